# Optimizing a Trainium2 kernel written in Bass

```python
import jax, jax.numpy as jnp
from jax import lax
import numpy as np

D_MODEL = 1024
BATCH = 16
SEQ = 4096
DEPTH = 1

CTX_LEN = 256
GRID_W = 64
HEAD_DIM = 64
ATTN_WIDTH = D_MODEL // 2
N_Q_HEADS = ATTN_WIDTH // HEAD_DIM
N_KV_HEADS = 2
Q_PER_KV = N_Q_HEADS // N_KV_HEADS
KV_WIDTH = N_KV_HEADS * HEAD_DIM
WINDOW = 128
BLOCK = 128
ROPE_BASE = 10000.0
LRU_WIDTH = D_MODEL - ATTN_WIDTH
LRU_BLOCKS = 8
LRU_BLOCK_DIM = LRU_WIDTH // LRU_BLOCKS
CONV_WIDTH = 4
LRU_C = 8.0
MIX_WIDTH = ATTN_WIDTH + LRU_WIDTH
IN_SPLITS = (ATTN_WIDTH, ATTN_WIDTH + KV_WIDTH, ATTN_WIDTH + 2 * KV_WIDTH,
             ATTN_WIDTH + 2 * KV_WIDTH + LRU_WIDTH)
IN_WIDTH = ATTN_WIDTH + 2 * KV_WIDTH + 2 * LRU_WIDTH
N_EXPERTS = 16
CAPACITY_FACTOR = 2
D_EXPERT = D_MODEL
EPS = 1e-6
NEG_INF = -1e30

kernel_name = "hybrid_attn_rglru_ec_moe_dit_layer"


def rmsnorm(x, g):
    xf = x.astype(jnp.float32)
    xf = xf * lax.rsqrt(jnp.mean(xf * xf, axis=-1, keepdims=True) + EPS)
    return (xf * g.astype(jnp.float32)).astype(x.dtype)


def modulate(h, shift, scale):
    return h * (1 + scale) + shift


def ada_params(cond, w_ada, b_ada):
    m = jax.nn.silu(cond) @ w_ada + b_ada
    return jnp.split(m[..., None, :], 6, axis=-1)


def rope_2d_tables(rows):
    row = jnp.broadcast_to(jnp.arange(rows)[:, None], (rows, GRID_W)).reshape(-1).astype(jnp.float32)
    col = jnp.broadcast_to(jnp.arange(GRID_W)[None, :], (rows, GRID_W)).reshape(-1).astype(jnp.float32)
    n_freq = HEAD_DIM // 4
    inv_freq = ROPE_BASE ** (-jnp.arange(n_freq, dtype=jnp.float32) / n_freq)
    ang = jnp.stack([row[:, None] * inv_freq, col[:, None] * inv_freq], axis=1)
    return jnp.cos(ang), jnp.sin(ang)


def apply_rope_2d(x, cos, sin):
    xs = x.astype(jnp.float32).reshape(x.shape[:-1] + (2, 2, HEAD_DIM // 4))
    x1, x2 = xs[..., 0, :], xs[..., 1, :]
    cs, sn = cos[None, :, None], sin[None, :, None]
    out = jnp.stack([x1 * cs - x2 * sn, x2 * cs + x1 * sn], axis=-2)
    return out.reshape(x.shape).astype(x.dtype)


def windowed_attention_with_context(q, k, v, k_c, v_c, sink):
    bsz, n = q.shape[0], q.shape[1]
    n_ctx = k_c.shape[1]
    nb = n // BLOCK
    scale = HEAD_DIM ** -0.5
    qb = q.reshape(bsz, nb, BLOCK, N_KV_HEADS, Q_PER_KV, HEAD_DIM).transpose(1, 0, 3, 4, 2, 5)
    pad = ((0, 0), (BLOCK, BLOCK), (0, 0), (0, 0))
    k_pad = jnp.pad(k, pad).transpose(0, 2, 1, 3)
    v_pad = jnp.pad(v, pad).transpose(0, 2, 1, 3)
    kc = k_c.transpose(0, 2, 1, 3)
    vc = v_c.transpose(0, 2, 1, 3)
    sink_l = sink.astype(jnp.float32).reshape(N_KV_HEADS, Q_PER_KV)[None, :, :, None, None]
    offs_q = jnp.arange(BLOCK)
    offs_k = jnp.arange(3 * BLOCK) - BLOCK

    def one_block(args):
        i, q_i = args
        start = i * BLOCK
        k_i = lax.dynamic_slice_in_dim(k_pad, start, 3 * BLOCK, axis=2)
        v_i = lax.dynamic_slice_in_dim(v_pad, start, 3 * BLOCK, axis=2)
        q_pos = start + offs_q
        k_pos = start + offs_k
        valid = ((jnp.abs(q_pos[:, None] - k_pos[None, :]) <= WINDOW)
                 & (k_pos >= 0)[None, :] & (k_pos < n)[None, :])
        s_loc = jnp.einsum('bhgqd,bhkd->bhgqk', q_i, k_i).astype(jnp.float32) * scale
        s_loc = jnp.where(valid, s_loc, NEG_INF)
        s_ctx = jnp.einsum('bhgqd,bhkd->bhgqk', q_i, kc).astype(jnp.float32) * scale
        sink_b = jnp.broadcast_to(sink_l, s_ctx.shape[:-1] + (1,))
        p = jax.nn.softmax(jnp.concatenate([s_loc, s_ctx, sink_b], axis=-1), axis=-1)
        p_loc = p[..., :3 * BLOCK].astype(v.dtype)
        p_ctx = p[..., 3 * BLOCK:3 * BLOCK + n_ctx].astype(v.dtype)
        return (jnp.einsum('bhgqk,bhkd->bhgqd', p_loc, v_i)
                + jnp.einsum('bhgqk,bhkd->bhgqd', p_ctx, vc))

    out = lax.map(one_block, (jnp.arange(nb), qb))
    return out.transpose(1, 0, 4, 2, 3, 5).reshape(bsz, n, N_Q_HEADS * HEAD_DIM)


def context_attention(q_c, k_c, v_c, sink):
    bsz, n_ctx = q_c.shape[0], q_c.shape[1]
    qg = q_c.reshape(bsz, n_ctx, N_KV_HEADS, Q_PER_KV, HEAD_DIM)
    s = jnp.einsum('bqhgd,bkhd->bhgqk', qg, k_c).astype(jnp.float32) * (HEAD_DIM ** -0.5)
    sink_b = jnp.broadcast_to(sink.astype(jnp.float32).reshape(N_KV_HEADS, Q_PER_KV)[None, :, :, None, None],
                              s.shape[:-1] + (1,))
    p = jax.nn.softmax(jnp.concatenate([s, sink_b], axis=-1), axis=-1)[..., :n_ctx].astype(v_c.dtype)
    o = jnp.einsum('bhgqk,bkhd->bqhgd', p, v_c)
    return o.reshape(bsz, n_ctx, N_Q_HEADS * HEAD_DIM)


def centred_depthwise_conv(x, w, b):
    n = x.shape[1]
    left = CONV_WIDTH // 2
    right = CONV_WIDTH - 1 - left
    xp = jnp.pad(x, ((0, 0), (left, right), (0, 0)))
    y = b + w[0] * xp[:, 0:n]
    for j in range(1, CONV_WIDTH):
        y = y + w[j] * xp[:, j:j + n]
    return y


def block_diag_linear(x, w, b):
    xb = x.reshape(x.shape[:-1] + (LRU_BLOCKS, LRU_BLOCK_DIM))
    return jnp.einsum('blnd,nde->blne', xb, w).reshape(x.shape) + b


def lru_coeffs(xc, w_r, b_r, w_i, b_i, lam):
    r = jax.nn.sigmoid(block_diag_linear(xc, w_r, b_r).astype(jnp.float32))
    i = jax.nn.sigmoid(block_diag_linear(xc, w_i, b_i).astype(jnp.float32))
    log_a = -LRU_C * r * jax.nn.softplus(-lam.astype(jnp.float32))
    a = jnp.exp(log_a)
    mult = jnp.sqrt(-jnp.expm1(2.0 * log_a))
    return a, mult * i * xc.astype(jnp.float32)


def _combine(left, right):
    a_l, b_l = left
    a_r, b_r = right
    return a_l * a_r, a_r * b_l + b_r


def linear_scan(a, b, h0, reverse):
    if reverse:
        a, b = jnp.flip(a, axis=1), jnp.flip(b, axis=1)
    a_cum, h = lax.associative_scan(_combine, (a, b), axis=1)
    h = h + a_cum * h0[:, None, :]
    if reverse:
        h = jnp.flip(h, axis=1)
    return h


def bidirectional_rglru(xr_lat, xr_ctx, conv_w, conv_b, w_r, b_r, w_i, b_i, lam):
    xc_lat = centred_depthwise_conv(xr_lat, conv_w, conv_b)
    xc_ctx = centred_depthwise_conv(xr_ctx, conv_w, conv_b)
    h_lat_sum, h_ctx_sum = None, None
    for d in range(2):
        reverse = d == 1
        a_c, b_c = lru_coeffs(xc_ctx, w_r[d], b_r[d], w_i[d], b_i[d], lam[d])
        h_c = linear_scan(a_c, b_c, jnp.zeros_like(a_c[:, 0]), reverse)
        h0 = h_c[:, 0] if reverse else h_c[:, -1]
        a_l, b_l = lru_coeffs(xc_lat, w_r[d], b_r[d], w_i[d], b_i[d], lam[d])
        h_l = linear_scan(a_l, b_l, h0, reverse)
        h_lat_sum = h_l if h_lat_sum is None else h_lat_sum + h_l
        h_ctx_sum = h_c if h_ctx_sum is None else h_ctx_sum + h_c
    return h_lat_sum.astype(xr_lat.dtype), h_ctx_sum.astype(xr_ctx.dtype)


def expert_choice_moe(h, w_router, w_gate, w_up, w_down):
    bsz, n, dm = h.shape
    cap = CAPACITY_FACTOR * n // N_EXPERTS
    affinity = jax.nn.softmax((h @ w_router).astype(jnp.float32), axis=-1)
    gate_vals, idx = lax.top_k(affinity.transpose(0, 2, 1), cap)
    idx_flat = idx.reshape(bsz, N_EXPERTS * cap)
    xs = jnp.take_along_axis(h, idx_flat[..., None], axis=1).reshape(bsz, N_EXPERTS, cap, dm)
    hid = (jax.nn.silu(jnp.einsum('becd,edf->becf', xs, w_gate))
           * jnp.einsum('becd,edf->becf', xs, w_up))
    ys = jnp.einsum('becf,efd->becd', hid, w_down) * gate_vals[..., None].astype(h.dtype)
    out = jnp.zeros_like(h).at[jnp.arange(bsz)[:, None], idx_flat].add(
        ys.reshape(bsz, N_EXPERTS * cap, dm))
    return out


def setup_inputs(seed: int = 0) -> dict:
    key = jax.random.key(seed)
    ks = jax.random.split(key, 24)
    f32 = jnp.float32
    nrm = lambda k, shape, s: jax.random.normal(k, shape, f32) * s
    u = jax.random.uniform(ks[16], (DEPTH, 2, LRU_WIDTH), f32, 0.9, 0.999)
    a0 = u ** (1.0 / LRU_C)
    lru_lambda = jnp.log(a0) - jnp.log1p(-a0)
    return {
        "x": nrm(ks[0], (BATCH, SEQ, D_MODEL), 1.0),
        "c": nrm(ks[1], (BATCH, D_MODEL), 1.0),
        "ctx": nrm(ks[2], (BATCH, CTX_LEN, D_MODEL), 1.0),
        "c_ctx": nrm(ks[3], (D_MODEL,), 1.0),
        "w_ada": nrm(ks[4], (DEPTH, D_MODEL, 6 * D_MODEL), 0.5 * D_MODEL ** -0.5),
        "b_ada": nrm(ks[5], (DEPTH, 6 * D_MODEL), 0.02),
        "norm1_g": 1.0 + nrm(ks[6], (DEPTH, D_MODEL), 0.05),
        "norm2_g": 1.0 + nrm(ks[7], (DEPTH, D_MODEL), 0.05),
        "w_in": nrm(ks[8], (DEPTH, D_MODEL, IN_WIDTH), D_MODEL ** -0.5),
        "q_norm_g": 1.0 + nrm(ks[9], (DEPTH, HEAD_DIM), 0.05),
        "k_norm_g": 1.0 + nrm(ks[10], (DEPTH, HEAD_DIM), 0.05),
        "attn_sink": nrm(ks[11], (DEPTH, N_Q_HEADS), 0.5),
        "conv_w": nrm(ks[12], (DEPTH, CONV_WIDTH, LRU_WIDTH), CONV_WIDTH ** -0.5),
        "conv_b": nrm(ks[13], (DEPTH, LRU_WIDTH), 0.02),
        "lru_w_r": nrm(ks[14], (DEPTH, 2, LRU_BLOCKS, LRU_BLOCK_DIM, LRU_BLOCK_DIM), LRU_BLOCK_DIM ** -0.5),
        "lru_b_r": nrm(ks[15], (DEPTH, 2, LRU_WIDTH), 0.02),
        "lru_w_i": nrm(ks[17], (DEPTH, 2, LRU_BLOCKS, LRU_BLOCK_DIM, LRU_BLOCK_DIM), LRU_BLOCK_DIM ** -0.5),
        "lru_b_i": nrm(ks[18], (DEPTH, 2, LRU_WIDTH), 0.02),
        "lru_lambda": lru_lambda,
        "w_out": nrm(ks[19], (DEPTH, MIX_WIDTH, D_MODEL), MIX_WIDTH ** -0.5),
        "w_router": nrm(ks[20], (DEPTH, D_MODEL, N_EXPERTS), D_MODEL ** -0.5),
        "w_gate": nrm(ks[21], (DEPTH, N_EXPERTS, D_MODEL, D_EXPERT), D_MODEL ** -0.5),
        "w_up": nrm(ks[22], (DEPTH, N_EXPERTS, D_MODEL, D_EXPERT), D_MODEL ** -0.5),
        "w_down": nrm(ks[23], (DEPTH, N_EXPERTS, D_EXPERT, D_MODEL), D_EXPERT ** -0.5),
    }


def reference(x, c, ctx, c_ctx, w_ada, b_ada, norm1_g, norm2_g, w_in, q_norm_g, k_norm_g,
              attn_sink, conv_w, conv_b, lru_w_r, lru_b_r, lru_w_i, lru_b_i, lru_lambda,
              w_out, w_router, w_gate, w_up, w_down):
    bsz, n, _ = x.shape
    n_ctx = ctx.shape[1]
    rows = n // GRID_W
    cos, sin = rope_2d_tables(rows)
    for layer in range(DEPTH):
        sh1, sc1, g1, sh2, sc2, g2 = ada_params(c, w_ada[layer], b_ada[layer])
        csh1, csc1, cg1, csh2, csc2, cg2 = ada_params(c_ctx, w_ada[layer], b_ada[layer])

        h = modulate(rmsnorm(x, norm1_g[layer]), sh1, sc1)
        hc = modulate(rmsnorm(ctx, norm1_g[layer]), csh1, csc1)
        q, k, v, xr, gr = jnp.split(h @ w_in[layer], IN_SPLITS, axis=-1)
        q_c, k_c, v_c, xr_c, gr_c = jnp.split(hc @ w_in[layer], IN_SPLITS, axis=-1)

        q = apply_rope_2d(rmsnorm(q.reshape(bsz, n, N_Q_HEADS, HEAD_DIM), q_norm_g[layer]), cos, sin)
        k = apply_rope_2d(rmsnorm(k.reshape(bsz, n, N_KV_HEADS, HEAD_DIM), k_norm_g[layer]), cos, sin)
        v = v.reshape(bsz, n, N_KV_HEADS, HEAD_DIM)
        q_c = rmsnorm(q_c.reshape(bsz, n_ctx, N_Q_HEADS, HEAD_DIM), q_norm_g[layer])
        k_c = rmsnorm(k_c.reshape(bsz, n_ctx, N_KV_HEADS, HEAD_DIM), k_norm_g[layer])
        v_c = v_c.reshape(bsz, n_ctx, N_KV_HEADS, HEAD_DIM)

        attn_lat = windowed_attention_with_context(q, k, v, k_c, v_c, attn_sink[layer])
        rnn_lat, rnn_ctx = bidirectional_rglru(xr, xr_c, conv_w[layer], conv_b[layer],
                                               lru_w_r[layer], lru_b_r[layer], lru_w_i[layer],
                                               lru_b_i[layer], lru_lambda[layer])
        y = jnp.concatenate([attn_lat, rnn_lat * jax.nn.gelu(gr)], axis=-1) @ w_out[layer]
        x = x + g1 * y

        h2 = modulate(rmsnorm(x, norm2_g[layer]), sh2, sc2)
        x = x + g2 * expert_choice_moe(h2, w_router[layer], w_gate[layer], w_up[layer], w_down[layer])

        if layer + 1 < DEPTH:
            attn_ctx = context_attention(q_c, k_c, v_c, attn_sink[layer])
            yc = jnp.concatenate([attn_ctx, rnn_ctx * jax.nn.gelu(gr_c)], axis=-1) @ w_out[layer]
            ctx = ctx + cg1 * yc
            hc2 = modulate(rmsnorm(ctx, norm2_g[layer]), csh2, csc2)
            ctx = ctx + cg2 * expert_choice_moe(hc2, w_router[layer], w_gate[layer], w_up[layer], w_down[layer])
    return x
```

```python
import numpy as np
import ml_dtypes
import concourse.bass as bass
import concourse.mybir as mybir
from concourse.bass_utils import run_bass_kernel_spmd
from contextlib import ExitStack

AF = mybir.ActivationFunctionType
ALU = mybir.AluOpType
AX = mybir.AxisListType
F32 = mybir.dt.float32
BF16 = mybir.dt.bfloat16
I32 = mybir.dt.int32

NCORES = 8
SEQ = 4096
NCTX = 256
NTOK = SEQ + NCTX
D = 1024
EPS = 1e-6
NEXP = 16
CAP = 512


class Op:
    __slots__ = ("eng", "fn", "deps", "dma", "key", "signal", "ticket")

    def __init__(self, eng, fn, dma, key):
        self.eng = eng
        self.fn = fn
        self.dma = dma
        self.key = key
        self.deps = []
        self.signal = False
        self.ticket = 0


ENGS = ["pe", "act", "dve", "pool", "sp"]


def C(name, *a, **k):
    return (name, a, k)


class Prog:
    def __init__(self, nc):
        self.nc = nc
        self.ops = []
        self.last_w = {}
        self.readers = {}
        self.dma_last = {}
        self.last_eng = {}
        self.bank_last = {}
        self.stack = ExitStack()

    def sb(self, name, shape, dtype):
        return self.stack.enter_context(self.nc.sbuf_tensor("sb_" + name, list(shape), dtype))

    def ps(self, name, shape, dtype):
        return self.stack.enter_context(self.nc.psum_tensor("ps_" + name, list(shape), dtype))

    dead = False

    def cut(self, n):
        import os
        if int(os.environ.get("CUT", "-1")) == n:
            self.dead = True

    def add(self, eng, fn, r=(), w=(), key=None):
        op = Op(eng, fn, key is not None, key)
        if self.dead:
            return op
        deps = {}
        for k in r:
            o = self.last_w.get(k)
            if o is not None:
                deps[id(o)] = o
        for k in w:
            o = self.last_w.get(k)
            if o is not None:
                deps[id(o)] = o
            for o in self.readers.get(k, {}).values():
                deps[id(o)] = o
        if op.dma:
            o = self.dma_last.get(key)
            if o is not None:
                deps[id(o)] = o
            self.dma_last[key] = op
        nm_, a_, k_ = fn
        for v in list(a_) + list(k_.values()):
            tn = getattr(getattr(v, "tensor", None), "name", "")
            if tn.startswith("ps_bank"):
                o = self.bank_last.get(tn)
                if o is not None and o.eng != eng:
                    deps[id(o)] = o
                self.bank_last[tn] = op
        for k in r:
            self.readers.setdefault(k, {})[(eng, key)] = op
        for k in w:
            self.last_w[k] = op
            self.readers[k] = {}
        for o in deps.values():
            if o is op:
                continue
            if (not o.dma) and (not op.dma) and o.eng == "pe" and op.eng == "pe":
                continue
            op.deps.append(o)
            o.signal = True
        if not op.dma:
            self.last_eng[eng] = op
        self.ops.append(op)
        return op

    def barrier(self):
        prev = [o for o in self.last_eng.values()] + list(self.dma_last.values())
        new = []
        for e in ENGS:
            op = Op(e, C("nop", ), False, None)
            for o in prev:
                if o.eng == e and not o.dma and e == "pe":
                    continue
                op.deps.append(o)
                o.signal = True
            self.ops.append(op)
            new.append(op)
        for op in new:
            self.last_eng[op.eng] = op
        self.last_w = {}
        self.readers = {}

    def emit(self):
        nc = self.nc
        cnt = {e: 0 for e in ENGS}
        dcnt = {}
        for op in self.ops:
            if op.dma:
                dcnt[op.key] = dcnt.get(op.key, 0) + 16
                op.ticket = dcnt[op.key]
            elif op.signal:
                cnt[op.eng] += 1
                op.ticket = cnt[op.eng]
        sems = {}
        for e in ENGS:
            if cnt[e]:
                sems[("c", e)] = self.stack.enter_context(nc.semaphore("s_" + e))
        for k in dcnt:
            sems[("d", k)] = self.stack.enter_context(nc.semaphore("d_" + str(k)))

        def semof(o):
            return sems[("d", o.key)] if o.dma else sems[("c", o.eng)]

        per = {e: [] for e in ENGS}
        for op in self.ops:
            per[op.eng].append(op)

        def run(name, eng):
            seen = {}
            for op in per[name]:
                need = {}
                for d in op.deps:
                    s = semof(d)
                    if need.get(id(s), (None, 0))[1] < d.ticket:
                        need[id(s)] = (s, d.ticket)
                for kk, (s, t) in need.items():
                    if seen.get(kk, 0) >= t:
                        continue
                    eng.wait_ge(s, t)
                    seen[kk] = t
                nm, a_, k_ = op.fn
                try:
                    ins = getattr(eng, nm)(*a_, **k_)
                except Exception:
                    print("EMIT FAIL at", per[name].index(op), "of", len(per[name]), "ndma_before", sum(1 for o in per[name][:per[name].index(op)] if o.dma), flush=True)
                    print("EMIT FAIL", name, nm, [repr(x)[:300] for x in a_], {kk: repr(vv)[:300] for kk, vv in k_.items()}, flush=True)
                    raise
                if op.dma:
                    ins.then_inc(semof(op), 16)
                elif op.signal:
                    ins.then_inc(semof(op), 1)

        with nc.Block() as block:
            @block.tensor
            def _(e):
                run("pe", e)

            @block.scalar
            def _(e):
                run("act", e)

            @block.vector
            def _(e):
                run("dve", e)

            @block.gpsimd
            def _(e):
                run("pool", e)

            @block.sync
            def _(e):
                run("sp", e)
        self.stats = dict(n={e: len(per[e]) for e in ENGS}, sig=cnt, nsems=len(sems))

    def close(self):
        self.stack.close()


class Arena:
    def __init__(self, P, name, nbytes):
        self.t = P.sb(name, [128, nbytes // 2], BF16)
        self.cap = nbytes
        self.off = 0

    def alloc(self, shape, dtype):
        esz = 2 if dtype == BF16 else 4
        n = int(np.prod(shape))
        nb = n * esz
        start = self.off
        self.off += (nb + 31) // 32 * 32
        assert self.off <= self.cap, ("arena overflow", self.off, self.cap)
        ap = self.t[:, start // 2:start // 2 + nb // 2]
        if dtype != BF16:
            ap = ap.bitcast(dtype)
        if len(shape) == 2:
            ap = ap.rearrange("p (a b) -> p a b", a=shape[0])
        elif len(shape) == 3:
            ap = ap.rearrange("p (a b c) -> p a b c", a=shape[0], b=shape[1])
        elif len(shape) == 4:
            ap = ap.rearrange("p (a b c d) -> p a b c d", a=shape[0], b=shape[1], c=shape[2])
        return ap

    def reset(self, off=0):
        self.off = off


V_N1G, V_N2G, V_BADA, V_GQ, V_GK, V_CONVW, V_CONVB, V_BR, V_BI, V_LAM = 0, 8, 16, 64, 65, 66, 82, 86, 94, 102
V_SINK, V_GQROW, V_GKROW, NV = 110, 118, 182, 246
CM_ID, CM_ONES, CM_SWAP, CM_IOP, CM_IOC, NCM = 0, 1, 2, 3, 4, 5


def build_program(stage="full"):
    nc = bass.Bass("TRN2", target_bir_lowering=False)

    def din(name, shape, dt=F32):
        return nc.dram_tensor(name, list(shape), dt, kind="ExternalInput").ap()

    x_d = din("x", [2 * SEQ, D])
    ctx_d = din("ctx", [2 * NCTX, D])
    ccT_d = din("ccT", [128, 8, 3])
    wada_d = din("w_ada", [D, 6 * D])
    badag_d = din("badag", [3, 2, D])
    vecs_d = din("vecs", [128, NV])
    cm_d = din("cm", [128, NCM, 128], BF16)
    identf_d = din("identf", [128, 128])
    masks_d = din("masks", [128, 2, 512], BF16)
    rope_d = din("rope", [128, 2, SEQ], BF16)
    tcol_d = din("tcol", [128, 32, 2])
    win_d = din("w_in", [D, 1792])
    lruw_d = din("lruw", [128, 16, 128])
    wout_d = din("w_out", [D, D])
    wr_d = din("w_router", [128, 8, 16])
    wg_d = din("w_gate", [NEXP, D, D])
    wu_d = din("w_up", [NEXP, D, D])
    wd_d = din("w_down", [NEXP, D, D])
    out_d = nc.dram_tensor("out", [2 * SEQ, D], F32, kind="ExternalOutput").ap()
    xn_d = nc.dram_tensor("xn_scr", [2 * SEQ, 1056], BF16, kind="Internal").ap()
    gsc_d = nc.dram_tensor("g_scr", [3, 2, D], F32, kind="Internal").ap()
    dbg = {}
    if stage != "full":
        dbg["qT"] = nc.dram_tensor("dbg_qT", [128, 4, SEQ], BF16, kind="ExternalOutput").ap()
        dbg["kT"] = nc.dram_tensor("dbg_kT", [128, NTOK], BF16, kind="ExternalOutput").ap()
        dbg["v"] = nc.dram_tensor("dbg_v", [128, 34, 2, 65], BF16, kind="ExternalOutput").ap()
        dbg["xr"] = nc.dram_tensor("dbg_xr", [128, 4, NTOK], BF16, kind="ExternalOutput").ap()
        dbg["gr"] = nc.dram_tensor("dbg_gr", [128, 4, SEQ], BF16, kind="ExternalOutput").ap()
        dbg["modv"] = nc.dram_tensor("dbg_modv", [128, 48, 3], F32, kind="ExternalOutput").ap()
        dbg["aall"] = nc.dram_tensor("dbg_aall", [128, 32, 32], F32, kind="ExternalOutput").ap()
        dbg["idx"] = nc.dram_tensor("dbg_idx", [128, 32, 4], I32, kind="ExternalOutput").ap()

    P = Prog(nc)
    add = P.add

    vecs = P.sb("vecs", [128, NV], F32)
    cm = P.sb("cm", [128, NCM, 128], BF16)
    identf = P.sb("identf", [128, 128], F32)
    masks = P.sb("masks", [128, 2, 512], BF16)
    lruW = P.sb("lruW", [128, 16, 128], BF16)
    wr = P.sb("wr", [128, 8, 16], BF16)
    tcol = P.sb("tcol", [128, 32, 2], F32)
    modv = P.sb("modv", [128, 48, 3], F32)
    A1 = P.sb("A1", [128, 8, 3], F32)
    A2 = P.sb("A2", [128, 8, 3], F32)
    lv = P.sb("lv", [128, 40], F32)
    av = P.sb("av", [128, 16], F32)
    Aall = P.sb("Aall", [128, 32, 32], F32)
    IDX = P.sb("IDX", [128, 32, 4], I32)
    ident = cm[:, CM_ID, :]
    onesblk = cm[:, CM_ONES, :]
    pswap = cm[:, CM_SWAP, :]
    B1 = modv[:, 0:8, :]
    B2 = modv[:, 24:32, :]
    HBR, HBI, CNEG, HCNEG = 0, 8, 16, 24
    negB = av[:, 0:1]
    esink = av[:, 1:9]

    banks = [P.ps("bank%d" % i, [128, 512], F32) for i in range(8)]
    ST = Arena(P, "ST", 116 * 1024)
    WK = Arena(P, "WK", 77 * 1024)

    def ld(eng, out, in_, w, key):
        return add(eng, C("dma_start", out=out, in_=in_), w=w, key=key)

    ld("sp", vecs[:], vecs_d[:, :], ["vecs"], "c0")
    ld("sp", cm[:], cm_d[:, :, :], ["cm"], "c1")
    ld("sp", identf[:], identf_d[:, :], ["identf"], "c2")
    ld("sp", masks[:], masks_d[:, :, :], ["masks"], "c3")
    ld("sp", tcol[:], tcol_d[:, :, :], ["tcol"], "c0")
    ld("pool", lruW[:], lruw_d[:, :, :], ["lruW"], "c4")
    ld("pool", wr[:], wr_d[:, :, :], ["wr"], "c5")
    ccT = WK.alloc([8, 3], F32)
    scT = WK.alloc([8, 3], F32)
    badag = WK.alloc([2, D], F32)
    grow = WK.alloc([2, D], F32)
    wa = [WK.alloc([8, 512], F32) for _ in range(2)]
    tmpv = WK.alloc([64], F32)
    ld("sp", ccT, ccT_d[:, :, :], ["ccT"], "c2")
    ld("sp", badag[0:3], badag_d[:, :, :], ["badag"], "c3")
    add("act", C("activation", out=scT, in_=ccT, func=AF.Silu), r=["ccT"], w=["scT"])
    pm = banks[0][:, 0:192].rearrange("p (a b) -> p a b", a=48)
    prow = banks[1]
    wada_v = wada_d.rearrange("(kc p) n -> p kc n", p=128)
    P.cut(1)
    pc = 0
    for m in range(6):
        for half in range(2):
            b = pc % 2
            pc += 1
            c0 = m * D + half * 512
            ld("sp", wa[b], wada_v[:, :, c0:c0 + 512], ["wa%d" % b], "wa%d" % b)
            for jj in range(4):
                j = m * 8 + half * 4 + jj
                for kc in range(8):
                    add("pe", C("matmul", pm[:, j, 0:3], lhsT=wa[b][:, kc, jj * 128:(jj + 1) * 128], rhs=scT[:, kc, :], start=(kc == 0), stop=(kc == 7)),
                        r=["wa%d" % b, "scT"], w=["pm"])
            if m in (2, 5):
                which = 0 if m == 2 else 1
                for kc in range(8):
                    add("pe", C("matmul", prow[0:3, :], lhsT=scT[:, kc, :], rhs=wa[b][:, kc, :], start=(kc == 0), stop=(kc == 7)),
                        r=["wa%d" % b, "scT"], w=["prow"])
                add("dve", C("tensor_tensor", out=grow[0:3, which, half * 512:(half + 1) * 512], in0=prow[0:3, :], in1=badag[0:3, which, half * 512:(half + 1) * 512], op=ALU.add),
                    r=["prow", "badag"], w=["grow"])
    P.cut(2)
    add("sp", C("dma_start", out=gsc_d[:, :, :], in_=grow[0:3]), r=["grow"], w=["gsc"], key="c1")
    P.cut(3)
    add("dve", C("tensor_tensor", out=modv[:], in0=pm[:, :, 0:3], in1=vecs[:, V_BADA:V_BADA + 48].unsqueeze(2).to_broadcast([128, 48, 3]), op=ALU.add),
        r=["pm", "vecs"], w=["modv"])
    add("dve", C("scalar_tensor_tensor", out=A1[:], in0=modv[:, 8:16, :], scalar=1.0, in1=vecs[:, V_N1G:V_N1G + 8].unsqueeze(2).to_broadcast([128, 8, 3]), op0=ALU.add, op1=ALU.mult),
        r=["modv", "vecs"], w=["A1"])
    add("dve", C("scalar_tensor_tensor", out=A2[:], in0=modv[:, 32:40, :], scalar=1.0, in1=vecs[:, V_N2G:V_N2G + 8].unsqueeze(2).to_broadcast([128, 8, 3]), op0=ALU.add, op1=ALU.mult),
        r=["modv", "vecs"], w=["A2"])
    P.cut(4)
    add("dve", C("tensor_scalar", out=lv[:, HBR:HBR + 16], in0=vecs[:, V_BR:V_BR + 16], scalar1=0.5, scalar2=None, op0=ALU.mult), r=["vecs"], w=["lv0"])
    add("act", C("activation", out=tmpv[:, 0:8], in_=vecs[:, V_LAM:V_LAM + 8], func=AF.Exp, scale=-1.0), r=["vecs"], w=["tmpv"])
    add("act", C("activation", out=tmpv[:, 8:16], in_=tmpv[:, 0:8], func=AF.Ln, bias=1.0), r=["tmpv"], w=["tmpv2"])
    add("dve", C("tensor_scalar", out=lv[:, CNEG:CNEG + 8], in0=tmpv[:, 8:16], scalar1=-8.0, scalar2=None, op0=ALU.mult), r=["tmpv2"], w=["lv1"])
    add("dve", C("tensor_scalar", out=lv[:, HCNEG:HCNEG + 8], in0=tmpv[:, 8:16], scalar1=-4.0, scalar2=None, op0=ALU.mult), r=["tmpv2"], w=["lv2"])
    P.cut(5)
    add("dve", C("tensor_reduce", out=tmpv[:, 16:17], in_=vecs[:, V_GQROW:V_GQROW + 64], axis=AX.X, op=ALU.max, apply_absolute_value=True), r=["vecs"], w=["tmpv3"])
    add("dve", C("tensor_reduce", out=tmpv[:, 17:18], in_=vecs[:, V_GKROW:V_GKROW + 64], axis=AX.X, op=ALU.max, apply_absolute_value=True), r=["vecs"], w=["tmpv4"])
    add("dve", C("scalar_tensor_tensor", out=av[:, 0:1], in0=tmpv[:, 16:17], scalar=-8.0, in1=tmpv[:, 17:18], op0=ALU.mult, op1=ALU.mult), r=["tmpv3", "tmpv4"], w=["av0"])
    add("act", C("activation", out=av[:, 1:9], in_=vecs[:, V_SINK:V_SINK + 8], func=AF.Exp, bias=av[:, 0:1]), r=["vecs", "av0"], w=["av1"])
    P.cut(6)
    if "modv" in dbg:
        add("sp", C("dma_start", out=dbg["modv"][:, :, :], in_=modv[:]), r=["modv"], key="dbg0")
    P.barrier()
    P.cut(7)

    ST.reset()
    qT = ST.alloc([4, SEQ], BF16)
    kT = ST.alloc([NTOK], BF16)
    vaug = ST.alloc([34, 2, 65], BF16)
    xrS = ST.alloc([4, NTOK], BF16)
    grS = ST.alloc([4, SEQ], BF16)
    add("pool", C("memset", vaug[:, :, :, 64:65], 1.0), w=["vones"])
    P.cut(8)
    P.barrier()
    P.cut(9)

    win_v = win_d.rearrange("(kc p) n -> p kc n", p=128)
    wout_v = wout_d.rearrange("(kc p) n -> p kc n", p=128)

    def phase_A(s):
        WK.reset()
        win = WK.alloc([8, 1792], BF16)
        ropeb = [WK.alloc([2, 512], BF16) for _ in range(2)]
        xt = [WK.alloc([2, D], F32) for _ in range(2)]
        junk = WK.alloc([D], BF16)
        xn = WK.alloc([4, D], BF16)
        hT = [WK.alloc([8, 512], BF16)] * 2
        tq = [[WK.alloc([512], BF16) for _ in range(4)] for _ in range(2)]
        rstd = [WK.alloc([512], F32)] * 2
        ssb = WK.alloc([8], F32)
        for h in range(2):
            add("pool", C("dma_start", out=win[:, h * 4:(h + 1) * 4, :], in_=win_v[:, h * 4:(h + 1) * 4, :]), w=["win%d" % h], key="win%d" % h)
        pT = [banks[i].bitcast(BF16).rearrange("p (a b) -> p a b", a=2) for i in range(4)]
        pA = [banks[4], banks[5]]
        pS = banks[6]
        pR = banks[7]
        st = dict(xh=0, grp=0, acc=0, tq=0)

        def group(rows_ap, ntiles, is_ctx, col0, tile0, lat0):
            ntok = ntiles * 128
            scol = 2 if is_ctx else s
            gi = st["grp"]
            st["grp"] += 1
            hb = gi % 2
            hbk = 0
            if not is_ctx:
                add("sp", C("dma_start", out=ropeb[hb], in_=rope_d[:, :, lat0:lat0 + 512]), w=["rope%d" % hb], key="rope%d" % hb)
            for half in range(ntiles // 2):
                xb = st["xh"] % 2
                st["xh"] += 1
                src = rows_ap[half * 256:(half + 1) * 256, :].rearrange("(j p) n -> p j n", p=128)
                add("sp", C("dma_start", out=xt[xb], in_=src), w=["xt%d" % xb], key="xt%d" % xb)
                for j in range(2):
                    add("act", C("activation", out=junk, in_=xt[xb][:, j, :], func=AF.Square, accum_out=ssb[:, xb * 2 + j:xb * 2 + j + 1]),
                        r=["xt%d" % xb], w=["junk", "ms%d" % xb])
                sl = ssb[:, xb * 2:xb * 2 + 2]
                add("dve", C("tensor_scalar", out=sl, in0=sl, scalar1=1.0 / D, scalar2=EPS, op0=ALU.mult, op1=ALU.add), r=["ms%d" % xb], w=["ms%d" % xb])
                add("act", C("activation", out=sl, in_=sl, func=AF.Sqrt), r=["ms%d" % xb], w=["ms%d" % xb])
                add("dve", C("reciprocal", out=sl, in_=sl), r=["ms%d" % xb], w=["ms%d" % xb])
                for j in range(2):
                    t = half * 2 + j
                    add("dve", C("tensor_scalar", out=xn[:, t, :], in0=xt[xb][:, j, :], scalar1=ssb[:, xb * 2 + j:xb * 2 + j + 1], scalar2=None, op0=ALU.mult),
                        r=["xt%d" % xb, "ms%d" % xb], w=["xn%d" % t])
            P.cut(10)
            for fc in range(8):
                for t in range(ntiles):
                    add("pe", C("transpose", out=pT[fc % 4][:, fc // 4, t * 128:(t + 1) * 128], in_=xn[:, t, fc * 128:(fc + 1) * 128], identity=ident),
                        r=["xn%d" % t, "cm"], w=["pT%d" % fc])
                eng = "act" if (fc % 4) % 2 == 0 else "dve"
                import os
                _f = os.environ.get("DBGF", "")
                if _f == "noevac":
                    continue
                if _f == "actonly":
                    eng = "act"
                if _f == "dveonly":
                    eng = "dve"
                if eng == "act":
                    add("act", C("activation", out=hT[hb][:, fc, 0:ntok], in_=pT[fc % 4][:, fc // 4, 0:ntok], func=AF.Identity, scale=A1[:, fc, scol:scol + 1], bias=B1[:, fc, scol:scol + 1]),
                        r=["pT%d" % fc, "A1", "modv"], w=["hT%d_%d" % (hbk, fc)])
                else:
                    add("dve", C("tensor_scalar", out=hT[hb][:, fc, 0:ntok], in0=pT[fc % 4][:, fc // 4, 0:ntok], scalar1=A1[:, fc, scol:scol + 1], scalar2=B1[:, fc, scol:scol + 1], op0=ALU.mult, op1=ALU.add),
                        r=["pT%d" % fc, "A1", "modv"], w=["hT%d_%d" % (hbk, fc)])
            hkeys = ["hT%d_%d" % (hbk, fc) for fc in range(8)]
            P.cut(11)
            blocks = [("k", 512), ("v", 640)]
            if not is_ctx:
                blocks += [("q%d" % b, b * 128) for b in range(4)]
            blocks += [("x%d" % c, 768 + c * 128) for c in range(4)]
            if not is_ctx:
                blocks += [("g%d" % c, 1280 + c * 128) for c in range(4)]
            for bi_, (name, c0) in enumerate(blocks):
                P.cut(12 + bi_ + 20 * gi)
                ab = st["acc"] % 2
                st["acc"] += 1
                acc = pA[ab]
                akey = "pA%d" % ab
                if name == "v":
                    accv = acc.rearrange("p (t c) -> p t c", t=4)
                    for t in range(ntiles):
                        for kc in range(8):
                            add("pe", C("matmul", accv[:, t, :], lhsT=hT[hb][:, kc, t * 128:(t + 1) * 128], rhs=win[:, kc, 640:768], start=(kc == 0), stop=(kc == 7)),
                                r=hkeys + ["win0", "win1"], w=[akey])
                    add("dve", C("tensor_copy", out=vaug[:, tile0:tile0 + ntiles, :, 0:64], in_=accv[:, 0:ntiles, :].rearrange("p t (h d) -> p t h d", h=2)),
                        r=[akey], w=["v%d" % (tile0 + t) for t in range(ntiles)])
                    continue
                for kc in range(8):
                    add("pe", C("matmul", acc[:, 0:ntok], lhsT=win[:, kc, c0:c0 + 128], rhs=hT[hb][:, kc, 0:ntok], start=(kc == 0), stop=(kc == 7)),
                        r=hkeys + ["win0", "win1"], w=[akey])
                if name[0] == "x":
                    c = int(name[1])
                    add("act", C("copy", out=xrS[:, c, col0:col0 + ntok], in_=acc[:, 0:ntok]), r=[akey], w=["xr%d" % c])
                    continue
                if name[0] == "g":
                    c = int(name[1])
                    add("dve", C("tensor_copy", out=grS[:, c, lat0:lat0 + ntok], in_=acc[:, 0:ntok]), r=[akey], w=["gr%d" % c])
                    continue
                tb = st["tq"] % 2
                st["tq"] += 1
                sq, qg, qc, qs = tq[tb]
                rs = rstd[tb]
                gcol = vecs[:, V_GK:V_GK + 1] if name == "k" else vecs[:, V_GQ:V_GQ + 1]
                add("act", C("activation", out=sq[:, 0:ntok], in_=acc[:, 0:ntok], func=AF.Square), r=[akey], w=["sq%d" % tb])
                add("pe", C("matmul", pS[:, 0:ntok], lhsT=onesblk, rhs=sq[:, 0:ntok], start=True, stop=True), r=["sq%d" % tb, "cm"], w=["pS"])
                add("act", C("activation", out=rs[:, 0:ntok], in_=pS[:, 0:ntok], func=AF.Sqrt, scale=1.0 / 64, bias=EPS), r=["pS"], w=["rs0"])
                add("dve", C("reciprocal", out=rs[:, 0:ntok], in_=rs[:, 0:ntok]), r=["rs0"], w=["rs0"])
                if is_ctx:
                    add("dve", C("scalar_tensor_tensor", out=kT[:, col0:col0 + ntok], in0=acc[:, 0:ntok], scalar=gcol, in1=rs[:, 0:ntok], op0=ALU.mult, op1=ALU.mult),
                        r=[akey, "rs0", "vecs"], w=["kT%d" % (col0 // 128 + t) for t in range(ntiles)])
                    continue
                add("dve", C("scalar_tensor_tensor", out=qg[:, 0:ntok], in0=acc[:, 0:ntok], scalar=gcol, in1=rs[:, 0:ntok], op0=ALU.mult, op1=ALU.mult),
                    r=[akey, "rs0", "vecs"], w=["qg%d" % tb])
                add("pool", C("tensor_tensor", out=qc[:, 0:ntok], in0=qg[:, 0:ntok], in1=ropeb[hb][:, 0, 0:ntok], op=ALU.mult), r=["qg%d" % tb, "rope%d" % hb], w=["qc%d" % tb])
                add("pool", C("tensor_tensor", out=qs[:, 0:ntok], in0=qg[:, 0:ntok], in1=ropeb[hb][:, 1, 0:ntok], op=ALU.mult), r=["qg%d" % tb, "rope%d" % hb], w=["qs%d" % tb])
                add("pe", C("matmul", pR[:, 0:ntok], lhsT=ident, rhs=qc[:, 0:ntok], start=True, stop=False), r=["qc%d" % tb, "cm"], w=["pR"])
                add("pe", C("matmul", pR[:, 0:ntok], lhsT=pswap, rhs=qs[:, 0:ntok], start=False, stop=True), r=["qs%d" % tb, "cm"], w=["pR"])
                if name == "k":
                    add("act", C("copy", out=kT[:, col0:col0 + ntok], in_=pR[:, 0:ntok]), r=["pR"], w=["kT%d" % (col0 // 128 + t) for t in range(ntiles)])
                else:
                    b = int(name[1])
                    add("act", C("copy", out=qT[:, b, lat0:lat0 + ntok], in_=pR[:, 0:ntok]), r=["pR"], w=["qT%d" % (lat0 // 128 + t) for t in range(ntiles)])

        group(ctx_d[s * NCTX:(s + 1) * NCTX, :], 2, True, 0, 0, 0)
        for g in range(8):
            group(x_d[s * SEQ + g * 512:s * SEQ + (g + 1) * 512, :], 4, False, NCTX + g * 512, 2 + g * 4, g * 512)

    def phase_B(s):
        WK.reset()
        expS = [WK.alloc([512], BF16) for _ in range(6)]
        On = [WK.alloc([4, 2, 64], BF16) for _ in range(2)]
        den = [WK.alloc([4], F32) for _ in range(4)]
        pSc = banks[0:4]
        pO = [banks[4], banks[5]]
        pTr = [banks[6].bitcast(BF16)[:, 0:512].rearrange("p (g q) -> p g q", g=4), banks[7].bitcast(BF16)[:, 0:512].rearrange("p (g q) -> p g q", g=4)]
        ctr = dict(sc=0, ex=0, o=0)
        for i in range(32):
            ob = i % 2
            for kvh in range(2):
                chunks = [(0, None), (1, None)]
                if i > 0:
                    chunks.append((2 + i - 1, 0))
                chunks.append((2 + i, None))
                if i < 31:
                    chunks.append((2 + i + 1, 1))
                po = pO[ctr["o"] % 2]
                pok = "pO%d" % (ctr["o"] % 2)
                dn = den[ctr["o"] % 4]
                dnk = "den%d" % (ctr["o"] % 4)
                ctr["o"] += 1
                pov = po[:, 0:260].rearrange("p (g c) -> p g c", g=4)
                lo, hi = kvh * 64, (kvh + 1) * 64
                for ci, (kt, mk) in enumerate(chunks):
                    sb_ = ctr["sc"] % 4
                    ctr["sc"] += 1
                    eb = ctr["ex"] % 6
                    ctr["ex"] += 1
                    add("pe", C("matmul", pSc[sb_][:, :], lhsT=kT[lo:hi, kt * 128:(kt + 1) * 128], rhs=qT[lo:hi, :, i * 128:(i + 1) * 128], start=True, stop=True),
                        r=["kT%d" % kt, "qT%d" % i], w=["pSc%d" % sb_])
                    add("act", C("activation", out=expS[eb], in_=pSc[sb_][:, :], func=AF.Exp, scale=0.125, bias=negB), r=["pSc%d" % sb_, "av0"], w=["ex%d" % eb])
                    if mk is not None:
                        add("pool", C("tensor_tensor", out=expS[eb], in0=expS[eb], in1=masks[:, mk, :], op=ALU.mult), r=["ex%d" % eb, "masks"], w=["ex%d" % eb])
                    for g in range(4):
                        first = (ci == 0 and g == 0)
                        add("pe", C("matmul", pov[:, g, :], lhsT=expS[eb][:, g * 128:(g + 1) * 128], rhs=vaug[:, kt, kvh, :], start=first, stop=(ci == len(chunks) - 1), skip_group_check=True),
                            r=["ex%d" % eb, "v%d" % kt, "vones"], w=[pok])
                add("dve", C("tensor_tensor", out=dn, in0=pov[:, :, 64], in1=esink[:, kvh * 4:(kvh + 1) * 4], op=ALU.add), r=[pok, "av1"], w=[dnk])
                add("dve", C("reciprocal", out=dn, in_=dn), r=[dnk], w=[dnk])
                add("dve", C("tensor_tensor", out=On[ob][:, :, kvh, :], in0=pov[:, :, 0:64], in1=dn.unsqueeze(2).to_broadcast([128, 4, 64]), op=ALU.mult),
                    r=[pok, dnk], w=["On%d_%d" % (ob, kvh)])
            for g in range(4):
                add("pe", C("transpose", out=pTr[ob][:, g, :], in_=On[ob][:, g, :, :].rearrange("p h d -> p (h d)"), identity=ident),
                    r=["On%d_0" % ob, "On%d_1" % ob, "cm"], w=["pTr%d" % ob])
            add("act", C("copy", out=qT[:, :, i * 128:(i + 1) * 128], in_=pTr[ob]), r=["pTr%d" % ob], w=["qT%d" % i])

    def phase_C(s):
        WK.reset()
        xcb = WK.alloc([NTOK], BF16)
        abuf = WK.alloc([NTOK], F32)
        bbuf = [WK.alloc([NTOK], F32) for _ in range(2)]
        tbuf = WK.alloc([NTOK], BF16)
        trb = [WK.alloc([512], F32) for _ in range(2)]
        tib = [WK.alloc([512], F32) for _ in range(2)]
        ctr = dict(b=0)
        segs = [(0, NCTX), (NCTX, NTOK)]
        for c in range(4):
            xcf = bbuf[0]
            cw = lambda j: vecs[:, V_CONVW + c * 4 + j:V_CONVW + c * 4 + j + 1]
            for (lo, hi) in segs:
                add("dve", C("tensor_scalar", out=xcf[:, lo:hi], in0=xrS[:, c, lo:hi], scalar1=cw(2), scalar2=vecs[:, V_CONVB + c:V_CONVB + c + 1], op0=ALU.mult, op1=ALU.add),
                    r=["xr%d" % c, "vecs"], w=["xcf"])
                for j in (0, 1, 3):
                    o = j - 2
                    a0, a1 = max(lo, lo - o), min(hi, hi - o)
                    add("dve", C("scalar_tensor_tensor", out=xcf[:, a0:a1], in0=xrS[:, c, a0 + o:a1 + o], scalar=cw(j), in1=xcf[:, a0:a1], op0=ALU.mult, op1=ALU.add),
                        r=["xr%d" % c, "vecs", "xcf"], w=["xcf"])
            add("act", C("copy", out=xcb, in_=xcf), r=["xcf"], w=["xcb"])
            for d in range(2):
                bb = bbuf[d]
                bk = "b%d" % d
                wi_r = (d * 2 + 0) * 4 + c
                wi_i = (d * 2 + 1) * 4 + c
                col = d * 4 + c
                blocks = [(0, NCTX)] + [(NCTX + g * 512, NCTX + (g + 1) * 512) for g in range(8)]
                for (lo, hi) in blocks:
                    n = hi - lo
                    pb = ctr["b"] % 2
                    ctr["b"] += 1
                    pr, pi = banks[pb * 2], banks[pb * 2 + 1]
                    tr, ti = trb[pb], tib[pb]
                    add("pe", C("matmul", pr[:, 0:n], lhsT=lruW[:, wi_r, :], rhs=xcb[:, lo:hi], start=True, stop=True), r=["xcb", "lruW"], w=["pr%d" % pb])
                    add("pe", C("matmul", pi[:, 0:n], lhsT=lruW[:, wi_i, :], rhs=xcb[:, lo:hi], start=True, stop=True), r=["xcb", "lruW"], w=["pi%d" % pb])
                    add("act", C("activation", out=tr[:, 0:n], in_=pr[:, 0:n], func=AF.Tanh, scale=0.5, bias=lv[:, HBR + col:HBR + col + 1]), r=["pr%d" % pb, "lv0"], w=["tr%d" % pb])
                    add("act", C("activation", out=ti[:, 0:n], in_=pi[:, 0:n], func=AF.Tanh, scale=0.5, bias=lv[:, HBI + col:HBI + col + 1]), r=["pi%d" % pb, "lv0"], w=["ti%d" % pb])
                    add("act", C("activation", out=abuf[:, lo:hi], in_=tr[:, 0:n], func=AF.Exp, scale=lv[:, HCNEG + col:HCNEG + col + 1], bias=lv[:, HCNEG + col:HCNEG + col + 1]), r=["tr%d" % pb, "lv2"], w=["a"])
                    add("act", C("activation", out=bb[:, lo:hi], in_=tr[:, 0:n], func=AF.Exp, scale=lv[:, CNEG + col:CNEG + col + 1], bias=lv[:, CNEG + col:CNEG + col + 1]), r=["tr%d" % pb, "lv1"], w=[bk])
                    add("dve", C("scalar_tensor_tensor", out=tbuf[:, lo:hi], in0=ti[:, 0:n], scalar=1.0, in1=xcb[:, lo:hi], op0=ALU.add, op1=ALU.mult), r=["ti%d" % pb, "xcb"], w=["t"])
                add("act", C("activation", out=bb, in_=bb, func=AF.Sqrt, scale=-0.25, bias=0.25), r=[bk], w=[bk])
                add("dve", C("tensor_tensor", out=bb, in0=bb, in1=tbuf, op=ALU.mult), r=[bk, "t"], w=[bk])
                if d == 0:
                    add("dve", C("tensor_tensor_scan", out=bb[:, 0:NCTX], data0=abuf[:, 0:NCTX], data1=bb[:, 0:NCTX], initial=0.0, op0=ALU.mult, op1=ALU.add), r=[bk, "a"], w=[bk])
                    add("dve", C("tensor_tensor_scan", out=bb[:, NCTX:NTOK], data0=abuf[:, NCTX:NTOK], data1=bb[:, NCTX:NTOK], initial=bb[:, NCTX - 1:NCTX], op0=ALU.mult, op1=ALU.add), r=[bk, "a"], w=[bk])
                else:
                    add("dve", C("tensor_tensor_scan", out=bb[:, 0:NCTX][:, ::-1], data0=abuf[:, 0:NCTX][:, ::-1], data1=bb[:, 0:NCTX][:, ::-1], initial=0.0, op0=ALU.mult, op1=ALU.add), r=[bk, "a"], w=[bk])
                    add("dve", C("tensor_tensor_scan", out=bb[:, NCTX:NTOK][:, ::-1], data0=abuf[:, NCTX:NTOK][:, ::-1], data1=bb[:, NCTX:NTOK][:, ::-1], initial=bb[:, 0:1], op0=ALU.mult, op1=ALU.add), r=[bk, "a"], w=[bk])
            add("pool", C("tensor_tensor", out=bbuf[0][:, NCTX:NTOK], in0=bbuf[0][:, NCTX:NTOK], in1=bbuf[1][:, NCTX:NTOK], op=ALU.add), r=["b0", "b1"], w=["b0"])
            add("act", C("activation", out=tbuf[:, 0:SEQ], in_=grS[:, c, :], func=AF.Gelu_apprx_tanh), r=["gr%d" % c, "t"], w=["t"])
            add("dve", C("tensor_tensor", out=grS[:, c, :], in0=bbuf[0][:, NCTX:NTOK], in1=tbuf[:, 0:SEQ], op=ALU.mult), r=["b0", "t"], w=["gr%d" % c])

    def phase_D(s):
        WK.reset()
        wout = WK.alloc([8, D], BF16)
        g1bc = WK.alloc([D], F32)
        xt2 = [WK.alloc([D], F32) for _ in range(2)]
        x1 = [WK.alloc([D], F32) for _ in range(2)]
        xn2 = [WK.alloc([1056], BF16) for _ in range(2)]
        h2T = [WK.alloc([8, 128], BF16) for _ in range(2)]
        junk = WK.alloc([D], BF16)
        sm = [WK.alloc([24], F32) for _ in range(2)]
        ex = [WK.alloc([16], F32) for _ in range(2)]
        for h in range(2):
            add("pool", C("dma_start", out=wout[:, h * 4:(h + 1) * 4, :], in_=wout_v[:, h * 4:(h + 1) * 4, :]), w=["wout%d" % h], key="win%d" % h)
        add("sp", C("dma_start", out=g1bc, in_=gsc_d[s, 0, :].partition_broadcast(128)), w=["g1bc"], key="c0")
        py = [[banks[0], banks[1]], [banks[2], banks[3]]]
        pT2 = [banks[4].bitcast(BF16).rearrange("p (a b) -> p a b", a=8), banks[5].bitcast(BF16).rearrange("p (a b) -> p a b", a=8)]
        pl = [banks[6], banks[7]]
        for tt in range(32):
            b = tt % 2
            cols = slice(tt * 128, (tt + 1) * 128)
            r0 = s * SEQ + tt * 128
            add("sp", C("dma_start", out=xt2[b], in_=x_d[r0:r0 + 128, :]), w=["xt2%d" % b], key="xt%d" % b)
            for half in range(2):
                for kc in range(8):
                    lhs = qT[:, kc, cols] if kc < 4 else grS[:, kc - 4, cols]
                    rk = "qT%d" % tt if kc < 4 else "gr%d" % (kc - 4)
                    add("pe", C("matmul", py[b][half][:, :], lhsT=lhs, rhs=wout[:, kc, half * 512:(half + 1) * 512], start=(kc == 0), stop=(kc == 7)),
                        r=[rk, "wout0", "wout1"], w=["py%d%d" % (b, half)])
            for half in range(2):
                hs = slice(half * 512, (half + 1) * 512)
                add("dve", C("tensor_tensor", out=x1[b][:, hs], in0=py[b][half][:, :], in1=g1bc[:, hs], op=ALU.mult), r=["py%d%d" % (b, half), "g1bc"], w=["x1%d_%d" % (b, half), "x1%d" % b])
            add("pool", C("tensor_tensor", out=x1[b], in0=x1[b], in1=xt2[b], op=ALU.add), r=["x1%d_0" % b, "x1%d_1" % b, "xt2%d" % b], w=["x1%d" % b])
            add("sp", C("dma_start", out=out_d[r0:r0 + 128, :], in_=x1[b]), r=["x1%d" % b], key="o%d" % b)
            add("act", C("activation", out=junk, in_=x1[b], func=AF.Square, accum_out=sm[b][:, 0:1]), r=["x1%d" % b], w=["junk", "sm%d" % b])
            add("dve", C("tensor_scalar", out=sm[b][:, 0:1], in0=sm[b][:, 0:1], scalar1=1.0 / D, scalar2=EPS, op0=ALU.mult, op1=ALU.add), r=["sm%d" % b], w=["sm%d" % b])
            add("act", C("activation", out=sm[b][:, 0:1], in_=sm[b][:, 0:1], func=AF.Sqrt), r=["sm%d" % b], w=["sm%d" % b])
            add("dve", C("reciprocal", out=sm[b][:, 0:1], in_=sm[b][:, 0:1]), r=["sm%d" % b], w=["sm%d" % b])
            add("dve", C("tensor_scalar", out=xn2[b][:, 0:D], in0=x1[b], scalar1=sm[b][:, 0:1], scalar2=None, op0=ALU.mult), r=["x1%d" % b, "sm%d" % b], w=["xn2%d" % b])
            for fc in range(8):
                add("pe", C("transpose", out=pT2[b][:, fc, :], in_=xn2[b][:, fc * 128:(fc + 1) * 128], identity=ident), r=["xn2%d" % b, "cm"], w=["pT2%d" % b])
            for fc in range(8):
                if b == 0:
                    add("act", C("activation", out=h2T[b][:, fc, :], in_=pT2[b][:, fc, :], func=AF.Identity, scale=A2[:, fc, s:s + 1], bias=B2[:, fc, s:s + 1]), r=["pT2%d" % b, "A2", "modv"], w=["h2T%d_%d" % (b, fc)])
                else:
                    add("dve", C("tensor_scalar", out=h2T[b][:, fc, :], in0=pT2[b][:, fc, :], scalar1=A2[:, fc, s:s + 1], scalar2=B2[:, fc, s:s + 1], op0=ALU.mult, op1=ALU.add), r=["pT2%d" % b, "A2", "modv"], w=["h2T%d_%d" % (b, fc)])
            for kc in range(8):
                add("pe", C("matmul", pl[b][:, 0:16], lhsT=h2T[b][:, kc, :], rhs=wr[:, kc, :], start=(kc == 0), stop=(kc == 7)), r=["h2T%d_%d" % (b, fc) for fc in range(8)] + ["wr"], w=["pl%d" % b])
            add("dve", C("tensor_reduce", out=sm[b][:, 1:2], in_=pl[b][:, 0:16], axis=AX.X, op=ALU.max), r=["pl%d" % b], w=["mx%d" % b])
            add("dve", C("tensor_scalar", out=sm[b][:, 1:2], in0=sm[b][:, 1:2], scalar1=-1.0, scalar2=None, op0=ALU.mult), r=["mx%d" % b], w=["mx%d" % b])
            add("act", C("activation", out=ex[b], in_=pl[b][:, 0:16], func=AF.Exp, bias=sm[b][:, 1:2], accum_out=sm[b][:, 2:3]), r=["pl%d" % b, "mx%d" % b], w=["ex%d" % b, "se%d" % b])
            add("dve", C("reciprocal", out=sm[b][:, 2:3], in_=sm[b][:, 2:3]), r=["se%d" % b], w=["se%d" % b])
            add("dve", C("tensor_scalar", out=Aall[:, tt, s * 16:(s + 1) * 16], in0=ex[b], scalar1=sm[b][:, 2:3], scalar2=None, op0=ALU.mult), r=["ex%d" % b, "se%d" % b], w=["Aall"])
            add("dve", C("tensor_copy", out=xn2[b][:, D:1056].bitcast(F32), in_=Aall[:, tt, s * 16:(s + 1) * 16]), r=["Aall"], w=["xn2a%d" % b])
            add("sp", C("dma_start", out=xn_d[r0:r0 + 128, :], in_=xn2[b]), r=["xn2%d" % b, "xn2a%d" % b], key="xn%d" % b)

    def phase_topk():
        WK.reset()
        ST.reset()
        AT = ST.alloc([SEQ], F32)
        ones = ST.alloc([SEQ], F32)
        Mk = ST.alloc([SEQ], F32)
        cs = ST.alloc([SEQ], F32)
        bs = WK.alloc([8], F32)
        for tt in range(32):
            pb = banks[tt % 2]
            add("pe", C("transpose", out=pb[0:32, 0:128], in_=Aall[:, tt, :], identity=identf[:]), r=["Aall", "identf"], w=["pAT%d" % (tt % 2)])
            add("act", C("copy", out=AT[0:32, tt * 128:(tt + 1) * 128], in_=pb[0:32, 0:128]), r=["pAT%d" % (tt % 2)], w=["AT"])
        add("pool", C("memset", ones[0:32, :], 1.0), w=["ones"])
        mid, cntc, stp = bs[0:32, 0:1], bs[0:32, 1:2], bs[0:32, 2:3]
        add("dve", C("memset", mid, 0.5), w=["mid"])
        NIT = 24
        for k in range(NIT):
            wk = 2.0 ** -(k + 1)
            wn = 2.0 ** -(k + 2)
            add("dve", C("tensor_scalar", out=Mk[0:32, :], in0=AT[0:32, :], scalar1=mid, scalar2=None, op0=ALU.is_ge, op1=ALU.add, accum_out=cntc), r=["AT", "mid"], w=["Mk", "cnt"])
            add("dve", C("tensor_scalar", out=stp, in0=cntc, scalar1=float(CAP), scalar2=wk, op0=ALU.is_ge, op1=ALU.mult), r=["cnt"], w=["stp"])
            last = (k == NIT - 1)
            delta = (-wk) if last else (wn - wk)
            add("dve", C("scalar_tensor_tensor", out=mid, in0=stp, scalar=delta, in1=mid, op0=ALU.add, op1=ALU.add), r=["stp", "mid"], w=["mid"])
        add("dve", C("tensor_scalar", out=Mk[0:32, :], in0=AT[0:32, :], scalar1=mid, scalar2=None, op0=ALU.is_ge), r=["AT", "mid"], w=["Mk"])
        add("dve", C("tensor_tensor_scan", out=cs[0:32, :], data0=ones[0:32, :], data1=Mk[0:32, :], initial=0.0, op0=ALU.mult, op1=ALU.add), r=["ones", "Mk"], w=["cs"])
        add("dve", C("tensor_tensor", out=cs[0:32, :], in0=cs[0:32, :], in1=Mk[0:32, :], op=ALU.mult), r=["cs", "Mk"], w=["cs"])
        for thr in (129.0, 257.0, 385.0):
            add("dve", C("scalar_tensor_tensor", out=Mk[0:32, :], in0=cs[0:32, :], scalar=thr, in1=Mk[0:32, :], op0=ALU.is_ge, op1=ALU.add), r=["cs", "Mk"], w=["Mk"])
        add("dve", C("scalar_tensor_tensor", out=cs[0:32, :], in0=Mk[0:32, :], scalar=-128.0, in1=cs[0:32, :], op0=ALU.mult, op1=ALU.add), r=["Mk", "cs"], w=["cs"])
        PH = WK.alloc([32, 32], BF16)
        PL = WK.alloc([32, 32], BF16)
        Lb = [WK.alloc([32, 128], BF16) for _ in range(2)]
        Ht = [WK.alloc([32, 4, 2], BF16) for _ in range(2)]
        eq = [WK.alloc([32, 4], F32) for _ in range(2)]
        pph = [banks[2], banks[3]]
        for tt in range(32):
            pb = pph[tt % 2]
            add("pe", C("transpose", out=pb[:, 0:32], in_=Mk[0:32, tt * 128:(tt + 1) * 128], identity=identf[0:32, 0:32]), r=["Mk", "identf"], w=["pph%d" % (tt % 2)])
            add("pe", C("transpose", out=pb[:, 32:64], in_=cs[0:32, tt * 128:(tt + 1) * 128], identity=identf[0:32, 0:32]), r=["cs", "identf"], w=["pph%d" % (tt % 2)])
            add("act", C("copy", out=PH[:, tt, :], in_=pb[:, 0:32]), r=["pph%d" % (tt % 2)], w=["PH%d" % tt])
            add("act", C("copy", out=PL[:, tt, :], in_=pb[:, 32:64]), r=["pph%d" % (tt % 2)], w=["PL%d" % tt])
        pIdx = banks[4][:, 0:256].rearrange("p (a c k) -> p a c k", a=32, c=4)
        iop = cm[:, CM_IOP, :]
        ioc = cm[:, CM_IOC, 0:4]
        for tt in range(32):
            b = tt % 2
            add("dve", C("tensor_tensor", out=Lb[b], in0=iop.unsqueeze(1).to_broadcast([128, 32, 128]), in1=PL[:, tt, :].unsqueeze(2).to_broadcast([128, 32, 128]), op=ALU.is_equal), r=["PL%d" % tt, "cm"], w=["Lb%d" % b])
            add("dve", C("tensor_tensor", out=eq[b], in0=ioc.unsqueeze(1).to_broadcast([128, 32, 4]), in1=PH[:, tt, :].unsqueeze(2).to_broadcast([128, 32, 4]), op=ALU.is_equal), r=["PH%d" % tt, "cm"], w=["eq%d" % b])
            for k2 in range(2):
                add("dve", C("tensor_scalar", out=Ht[b][:, :, :, k2], in0=eq[b], scalar1=tcol[:, tt, k2:k2 + 1], scalar2=None, op0=ALU.mult), r=["eq%d" % b, "tcol"], w=["Ht%d_%d" % (b, k2)])
            for se in range(32):
                first = (tt == 0 and se == 0)
                add("pe", C("matmul", pIdx[:, se, :, :].rearrange("p c k -> p (c k)"), lhsT=Lb[b][:, se, :], rhs=Ht[b][:, se, :, :].rearrange("p c k -> p (c k)"), start=first, stop=(tt == 31), skip_group_check=True),
                    r=["Lb%d" % b, "Ht%d_0" % b, "Ht%d_1" % b], w=["pIdx"])
        idf = WK.alloc([32, 4], F32)
        add("dve", C("tensor_copy", out=idf, in_=pIdx[:, :, :, 1]), r=["pIdx"], w=["idf"])
        add("dve", C("scalar_tensor_tensor", out=idf, in0=pIdx[:, :, :, 0], scalar=64.0, in1=idf, op0=ALU.mult, op1=ALU.add), r=["pIdx", "idf"], w=["idf"])
        add("dve", C("tensor_scalar", out=idf[:, 16:32, :], in0=idf[:, 16:32, :], scalar1=float(SEQ), scalar2=None, op0=ALU.add), r=["idf"], w=["idf"])
        add("dve", C("tensor_copy", out=IDX[:], in_=idf), r=["idf"], w=["IDX"])
        if "idx" in dbg:
            add("sp", C("dma_start", out=dbg["idx"][:, :, :], in_=IDX[:]), r=["IDX"], key="dbg1")

    def phase_moe():
        ST.reset()
        WK.reset()
        wts = [[ST.alloc([8, D], BF16) for _ in range(3)] for _ in range(2)]
        g2bc = [ST.alloc([D], F32) for _ in range(2)]
        xgs = [[WK.alloc([1056], BF16) for _ in range(8)] for _ in range(2)]
        xsT = WK.alloc([8, 1024], BF16)
        hidT = WK.alloc([8, 1024], BF16)
        sg = [WK.alloc([512], BF16) for _ in range(2)]
        yo = [WK.alloc([D], F32) for _ in range(2)]
        for s in range(2):
            add("sp", C("dma_start", out=g2bc[s], in_=gsc_d[s, 1, :].partition_broadcast(128)), w=["g2bc%d" % s], key="c%d" % s)
        pX = [banks[0].bitcast(BF16).rearrange("p (a b) -> p a b", a=4), banks[1].bitcast(BF16).rearrange("p (a b) -> p a b", a=4)]
        pG = [banks[2], banks[3]]
        pU = [banks[4], banks[5]]
        pY = [banks[6], banks[7]]
        srcs = [wg_d, wu_d, wd_d]
        ctr = dict(y=0, gu=0)

        def load_w(e_):
            wb = e_ % 2
            for m in range(3):
                v = srcs[m][e_].rearrange("(kc p) n -> p kc n", p=128)
                for h in range(2):
                    add("pool", C("dma_start", out=wts[wb][m][:, h * 4:(h + 1) * 4, :], in_=v[:, h * 4:(h + 1) * 4, :]), w=["w%d_%d_%d" % (wb, m, h)], key="w%d%d" % (m, h))

        def gather(e_):
            gb_ = e_ % 2
            for s in range(2):
                for c in range(4):
                    st_ = s * 4 + c
                    add("pool", C("indirect_dma_start", out=xgs[gb_][st_], out_offset=None, in_=xn_d[:, :], in_offset=bass.IndirectOffsetOnAxis(ap=IDX[:, s * 16 + e_, c:c + 1], axis=0)),
                        r=["IDX"], w=["xg%d_%d" % (gb_, st_)], key="g%d_%d" % (gb_, st_))

        gather(0)
        load_w(0)
        for e_ in range(NEXP):
            wb = e_ % 2
            wgt, wut, wdt = wts[wb]
            xg = xgs[e_ % 2]
            xgk = lambda st_: "xg%d_%d" % (e_ % 2, st_)
            if e_ + 1 < NEXP:
                gather(e_ + 1)
                load_w(e_ + 1)
            for pair in range(4):
                s = pair // 2
                for hf in range(2):
                    pb = (pair * 2 + hf) % 2
                    for st2 in range(2):
                        st_ = pair * 2 + st2
                        for f4 in range(4):
                            fc = hf * 4 + f4
                            add("pe", C("transpose", out=pX[pb][:, f4, st2 * 128:(st2 + 1) * 128], in_=xg[st_][:, fc * 128:(fc + 1) * 128], identity=ident),
                                r=[xgk(st_), "cm"], w=["pX%d" % pb])
                    for f4 in range(4):
                        fc = hf * 4 + f4
                        dst = xsT[:, fc, pair * 256:(pair + 1) * 256]
                        if pb == 0:
                            add("act", C("activation", out=dst, in_=pX[pb][:, f4, :], func=AF.Identity, scale=A2[:, fc, s:s + 1], bias=B2[:, fc, s:s + 1]), r=["pX%d" % pb, "A2", "modv"], w=["xsT%d" % pair])
                        else:
                            add("dve", C("tensor_scalar", out=dst, in0=pX[pb][:, f4, :], scalar1=A2[:, fc, s:s + 1], scalar2=B2[:, fc, s:s + 1], op0=ALU.mult, op1=ALU.add), r=["pX%d" % pb, "A2", "modv"], w=["xsT%d" % pair])
            for f in range(8):
                for half in range(2):
                    gb = ctr["gu"] % 2
                    ctr["gu"] += 1
                    xk = ["xsT%d" % (half * 2), "xsT%d" % (half * 2 + 1)]
                    for kc in range(8):
                        add("pe", C("matmul", pG[gb][:, :], lhsT=wgt[:, kc, f * 128:(f + 1) * 128], rhs=xsT[:, kc, half * 512:(half + 1) * 512], start=(kc == 0), stop=(kc == 7)),
                            r=xk + ["w%d_0_0" % wb, "w%d_0_1" % wb], w=["pG%d" % gb])
                    for kc in range(8):
                        add("pe", C("matmul", pU[gb][:, :], lhsT=wut[:, kc, f * 128:(f + 1) * 128], rhs=xsT[:, kc, half * 512:(half + 1) * 512], start=(kc == 0), stop=(kc == 7)),
                            r=xk + ["w%d_1_0" % wb, "w%d_1_1" % wb], w=["pU%d" % gb])
                    add("act", C("activation", out=sg[gb], in_=pG[gb][:, :], func=AF.Silu), r=["pG%d" % gb], w=["sg%d" % gb])
                    add("dve", C("tensor_tensor", out=hidT[:, f, half * 512:(half + 1) * 512], in0=pU[gb][:, :], in1=sg[gb], op=ALU.mult), r=["pU%d" % gb, "sg%d" % gb], w=["hid%d" % half])
            for st_ in range(8):
                s, c = st_ // 4, st_ % 4
                for half in range(2):
                    for fk in range(8):
                        add("pe", C("matmul", pY[half][:, :], lhsT=hidT[:, fk, st_ * 128:(st_ + 1) * 128], rhs=wdt[:, fk, half * 512:(half + 1) * 512], start=(fk == 0), stop=(fk == 7)),
                            r=["hid%d" % s, "w%d_2_0" % wb, "w%d_2_1" % wb], w=["pY%d" % half])
                yb = ctr["y"] % 2
                ctr["y"] += 1
                gate = xg[st_][:, D:1056].bitcast(F32)[:, e_:e_ + 1]
                for half in range(2):
                    hs = slice(half * 512, (half + 1) * 512)
                    add("dve", C("scalar_tensor_tensor", out=yo[yb][:, hs], in0=pY[half][:, :], scalar=gate, in1=g2bc[s][:, hs], op0=ALU.mult, op1=ALU.mult),
                        r=["pY%d" % half, xgk(st_), "g2bc%d" % s], w=["yo%d_%d" % (yb, half)])
                prev = ["sc_%d_%d_%d" % (s, e_ - 1, cc) for cc in range(4)] if e_ > 0 else []
                add("pool", C("indirect_dma_start", out=out_d[:, :], out_offset=bass.IndirectOffsetOnAxis(ap=IDX[:, s * 16 + e_, c:c + 1], axis=0), in_=yo[yb], in_offset=None, compute_op=ALU.add, oob_is_err=True),
                    r=["yo%d_0" % yb, "yo%d_1" % yb, "IDX"] + prev, w=["sc_%d_%d_%d" % (s, e_, c)], key="sc%d" % st_)

    nsamp = 2
    stages = ["A", "B", "C", "D", "T", "full"]
    lvl = stages.index(stage) if stage in stages else -1
    if stage == "0":
        nsamp = 0
    if stage == "A1":
        nsamp = 1
    import os as _os
    if _os.environ.get("NSAMP"):
        nsamp = int(_os.environ["NSAMP"])
    for s in range(nsamp):
        phase_A(s)
        P.barrier()
        if stage == "A" and s == 0:
            add("sp", C("dma_start", out=dbg["qT"][:, :, :], in_=qT), key="dbg0")
            add("sp", C("dma_start", out=dbg["kT"][:, :], in_=kT), key="dbg1")
            add("sp", C("dma_start", out=dbg["v"][:, :, :, :], in_=vaug), key="dbg0")
            add("sp", C("dma_start", out=dbg["xr"][:, :, :], in_=xrS), key="dbg1")
            add("sp", C("dma_start", out=dbg["gr"][:, :, :], in_=grS), key="dbg0")
            P.barrier()
        if lvl >= 1:
            phase_B(s)
            P.barrier()
            if stage == "B" and s == 0:
                add("sp", C("dma_start", out=dbg["qT"][:, :, :], in_=qT), key="dbg0")
                P.barrier()
        if lvl >= 2:
            phase_C(s)
            P.barrier()
            if stage == "C" and s == 0:
                add("sp", C("dma_start", out=dbg["gr"][:, :, :], in_=grS), key="dbg0")
                add("sp", C("dma_start", out=dbg["qT"][:, :, :], in_=qT), key="dbg1")
                P.barrier()
        if lvl >= 3:
            phase_D(s)
            P.barrier()
    if lvl >= 4:
        phase_topk()
        P.barrier()
        if "aall" in dbg:
            add("sp", C("dma_start", out=dbg["aall"][:, :, :], in_=Aall[:]), key="dbg0")
    if lvl >= 5:
        phase_moe()
    P.barrier()
    P.emit()
    P.close()
    return nc, P.stats


def _swap_idx():
    d = np.arange(64)
    return np.where((d % 32) < 16, d + 16, d - 16)


def _consts():
    bf = ml_dtypes.bfloat16
    cm = np.zeros((128, NCM, 128), np.float32)
    cm[:, CM_ID, :] = np.eye(128)
    for h in range(2):
        cm[h * 64:(h + 1) * 64, CM_ONES, h * 64:(h + 1) * 64] = 1.0
    sw = _swap_idx()
    for m in range(128):
        k = (m // 64) * 64 + sw[m % 64]
        cm[k, CM_SWAP, m] = 1.0
    cm[:, CM_IOP, :] = (np.arange(128) - 127)[None, :]
    cm[:, CM_IOC, 0:4] = np.arange(1, 5)[None, :]
    identf = np.eye(128, dtype=np.float32)
    a = np.arange(128)
    prev = (a[None, :] <= a[:, None]).astype(np.float32)
    nxt = (a[:, None] <= a[None, :]).astype(np.float32)
    masks = np.stack([np.tile(prev, (1, 4)), np.tile(nxt, (1, 4))], axis=1)
    t = np.arange(SEQ)
    row = (t // 64).astype(np.float32)
    col = (t % 64).astype(np.float32)
    inv = (10000.0 ** (-np.arange(16, dtype=np.float32) / 16)).astype(np.float32)
    ar = (row[None, :] * inv[:, None]).astype(np.float32)
    ac = (col[None, :] * inv[:, None]).astype(np.float32)
    C = np.zeros((64, SEQ), np.float32)
    S2 = np.zeros((64, SEQ), np.float32)
    C[0:16] = np.cos(ar); C[16:32] = np.cos(ar); C[32:48] = np.cos(ac); C[48:64] = np.cos(ac)
    S2[0:16] = np.sin(ar); S2[16:32] = -np.sin(ar); S2[32:48] = np.sin(ac); S2[48:64] = -np.sin(ac)
    rope = np.stack([np.tile(C, (2, 1)), np.tile(S2, (2, 1))], axis=1)
    tok = np.arange(32)[None, :] * 128 + np.arange(128)[:, None]
    tcol = np.stack([tok // 64, tok % 64], axis=2).astype(np.float32)
    return dict(cm=cm.astype(bf), identf=identf, masks=masks.astype(bf), rope=rope.astype(bf), tcol=tcol)


def prep_inputs(inp):
    f = np.float32
    g = lambda k: np.asarray(inp[k], dtype=f)
    x, c, ctx, c_ctx = g("x"), g("c"), g("ctx"), g("c_ctx")
    L = 0
    w_in = g("w_in")[L]
    qperm = np.concatenate([np.r_[b * 64:(b + 1) * 64, (b + 4) * 64:(b + 5) * 64] for b in range(4)])
    w_in_p = np.ascontiguousarray(np.concatenate([w_in[:, qperm], w_in[:, 512:]], axis=1))
    w_out = g("w_out")[L]
    w_out_p = np.ascontiguousarray(np.concatenate([w_out[qperm, :], w_out[512:, :]], axis=0))
    b_ada = g("b_ada")[L]
    vecs = np.zeros((128, NV), f)
    pc = lambda v, n: v.reshape(n, 128).T
    vecs[:, V_N1G:V_N1G + 8] = pc(g("norm1_g")[L], 8)
    vecs[:, V_N2G:V_N2G + 8] = pc(g("norm2_g")[L], 8)
    vecs[:, V_BADA:V_BADA + 48] = pc(b_ada, 48)
    gq, gk = g("q_norm_g")[L], g("k_norm_g")[L]
    vecs[:, V_GQ] = np.tile(gq, 2)
    vecs[:, V_GK] = np.tile(gk, 2)
    cw = g("conv_w")[L]
    for cch in range(4):
        for j in range(4):
            vecs[:, V_CONVW + cch * 4 + j] = cw[j, cch * 128:(cch + 1) * 128]
    vecs[:, V_CONVB:V_CONVB + 4] = pc(g("conv_b")[L], 4)
    for d in range(2):
        vecs[:, V_BR + d * 4:V_BR + d * 4 + 4] = pc(g("lru_b_r")[L][d], 4)
        vecs[:, V_BI + d * 4:V_BI + d * 4 + 4] = pc(g("lru_b_i")[L][d], 4)
        vecs[:, V_LAM + d * 4:V_LAM + d * 4 + 4] = pc(g("lru_lambda")[L][d], 4)
    vecs[:, V_SINK:V_SINK + 8] = g("attn_sink")[L][None, :]
    vecs[:, V_GQROW:V_GQROW + 64] = gq[None, :]
    vecs[:, V_GKROW:V_GKROW + 64] = gk[None, :]
    lruw = np.zeros((128, 16, 128), f)
    for d in range(2):
        for gi, nm in enumerate(["lru_w_r", "lru_w_i"]):
            w = g(nm)[L][d]
            for cch in range(4):
                for nl in range(2):
                    lruw[nl * 64:(nl + 1) * 64, (d * 2 + gi) * 4 + cch, nl * 64:(nl + 1) * 64] = w[cch * 2 + nl]
    badag = np.stack([np.stack([b_ada[2 * D:3 * D], b_ada[5 * D:6 * D]])] * 3)
    wrt = np.ascontiguousarray(g("w_router")[L].reshape(8, 128, 16).transpose(1, 0, 2))
    consts = _consts()
    shared = dict(w_ada=g("w_ada")[L], badag=badag, vecs=vecs, w_in=w_in_p, lruw=lruw, w_out=w_out_p, w_router=wrt,
                  w_gate=g("w_gate")[L], w_up=g("w_up")[L], w_down=g("w_down")[L], **consts)
    maps = []
    for core in range(NCORES):
        s0 = 2 * core
        cc = np.stack([c[s0], c[s0 + 1], c_ctx])
        ccT = np.ascontiguousarray(cc.reshape(3, 8, 128).transpose(2, 1, 0))
        m = dict(shared)
        m["x"] = np.ascontiguousarray(x[s0:s0 + 2].reshape(2 * SEQ, D))
        m["ctx"] = np.ascontiguousarray(ctx[s0:s0 + 2].reshape(2 * NCTX, D))
        m["ccT"] = ccT
        maps.append(m)
    return maps


_CACHE = {}


def kernel(**inputs):
    maps = prep_inputs(inputs)
    if "nc" not in _CACHE:
        _CACHE["nc"] = build_program("full")[0]
    nc = _CACHE["nc"]
    res = run_bass_kernel_spmd(nc, maps, core_ids=list(range(NCORES)))
    outs = [np.asarray(r["out"]).reshape(2, SEQ, D) for r in res.results]
    return np.concatenate(outs, axis=0).astype(np.float32)
```

```python
import numpy as np
import ml_dtypes
import concourse.bass as bass
import concourse.mybir as mybir
from concourse.bass_utils import run_bass_kernel_spmd
from contextlib import ExitStack

AF = mybir.ActivationFunctionType
ALU = mybir.AluOpType
AX = mybir.AxisListType
F32 = mybir.dt.float32
BF16 = mybir.dt.bfloat16
I32 = mybir.dt.int32

NCORES = 8
SEQ = 4096
NCTX = 256
NTOK = SEQ + NCTX
D = 1024
EPS = 1e-6
NEXP = 16
CAP = 512


class Op:
    __slots__ = ("eng", "fn", "deps", "dma", "key", "signal", "ticket")

    def __init__(self, eng, fn, dma, key):
        self.eng = eng
        self.fn = fn
        self.dma = dma
        self.key = key
        self.deps = []
        self.signal = False
        self.ticket = 0


ENGS = ["pe", "act", "dve", "pool", "sp"]


def C(name, *a, **k):
    return (name, a, k)


class Prog:
    def __init__(self, nc):
        self.nc = nc
        self.ops = []
        self.last_w = {}
        self.readers = {}
        self.dma_last = {}
        self.last_eng = {}
        self.bank_last = {}
        self.stack = ExitStack()

    def sb(self, name, shape, dtype):
        return self.stack.enter_context(self.nc.sbuf_tensor("sb_" + name, list(shape), dtype))

    def ps(self, name, shape, dtype):
        return self.stack.enter_context(self.nc.psum_tensor("ps_" + name, list(shape), dtype))

    dead = False

    def cut(self, n):
        import os
        if int(os.environ.get("CUT", "-1")) == n:
            self.dead = True

    def add(self, eng, fn, r=(), w=(), key=None):
        op = Op(eng, fn, key is not None, key)
        if self.dead:
            return op
        deps = {}
        for k in r:
            o = self.last_w.get(k)
            if o is not None:
                deps[id(o)] = o
        for k in w:
            o = self.last_w.get(k)
            if o is not None:
                deps[id(o)] = o
            for o in self.readers.get(k, {}).values():
                deps[id(o)] = o
        if op.dma:
            o = self.dma_last.get(key)
            if o is not None:
                deps[id(o)] = o
            self.dma_last[key] = op
        nm_, a_, k_ = fn
        for v in list(a_) + list(k_.values()):
            tn = getattr(getattr(v, "tensor", None), "name", "")
            if tn.startswith("ps_bank"):
                o = self.bank_last.get(tn)
                if o is not None and o.eng != eng:
                    deps[id(o)] = o
                self.bank_last[tn] = op
        for k in r:
            self.readers.setdefault(k, {})[(eng, key)] = op
        for k in w:
            self.last_w[k] = op
            self.readers[k] = {}
        for o in deps.values():
            if o is op:
                continue
            if (not o.dma) and (not op.dma) and o.eng == "pe" and op.eng == "pe":
                continue
            op.deps.append(o)
            o.signal = True
        if not op.dma:
            self.last_eng[eng] = op
        self.ops.append(op)
        return op

    def barrier(self):
        prev = [o for o in self.last_eng.values()] + list(self.dma_last.values())
        new = []
        for e in ENGS:
            op = Op(e, C("nop", ), False, None)
            for o in prev:
                if o.eng == e and not o.dma and e == "pe":
                    continue
                op.deps.append(o)
                o.signal = True
            self.ops.append(op)
            new.append(op)
        for op in new:
            self.last_eng[op.eng] = op
        self.last_w = {}
        self.readers = {}

    def emit(self):
        nc = self.nc
        cnt = {e: 0 for e in ENGS}
        dcnt = {}
        for op in self.ops:
            if op.dma:
                dcnt[op.key] = dcnt.get(op.key, 0) + 16
                op.ticket = dcnt[op.key]
            elif op.signal:
                cnt[op.eng] += 1
                op.ticket = cnt[op.eng]
        sems = {}
        for e in ENGS:
            if cnt[e]:
                sems[("c", e)] = self.stack.enter_context(nc.semaphore("s_" + e))
        for k in dcnt:
            sems[("d", k)] = self.stack.enter_context(nc.semaphore("d_" + str(k)))

        def semof(o):
            return sems[("d", o.key)] if o.dma else sems[("c", o.eng)]

        per = {e: [] for e in ENGS}
        for op in self.ops:
            per[op.eng].append(op)

        def run(name, eng):
            seen = {}
            for op in per[name]:
                need = {}
                for d in op.deps:
                    s = semof(d)
                    if need.get(id(s), (None, 0))[1] < d.ticket:
                        need[id(s)] = (s, d.ticket)
                for kk, (s, t) in need.items():
                    if seen.get(kk, 0) >= t:
                        continue
                    eng.wait_ge(s, t)
                    seen[kk] = t
                nm, a_, k_ = op.fn
                try:
                    ins = getattr(eng, nm)(*a_, **k_)
                except Exception:
                    print("EMIT FAIL at", per[name].index(op), "of", len(per[name]), "ndma_before", sum(1 for o in per[name][:per[name].index(op)] if o.dma), flush=True)
                    print("EMIT FAIL", name, nm, [repr(x)[:300] for x in a_], {kk: repr(vv)[:300] for kk, vv in k_.items()}, flush=True)
                    raise
                if op.dma:
                    ins.then_inc(semof(op), 16)
                elif op.signal:
                    ins.then_inc(semof(op), 1)

        with nc.Block() as block:
            @block.tensor
            def _(e):
                run("pe", e)

            @block.scalar
            def _(e):
                run("act", e)

            @block.vector
            def _(e):
                run("dve", e)

            @block.gpsimd
            def _(e):
                run("pool", e)

            @block.sync
            def _(e):
                run("sp", e)
        self.stats = dict(n={e: len(per[e]) for e in ENGS}, sig=cnt, nsems=len(sems))

    def close(self):
        self.stack.close()


class Arena:
    def __init__(self, P, name, nbytes):
        self.t = P.sb(name, [128, nbytes // 2], BF16)
        self.cap = nbytes
        self.off = 0

    def alloc(self, shape, dtype):
        esz = 2 if dtype == BF16 else 4
        n = int(np.prod(shape))
        nb = n * esz
        start = self.off
        self.off += (nb + 31) // 32 * 32
        assert self.off <= self.cap, ("arena overflow", self.off, self.cap)
        ap = self.t[:, start // 2:start // 2 + nb // 2]
        if dtype != BF16:
            ap = ap.bitcast(dtype)
        if len(shape) == 2:
            ap = ap.rearrange("p (a b) -> p a b", a=shape[0])
        elif len(shape) == 3:
            ap = ap.rearrange("p (a b c) -> p a b c", a=shape[0], b=shape[1])
        elif len(shape) == 4:
            ap = ap.rearrange("p (a b c d) -> p a b c d", a=shape[0], b=shape[1], c=shape[2])
        return ap

    def reset(self, off=0):
        self.off = off


V_N1G, V_N2G, V_BADA, V_GQ, V_GK, V_CONVW, V_CONVB, V_BR, V_BI, V_LAM = 0, 8, 16, 64, 65, 66, 82, 86, 94, 102
V_SINK, V_GQROW, V_GKROW, NV = 110, 118, 182, 246
CM_ID, CM_ONES, CM_SWAP, CM_IOP, CM_IOC, NCM = 0, 1, 2, 3, 4, 5


def build_program(stage="full"):
    nc = bass.Bass("TRN2", target_bir_lowering=False)

    def din(name, shape, dt=F32):
        return nc.dram_tensor(name, list(shape), dt, kind="ExternalInput").ap()

    x_d = din("x", [2 * SEQ, D])
    ctx_d = din("ctx", [2 * NCTX, D])
    ccT_d = din("ccT", [128, 8, 3])
    wada_d = din("w_ada", [D, 6 * D])
    badag_d = din("badag", [3, 2, D])
    vecs_d = din("vecs", [128, NV])
    cm_d = din("cm", [128, NCM, 128], BF16)
    identf_d = din("identf", [128, 128])
    masks_d = din("masks", [128, 2, 512], BF16)
    rope_d = din("rope", [128, 2, SEQ], BF16)
    tcol_d = din("tcol", [128, 32, 2])
    win_d = din("w_in", [D, 1792])
    lruw_d = din("lruw", [128, 16, 128])
    wout_d = din("w_out", [D, D])
    wr_d = din("w_router", [128, 8, 16])
    wg_d = din("w_gate", [NEXP, D, D])
    wu_d = din("w_up", [NEXP, D, D])
    wd_d = din("w_down", [NEXP, D, D])
    out_d = nc.dram_tensor("out", [2 * SEQ, D], F32, kind="ExternalOutput").ap()
    xn_d = nc.dram_tensor("xn_scr", [2 * SEQ, 1056], BF16, kind="Internal").ap()
    gsc_d = nc.dram_tensor("g_scr", [3, 2, D], F32, kind="Internal").ap()
    dbg = {}
    if stage != "full":
        dbg["qT"] = nc.dram_tensor("dbg_qT", [128, 4, SEQ], BF16, kind="ExternalOutput").ap()
        dbg["kT"] = nc.dram_tensor("dbg_kT", [128, NTOK], BF16, kind="ExternalOutput").ap()
        dbg["v"] = nc.dram_tensor("dbg_v", [128, 34, 2, 65], BF16, kind="ExternalOutput").ap()
        dbg["xr"] = nc.dram_tensor("dbg_xr", [128, 4, NTOK], BF16, kind="ExternalOutput").ap()
        dbg["gr"] = nc.dram_tensor("dbg_gr", [128, 4, SEQ], BF16, kind="ExternalOutput").ap()
        dbg["modv"] = nc.dram_tensor("dbg_modv", [128, 48, 3], F32, kind="ExternalOutput").ap()
        dbg["aall"] = nc.dram_tensor("dbg_aall", [128, 32, 32], F32, kind="ExternalOutput").ap()
        dbg["idx"] = nc.dram_tensor("dbg_idx", [128, 32, 4], I32, kind="ExternalOutput").ap()

    P = Prog(nc)
    add = P.add

    vecs = P.sb("vecs", [128, NV], F32)
    cm = P.sb("cm", [128, NCM, 128], BF16)
    identf = P.sb("identf", [128, 128], F32)
    masks = P.sb("masks", [128, 2, 512], BF16)
    lruW = P.sb("lruW", [128, 16, 128], BF16)
    wr = P.sb("wr", [128, 8, 16], BF16)
    tcol = P.sb("tcol", [128, 32, 2], F32)
    modv = P.sb("modv", [128, 48, 3], F32)
    A1 = P.sb("A1", [128, 8, 3], F32)
    A2 = P.sb("A2", [128, 8, 3], F32)
    lv = P.sb("lv", [128, 40], F32)
    av = P.sb("av", [128, 16], F32)
    Aall = P.sb("Aall", [128, 32, 32], F32)
    IDX = P.sb("IDX", [128, 32, 4], I32)
    ident = cm[:, CM_ID, :]
    onesblk = cm[:, CM_ONES, :]
    pswap = cm[:, CM_SWAP, :]
    B1 = modv[:, 0:8, :]
    B2 = modv[:, 24:32, :]
    HBR, HBI, CNEG, HCNEG = 0, 8, 16, 24
    negB = av[:, 0:1]
    esink = av[:, 1:9]

    banks = [P.ps("bank%d" % i, [128, 512], F32) for i in range(8)]
    ST = Arena(P, "ST", 116 * 1024)
    WK = Arena(P, "WK", 77 * 1024)

    def ld(eng, out, in_, w, key):
        return add(eng, C("dma_start", out=out, in_=in_), w=w, key=key)

    ld("sp", vecs[:], vecs_d[:, :], ["vecs"], "c0")
    ld("sp", cm[:], cm_d[:, :, :], ["cm"], "c1")
    ld("sp", identf[:], identf_d[:, :], ["identf"], "c2")
    ld("sp", masks[:], masks_d[:, :, :], ["masks"], "c3")
    ld("sp", tcol[:], tcol_d[:, :, :], ["tcol"], "c0")
    ld("pool", lruW[:], lruw_d[:, :, :], ["lruW"], "c4")
    ld("pool", wr[:], wr_d[:, :, :], ["wr"], "c5")
    ccT = WK.alloc([8, 3], F32)
    scT = WK.alloc([8, 3], F32)
    badag = WK.alloc([2, D], F32)
    grow = WK.alloc([2, D], F32)
    wa = [WK.alloc([8, 512], F32) for _ in range(2)]
    tmpv = WK.alloc([64], F32)
    ld("sp", ccT, ccT_d[:, :, :], ["ccT"], "c2")
    ld("sp", badag[0:3], badag_d[:, :, :], ["badag"], "c3")
    add("act", C("activation", out=scT, in_=ccT, func=AF.Silu), r=["ccT"], w=["scT"])
    pm = banks[0][:, 0:192].rearrange("p (a b) -> p a b", a=48)
    prow = banks[1]
    wada_v = wada_d.rearrange("(kc p) n -> p kc n", p=128)
    P.cut(1)
    pc = 0
    for m in range(6):
        for half in range(2):
            b = pc % 2
            pc += 1
            c0 = m * D + half * 512
            ld("sp", wa[b], wada_v[:, :, c0:c0 + 512], ["wa%d" % b], "wa%d" % b)
            for jj in range(4):
                j = m * 8 + half * 4 + jj
                for kc in range(8):
                    add("pe", C("matmul", pm[:, j, 0:3], lhsT=wa[b][:, kc, jj * 128:(jj + 1) * 128], rhs=scT[:, kc, :], start=(kc == 0), stop=(kc == 7)),
                        r=["wa%d" % b, "scT"], w=["pm"])
            if m in (2, 5):
                which = 0 if m == 2 else 1
                for kc in range(8):
                    add("pe", C("matmul", prow[0:3, :], lhsT=scT[:, kc, :], rhs=wa[b][:, kc, :], start=(kc == 0), stop=(kc == 7)),
                        r=["wa%d" % b, "scT"], w=["prow"])
                add("dve", C("tensor_tensor", out=grow[0:3, which, half * 512:(half + 1) * 512], in0=prow[0:3, :], in1=badag[0:3, which, half * 512:(half + 1) * 512], op=ALU.add),
                    r=["prow", "badag"], w=["grow"])
    P.cut(2)
    add("sp", C("dma_start", out=gsc_d[:, :, :], in_=grow[0:3]), r=["grow"], w=["gsc"], key="c1")
    P.cut(3)
    add("dve", C("tensor_tensor", out=modv[:], in0=pm[:, :, 0:3], in1=vecs[:, V_BADA:V_BADA + 48].unsqueeze(2).to_broadcast([128, 48, 3]), op=ALU.add),
        r=["pm", "vecs"], w=["modv"])
    add("dve", C("scalar_tensor_tensor", out=A1[:], in0=modv[:, 8:16, :], scalar=1.0, in1=vecs[:, V_N1G:V_N1G + 8].unsqueeze(2).to_broadcast([128, 8, 3]), op0=ALU.add, op1=ALU.mult),
        r=["modv", "vecs"], w=["A1"])
    add("dve", C("scalar_tensor_tensor", out=A2[:], in0=modv[:, 32:40, :], scalar=1.0, in1=vecs[:, V_N2G:V_N2G + 8].unsqueeze(2).to_broadcast([128, 8, 3]), op0=ALU.add, op1=ALU.mult),
        r=["modv", "vecs"], w=["A2"])
    P.cut(4)
    add("dve", C("tensor_scalar", out=lv[:, HBR:HBR + 16], in0=vecs[:, V_BR:V_BR + 16], scalar1=0.5, scalar2=None, op0=ALU.mult), r=["vecs"], w=["lv0"])
    add("act", C("activation", out=tmpv[:, 0:8], in_=vecs[:, V_LAM:V_LAM + 8], func=AF.Exp, scale=-1.0), r=["vecs"], w=["tmpv"])
    add("act", C("activation", out=tmpv[:, 8:16], in_=tmpv[:, 0:8], func=AF.Ln, bias=1.0), r=["tmpv"], w=["tmpv2"])
    add("dve", C("tensor_scalar", out=lv[:, CNEG:CNEG + 8], in0=tmpv[:, 8:16], scalar1=-8.0, scalar2=None, op0=ALU.mult), r=["tmpv2"], w=["lv1"])
    add("dve", C("tensor_scalar", out=lv[:, HCNEG:HCNEG + 8], in0=tmpv[:, 8:16], scalar1=-4.0, scalar2=None, op0=ALU.mult), r=["tmpv2"], w=["lv2"])
    P.cut(5)
    add("dve", C("tensor_reduce", out=tmpv[:, 16:17], in_=vecs[:, V_GQROW:V_GQROW + 64], axis=AX.X, op=ALU.max, apply_absolute_value=True), r=["vecs"], w=["tmpv3"])
    add("dve", C("tensor_reduce", out=tmpv[:, 17:18], in_=vecs[:, V_GKROW:V_GKROW + 64], axis=AX.X, op=ALU.max, apply_absolute_value=True), r=["vecs"], w=["tmpv4"])
    add("dve", C("scalar_tensor_tensor", out=av[:, 0:1], in0=tmpv[:, 16:17], scalar=-8.0, in1=tmpv[:, 17:18], op0=ALU.mult, op1=ALU.mult), r=["tmpv3", "tmpv4"], w=["av0"])
    add("act", C("activation", out=av[:, 1:9], in_=vecs[:, V_SINK:V_SINK + 8], func=AF.Exp, bias=av[:, 0:1]), r=["vecs", "av0"], w=["av1"])
    P.cut(6)
    if "modv" in dbg:
        add("sp", C("dma_start", out=dbg["modv"][:, :, :], in_=modv[:]), r=["modv"], key="dbg0")
    P.barrier()
    P.cut(7)

    ST.reset()
    qT = ST.alloc([4, SEQ], BF16)
    kT = ST.alloc([NTOK], BF16)
    vaug = ST.alloc([34, 2, 65], BF16)
    xrS = ST.alloc([4, NTOK], BF16)
    grS = ST.alloc([4, SEQ], BF16)
    add("pool", C("memset", vaug[:, :, :, 64:65], 1.0), w=["vones"])
    P.cut(8)
    P.barrier()
    P.cut(9)

    win_v = win_d.rearrange("(kc p) n -> p kc n", p=128)
    wout_v = wout_d.rearrange("(kc p) n -> p kc n", p=128)

    def phase_A(s):
        WK.reset()
        win = WK.alloc([8, 1792], BF16)
        ropeb = [WK.alloc([2, 512], BF16) for _ in range(2)]
        xt = [WK.alloc([2, D], F32) for _ in range(2)]
        junk = WK.alloc([D], BF16)
        xn = WK.alloc([4, D], BF16)
        hT = [WK.alloc([8, 512], BF16)] * 2
        tq = [[WK.alloc([512], BF16) for _ in range(4)] for _ in range(2)]
        rstd = [WK.alloc([512], F32)] * 2
        ssb = WK.alloc([8], F32)
        for h in range(2):
            add("pool", C("dma_start", out=win[:, h * 4:(h + 1) * 4, :], in_=win_v[:, h * 4:(h + 1) * 4, :]), w=["win%d" % h], key="win%d" % h)
        pT = [banks[i].bitcast(BF16).rearrange("p (a b) -> p a b", a=2) for i in range(4)]
        pA = [banks[4], banks[5]]
        pS = banks[6]
        pR = banks[7]
        st = dict(xh=0, grp=0, acc=0, tq=0)

        def group(rows_ap, ntiles, is_ctx, col0, tile0, lat0):
            ntok = ntiles * 128
            scol = 2 if is_ctx else s
            gi = st["grp"]
            st["grp"] += 1
            hb = gi % 2
            hbk = 0
            if not is_ctx:
                add("sp", C("dma_start", out=ropeb[hb], in_=rope_d[:, :, lat0:lat0 + 512]), w=["rope%d" % hb], key="rope%d" % hb)
            for half in range(ntiles // 2):
                xb = st["xh"] % 2
                st["xh"] += 1
                src = rows_ap[half * 256:(half + 1) * 256, :].rearrange("(j p) n -> p j n", p=128)
                add("sp", C("dma_start", out=xt[xb], in_=src), w=["xt%d" % xb], key="xt%d" % xb)
                for j in range(2):
                    add("act", C("activation", out=junk, in_=xt[xb][:, j, :], func=AF.Square, accum_out=ssb[:, xb * 2 + j:xb * 2 + j + 1]),
                        r=["xt%d" % xb], w=["junk", "ms%d" % xb])
                sl = ssb[:, xb * 2:xb * 2 + 2]
                add("dve", C("tensor_scalar", out=sl, in0=sl, scalar1=1.0 / D, scalar2=EPS, op0=ALU.mult, op1=ALU.add), r=["ms%d" % xb], w=["ms%d" % xb])
                add("act", C("activation", out=sl, in_=sl, func=AF.Sqrt), r=["ms%d" % xb], w=["ms%d" % xb])
                add("dve", C("reciprocal", out=sl, in_=sl), r=["ms%d" % xb], w=["ms%d" % xb])
                for j in range(2):
                    t = half * 2 + j
                    add("dve", C("tensor_scalar", out=xn[:, t, :], in0=xt[xb][:, j, :], scalar1=ssb[:, xb * 2 + j:xb * 2 + j + 1], scalar2=None, op0=ALU.mult),
                        r=["xt%d" % xb, "ms%d" % xb], w=["xn%d" % t])
            P.cut(10)
            for fc in range(8):
                for t in range(ntiles):
                    add("pe", C("transpose", out=pT[fc % 4][:, fc // 4, t * 128:(t + 1) * 128], in_=xn[:, t, fc * 128:(fc + 1) * 128], identity=ident),
                        r=["xn%d" % t, "cm"], w=["pT%d" % fc])
                eng = "act" if (fc % 4) % 2 == 0 else "dve"
                import os
                _f = os.environ.get("DBGF", "")
                if _f == "noevac":
                    continue
                if _f == "actonly":
                    eng = "act"
                if _f == "dveonly":
                    eng = "dve"
                if eng == "act":
                    add("act", C("activation", out=hT[hb][:, fc, 0:ntok], in_=pT[fc % 4][:, fc // 4, 0:ntok], func=AF.Identity, scale=A1[:, fc, scol:scol + 1], bias=B1[:, fc, scol:scol + 1]),
                        r=["pT%d" % fc, "A1", "modv"], w=["hT%d_%d" % (hbk, fc)])
                else:
                    add("dve", C("tensor_scalar", out=hT[hb][:, fc, 0:ntok], in0=pT[fc % 4][:, fc // 4, 0:ntok], scalar1=A1[:, fc, scol:scol + 1], scalar2=B1[:, fc, scol:scol + 1], op0=ALU.mult, op1=ALU.add),
                        r=["pT%d" % fc, "A1", "modv"], w=["hT%d_%d" % (hbk, fc)])
            hkeys = ["hT%d_%d" % (hbk, fc) for fc in range(8)]
            P.cut(11)
            blocks = [("k", 512), ("v", 640)]
            if not is_ctx:
                blocks += [("q%d" % b, b * 128) for b in range(4)]
            blocks += [("x%d" % c, 768 + c * 128) for c in range(4)]
            if not is_ctx:
                blocks += [("g%d" % c, 1280 + c * 128) for c in range(4)]
            for bi_, (name, c0) in enumerate(blocks):
                P.cut(12 + bi_ + 20 * gi)
                ab = st["acc"] % 2
                st["acc"] += 1
                acc = pA[ab]
                akey = "pA%d" % ab
                if name == "v":
                    accv = acc.rearrange("p (t c) -> p t c", t=4)
                    for t in range(ntiles):
                        for kc in range(8):
                            add("pe", C("matmul", accv[:, t, :], lhsT=hT[hb][:, kc, t * 128:(t + 1) * 128], rhs=win[:, kc, 640:768], start=(kc == 0), stop=(kc == 7)),
                                r=hkeys + ["win0", "win1"], w=[akey])
                    add("dve", C("tensor_copy", out=vaug[:, tile0:tile0 + ntiles, :, 0:64], in_=accv[:, 0:ntiles, :].rearrange("p t (h d) -> p t h d", h=2)),
                        r=[akey], w=["v%d" % (tile0 + t) for t in range(ntiles)])
                    continue
                for kc in range(8):
                    add("pe", C("matmul", acc[:, 0:ntok], lhsT=win[:, kc, c0:c0 + 128], rhs=hT[hb][:, kc, 0:ntok], start=(kc == 0), stop=(kc == 7)),
                        r=hkeys + ["win0", "win1"], w=[akey])
                if name[0] == "x":
                    c = int(name[1])
                    add("act", C("copy", out=xrS[:, c, col0:col0 + ntok], in_=acc[:, 0:ntok]), r=[akey], w=["xr%d" % c])
                    continue
                if name[0] == "g":
                    c = int(name[1])
                    add("dve", C("tensor_copy", out=grS[:, c, lat0:lat0 + ntok], in_=acc[:, 0:ntok]), r=[akey], w=["gr%d" % c])
                    continue
                tb = st["tq"] % 2
                st["tq"] += 1
                sq, qg, qc, qs = tq[tb]
                rs = rstd[tb]
                gcol = vecs[:, V_GK:V_GK + 1] if name == "k" else vecs[:, V_GQ:V_GQ + 1]
                add("act", C("activation", out=sq[:, 0:ntok], in_=acc[:, 0:ntok], func=AF.Square), r=[akey], w=["sq%d" % tb])
                add("pe", C("matmul", pS[:, 0:ntok], lhsT=onesblk, rhs=sq[:, 0:ntok], start=True, stop=True), r=["sq%d" % tb, "cm"], w=["pS"])
                add("act", C("activation", out=rs[:, 0:ntok], in_=pS[:, 0:ntok], func=AF.Sqrt, scale=1.0 / 64, bias=EPS), r=["pS"], w=["rs0"])
                add("dve", C("reciprocal", out=rs[:, 0:ntok], in_=rs[:, 0:ntok]), r=["rs0"], w=["rs0"])
                if is_ctx:
                    add("dve", C("scalar_tensor_tensor", out=kT[:, col0:col0 + ntok], in0=acc[:, 0:ntok], scalar=gcol, in1=rs[:, 0:ntok], op0=ALU.mult, op1=ALU.mult),
                        r=[akey, "rs0", "vecs"], w=["kT%d" % (col0 // 128 + t) for t in range(ntiles)])
                    continue
                add("dve", C("scalar_tensor_tensor", out=qg[:, 0:ntok], in0=acc[:, 0:ntok], scalar=gcol, in1=rs[:, 0:ntok], op0=ALU.mult, op1=ALU.mult),
                    r=[akey, "rs0", "vecs"], w=["qg%d" % tb])
                add("pool", C("tensor_tensor", out=qc[:, 0:ntok], in0=qg[:, 0:ntok], in1=ropeb[hb][:, 0, 0:ntok], op=ALU.mult), r=["qg%d" % tb, "rope%d" % hb], w=["qc%d" % tb])
                add("pool", C("tensor_tensor", out=qs[:, 0:ntok], in0=qg[:, 0:ntok], in1=ropeb[hb][:, 1, 0:ntok], op=ALU.mult), r=["qg%d" % tb, "rope%d" % hb], w=["qs%d" % tb])
                add("pe", C("matmul", pR[:, 0:ntok], lhsT=ident, rhs=qc[:, 0:ntok], start=True, stop=False), r=["qc%d" % tb, "cm"], w=["pR"])
                add("pe", C("matmul", pR[:, 0:ntok], lhsT=pswap, rhs=qs[:, 0:ntok], start=False, stop=True), r=["qs%d" % tb, "cm"], w=["pR"])
                if name == "k":
                    add("act", C("copy", out=kT[:, col0:col0 + ntok], in_=pR[:, 0:ntok]), r=["pR"], w=["kT%d" % (col0 // 128 + t) for t in range(ntiles)])
                else:
                    b = int(name[1])
                    add("act", C("copy", out=qT[:, b, lat0:lat0 + ntok], in_=pR[:, 0:ntok]), r=["pR"], w=["qT%d" % (lat0 // 128 + t) for t in range(ntiles)])

        group(ctx_d[s * NCTX:(s + 1) * NCTX, :], 2, True, 0, 0, 0)
        for g in range(8):
            group(x_d[s * SEQ + g * 512:s * SEQ + (g + 1) * 512, :], 4, False, NCTX + g * 512, 2 + g * 4, g * 512)

    def phase_B(s):
        WK.reset()
        kTz = [WK.alloc([NTOK], BF16) for _ in range(2)]
        NEX = 16
        expS = [WK.alloc([512], BF16) for _ in range(NEX)]
        On = [WK.alloc([4, 2, 64], BF16) for _ in range(2)]
        den = [WK.alloc([4], F32) for _ in range(4)]
        kkeys = ["kT%d" % t for t in range(34)]
        add("pool", C("memset", kTz[0][64:128, :], 0.0), w=["kz0"])
        add("pool", C("memset", kTz[1][0:64, :], 0.0), w=["kz1"])
        add("act", C("copy", out=kTz[0][0:64, :], in_=kT[0:64, :]), r=kkeys, w=["kz0"])
        add("dve", C("tensor_copy", out=kTz[1][64:128, :], in_=kT[64:128, :]), r=kkeys, w=["kz1"])
        pSc = banks[0:4]
        pO = [banks[4], banks[5]]
        pTr = [banks[6].bitcast(BF16)[:, 0:512].rearrange("p (g q) -> p g q", g=4), banks[7].bitcast(BF16)[:, 0:512].rearrange("p (g q) -> p g q", g=4)]
        ctr = dict(sc=0, ex=0)
        its = [(i, kvh) for i in range(32) for kvh in range(2)]
        info = {}

        def chunks_of(i):
            ch = [(0, None), (1, None)]
            if i > 0:
                ch.append((2 + i - 1, 0))
            ch.append((2 + i, None))
            if i < 31:
                ch.append((2 + i + 1, 1))
            return ch

        def stage_sc(n):
            i, kvh = its[n]
            ebs = []
            for ci, (kt, mk) in enumerate(chunks_of(i)):
                sb_ = ctr["sc"] % 4
                ctr["sc"] += 1
                eb = ctr["ex"] % NEX
                ctr["ex"] += 1
                ebs.append(eb)
                add("pe", C("matmul", pSc[sb_][:, :], lhsT=kTz[kvh][:, kt * 128:(kt + 1) * 128], rhs=qT[:, :, i * 128:(i + 1) * 128], start=True, stop=True),
                    r=["kz%d" % kvh, "qT%d" % i], w=["pSc%d" % sb_])
                add("act", C("activation", out=expS[eb], in_=pSc[sb_][:, :], func=AF.Exp, scale=0.125, bias=negB), r=["pSc%d" % sb_, "av0"], w=["ex%d" % eb])
                if mk is not None:
                    add("pool", C("tensor_tensor", out=expS[eb], in0=expS[eb], in1=masks[:, mk, :], op=ALU.mult), r=["ex%d" % eb, "masks"], w=["ex%d" % eb])
            info[n] = ebs

        def stage_pv(n):
            i, kvh = its[n]
            ob = i % 2
            chunks = chunks_of(i)
            ebs = info.pop(n)
            po = pO[n % 2]
            pok = "pO%d" % (n % 2)
            dn = den[n % 4]
            dnk = "den%d" % (n % 4)
            pov = po[:, 0:260].rearrange("p (g c) -> p g c", g=4)
            for ci, (kt, mk) in enumerate(chunks):
                eb = ebs[ci]
                for g in range(4):
                    first = (ci == 0 and g == 0)
                    add("pe", C("matmul", pov[:, g, :], lhsT=expS[eb][:, g * 128:(g + 1) * 128], rhs=vaug[:, kt, kvh, :], start=first, stop=(ci == len(chunks) - 1), skip_group_check=True),
                        r=["ex%d" % eb, "v%d" % kt, "vones"], w=[pok])
            add("dve", C("tensor_tensor", out=dn, in0=pov[:, :, 64], in1=esink[:, kvh * 4:(kvh + 1) * 4], op=ALU.add), r=[pok, "av1"], w=[dnk])
            add("dve", C("reciprocal", out=dn, in_=dn), r=[dnk], w=[dnk])
            add("dve", C("tensor_tensor", out=On[ob][:, :, kvh, :], in0=pov[:, :, 0:64], in1=dn.unsqueeze(2).to_broadcast([128, 4, 64]), op=ALU.mult),
                r=[pok, dnk], w=["On%d_%d" % (ob, kvh)])
            if kvh == 1:
                for g in range(4):
                    add("pe", C("transpose", out=pTr[ob][:, g, :], in_=On[ob][:, g, :, :].rearrange("p h d -> p (h d)"), identity=ident),
                        r=["On%d_0" % ob, "On%d_1" % ob, "cm"], w=["pTr%d" % ob])
                add("act", C("copy", out=qT[:, :, i * 128:(i + 1) * 128], in_=pTr[ob]), r=["pTr%d" % ob], w=["qT%d" % i])

        LAG = 2
        for step in range(len(its) + LAG):
            if step < len(its):
                stage_sc(step)
            if step - LAG >= 0:
                stage_pv(step - LAG)

    def phase_C(s):
        WK.reset()
        xcb = WK.alloc([NTOK], BF16)
        abuf = WK.alloc([NTOK], F32)
        bbuf = [WK.alloc([NTOK], F32) for _ in range(2)]
        tbuf = WK.alloc([NTOK], BF16)
        trb = [WK.alloc([512], F32) for _ in range(2)]
        tib = [WK.alloc([512], F32) for _ in range(2)]
        ctr = dict(b=0)
        segs = [(0, NCTX), (NCTX, NTOK)]
        for c in range(4):
            xcf = bbuf[0]
            cw = lambda j: vecs[:, V_CONVW + c * 4 + j:V_CONVW + c * 4 + j + 1]
            for (lo, hi) in segs:
                add("dve", C("tensor_scalar", out=xcf[:, lo:hi], in0=xrS[:, c, lo:hi], scalar1=cw(2), scalar2=vecs[:, V_CONVB + c:V_CONVB + c + 1], op0=ALU.mult, op1=ALU.add),
                    r=["xr%d" % c, "vecs"], w=["xcf"])
                for j in (0, 1, 3):
                    o = j - 2
                    a0, a1 = max(lo, lo - o), min(hi, hi - o)
                    add("dve", C("scalar_tensor_tensor", out=xcf[:, a0:a1], in0=xrS[:, c, a0 + o:a1 + o], scalar=cw(j), in1=xcf[:, a0:a1], op0=ALU.mult, op1=ALU.add),
                        r=["xr%d" % c, "vecs", "xcf"], w=["xcf"])
            add("act", C("copy", out=xcb, in_=xcf), r=["xcf"], w=["xcb"])
            for d in range(2):
                bb = bbuf[d]
                bk = "b%d" % d
                wi_r = (d * 2 + 0) * 4 + c
                wi_i = (d * 2 + 1) * 4 + c
                col = d * 4 + c
                blocks = [(0, NCTX)] + [(NCTX + g * 512, NCTX + (g + 1) * 512) for g in range(8)]
                for (lo, hi) in blocks:
                    n = hi - lo
                    pb = ctr["b"] % 2
                    ctr["b"] += 1
                    pr, pi = banks[pb * 2], banks[pb * 2 + 1]
                    tr, ti = trb[pb], tib[pb]
                    add("pe", C("matmul", pr[:, 0:n], lhsT=lruW[:, wi_r, :], rhs=xcb[:, lo:hi], start=True, stop=True), r=["xcb", "lruW"], w=["pr%d" % pb])
                    add("pe", C("matmul", pi[:, 0:n], lhsT=lruW[:, wi_i, :], rhs=xcb[:, lo:hi], start=True, stop=True), r=["xcb", "lruW"], w=["pi%d" % pb])
                    add("act", C("activation", out=tr[:, 0:n], in_=pr[:, 0:n], func=AF.Tanh, scale=0.5, bias=lv[:, HBR + col:HBR + col + 1]), r=["pr%d" % pb, "lv0"], w=["tr%d" % pb])
                    add("act", C("activation", out=ti[:, 0:n], in_=pi[:, 0:n], func=AF.Tanh, scale=0.5, bias=lv[:, HBI + col:HBI + col + 1]), r=["pi%d" % pb, "lv0"], w=["ti%d" % pb])
                    add("act", C("activation", out=abuf[:, lo:hi], in_=tr[:, 0:n], func=AF.Exp, scale=lv[:, HCNEG + col:HCNEG + col + 1], bias=lv[:, HCNEG + col:HCNEG + col + 1]), r=["tr%d" % pb, "lv2"], w=["a"])
                    add("act", C("activation", out=bb[:, lo:hi], in_=tr[:, 0:n], func=AF.Exp, scale=lv[:, CNEG + col:CNEG + col + 1], bias=lv[:, CNEG + col:CNEG + col + 1]), r=["tr%d" % pb, "lv1"], w=[bk])
                    add("dve", C("scalar_tensor_tensor", out=tbuf[:, lo:hi], in0=ti[:, 0:n], scalar=1.0, in1=xcb[:, lo:hi], op0=ALU.add, op1=ALU.mult), r=["ti%d" % pb, "xcb"], w=["t"])
                add("act", C("activation", out=bb, in_=bb, func=AF.Sqrt, scale=-0.25, bias=0.25), r=[bk], w=[bk])
                add("dve", C("tensor_tensor", out=bb, in0=bb, in1=tbuf, op=ALU.mult), r=[bk, "t"], w=[bk])
                if d == 0:
                    add("dve", C("tensor_tensor_scan", out=bb[:, 0:NCTX], data0=abuf[:, 0:NCTX], data1=bb[:, 0:NCTX], initial=0.0, op0=ALU.mult, op1=ALU.add), r=[bk, "a"], w=[bk])
                    add("dve", C("tensor_tensor_scan", out=bb[:, NCTX:NTOK], data0=abuf[:, NCTX:NTOK], data1=bb[:, NCTX:NTOK], initial=bb[:, NCTX - 1:NCTX], op0=ALU.mult, op1=ALU.add), r=[bk, "a"], w=[bk])
                else:
                    add("dve", C("tensor_tensor_scan", out=bb[:, 0:NCTX][:, ::-1], data0=abuf[:, 0:NCTX][:, ::-1], data1=bb[:, 0:NCTX][:, ::-1], initial=0.0, op0=ALU.mult, op1=ALU.add), r=[bk, "a"], w=[bk])
                    add("dve", C("tensor_tensor_scan", out=bb[:, NCTX:NTOK][:, ::-1], data0=abuf[:, NCTX:NTOK][:, ::-1], data1=bb[:, NCTX:NTOK][:, ::-1], initial=bb[:, 0:1], op0=ALU.mult, op1=ALU.add), r=[bk, "a"], w=[bk])
            add("pool", C("tensor_tensor", out=bbuf[0][:, NCTX:NTOK], in0=bbuf[0][:, NCTX:NTOK], in1=bbuf[1][:, NCTX:NTOK], op=ALU.add), r=["b0", "b1"], w=["b0"])
            add("act", C("activation", out=tbuf[:, 0:SEQ], in_=grS[:, c, :], func=AF.Gelu_apprx_tanh), r=["gr%d" % c, "t"], w=["t"])
            add("dve", C("tensor_tensor", out=grS[:, c, :], in0=bbuf[0][:, NCTX:NTOK], in1=tbuf[:, 0:SEQ], op=ALU.mult), r=["b0", "t"], w=["gr%d" % c])

    def phase_D(s):
        WK.reset()
        wout = WK.alloc([8, D], BF16)
        g1bc = WK.alloc([D], F32)
        xt2 = [WK.alloc([D], F32) for _ in range(2)]
        x1 = [WK.alloc([D], F32) for _ in range(2)]
        xn2 = [WK.alloc([1056], BF16) for _ in range(2)]
        h2T = [WK.alloc([8, 128], BF16) for _ in range(2)]
        junk = WK.alloc([D], BF16)
        sm = [WK.alloc([24], F32) for _ in range(2)]
        ex = [WK.alloc([16], F32) for _ in range(2)]
        for h in range(2):
            add("pool", C("dma_start", out=wout[:, h * 4:(h + 1) * 4, :], in_=wout_v[:, h * 4:(h + 1) * 4, :]), w=["wout%d" % h], key="win%d" % h)
        add("sp", C("dma_start", out=g1bc, in_=gsc_d[s, 0, :].partition_broadcast(128)), w=["g1bc"], key="c0")
        py = [[banks[0], banks[1]], [banks[2], banks[3]]]
        pT2 = [banks[4].bitcast(BF16).rearrange("p (a b) -> p a b", a=8), banks[5].bitcast(BF16).rearrange("p (a b) -> p a b", a=8)]
        pl = [banks[6], banks[7]]

        def s1(tt):
            b = tt % 2
            cols = slice(tt * 128, (tt + 1) * 128)
            r0 = s * SEQ + tt * 128
            add("sp", C("dma_start", out=xt2[b], in_=x_d[r0:r0 + 128, :]), w=["xt2%d" % b], key="xt%d" % b)
            for half in range(2):
                for kc in range(8):
                    lhs = qT[:, kc, cols] if kc < 4 else grS[:, kc - 4, cols]
                    rk = "qT%d" % tt if kc < 4 else "gr%d" % (kc - 4)
                    add("pe", C("matmul", py[b][half][:, :], lhsT=lhs, rhs=wout[:, kc, half * 512:(half + 1) * 512], start=(kc == 0), stop=(kc == 7)),
                        r=[rk, "wout0", "wout1"], w=["py%d%d" % (b, half)])

        def s2(tt):
            b = tt % 2
            r0 = s * SEQ + tt * 128
            for half in range(2):
                hs = slice(half * 512, (half + 1) * 512)
                add("dve", C("tensor_tensor", out=x1[b][:, hs], in0=py[b][half][:, :], in1=g1bc[:, hs], op=ALU.mult), r=["py%d%d" % (b, half), "g1bc"], w=["x1%d_%d" % (b, half), "x1%d" % b])
            add("pool", C("tensor_tensor", out=x1[b], in0=x1[b], in1=xt2[b], op=ALU.add), r=["x1%d_0" % b, "x1%d_1" % b, "xt2%d" % b], w=["x1%d" % b])
            add("sp", C("dma_start", out=out_d[r0:r0 + 128, :], in_=x1[b]), r=["x1%d" % b], key="o%d" % b)
            add("act", C("activation", out=junk, in_=x1[b], func=AF.Square, accum_out=sm[b][:, 0:1]), r=["x1%d" % b], w=["junk", "sm%d" % b])

        def s3(tt):
            b = tt % 2
            add("dve", C("tensor_scalar", out=sm[b][:, 0:1], in0=sm[b][:, 0:1], scalar1=1.0 / D, scalar2=EPS, op0=ALU.mult, op1=ALU.add), r=["sm%d" % b], w=["sm%d" % b])
            add("act", C("activation", out=sm[b][:, 0:1], in_=sm[b][:, 0:1], func=AF.Sqrt), r=["sm%d" % b], w=["sm%d" % b])
            add("dve", C("reciprocal", out=sm[b][:, 0:1], in_=sm[b][:, 0:1]), r=["sm%d" % b], w=["sm%d" % b])
            add("dve", C("tensor_scalar", out=xn2[b][:, 0:D], in0=x1[b], scalar1=sm[b][:, 0:1], scalar2=None, op0=ALU.mult), r=["x1%d" % b, "sm%d" % b], w=["xn2%d" % b])

        def s4(tt):
            b = tt % 2
            for fc in range(8):
                add("pe", C("transpose", out=pT2[b][:, fc, :], in_=xn2[b][:, fc * 128:(fc + 1) * 128], identity=ident), r=["xn2%d" % b, "cm"], w=["pT2%d" % b])
            for fc in range(8):
                if b == 0:
                    add("act", C("activation", out=h2T[b][:, fc, :], in_=pT2[b][:, fc, :], func=AF.Identity, scale=A2[:, fc, s:s + 1], bias=B2[:, fc, s:s + 1]), r=["pT2%d" % b, "A2", "modv"], w=["h2T%d_%d" % (b, fc)])
                else:
                    add("dve", C("tensor_scalar", out=h2T[b][:, fc, :], in0=pT2[b][:, fc, :], scalar1=A2[:, fc, s:s + 1], scalar2=B2[:, fc, s:s + 1], op0=ALU.mult, op1=ALU.add), r=["pT2%d" % b, "A2", "modv"], w=["h2T%d_%d" % (b, fc)])
            for kc in range(8):
                add("pe", C("matmul", pl[b][:, 0:16], lhsT=h2T[b][:, kc, :], rhs=wr[:, kc, :], start=(kc == 0), stop=(kc == 7)), r=["h2T%d_%d" % (b, fc) for fc in range(8)] + ["wr"], w=["pl%d" % b])

        def s5(tt):
            b = tt % 2
            r0 = s * SEQ + tt * 128
            add("dve", C("tensor_reduce", out=sm[b][:, 1:2], in_=pl[b][:, 0:16], axis=AX.X, op=ALU.max), r=["pl%d" % b], w=["mx%d" % b])
            add("dve", C("tensor_scalar", out=sm[b][:, 1:2], in0=sm[b][:, 1:2], scalar1=-1.0, scalar2=None, op0=ALU.mult), r=["mx%d" % b], w=["mx%d" % b])
            add("act", C("activation", out=ex[b], in_=pl[b][:, 0:16], func=AF.Exp, bias=sm[b][:, 1:2], accum_out=sm[b][:, 2:3]), r=["pl%d" % b, "mx%d" % b], w=["ex%d" % b, "se%d" % b])
            add("dve", C("reciprocal", out=sm[b][:, 2:3], in_=sm[b][:, 2:3]), r=["se%d" % b], w=["se%d" % b])
            add("dve", C("tensor_scalar", out=Aall[:, tt, s * 16:(s + 1) * 16], in0=ex[b], scalar1=sm[b][:, 2:3], scalar2=None, op0=ALU.mult), r=["ex%d" % b, "se%d" % b], w=["Aall"])
            add("dve", C("tensor_copy", out=xn2[b][:, D:1056].bitcast(F32), in_=Aall[:, tt, s * 16:(s + 1) * 16]), r=["Aall"], w=["xn2a%d" % b])
            add("sp", C("dma_start", out=xn_d[r0:r0 + 128, :], in_=xn2[b]), r=["xn2%d" % b, "xn2a%d" % b], key="xn%d" % b)

        stages = [s1, s2, s3, s4, s5]
        NT = 32
        for step in range(NT + len(stages) - 1):
            for si in reversed(range(len(stages))):
                tt = step - si
                if 0 <= tt < NT:
                    stages[si](tt)

    def phase_topk():
        WK.reset()
        ST.reset()
        AT = ST.alloc([SEQ], F32)
        ones = ST.alloc([SEQ], F32)
        Mk = ST.alloc([SEQ], F32)
        cs = ST.alloc([SEQ], F32)
        bs = WK.alloc([8], F32)
        for tt in range(32):
            pb = banks[tt % 2]
            add("pe", C("transpose", out=pb[0:32, 0:128], in_=Aall[:, tt, :], identity=identf[:]), r=["Aall", "identf"], w=["pAT%d" % (tt % 2)])
            add("act", C("copy", out=AT[0:32, tt * 128:(tt + 1) * 128], in_=pb[0:32, 0:128]), r=["pAT%d" % (tt % 2)], w=["AT"])
        add("pool", C("memset", ones[0:32, :], 1.0), w=["ones"])
        mid, cntc, stp = bs[0:32, 0:1], bs[0:32, 1:2], bs[0:32, 2:3]
        add("dve", C("memset", mid, 0.5), w=["mid"])
        NIT = 24
        for k in range(NIT):
            wk = 2.0 ** -(k + 1)
            wn = 2.0 ** -(k + 2)
            add("dve", C("tensor_scalar", out=Mk[0:32, :], in0=AT[0:32, :], scalar1=mid, scalar2=None, op0=ALU.is_ge, op1=ALU.add, accum_out=cntc), r=["AT", "mid"], w=["Mk", "cnt"])
            add("dve", C("tensor_scalar", out=stp, in0=cntc, scalar1=float(CAP), scalar2=wk, op0=ALU.is_ge, op1=ALU.mult), r=["cnt"], w=["stp"])
            last = (k == NIT - 1)
            delta = (-wk) if last else (wn - wk)
            add("dve", C("scalar_tensor_tensor", out=mid, in0=stp, scalar=delta, in1=mid, op0=ALU.add, op1=ALU.add), r=["stp", "mid"], w=["mid"])
        add("dve", C("tensor_scalar", out=Mk[0:32, :], in0=AT[0:32, :], scalar1=mid, scalar2=None, op0=ALU.is_ge), r=["AT", "mid"], w=["Mk"])
        add("dve", C("tensor_tensor_scan", out=cs[0:32, :], data0=ones[0:32, :], data1=Mk[0:32, :], initial=0.0, op0=ALU.mult, op1=ALU.add), r=["ones", "Mk"], w=["cs"])
        add("dve", C("tensor_tensor", out=cs[0:32, :], in0=cs[0:32, :], in1=Mk[0:32, :], op=ALU.mult), r=["cs", "Mk"], w=["cs"])
        for thr in (129.0, 257.0, 385.0):
            add("dve", C("scalar_tensor_tensor", out=Mk[0:32, :], in0=cs[0:32, :], scalar=thr, in1=Mk[0:32, :], op0=ALU.is_ge, op1=ALU.add), r=["cs", "Mk"], w=["Mk"])
        add("dve", C("scalar_tensor_tensor", out=cs[0:32, :], in0=Mk[0:32, :], scalar=-128.0, in1=cs[0:32, :], op0=ALU.mult, op1=ALU.add), r=["Mk", "cs"], w=["cs"])
        PH = WK.alloc([32, 32], BF16)
        PL = WK.alloc([32, 32], BF16)
        Lb = [WK.alloc([32, 128], BF16) for _ in range(2)]
        Ht = [WK.alloc([32, 4, 2], BF16) for _ in range(2)]
        eq = [WK.alloc([32, 4], F32) for _ in range(2)]
        pph = [banks[2], banks[3]]
        for tt in range(32):
            pb = pph[tt % 2]
            add("pe", C("transpose", out=pb[:, 0:32], in_=Mk[0:32, tt * 128:(tt + 1) * 128], identity=identf[0:32, 0:32]), r=["Mk", "identf"], w=["pph%d" % (tt % 2)])
            add("pe", C("transpose", out=pb[:, 32:64], in_=cs[0:32, tt * 128:(tt + 1) * 128], identity=identf[0:32, 0:32]), r=["cs", "identf"], w=["pph%d" % (tt % 2)])
            add("act", C("copy", out=PH[:, tt, :], in_=pb[:, 0:32]), r=["pph%d" % (tt % 2)], w=["PH%d" % tt])
            add("act", C("copy", out=PL[:, tt, :], in_=pb[:, 32:64]), r=["pph%d" % (tt % 2)], w=["PL%d" % tt])
        pIdx = banks[4][:, 0:256].rearrange("p (a c k) -> p a c k", a=32, c=4)
        iop = cm[:, CM_IOP, :]
        ioc = cm[:, CM_IOC, 0:4]
        for tt in range(32):
            b = tt % 2
            add("dve", C("tensor_tensor", out=Lb[b], in0=iop.unsqueeze(1).to_broadcast([128, 32, 128]), in1=PL[:, tt, :].unsqueeze(2).to_broadcast([128, 32, 128]), op=ALU.is_equal), r=["PL%d" % tt, "cm"], w=["Lb%d" % b])
            add("dve", C("tensor_tensor", out=eq[b], in0=ioc.unsqueeze(1).to_broadcast([128, 32, 4]), in1=PH[:, tt, :].unsqueeze(2).to_broadcast([128, 32, 4]), op=ALU.is_equal), r=["PH%d" % tt, "cm"], w=["eq%d" % b])
            for k2 in range(2):
                add("dve", C("tensor_scalar", out=Ht[b][:, :, :, k2], in0=eq[b], scalar1=tcol[:, tt, k2:k2 + 1], scalar2=None, op0=ALU.mult), r=["eq%d" % b, "tcol"], w=["Ht%d_%d" % (b, k2)])
            for se in range(32):
                first = (tt == 0 and se == 0)
                add("pe", C("matmul", pIdx[:, se, :, :].rearrange("p c k -> p (c k)"), lhsT=Lb[b][:, se, :], rhs=Ht[b][:, se, :, :].rearrange("p c k -> p (c k)"), start=first, stop=(tt == 31), skip_group_check=True),
                    r=["Lb%d" % b, "Ht%d_0" % b, "Ht%d_1" % b], w=["pIdx"])
        idf = WK.alloc([32, 4], F32)
        add("dve", C("tensor_copy", out=idf, in_=pIdx[:, :, :, 1]), r=["pIdx"], w=["idf"])
        add("dve", C("scalar_tensor_tensor", out=idf, in0=pIdx[:, :, :, 0], scalar=64.0, in1=idf, op0=ALU.mult, op1=ALU.add), r=["pIdx", "idf"], w=["idf"])
        add("dve", C("tensor_scalar", out=idf[:, 16:32, :], in0=idf[:, 16:32, :], scalar1=float(SEQ), scalar2=None, op0=ALU.add), r=["idf"], w=["idf"])
        add("dve", C("tensor_copy", out=IDX[:], in_=idf), r=["idf"], w=["IDX"])
        if "idx" in dbg:
            add("sp", C("dma_start", out=dbg["idx"][:, :, :], in_=IDX[:]), r=["IDX"], key="dbg1")

    def phase_moe():
        ST.reset()
        WK.reset()
        wts = [[ST.alloc([8, D], BF16) for _ in range(3)] for _ in range(2)]
        g2bc = [ST.alloc([D], F32) for _ in range(2)]
        xgs = [[WK.alloc([1056], BF16) for _ in range(8)] for _ in range(2)]
        xsT = WK.alloc([8, 1024], BF16)
        hidT = WK.alloc([8, 1024], BF16)
        sg = [WK.alloc([512], BF16) for _ in range(2)]
        yo = [WK.alloc([D], F32) for _ in range(2)]
        for s in range(2):
            add("sp", C("dma_start", out=g2bc[s], in_=gsc_d[s, 1, :].partition_broadcast(128)), w=["g2bc%d" % s], key="c%d" % s)
        pX = [banks[0].bitcast(BF16).rearrange("p (a b) -> p a b", a=4), banks[1].bitcast(BF16).rearrange("p (a b) -> p a b", a=4)]
        pG = [banks[2], banks[3]]
        pU = [banks[4], banks[5]]
        pY = [banks[6], banks[7]]
        srcs = [wg_d, wu_d, wd_d]
        ctr = dict(y=0, gu=0)

        def load_w(e_):
            wb = e_ % 2
            for m in range(3):
                v = srcs[m][e_].rearrange("(kc p) n -> p kc n", p=128)
                for h in range(2):
                    add("pool", C("dma_start", out=wts[wb][m][:, h * 4:(h + 1) * 4, :], in_=v[:, h * 4:(h + 1) * 4, :]), w=["w%d_%d_%d" % (wb, m, h)], key="w%d%d" % (m, h))

        def gather(e_):
            gb_ = e_ % 2
            for s in range(2):
                for c in range(4):
                    st_ = s * 4 + c
                    add("pool", C("indirect_dma_start", out=xgs[gb_][st_], out_offset=None, in_=xn_d[:, :], in_offset=bass.IndirectOffsetOnAxis(ap=IDX[:, s * 16 + e_, c:c + 1], axis=0)),
                        r=["IDX"], w=["xg%d_%d" % (gb_, st_)], key="g%d_%d" % (gb_, st_))

        gather(0)
        load_w(0)
        for e_ in range(NEXP):
            wb = e_ % 2
            wgt, wut, wdt = wts[wb]
            xg = xgs[e_ % 2]
            xgk = lambda st_: "xg%d_%d" % (e_ % 2, st_)
            if e_ + 1 < NEXP:
                gather(e_ + 1)
                load_w(e_ + 1)
            for pair in range(4):
                s = pair // 2
                for hf in range(2):
                    pb = (pair * 2 + hf) % 2
                    for st2 in range(2):
                        st_ = pair * 2 + st2
                        for f4 in range(4):
                            fc = hf * 4 + f4
                            add("pe", C("transpose", out=pX[pb][:, f4, st2 * 128:(st2 + 1) * 128], in_=xg[st_][:, fc * 128:(fc + 1) * 128], identity=ident),
                                r=[xgk(st_), "cm"], w=["pX%d" % pb])
                    for f4 in range(4):
                        fc = hf * 4 + f4
                        dst = xsT[:, fc, pair * 256:(pair + 1) * 256]
                        if pb == 0:
                            add("act", C("activation", out=dst, in_=pX[pb][:, f4, :], func=AF.Identity, scale=A2[:, fc, s:s + 1], bias=B2[:, fc, s:s + 1]), r=["pX%d" % pb, "A2", "modv"], w=["xsT%d" % pair])
                        else:
                            add("dve", C("tensor_scalar", out=dst, in0=pX[pb][:, f4, :], scalar1=A2[:, fc, s:s + 1], scalar2=B2[:, fc, s:s + 1], op0=ALU.mult, op1=ALU.add), r=["pX%d" % pb, "A2", "modv"], w=["xsT%d" % pair])
            for f in range(8):
                for half in range(2):
                    gb = ctr["gu"] % 2
                    ctr["gu"] += 1
                    xk = ["xsT%d" % (half * 2), "xsT%d" % (half * 2 + 1)]
                    for kc in range(8):
                        add("pe", C("matmul", pG[gb][:, :], lhsT=wgt[:, kc, f * 128:(f + 1) * 128], rhs=xsT[:, kc, half * 512:(half + 1) * 512], start=(kc == 0), stop=(kc == 7)),
                            r=xk + ["w%d_0_0" % wb, "w%d_0_1" % wb], w=["pG%d" % gb])
                    for kc in range(8):
                        add("pe", C("matmul", pU[gb][:, :], lhsT=wut[:, kc, f * 128:(f + 1) * 128], rhs=xsT[:, kc, half * 512:(half + 1) * 512], start=(kc == 0), stop=(kc == 7)),
                            r=xk + ["w%d_1_0" % wb, "w%d_1_1" % wb], w=["pU%d" % gb])
                    add("act", C("activation", out=sg[gb], in_=pG[gb][:, :], func=AF.Silu), r=["pG%d" % gb], w=["sg%d" % gb])
                    add("dve", C("tensor_tensor", out=hidT[:, f, half * 512:(half + 1) * 512], in0=pU[gb][:, :], in1=sg[gb], op=ALU.mult), r=["pU%d" % gb, "sg%d" % gb], w=["hid%d" % half])
            for st_ in range(8):
                s, c = st_ // 4, st_ % 4
                for half in range(2):
                    for fk in range(8):
                        add("pe", C("matmul", pY[half][:, :], lhsT=hidT[:, fk, st_ * 128:(st_ + 1) * 128], rhs=wdt[:, fk, half * 512:(half + 1) * 512], start=(fk == 0), stop=(fk == 7)),
                            r=["hid%d" % s, "w%d_2_0" % wb, "w%d_2_1" % wb], w=["pY%d" % half])
                yb = ctr["y"] % 2
                ctr["y"] += 1
                gate = xg[st_][:, D:1056].bitcast(F32)[:, e_:e_ + 1]
                for half in range(2):
                    hs = slice(half * 512, (half + 1) * 512)
                    add("dve", C("scalar_tensor_tensor", out=yo[yb][:, hs], in0=pY[half][:, :], scalar=gate, in1=g2bc[s][:, hs], op0=ALU.mult, op1=ALU.mult),
                        r=["pY%d" % half, xgk(st_), "g2bc%d" % s], w=["yo%d_%d" % (yb, half)])
                prev = ["sc_%d_%d_%d" % (s, e_ - 1, cc) for cc in range(4)] if e_ > 0 else []
                add("pool", C("indirect_dma_start", out=out_d[:, :], out_offset=bass.IndirectOffsetOnAxis(ap=IDX[:, s * 16 + e_, c:c + 1], axis=0), in_=yo[yb], in_offset=None, compute_op=ALU.add, oob_is_err=True),
                    r=["yo%d_0" % yb, "yo%d_1" % yb, "IDX"] + prev, w=["sc_%d_%d_%d" % (s, e_, c)], key="sc%d" % st_)

    nsamp = 2
    stages = ["A", "B", "C", "D", "T", "full"]
    lvl = stages.index(stage) if stage in stages else -1
    if stage == "0":
        nsamp = 0
    if stage == "A1":
        nsamp = 1
    import os as _os
    if _os.environ.get("NSAMP"):
        nsamp = int(_os.environ["NSAMP"])
    for s in range(nsamp):
        phase_A(s)
        P.barrier()
        if stage == "A" and s == 0:
            add("sp", C("dma_start", out=dbg["qT"][:, :, :], in_=qT), key="dbg0")
            add("sp", C("dma_start", out=dbg["kT"][:, :], in_=kT), key="dbg1")
            add("sp", C("dma_start", out=dbg["v"][:, :, :, :], in_=vaug), key="dbg0")
            add("sp", C("dma_start", out=dbg["xr"][:, :, :], in_=xrS), key="dbg1")
            add("sp", C("dma_start", out=dbg["gr"][:, :, :], in_=grS), key="dbg0")
            P.barrier()
        if lvl >= 1:
            phase_B(s)
            P.barrier()
            if stage == "B" and s == 0:
                add("sp", C("dma_start", out=dbg["qT"][:, :, :], in_=qT), key="dbg0")
                P.barrier()
        if lvl >= 2:
            phase_C(s)
            P.barrier()
            if stage == "C" and s == 0:
                add("sp", C("dma_start", out=dbg["gr"][:, :, :], in_=grS), key="dbg0")
                add("sp", C("dma_start", out=dbg["qT"][:, :, :], in_=qT), key="dbg1")
                P.barrier()
        if lvl >= 3:
            phase_D(s)
            P.barrier()
    if lvl >= 4:
        phase_topk()
        P.barrier()
        if "aall" in dbg:
            add("sp", C("dma_start", out=dbg["aall"][:, :, :], in_=Aall[:]), key="dbg0")
    if lvl >= 5:
        phase_moe()
    P.barrier()
    P.emit()
    P.close()
    return nc, P.stats


def _swap_idx():
    d = np.arange(64)
    return np.where((d % 32) < 16, d + 16, d - 16)


def _consts():
    bf = ml_dtypes.bfloat16
    cm = np.zeros((128, NCM, 128), np.float32)
    cm[:, CM_ID, :] = np.eye(128)
    for h in range(2):
        cm[h * 64:(h + 1) * 64, CM_ONES, h * 64:(h + 1) * 64] = 1.0
    sw = _swap_idx()
    for m in range(128):
        k = (m // 64) * 64 + sw[m % 64]
        cm[k, CM_SWAP, m] = 1.0
    cm[:, CM_IOP, :] = (np.arange(128) - 127)[None, :]
    cm[:, CM_IOC, 0:4] = np.arange(1, 5)[None, :]
    identf = np.eye(128, dtype=np.float32)
    a = np.arange(128)
    prev = (a[None, :] <= a[:, None]).astype(np.float32)
    nxt = (a[:, None] <= a[None, :]).astype(np.float32)
    masks = np.stack([np.tile(prev, (1, 4)), np.tile(nxt, (1, 4))], axis=1)
    t = np.arange(SEQ)
    row = (t // 64).astype(np.float32)
    col = (t % 64).astype(np.float32)
    inv = (10000.0 ** (-np.arange(16, dtype=np.float32) / 16)).astype(np.float32)
    ar = (row[None, :] * inv[:, None]).astype(np.float32)
    ac = (col[None, :] * inv[:, None]).astype(np.float32)
    C = np.zeros((64, SEQ), np.float32)
    S2 = np.zeros((64, SEQ), np.float32)
    C[0:16] = np.cos(ar); C[16:32] = np.cos(ar); C[32:48] = np.cos(ac); C[48:64] = np.cos(ac)
    S2[0:16] = np.sin(ar); S2[16:32] = -np.sin(ar); S2[32:48] = np.sin(ac); S2[48:64] = -np.sin(ac)
    rope = np.stack([np.tile(C, (2, 1)), np.tile(S2, (2, 1))], axis=1)
    tok = np.arange(32)[None, :] * 128 + np.arange(128)[:, None]
    tcol = np.stack([tok // 64, tok % 64], axis=2).astype(np.float32)
    return dict(cm=cm.astype(bf), identf=identf, masks=masks.astype(bf), rope=rope.astype(bf), tcol=tcol)


def prep_inputs(inp):
    f = np.float32
    g = lambda k: np.asarray(inp[k], dtype=f)
    x, c, ctx, c_ctx = g("x"), g("c"), g("ctx"), g("c_ctx")
    L = 0
    w_in = g("w_in")[L]
    qperm = np.concatenate([np.r_[b * 64:(b + 1) * 64, (b + 4) * 64:(b + 5) * 64] for b in range(4)])
    w_in_p = np.ascontiguousarray(np.concatenate([w_in[:, qperm], w_in[:, 512:]], axis=1))
    w_out = g("w_out")[L]
    w_out_p = np.ascontiguousarray(np.concatenate([w_out[qperm, :], w_out[512:, :]], axis=0))
    b_ada = g("b_ada")[L]
    vecs = np.zeros((128, NV), f)
    pc = lambda v, n: v.reshape(n, 128).T
    vecs[:, V_N1G:V_N1G + 8] = pc(g("norm1_g")[L], 8)
    vecs[:, V_N2G:V_N2G + 8] = pc(g("norm2_g")[L], 8)
    vecs[:, V_BADA:V_BADA + 48] = pc(b_ada, 48)
    gq, gk = g("q_norm_g")[L], g("k_norm_g")[L]
    vecs[:, V_GQ] = np.tile(gq, 2)
    vecs[:, V_GK] = np.tile(gk, 2)
    cw = g("conv_w")[L]
    for cch in range(4):
        for j in range(4):
            vecs[:, V_CONVW + cch * 4 + j] = cw[j, cch * 128:(cch + 1) * 128]
    vecs[:, V_CONVB:V_CONVB + 4] = pc(g("conv_b")[L], 4)
    for d in range(2):
        vecs[:, V_BR + d * 4:V_BR + d * 4 + 4] = pc(g("lru_b_r")[L][d], 4)
        vecs[:, V_BI + d * 4:V_BI + d * 4 + 4] = pc(g("lru_b_i")[L][d], 4)
        vecs[:, V_LAM + d * 4:V_LAM + d * 4 + 4] = pc(g("lru_lambda")[L][d], 4)
    vecs[:, V_SINK:V_SINK + 8] = g("attn_sink")[L][None, :]
    vecs[:, V_GQROW:V_GQROW + 64] = gq[None, :]
    vecs[:, V_GKROW:V_GKROW + 64] = gk[None, :]
    lruw = np.zeros((128, 16, 128), f)
    for d in range(2):
        for gi, nm in enumerate(["lru_w_r", "lru_w_i"]):
            w = g(nm)[L][d]
            for cch in range(4):
                for nl in range(2):
                    lruw[nl * 64:(nl + 1) * 64, (d * 2 + gi) * 4 + cch, nl * 64:(nl + 1) * 64] = w[cch * 2 + nl]
    badag = np.stack([np.stack([b_ada[2 * D:3 * D], b_ada[5 * D:6 * D]])] * 3)
    wrt = np.ascontiguousarray(g("w_router")[L].reshape(8, 128, 16).transpose(1, 0, 2))
    consts = _consts()
    shared = dict(w_ada=g("w_ada")[L], badag=badag, vecs=vecs, w_in=w_in_p, lruw=lruw, w_out=w_out_p, w_router=wrt,
                  w_gate=g("w_gate")[L], w_up=g("w_up")[L], w_down=g("w_down")[L], **consts)
    maps = []
    for core in range(NCORES):
        s0 = 2 * core
        cc = np.stack([c[s0], c[s0 + 1], c_ctx])
        ccT = np.ascontiguousarray(cc.reshape(3, 8, 128).transpose(2, 1, 0))
        m = dict(shared)
        m["x"] = np.ascontiguousarray(x[s0:s0 + 2].reshape(2 * SEQ, D))
        m["ctx"] = np.ascontiguousarray(ctx[s0:s0 + 2].reshape(2 * NCTX, D))
        m["ccT"] = ccT
        maps.append(m)
    return maps


_CACHE = {}


def kernel(**inputs):
    maps = prep_inputs(inputs)
    if "nc" not in _CACHE:
        _CACHE["nc"] = build_program("full")[0]
    nc = _CACHE["nc"]
    res = run_bass_kernel_spmd(nc, maps, core_ids=list(range(NCORES)))
    outs = [np.asarray(r["out"]).reshape(2, SEQ, D) for r in res.results]
    return np.concatenate(outs, axis=0).astype(np.float32)
```

```python
import numpy as np
import ml_dtypes
import concourse.bass as bass
import concourse.mybir as mybir
from concourse.bass_utils import run_bass_kernel_spmd
from contextlib import ExitStack

AF = mybir.ActivationFunctionType
ALU = mybir.AluOpType
AX = mybir.AxisListType
F32 = mybir.dt.float32
BF16 = mybir.dt.bfloat16
I32 = mybir.dt.int32

NCORES = 8
SEQ = 4096
NCTX = 256
NTOK = SEQ + NCTX
D = 1024
EPS = 1e-6
NEXP = 16
CAP = 512


class Op:
    __slots__ = ("eng", "fn", "deps", "dma", "key", "signal", "ticket")

    def __init__(self, eng, fn, dma, key):
        self.eng = eng
        self.fn = fn
        self.dma = dma
        self.key = key
        self.deps = []
        self.signal = False
        self.ticket = 0


ENGS = ["pe", "act", "dve", "pool", "sp"]


def C(name, *a, **k):
    return (name, a, k)


class Prog:
    def __init__(self, nc):
        self.nc = nc
        self.ops = []
        self.last_w = {}
        self.readers = {}
        self.dma_last = {}
        self.last_eng = {}
        self.bank_last = {}
        self.stack = ExitStack()

    def sb(self, name, shape, dtype):
        return self.stack.enter_context(self.nc.sbuf_tensor("sb_" + name, list(shape), dtype))

    def ps(self, name, shape, dtype):
        return self.stack.enter_context(self.nc.psum_tensor("ps_" + name, list(shape), dtype))

    dead = False

    def cut(self, n):
        import os
        if int(os.environ.get("CUT", "-1")) == n:
            self.dead = True

    def add(self, eng, fn, r=(), w=(), key=None):
        op = Op(eng, fn, key is not None, key)
        if self.dead:
            return op
        deps = {}
        for k in r:
            o = self.last_w.get(k)
            if o is not None:
                deps[id(o)] = o
        for k in w:
            o = self.last_w.get(k)
            if o is not None:
                deps[id(o)] = o
            for o in self.readers.get(k, {}).values():
                deps[id(o)] = o
        if op.dma:
            o = self.dma_last.get(key)
            if o is not None:
                deps[id(o)] = o
            self.dma_last[key] = op
        nm_, a_, k_ = fn
        for v in list(a_) + list(k_.values()):
            tn = getattr(getattr(v, "tensor", None), "name", "")
            if tn.startswith("ps_bank"):
                o = self.bank_last.get(tn)
                if o is not None and o.eng != eng:
                    deps[id(o)] = o
                self.bank_last[tn] = op
        for k in r:
            self.readers.setdefault(k, {})[(eng, key)] = op
        for k in w:
            self.last_w[k] = op
            self.readers[k] = {}
        for o in deps.values():
            if o is op:
                continue
            if (not o.dma) and (not op.dma) and o.eng == "pe" and op.eng == "pe":
                continue
            op.deps.append(o)
            o.signal = True
        if not op.dma:
            self.last_eng[eng] = op
        self.ops.append(op)
        return op

    def barrier(self):
        prev = [o for o in self.last_eng.values()] + list(self.dma_last.values())
        new = []
        for e in ENGS:
            op = Op(e, C("nop", ), False, None)
            for o in prev:
                if o.eng == e and not o.dma and e == "pe":
                    continue
                op.deps.append(o)
                o.signal = True
            self.ops.append(op)
            new.append(op)
        for op in new:
            self.last_eng[op.eng] = op
        self.last_w = {}
        self.readers = {}

    def emit(self):
        nc = self.nc
        cnt = {e: 0 for e in ENGS}
        dcnt = {}
        for op in self.ops:
            if op.dma:
                dcnt[op.key] = dcnt.get(op.key, 0) + 16
                op.ticket = dcnt[op.key]
            elif op.signal:
                cnt[op.eng] += 1
                op.ticket = cnt[op.eng]
        sems = {}
        for e in ENGS:
            if cnt[e]:
                sems[("c", e)] = self.stack.enter_context(nc.semaphore("s_" + e))
        for k in dcnt:
            sems[("d", k)] = self.stack.enter_context(nc.semaphore("d_" + str(k)))

        def semof(o):
            return sems[("d", o.key)] if o.dma else sems[("c", o.eng)]

        per = {e: [] for e in ENGS}
        for op in self.ops:
            per[op.eng].append(op)

        def run(name, eng):
            seen = {}
            for op in per[name]:
                need = {}
                for d in op.deps:
                    s = semof(d)
                    if need.get(id(s), (None, 0))[1] < d.ticket:
                        need[id(s)] = (s, d.ticket)
                for kk, (s, t) in need.items():
                    if seen.get(kk, 0) >= t:
                        continue
                    eng.wait_ge(s, t)
                    seen[kk] = t
                nm, a_, k_ = op.fn
                try:
                    ins = getattr(eng, nm)(*a_, **k_)
                except Exception:
                    print("EMIT FAIL at", per[name].index(op), "of", len(per[name]), "ndma_before", sum(1 for o in per[name][:per[name].index(op)] if o.dma), flush=True)
                    print("EMIT FAIL", name, nm, [repr(x)[:300] for x in a_], {kk: repr(vv)[:300] for kk, vv in k_.items()}, flush=True)
                    raise
                if op.dma:
                    ins.then_inc(semof(op), 16)
                elif op.signal:
                    ins.then_inc(semof(op), 1)

        with nc.Block() as block:
            @block.tensor
            def _(e):
                run("pe", e)

            @block.scalar
            def _(e):
                run("act", e)

            @block.vector
            def _(e):
                run("dve", e)

            @block.gpsimd
            def _(e):
                run("pool", e)

            @block.sync
            def _(e):
                run("sp", e)
        self.stats = dict(n={e: len(per[e]) for e in ENGS}, sig=cnt, nsems=len(sems))

    def close(self):
        self.stack.close()


class Arena:
    def __init__(self, P, name, nbytes):
        self.t = P.sb(name, [128, nbytes // 2], BF16)
        self.cap = nbytes
        self.off = 0

    def alloc(self, shape, dtype):
        esz = 2 if dtype == BF16 else 4
        n = int(np.prod(shape))
        nb = n * esz
        start = self.off
        self.off += (nb + 31) // 32 * 32
        assert self.off <= self.cap, ("arena overflow", self.off, self.cap)
        ap = self.t[:, start // 2:start // 2 + nb // 2]
        if dtype != BF16:
            ap = ap.bitcast(dtype)
        if len(shape) == 2:
            ap = ap.rearrange("p (a b) -> p a b", a=shape[0])
        elif len(shape) == 3:
            ap = ap.rearrange("p (a b c) -> p a b c", a=shape[0], b=shape[1])
        elif len(shape) == 4:
            ap = ap.rearrange("p (a b c d) -> p a b c d", a=shape[0], b=shape[1], c=shape[2])
        return ap

    def reset(self, off=0):
        self.off = off


V_N1G, V_N2G, V_BADA, V_GQ, V_GK, V_CONVW, V_CONVB, V_BR, V_BI, V_LAM = 0, 8, 16, 64, 65, 66, 82, 86, 94, 102
V_SINK, V_GQROW, V_GKROW, NV = 110, 118, 182, 246
CM_ID, CM_ONES, CM_SWAP, CM_IOP, CM_IOC, NCM = 0, 1, 2, 3, 4, 5


def build_program(stage="full"):
    nc = bass.Bass("TRN2", target_bir_lowering=False)

    def din(name, shape, dt=F32):
        return nc.dram_tensor(name, list(shape), dt, kind="ExternalInput").ap()

    x_d = din("x", [2 * SEQ, D])
    ctx_d = din("ctx", [2 * NCTX, D])
    ccT_d = din("ccT", [128, 8, 3])
    wada_d = din("w_ada", [D, 6 * D])
    badag_d = din("badag", [3, 2, D])
    vecs_d = din("vecs", [128, NV])
    cm_d = din("cm", [128, NCM, 128], BF16)
    identf_d = din("identf", [128, 128])
    masks_d = din("masks", [128, 2, 512], BF16)
    rope_d = din("rope", [128, 2, SEQ], BF16)
    tcol_d = din("tcol", [128, 32, 2])
    win_d = din("w_in", [D, 1792])
    lruw_d = din("lruw", [128, 16, 128])
    wout_d = din("w_out", [D, D])
    wr_d = din("w_router", [128, 8, 16])
    wg_d = din("w_gate", [NEXP, D, D])
    wu_d = din("w_up", [NEXP, D, D])
    wd_d = din("w_down", [NEXP, D, D])
    out_d = nc.dram_tensor("out", [2 * SEQ, D], F32, kind="ExternalOutput").ap()
    xn_d = nc.dram_tensor("xn_scr", [2 * SEQ, 1056], BF16, kind="Internal").ap()
    gsc_d = nc.dram_tensor("g_scr", [3, 2, D], F32, kind="Internal").ap()
    dbg = {}
    if stage != "full":
        dbg["qT"] = nc.dram_tensor("dbg_qT", [128, 4, SEQ], BF16, kind="ExternalOutput").ap()
        dbg["kT"] = nc.dram_tensor("dbg_kT", [128, NTOK], BF16, kind="ExternalOutput").ap()
        dbg["v"] = nc.dram_tensor("dbg_v", [128, 34, 2, 65], BF16, kind="ExternalOutput").ap()
        dbg["xr"] = nc.dram_tensor("dbg_xr", [128, 4, NTOK], BF16, kind="ExternalOutput").ap()
        dbg["gr"] = nc.dram_tensor("dbg_gr", [128, 4, SEQ], BF16, kind="ExternalOutput").ap()
        dbg["modv"] = nc.dram_tensor("dbg_modv", [128, 48, 3], F32, kind="ExternalOutput").ap()
        dbg["aall"] = nc.dram_tensor("dbg_aall", [128, 32, 32], F32, kind="ExternalOutput").ap()
        dbg["idx"] = nc.dram_tensor("dbg_idx", [128, 32, 4], I32, kind="ExternalOutput").ap()

    P = Prog(nc)
    add = P.add

    vecs = P.sb("vecs", [128, NV], F32)
    cm = P.sb("cm", [128, NCM, 128], BF16)
    identf = P.sb("identf", [128, 128], F32)
    masks = P.sb("masks", [128, 2, 512], BF16)
    lruW = P.sb("lruW", [128, 16, 128], BF16)
    wr = P.sb("wr", [128, 8, 16], BF16)
    tcol = P.sb("tcol", [128, 32, 2], F32)
    modv = P.sb("modv", [128, 48, 3], F32)
    A1 = P.sb("A1", [128, 8, 3], F32)
    A2 = P.sb("A2", [128, 8, 3], F32)
    lv = P.sb("lv", [128, 40], F32)
    av = P.sb("av", [128, 16], F32)
    Aall = P.sb("Aall", [128, 32, 32], F32)
    IDX = P.sb("IDX", [128, 32, 4], I32)
    ident = cm[:, CM_ID, :]
    onesblk = cm[:, CM_ONES, :]
    pswap = cm[:, CM_SWAP, :]
    B1 = modv[:, 0:8, :]
    B2 = modv[:, 24:32, :]
    HBR, HBI, CNEG, HCNEG = 0, 8, 16, 24
    negB = av[:, 0:1]
    esink = av[:, 1:9]

    banks = [P.ps("bank%d" % i, [128, 512], F32) for i in range(8)]
    ST = Arena(P, "ST", 116 * 1024)
    WK = Arena(P, "WK", 77 * 1024)

    def ld(eng, out, in_, w, key):
        return add(eng, C("dma_start", out=out, in_=in_), w=w, key=key)

    ld("sp", vecs[:], vecs_d[:, :], ["vecs"], "c0")
    ld("sp", cm[:], cm_d[:, :, :], ["cm"], "c1")
    ld("sp", identf[:], identf_d[:, :], ["identf"], "c2")
    ld("sp", masks[:], masks_d[:, :, :], ["masks"], "c3")
    ld("sp", tcol[:], tcol_d[:, :, :], ["tcol"], "c0")
    ld("pool", lruW[:], lruw_d[:, :, :], ["lruW"], "c4")
    ld("pool", wr[:], wr_d[:, :, :], ["wr"], "c5")
    ccT = WK.alloc([8, 3], F32)
    scT = WK.alloc([8, 3], F32)
    badag = WK.alloc([2, D], F32)
    grow = WK.alloc([2, D], F32)
    wa = [WK.alloc([8, 512], F32) for _ in range(2)]
    tmpv = WK.alloc([64], F32)
    ld("sp", ccT, ccT_d[:, :, :], ["ccT"], "c2")
    ld("sp", badag[0:3], badag_d[:, :, :], ["badag"], "c3")
    add("act", C("activation", out=scT, in_=ccT, func=AF.Silu), r=["ccT"], w=["scT"])
    pm = banks[0][:, 0:192].rearrange("p (a b) -> p a b", a=48)
    prow = banks[1]
    wada_v = wada_d.rearrange("(kc p) n -> p kc n", p=128)
    P.cut(1)
    pc = 0
    for m in range(6):
        for half in range(2):
            b = pc % 2
            pc += 1
            c0 = m * D + half * 512
            ld("sp", wa[b], wada_v[:, :, c0:c0 + 512], ["wa%d" % b], "wa%d" % b)
            for jj in range(4):
                j = m * 8 + half * 4 + jj
                for kc in range(8):
                    add("pe", C("matmul", pm[:, j, 0:3], lhsT=wa[b][:, kc, jj * 128:(jj + 1) * 128], rhs=scT[:, kc, :], start=(kc == 0), stop=(kc == 7)),
                        r=["wa%d" % b, "scT"], w=["pm"])
            if m in (2, 5):
                which = 0 if m == 2 else 1
                for kc in range(8):
                    add("pe", C("matmul", prow[0:3, :], lhsT=scT[:, kc, :], rhs=wa[b][:, kc, :], start=(kc == 0), stop=(kc == 7)),
                        r=["wa%d" % b, "scT"], w=["prow"])
                add("dve", C("tensor_tensor", out=grow[0:3, which, half * 512:(half + 1) * 512], in0=prow[0:3, :], in1=badag[0:3, which, half * 512:(half + 1) * 512], op=ALU.add),
                    r=["prow", "badag"], w=["grow"])
    P.cut(2)
    add("sp", C("dma_start", out=gsc_d[:, :, :], in_=grow[0:3]), r=["grow"], w=["gsc"], key="c1")
    P.cut(3)
    add("dve", C("tensor_tensor", out=modv[:], in0=pm[:, :, 0:3], in1=vecs[:, V_BADA:V_BADA + 48].unsqueeze(2).to_broadcast([128, 48, 3]), op=ALU.add),
        r=["pm", "vecs"], w=["modv"])
    add("dve", C("scalar_tensor_tensor", out=A1[:], in0=modv[:, 8:16, :], scalar=1.0, in1=vecs[:, V_N1G:V_N1G + 8].unsqueeze(2).to_broadcast([128, 8, 3]), op0=ALU.add, op1=ALU.mult),
        r=["modv", "vecs"], w=["A1"])
    add("dve", C("scalar_tensor_tensor", out=A2[:], in0=modv[:, 32:40, :], scalar=1.0, in1=vecs[:, V_N2G:V_N2G + 8].unsqueeze(2).to_broadcast([128, 8, 3]), op0=ALU.add, op1=ALU.mult),
        r=["modv", "vecs"], w=["A2"])
    P.cut(4)
    add("dve", C("tensor_scalar", out=lv[:, HBR:HBR + 16], in0=vecs[:, V_BR:V_BR + 16], scalar1=0.5, scalar2=None, op0=ALU.mult), r=["vecs"], w=["lv0"])
    add("act", C("activation", out=tmpv[:, 0:8], in_=vecs[:, V_LAM:V_LAM + 8], func=AF.Exp, scale=-1.0), r=["vecs"], w=["tmpv"])
    add("act", C("activation", out=tmpv[:, 8:16], in_=tmpv[:, 0:8], func=AF.Ln, bias=1.0), r=["tmpv"], w=["tmpv2"])
    add("dve", C("tensor_scalar", out=lv[:, CNEG:CNEG + 8], in0=tmpv[:, 8:16], scalar1=-8.0, scalar2=None, op0=ALU.mult), r=["tmpv2"], w=["lv1"])
    add("dve", C("tensor_scalar", out=lv[:, HCNEG:HCNEG + 8], in0=tmpv[:, 8:16], scalar1=-4.0, scalar2=None, op0=ALU.mult), r=["tmpv2"], w=["lv2"])
    P.cut(5)
    add("dve", C("tensor_reduce", out=tmpv[:, 16:17], in_=vecs[:, V_GQROW:V_GQROW + 64], axis=AX.X, op=ALU.max, apply_absolute_value=True), r=["vecs"], w=["tmpv3"])
    add("dve", C("tensor_reduce", out=tmpv[:, 17:18], in_=vecs[:, V_GKROW:V_GKROW + 64], axis=AX.X, op=ALU.max, apply_absolute_value=True), r=["vecs"], w=["tmpv4"])
    add("dve", C("scalar_tensor_tensor", out=av[:, 0:1], in0=tmpv[:, 16:17], scalar=-8.0, in1=tmpv[:, 17:18], op0=ALU.mult, op1=ALU.mult), r=["tmpv3", "tmpv4"], w=["av0"])
    add("act", C("activation", out=av[:, 1:9], in_=vecs[:, V_SINK:V_SINK + 8], func=AF.Exp, bias=av[:, 0:1]), r=["vecs", "av0"], w=["av1"])
    P.cut(6)
    if "modv" in dbg:
        add("sp", C("dma_start", out=dbg["modv"][:, :, :], in_=modv[:]), r=["modv"], key="dbg0")
    P.barrier()
    P.cut(7)

    ST.reset()
    qT = ST.alloc([4, SEQ], BF16)
    kT = ST.alloc([NTOK], BF16)
    vaug = ST.alloc([34, 2, 65], BF16)
    xrS = ST.alloc([4, NTOK], BF16)
    grS = ST.alloc([4, SEQ], BF16)
    add("pool", C("memset", vaug[:, :, :, 64:65], 1.0), w=["vones"])
    P.cut(8)
    P.barrier()
    P.cut(9)

    win_v = win_d.rearrange("(kc p) n -> p kc n", p=128)
    wout_v = wout_d.rearrange("(kc p) n -> p kc n", p=128)

    def phase_A(s):
        WK.reset()
        win = WK.alloc([8, 1792], BF16)
        ropeb = [WK.alloc([2, 512], BF16) for _ in range(2)]
        xt = [WK.alloc([D], F32) for _ in range(3)]
        xn = WK.alloc([4, D], BF16)
        hT = WK.alloc([8, 512], BF16)
        NSET = 3
        tq = [[WK.alloc([512], BF16) for _ in range(3)] for _ in range(NSET)]
        rstd = [WK.alloc([512], F32) for _ in range(NSET)]
        ssb = WK.alloc([8], F32)
        for h in range(2):
            add("pool", C("dma_start", out=win[:, h * 4:(h + 1) * 4, :], in_=win_v[:, h * 4:(h + 1) * 4, :]), w=["win%d" % h], key="win%d" % h)
        add("pool", C("memset", vaug[:, :, :, 64:65], 1.0), w=["vones"])
        pT = [banks[i].bitcast(BF16).rearrange("p (a b) -> p a b", a=2) for i in range(4)]
        accs = [(banks[4], 4), (banks[5], 5), (banks[0], 0), (banks[1], 1), (banks[2], 2), (banks[3], 3)]
        pS = banks[6]
        pR = banks[7]
        st = dict(xh=0, grp=0, acc=0, tq=0, ev=0)

        def group(rows_ap, ntiles, is_ctx, col0, tile0, lat0):
            ntok = ntiles * 128
            scol = 2 if is_ctx else s
            gi = st["grp"]
            st["grp"] += 1
            hb = gi % 2
            if not is_ctx:
                add("sp", C("dma_start", out=ropeb[hb], in_=rope_d[:, :, lat0:lat0 + 512]), w=["rope%d" % hb], key="rope%d" % hb)
            for t in range(ntiles):
                xb = st["xh"] % 3
                st["xh"] += 1
                add("sp", C("dma_start", out=xt[xb], in_=rows_ap[t * 128:(t + 1) * 128, :]), w=["xt%d" % xb], key="xt%d" % xb)
                sc_ = ssb[:, t:t + 1]
                add("act", C("activation", out=xn[:, t, :], in_=xt[xb], func=AF.Square, accum_out=sc_), r=["xt%d" % xb], w=["xn%d" % t, "ms%d" % t])
                add("act", C("activation", out=sc_, in_=sc_, func=AF.Ln, scale=1.0 / D, bias=EPS), r=["ms%d" % t], w=["ms%d" % t])
                add("act", C("activation", out=sc_, in_=sc_, func=AF.Exp, scale=-0.5), r=["ms%d" % t], w=["ms%d" % t])
                add("dve", C("tensor_scalar", out=xn[:, t, :], in0=xt[xb], scalar1=sc_, scalar2=None, op0=ALU.mult), r=["xt%d" % xb, "ms%d" % t], w=["xn%d" % t])
            for fc in range(8):
                bk = fc % 4
                for t in range(ntiles):
                    add("pe", C("transpose", out=pT[bk][:, fc // 4, t * 128:(t + 1) * 128], in_=xn[:, t, fc * 128:(fc + 1) * 128], identity=ident),
                        r=["xn%d" % t, "cm"], w=["pT%d" % fc, "acc%d" % bk])
                if bk % 2 == 0:
                    add("act", C("activation", out=hT[:, fc, 0:ntok], in_=pT[bk][:, fc // 4, 0:ntok], func=AF.Identity, scale=A1[:, fc, scol:scol + 1], bias=B1[:, fc, scol:scol + 1]),
                        r=["pT%d" % fc, "A1", "modv"], w=["hT%d" % fc])
                else:
                    add("dve", C("tensor_scalar", out=hT[:, fc, 0:ntok], in0=pT[bk][:, fc // 4, 0:ntok], scalar1=A1[:, fc, scol:scol + 1], scalar2=B1[:, fc, scol:scol + 1], op0=ALU.mult, op1=ALU.add),
                        r=["pT%d" % fc, "A1", "modv"], w=["hT%d" % fc])
            hkeys = ["hT%d" % fc for fc in range(8)]
            cols = {"k": 512, "v": 640}
            for b in range(4):
                cols["q%d" % b] = b * 128
                cols["x%d" % b] = 768 + b * 128
                cols["g%d" % b] = 1280 + b * 128
            if is_ctx:
                order = ["k", "v", "x0", "x1", "x2", "x3"]
            else:
                order = ["k", "v", "q0", "x0", "q1", "x1", "q2", "x2", "q3", "x3", "g0", "g1", "g2", "g3"]
            pend = []

            def evac_copy(dst, src_, rk, wk):
                st["ev"] += 1
                if st["ev"] % 2 == 0:
                    add("act", C("copy", out=dst, in_=src_), r=rk, w=wk)
                else:
                    add("dve", C("tensor_copy", out=dst, in_=src_), r=rk, w=wk)

            def qk_stage1(name, acc, akey, ts):
                sq, qg, qs = tq[ts]
                qc = sq
                rs = rstd[ts]
                gcol = vecs[:, V_GK:V_GK + 1] if name == "k" else vecs[:, V_GQ:V_GQ + 1]
                add("pe", C("matmul", pS[:, 0:ntok], lhsT=onesblk, rhs=sq[:, 0:ntok], start=True, stop=True), r=["sq%d" % ts, "cm"], w=["pS"])
                add("act", C("activation", out=rs[:, 0:ntok], in_=pS[:, 0:ntok], func=AF.Ln, scale=1.0 / 64, bias=EPS), r=["pS"], w=["rs%d" % ts])
                add("act", C("activation", out=rs[:, 0:ntok], in_=rs[:, 0:ntok], func=AF.Exp, scale=-0.5), r=["rs%d" % ts], w=["rs%d" % ts])
                if is_ctx:
                    add("dve", C("scalar_tensor_tensor", out=kT[:, col0:col0 + ntok], in0=acc[:, 0:ntok], scalar=gcol, in1=rs[:, 0:ntok], op0=ALU.mult, op1=ALU.mult),
                        r=[akey, "rs%d" % ts, "vecs"], w=["kT%d" % (col0 // 128 + t) for t in range(ntiles)])
                    return
                add("dve", C("scalar_tensor_tensor", out=qg[:, 0:ntok], in0=acc[:, 0:ntok], scalar=gcol, in1=rs[:, 0:ntok], op0=ALU.mult, op1=ALU.mult),
                    r=[akey, "rs%d" % ts, "vecs"], w=["qg%d" % ts])
                add("dve", C("tensor_tensor", out=qc[:, 0:ntok], in0=qg[:, 0:ntok], in1=ropeb[hb][:, 0, 0:ntok], op=ALU.mult), r=["qg%d" % ts, "rope%d" % hb, "sq%d" % ts], w=["sq%d" % ts])
                add("dve", C("tensor_tensor", out=qs[:, 0:ntok], in0=qg[:, 0:ntok], in1=ropeb[hb][:, 1, 0:ntok], op=ALU.mult), r=["qg%d" % ts, "rope%d" % hb], w=["qs%d" % ts])

            def qk_stage2(name, ts):
                sq, qg, qs = tq[ts]
                qc = sq
                add("pe", C("matmul", pR[:, 0:ntok], lhsT=ident, rhs=qc[:, 0:ntok], start=True, stop=False), r=["sq%d" % ts, "cm"], w=["pR"])
                add("pe", C("matmul", pR[:, 0:ntok], lhsT=pswap, rhs=qs[:, 0:ntok], start=False, stop=True), r=["qs%d" % ts, "cm"], w=["pR"])
                if name == "k":
                    add("act", C("copy", out=kT[:, col0:col0 + ntok], in_=pR[:, 0:ntok]), r=["pR"], w=["kT%d" % (col0 // 128 + t) for t in range(ntiles)])
                else:
                    b = int(name[1])
                    add("act", C("copy", out=qT[:, b, lat0:lat0 + ntok], in_=pR[:, 0:ntok]), r=["pR"], w=["qT%d" % (lat0 // 128 + t) for t in range(ntiles)])

            def run_pending(upto):
                keep = []
                for due, fn_ in pend:
                    if due <= upto:
                        fn_()
                    else:
                        keep.append((due, fn_))
                pend[:] = keep

            for bi, name in enumerate(order):
                c0 = cols[name]
                acc, bno = accs[st["acc"] % len(accs)]
                st["acc"] += 1
                akey = "acc%d" % bno
                wk = [akey] + (["pT%d" % bno, "pT%d" % (bno + 4)] if bno < 4 else [])
                if name == "v":
                    accv = acc.rearrange("p (t c) -> p t c", t=4)
                    for t in range(ntiles):
                        for kc in range(8):
                            add("pe", C("matmul", accv[:, t, :], lhsT=hT[:, kc, t * 128:(t + 1) * 128], rhs=win[:, kc, 640:768], start=(kc == 0), stop=(kc == 7)),
                                r=hkeys + ["win0", "win1"], w=wk)
                    add("dve", C("tensor_copy", out=vaug[:, tile0:tile0 + ntiles, :, 0:64], in_=accv[:, 0:ntiles, :].rearrange("p t (h d) -> p t h d", h=2)),
                        r=[akey], w=["v%d" % (tile0 + t) for t in range(ntiles)])
                else:
                    for kc in range(8):
                        add("pe", C("matmul", acc[:, 0:ntok], lhsT=win[:, kc, c0:c0 + 128], rhs=hT[:, kc, 0:ntok], start=(kc == 0), stop=(kc == 7)),
                            r=hkeys + ["win0", "win1"], w=wk)
                    if name[0] == "x":
                        c = int(name[1])
                        evac_copy(xrS[:, c, col0:col0 + ntok], acc[:, 0:ntok], [akey], ["xr%d" % c])
                    elif name[0] == "g":
                        c = int(name[1])
                        evac_copy(grS[:, c, lat0:lat0 + ntok], acc[:, 0:ntok], [akey], ["gr%d" % c])
                    else:
                        ts = st["tq"] % NSET
                        st["tq"] += 1
                        add("act", C("activation", out=tq[ts][0][:, 0:ntok], in_=acc[:, 0:ntok], func=AF.Square), r=[akey], w=["sq%d" % ts])
                        pend.append((bi + 1, (lambda name=name, acc=acc, akey=akey, ts=ts: qk_stage1(name, acc, akey, ts))))
                        if not is_ctx:
                            pend.append((bi + 3, (lambda name=name, ts=ts: qk_stage2(name, ts))))
                run_pending(bi)
            run_pending(10 ** 9)

        group(ctx_d[s * NCTX:(s + 1) * NCTX, :], 2, True, 0, 0, 0)
        for g in range(8):
            group(x_d[s * SEQ + g * 512:s * SEQ + (g + 1) * 512, :], 4, False, NCTX + g * 512, 2 + g * 4, g * 512)

    def phase_B(s):
        WK.reset()
        kTz = [WK.alloc([NTOK], BF16) for _ in range(2)]
        NEX = 16
        expS = [WK.alloc([512], BF16) for _ in range(NEX)]
        On = [WK.alloc([4, 2, 64], BF16) for _ in range(2)]
        den = [WK.alloc([4], F32) for _ in range(4)]
        kkeys = ["kT%d" % t for t in range(34)]
        add("pool", C("memset", kTz[0][64:128, :], 0.0), w=["kz0"])
        add("pool", C("memset", kTz[1][0:64, :], 0.0), w=["kz1"])
        add("act", C("copy", out=kTz[0][0:64, :], in_=kT[0:64, :]), r=kkeys, w=["kz0"])
        add("dve", C("tensor_copy", out=kTz[1][64:128, :], in_=kT[64:128, :]), r=kkeys, w=["kz1"])
        pSc = banks[0:4]
        pO = [banks[4], banks[5]]
        pTr = [banks[6].bitcast(BF16)[:, 0:512].rearrange("p (g q) -> p g q", g=4), banks[7].bitcast(BF16)[:, 0:512].rearrange("p (g q) -> p g q", g=4)]
        ctr = dict(sc=0, ex=0)
        its = [(i, kvh) for i in range(32) for kvh in range(2)]
        info = {}

        def chunks_of(i):
            ch = [(0, None), (1, None)]
            if i > 0:
                ch.append((2 + i - 1, 0))
            ch.append((2 + i, None))
            if i < 31:
                ch.append((2 + i + 1, 1))
            return ch

        def stage_sc(n):
            i, kvh = its[n]
            ebs = []
            for ci, (kt, mk) in enumerate(chunks_of(i)):
                sb_ = ctr["sc"] % 4
                ctr["sc"] += 1
                eb = ctr["ex"] % NEX
                ctr["ex"] += 1
                ebs.append(eb)
                add("pe", C("matmul", pSc[sb_][:, :], lhsT=kTz[kvh][:, kt * 128:(kt + 1) * 128], rhs=qT[:, :, i * 128:(i + 1) * 128], start=True, stop=True),
                    r=["kz%d" % kvh, "qT%d" % i], w=["pSc%d" % sb_])
                add("act", C("activation", out=expS[eb], in_=pSc[sb_][:, :], func=AF.Exp, scale=0.125, bias=negB), r=["pSc%d" % sb_, "av0"], w=["ex%d" % eb])
                if mk is not None:
                    add("pool", C("tensor_tensor", out=expS[eb], in0=expS[eb], in1=masks[:, mk, :], op=ALU.mult), r=["ex%d" % eb, "masks"], w=["ex%d" % eb])
            info[n] = ebs

        def stage_pv(n):
            i, kvh = its[n]
            ob = i % 2
            chunks = chunks_of(i)
            ebs = info.pop(n)
            po = pO[n % 2]
            pok = "pO%d" % (n % 2)
            dn = den[n % 4]
            dnk = "den%d" % (n % 4)
            pov = po[:, 0:260].rearrange("p (g c) -> p g c", g=4)
            for ci, (kt, mk) in enumerate(chunks):
                eb = ebs[ci]
                for g in range(4):
                    first = (ci == 0 and g == 0)
                    add("pe", C("matmul", pov[:, g, :], lhsT=expS[eb][:, g * 128:(g + 1) * 128], rhs=vaug[:, kt, kvh, :], start=first, stop=(ci == len(chunks) - 1), skip_group_check=True),
                        r=["ex%d" % eb, "v%d" % kt, "vones"], w=[pok])
            add("dve", C("tensor_tensor", out=dn, in0=pov[:, :, 64], in1=esink[:, kvh * 4:(kvh + 1) * 4], op=ALU.add), r=[pok, "av1"], w=[dnk])
            add("dve", C("reciprocal", out=dn, in_=dn), r=[dnk], w=[dnk])
            add("dve", C("tensor_tensor", out=On[ob][:, :, kvh, :], in0=pov[:, :, 0:64], in1=dn.unsqueeze(2).to_broadcast([128, 4, 64]), op=ALU.mult),
                r=[pok, dnk], w=["On%d_%d" % (ob, kvh)])
            if kvh == 1:
                for g in range(4):
                    add("pe", C("transpose", out=pTr[ob][:, g, :], in_=On[ob][:, g, :, :].rearrange("p h d -> p (h d)"), identity=ident),
                        r=["On%d_0" % ob, "On%d_1" % ob, "cm"], w=["pTr%d" % ob])
                add("act", C("copy", out=qT[:, :, i * 128:(i + 1) * 128], in_=pTr[ob]), r=["pTr%d" % ob], w=["qT%d" % i])

        LAG = 2
        for step in range(len(its) + LAG):
            if step < len(its):
                stage_sc(step)
            if step - LAG >= 0:
                stage_pv(step - LAG)

    def phase_C(s):
        WK.reset()
        xcb = WK.alloc([NTOK], BF16)
        abuf0 = WK.alloc([NTOK], F32)
        abuf1 = ST.t[:, 16384:16384 + 8704].bitcast(F32)
        abufs = [abuf0, abuf1]
        bbuf = [WK.alloc([NTOK], F32) for _ in range(2)]
        tbuf = WK.alloc([NTOK], BF16)
        trb = [WK.alloc([512], F32) for _ in range(2)]
        tib = [WK.alloc([512], F32) for _ in range(2)]
        ctr = dict(b=0)
        segs = [(0, NCTX), (NCTX, NTOK)]
        for c in range(4):
            xcf = bbuf[0]
            cw = lambda j: vecs[:, V_CONVW + c * 4 + j:V_CONVW + c * 4 + j + 1]
            for (lo, hi) in segs:
                add("dve", C("tensor_scalar", out=xcf[:, lo:hi], in0=xrS[:, c, lo:hi], scalar1=cw(2), scalar2=vecs[:, V_CONVB + c:V_CONVB + c + 1], op0=ALU.mult, op1=ALU.add),
                    r=["xr%d" % c, "vecs"], w=["xcf"])
                for j in (0, 1, 3):
                    o = j - 2
                    a0, a1 = max(lo, lo - o), min(hi, hi - o)
                    add("dve", C("scalar_tensor_tensor", out=xcf[:, a0:a1], in0=xrS[:, c, a0 + o:a1 + o], scalar=cw(j), in1=xcf[:, a0:a1], op0=ALU.mult, op1=ALU.add),
                        r=["xr%d" % c, "vecs", "xcf"], w=["xcf"])
            add("act", C("copy", out=xcb, in_=xcf), r=["xcf"], w=["xcb"])
            for d in range(2):
                bb = bbuf[d]
                bk = "b%d" % d
                abuf = abufs[d]
                ak = "a%d" % d
                wi_r = (d * 2 + 0) * 4 + c
                wi_i = (d * 2 + 1) * 4 + c
                col = d * 4 + c
                blocks = [(0, NCTX)] + [(NCTX + g * 512, NCTX + (g + 1) * 512) for g in range(8)]
                for (lo, hi) in blocks:
                    n = hi - lo
                    pb = ctr["b"] % 2
                    ctr["b"] += 1
                    pr, pi = banks[pb * 2], banks[pb * 2 + 1]
                    tr, ti = trb[pb], tib[pb]
                    add("pe", C("matmul", pr[:, 0:n], lhsT=lruW[:, wi_r, :], rhs=xcb[:, lo:hi], start=True, stop=True), r=["xcb", "lruW"], w=["pr%d" % pb])
                    add("pe", C("matmul", pi[:, 0:n], lhsT=lruW[:, wi_i, :], rhs=xcb[:, lo:hi], start=True, stop=True), r=["xcb", "lruW"], w=["pi%d" % pb])
                    add("act", C("activation", out=tr[:, 0:n], in_=pr[:, 0:n], func=AF.Tanh, scale=0.5, bias=lv[:, HBR + col:HBR + col + 1]), r=["pr%d" % pb, "lv0"], w=["tr%d" % pb])
                    add("act", C("activation", out=ti[:, 0:n], in_=pi[:, 0:n], func=AF.Tanh, scale=0.5, bias=lv[:, HBI + col:HBI + col + 1]), r=["pi%d" % pb, "lv0"], w=["ti%d" % pb])
                    add("act", C("activation", out=abuf[:, lo:hi], in_=tr[:, 0:n], func=AF.Exp, scale=lv[:, HCNEG + col:HCNEG + col + 1], bias=lv[:, HCNEG + col:HCNEG + col + 1]), r=["tr%d" % pb, "lv2"], w=[ak])
                    add("act", C("activation", out=bb[:, lo:hi], in_=tr[:, 0:n], func=AF.Exp, scale=lv[:, CNEG + col:CNEG + col + 1], bias=lv[:, CNEG + col:CNEG + col + 1]), r=["tr%d" % pb, "lv1"], w=[bk])
                    add("dve", C("scalar_tensor_tensor", out=tbuf[:, lo:hi], in0=ti[:, 0:n], scalar=1.0, in1=xcb[:, lo:hi], op0=ALU.add, op1=ALU.mult), r=["ti%d" % pb, "xcb"], w=["t"])
                add("act", C("activation", out=bb, in_=bb, func=AF.Sqrt, scale=-0.25, bias=0.25), r=[bk], w=[bk])
                add("dve", C("tensor_tensor", out=bb, in0=bb, in1=tbuf, op=ALU.mult), r=[bk, "t"], w=[bk])
                if d == 0:
                    add("dve", C("tensor_tensor_scan", out=bb[:, 0:NCTX], data0=abuf[:, 0:NCTX], data1=bb[:, 0:NCTX], initial=0.0, op0=ALU.mult, op1=ALU.add), r=[bk, ak], w=[bk])
                    add("dve", C("tensor_tensor_scan", out=bb[:, NCTX:NTOK], data0=abuf[:, NCTX:NTOK], data1=bb[:, NCTX:NTOK], initial=bb[:, NCTX - 1:NCTX], op0=ALU.mult, op1=ALU.add), r=[bk, ak], w=[bk])
                else:
                    add("dve", C("tensor_tensor_scan", out=bb[:, 0:NCTX][:, ::-1], data0=abuf[:, 0:NCTX][:, ::-1], data1=bb[:, 0:NCTX][:, ::-1], initial=0.0, op0=ALU.mult, op1=ALU.add), r=[bk, ak], w=[bk])
                    add("dve", C("tensor_tensor_scan", out=bb[:, NCTX:NTOK][:, ::-1], data0=abuf[:, NCTX:NTOK][:, ::-1], data1=bb[:, NCTX:NTOK][:, ::-1], initial=bb[:, 0:1], op0=ALU.mult, op1=ALU.add), r=[bk, ak], w=[bk])
            add("pool", C("tensor_tensor", out=bbuf[0][:, NCTX:NTOK], in0=bbuf[0][:, NCTX:NTOK], in1=bbuf[1][:, NCTX:NTOK], op=ALU.add), r=["b0", "b1"], w=["b0"])
            add("act", C("activation", out=tbuf[:, 0:SEQ], in_=grS[:, c, :], func=AF.Gelu_apprx_tanh), r=["gr%d" % c, "t"], w=["t"])
            add("dve", C("tensor_tensor", out=grS[:, c, :], in0=bbuf[0][:, NCTX:NTOK], in1=tbuf[:, 0:SEQ], op=ALU.mult), r=["b0", "t"], w=["gr%d" % c])

    def phase_D(s):
        WK.reset()
        wout = WK.alloc([8, D], BF16)
        g1bc = WK.alloc([D], F32)
        xt2 = [WK.alloc([D], F32) for _ in range(2)]
        x1 = [WK.alloc([D], F32) for _ in range(2)]
        xn2 = [WK.alloc([1056], BF16) for _ in range(2)]
        h2T = [WK.alloc([8, 128], BF16) for _ in range(2)]
        junk = WK.alloc([D], BF16)
        sm = [WK.alloc([24], F32) for _ in range(2)]
        ex = [WK.alloc([16], F32) for _ in range(2)]
        for h in range(2):
            add("pool", C("dma_start", out=wout[:, h * 4:(h + 1) * 4, :], in_=wout_v[:, h * 4:(h + 1) * 4, :]), w=["wout%d" % h], key="win%d" % h)
        add("sp", C("dma_start", out=g1bc, in_=gsc_d[s, 0, :].partition_broadcast(128)), w=["g1bc"], key="c0")
        for kc in range(8):
            add("dve", C("tensor_tensor", out=wout[:, kc, :], in0=wout[:, kc, :], in1=g1bc, op=ALU.mult), r=["wout0", "wout1", "g1bc"], w=["wout%d" % (kc // 4)])
        py = [[banks[0], banks[1]], [banks[2], banks[3]]]
        pT2 = [banks[4].bitcast(BF16).rearrange("p (a b) -> p a b", a=8), banks[5].bitcast(BF16).rearrange("p (a b) -> p a b", a=8)]
        pl = [banks[6], banks[7]]

        def s1(tt):
            b = tt % 2
            cols = slice(tt * 128, (tt + 1) * 128)
            r0 = s * SEQ + tt * 128
            add("sp", C("dma_start", out=xt2[b], in_=x_d[r0:r0 + 128, :]), w=["xt2%d" % b], key="xt%d" % b)
            for half in range(2):
                for kc in range(8):
                    lhs = qT[:, kc, cols] if kc < 4 else grS[:, kc - 4, cols]
                    rk = "qT%d" % tt if kc < 4 else "gr%d" % (kc - 4)
                    add("pe", C("matmul", py[b][half][:, :], lhsT=lhs, rhs=wout[:, kc, half * 512:(half + 1) * 512], start=(kc == 0), stop=(kc == 7)),
                        r=[rk, "wout0", "wout1"], w=["py%d%d" % (b, half)])

        def s2(tt):
            b = tt % 2
            r0 = s * SEQ + tt * 128
            for half in range(2):
                hs = slice(half * 512, (half + 1) * 512)
                add("dve", C("tensor_tensor", out=x1[b][:, hs], in0=py[b][half][:, :], in1=xt2[b][:, hs], op=ALU.add), r=["py%d%d" % (b, half), "xt2%d" % b], w=["x1%d" % b])
            add("sp", C("dma_start", out=out_d[r0:r0 + 128, :], in_=x1[b]), r=["x1%d" % b], key="o%d" % b)
            add("act", C("activation", out=junk, in_=x1[b], func=AF.Square, accum_out=sm[b][:, 0:1]), r=["x1%d" % b], w=["junk", "sm%d" % b])

        def s3(tt):
            b = tt % 2
            add("act", C("activation", out=sm[b][:, 0:1], in_=sm[b][:, 0:1], func=AF.Ln, scale=1.0 / D, bias=EPS), r=["sm%d" % b], w=["sm%d" % b])
            add("act", C("activation", out=sm[b][:, 0:1], in_=sm[b][:, 0:1], func=AF.Exp, scale=-0.5), r=["sm%d" % b], w=["sm%d" % b])
            add("dve", C("tensor_scalar", out=xn2[b][:, 0:D], in0=x1[b], scalar1=sm[b][:, 0:1], scalar2=None, op0=ALU.mult), r=["x1%d" % b, "sm%d" % b], w=["xn2%d" % b])

        def s4(tt):
            b = tt % 2
            for fc in range(8):
                add("pe", C("transpose", out=pT2[b][:, fc, :], in_=xn2[b][:, fc * 128:(fc + 1) * 128], identity=ident), r=["xn2%d" % b, "cm"], w=["pT2%d" % b])
            for fc in range(8):
                if b == 0:
                    add("act", C("activation", out=h2T[b][:, fc, :], in_=pT2[b][:, fc, :], func=AF.Identity, scale=A2[:, fc, s:s + 1], bias=B2[:, fc, s:s + 1]), r=["pT2%d" % b, "A2", "modv"], w=["h2T%d_%d" % (b, fc)])
                else:
                    add("dve", C("tensor_scalar", out=h2T[b][:, fc, :], in0=pT2[b][:, fc, :], scalar1=A2[:, fc, s:s + 1], scalar2=B2[:, fc, s:s + 1], op0=ALU.mult, op1=ALU.add), r=["pT2%d" % b, "A2", "modv"], w=["h2T%d_%d" % (b, fc)])
            for kc in range(8):
                add("pe", C("matmul", pl[b][:, 0:16], lhsT=h2T[b][:, kc, :], rhs=wr[:, kc, :], start=(kc == 0), stop=(kc == 7)), r=["h2T%d_%d" % (b, fc) for fc in range(8)] + ["wr"], w=["pl%d" % b])

        def s5(tt):
            b = tt % 2
            r0 = s * SEQ + tt * 128
            add("dve", C("tensor_reduce", out=sm[b][:, 1:2], in_=pl[b][:, 0:16], axis=AX.X, op=ALU.max), r=["pl%d" % b], w=["mx%d" % b])
            add("dve", C("tensor_scalar", out=sm[b][:, 1:2], in0=sm[b][:, 1:2], scalar1=-1.0, scalar2=None, op0=ALU.mult), r=["mx%d" % b], w=["mx%d" % b])
            add("act", C("activation", out=ex[b], in_=pl[b][:, 0:16], func=AF.Exp, bias=sm[b][:, 1:2], accum_out=sm[b][:, 2:3]), r=["pl%d" % b, "mx%d" % b], w=["ex%d" % b, "se%d" % b])
            add("dve", C("reciprocal", out=sm[b][:, 2:3], in_=sm[b][:, 2:3]), r=["se%d" % b], w=["se%d" % b])
            add("dve", C("tensor_scalar", out=Aall[:, tt, s * 16:(s + 1) * 16], in0=ex[b], scalar1=sm[b][:, 2:3], scalar2=None, op0=ALU.mult), r=["ex%d" % b, "se%d" % b], w=["Aall"])
            add("dve", C("tensor_copy", out=xn2[b][:, D:1056].bitcast(F32), in_=Aall[:, tt, s * 16:(s + 1) * 16]), r=["Aall"], w=["xn2a%d" % b])
            add("sp", C("dma_start", out=xn_d[r0:r0 + 128, :], in_=xn2[b]), r=["xn2%d" % b, "xn2a%d" % b], key="xn%d" % b)

        stages = [s1, s2, s3, s4, s5]
        NT = 32
        for step in range(NT + len(stages) - 1):
            for si in reversed(range(len(stages))):
                tt = step - si
                if 0 <= tt < NT:
                    stages[si](tt)

    def phase_topk():
        WK.reset()
        ST.reset()
        AT = ST.alloc([SEQ], F32)
        ones = ST.alloc([SEQ], F32)
        Mk = ST.alloc([SEQ], F32)
        cs = ST.alloc([SEQ], F32)
        bs = WK.alloc([8], F32)
        for tt in range(32):
            pb = banks[tt % 2]
            add("pe", C("transpose", out=pb[0:32, 0:128], in_=Aall[:, tt, :], identity=identf[:]), r=["Aall", "identf"], w=["pAT%d" % (tt % 2)])
            add("act", C("copy", out=AT[0:32, tt * 128:(tt + 1) * 128], in_=pb[0:32, 0:128]), r=["pAT%d" % (tt % 2)], w=["AT"])
        add("pool", C("memset", ones[0:32, :], 1.0), w=["ones"])
        mid, cntc, stp = bs[0:32, 0:1], bs[0:32, 1:2], bs[0:32, 2:3]
        add("dve", C("memset", mid, 0.5), w=["mid"])
        NIT = 24
        for k in range(NIT):
            wk = 2.0 ** -(k + 1)
            wn = 2.0 ** -(k + 2)
            add("dve", C("tensor_scalar", out=Mk[0:32, :], in0=AT[0:32, :], scalar1=mid, scalar2=None, op0=ALU.is_ge, op1=ALU.add, accum_out=cntc), r=["AT", "mid"], w=["Mk", "cnt"])
            add("dve", C("tensor_scalar", out=stp, in0=cntc, scalar1=float(CAP), scalar2=wk, op0=ALU.is_ge, op1=ALU.mult), r=["cnt"], w=["stp"])
            last = (k == NIT - 1)
            delta = (-wk) if last else (wn - wk)
            add("dve", C("scalar_tensor_tensor", out=mid, in0=stp, scalar=delta, in1=mid, op0=ALU.add, op1=ALU.add), r=["stp", "mid"], w=["mid"])
        add("dve", C("tensor_scalar", out=Mk[0:32, :], in0=AT[0:32, :], scalar1=mid, scalar2=None, op0=ALU.is_ge), r=["AT", "mid"], w=["Mk"])
        add("dve", C("tensor_tensor_scan", out=cs[0:32, :], data0=ones[0:32, :], data1=Mk[0:32, :], initial=0.0, op0=ALU.mult, op1=ALU.add), r=["ones", "Mk"], w=["cs"])
        add("dve", C("tensor_tensor", out=cs[0:32, :], in0=cs[0:32, :], in1=Mk[0:32, :], op=ALU.mult), r=["cs", "Mk"], w=["cs"])
        for thr in (129.0, 257.0, 385.0):
            add("dve", C("scalar_tensor_tensor", out=Mk[0:32, :], in0=cs[0:32, :], scalar=thr, in1=Mk[0:32, :], op0=ALU.is_ge, op1=ALU.add), r=["cs", "Mk"], w=["Mk"])
        add("dve", C("scalar_tensor_tensor", out=cs[0:32, :], in0=Mk[0:32, :], scalar=-128.0, in1=cs[0:32, :], op0=ALU.mult, op1=ALU.add), r=["Mk", "cs"], w=["cs"])
        PH = WK.alloc([32, 32], BF16)
        PL = WK.alloc([32, 32], BF16)
        Lb = [WK.alloc([32, 128], BF16) for _ in range(2)]
        Ht = [WK.alloc([32, 4, 2], BF16) for _ in range(2)]
        eq = [WK.alloc([32, 4], F32) for _ in range(2)]
        pph = [banks[2], banks[3]]
        for tt in range(32):
            pb = pph[tt % 2]
            add("pe", C("transpose", out=pb[:, 0:32], in_=Mk[0:32, tt * 128:(tt + 1) * 128], identity=identf[0:32, 0:32]), r=["Mk", "identf"], w=["pph%d" % (tt % 2)])
            add("pe", C("transpose", out=pb[:, 32:64], in_=cs[0:32, tt * 128:(tt + 1) * 128], identity=identf[0:32, 0:32]), r=["cs", "identf"], w=["pph%d" % (tt % 2)])
            add("act", C("copy", out=PH[:, tt, :], in_=pb[:, 0:32]), r=["pph%d" % (tt % 2)], w=["PH%d" % tt])
            add("act", C("copy", out=PL[:, tt, :], in_=pb[:, 32:64]), r=["pph%d" % (tt % 2)], w=["PL%d" % tt])
        pIdx = banks[4][:, 0:256].rearrange("p (a c k) -> p a c k", a=32, c=4)
        iop = cm[:, CM_IOP, :]
        ioc = cm[:, CM_IOC, 0:4]
        for tt in range(32):
            b = tt % 2
            add("dve", C("tensor_tensor", out=Lb[b], in0=iop.unsqueeze(1).to_broadcast([128, 32, 128]), in1=PL[:, tt, :].unsqueeze(2).to_broadcast([128, 32, 128]), op=ALU.is_equal), r=["PL%d" % tt, "cm"], w=["Lb%d" % b])
            add("dve", C("tensor_tensor", out=eq[b], in0=ioc.unsqueeze(1).to_broadcast([128, 32, 4]), in1=PH[:, tt, :].unsqueeze(2).to_broadcast([128, 32, 4]), op=ALU.is_equal), r=["PH%d" % tt, "cm"], w=["eq%d" % b])
            for k2 in range(2):
                add("dve", C("tensor_scalar", out=Ht[b][:, :, :, k2], in0=eq[b], scalar1=tcol[:, tt, k2:k2 + 1], scalar2=None, op0=ALU.mult), r=["eq%d" % b, "tcol"], w=["Ht%d_%d" % (b, k2)])
            for se in range(32):
                first = (tt == 0 and se == 0)
                add("pe", C("matmul", pIdx[:, se, :, :].rearrange("p c k -> p (c k)"), lhsT=Lb[b][:, se, :], rhs=Ht[b][:, se, :, :].rearrange("p c k -> p (c k)"), start=first, stop=(tt == 31), skip_group_check=True),
                    r=["Lb%d" % b, "Ht%d_0" % b, "Ht%d_1" % b], w=["pIdx"])
        idf = WK.alloc([32, 4], F32)
        add("dve", C("tensor_copy", out=idf, in_=pIdx[:, :, :, 1]), r=["pIdx"], w=["idf"])
        add("dve", C("scalar_tensor_tensor", out=idf, in0=pIdx[:, :, :, 0], scalar=64.0, in1=idf, op0=ALU.mult, op1=ALU.add), r=["pIdx", "idf"], w=["idf"])
        add("dve", C("tensor_scalar", out=idf[:, 16:32, :], in0=idf[:, 16:32, :], scalar1=float(SEQ), scalar2=None, op0=ALU.add), r=["idf"], w=["idf"])
        add("dve", C("tensor_copy", out=IDX[:], in_=idf), r=["idf"], w=["IDX"])
        if "idx" in dbg:
            add("sp", C("dma_start", out=dbg["idx"][:, :, :], in_=IDX[:]), r=["IDX"], key="dbg1")

    def phase_moe():
        ST.reset()
        WK.reset()
        wts = [[ST.alloc([8, D], BF16) for _ in range(3)] for _ in range(2)]
        g2bc = [ST.alloc([D], F32) for _ in range(2)]
        xgs = [[WK.alloc([1056], BF16) for _ in range(8)] for _ in range(2)]
        xsT = WK.alloc([8, 1024], BF16)
        hidT = WK.alloc([8, 1024], BF16)
        sg = [WK.alloc([512], BF16) for _ in range(2)]
        yo = [WK.alloc([D], F32) for _ in range(2)]
        for s in range(2):
            add("sp", C("dma_start", out=g2bc[s], in_=gsc_d[s, 1, :].partition_broadcast(128)), w=["g2bc%d" % s], key="c%d" % s)
        pX = [banks[0].bitcast(BF16).rearrange("p (a b) -> p a b", a=4), banks[1].bitcast(BF16).rearrange("p (a b) -> p a b", a=4)]
        pG = [banks[2], banks[3]]
        pU = [banks[4], banks[5]]
        pY = [banks[6], banks[7]]
        srcs = [wg_d, wu_d, wd_d]
        ctr = dict(y=0, gu=0)

        def load_w(e_):
            wb = e_ % 2
            for m in range(3):
                v = srcs[m][e_].rearrange("(kc p) n -> p kc n", p=128)
                for h in range(2):
                    add("pool", C("dma_start", out=wts[wb][m][:, h * 4:(h + 1) * 4, :], in_=v[:, h * 4:(h + 1) * 4, :]), w=["w%d_%d_%d" % (wb, m, h)], key="w%d%d" % (m, h))

        def gather(e_):
            gb_ = e_ % 2
            for s in range(2):
                for c in range(4):
                    st_ = s * 4 + c
                    add("pool", C("indirect_dma_start", out=xgs[gb_][st_], out_offset=None, in_=xn_d[:, :], in_offset=bass.IndirectOffsetOnAxis(ap=IDX[:, s * 16 + e_, c:c + 1], axis=0)),
                        r=["IDX"], w=["xg%d_%d" % (gb_, st_)], key="g%d_%d" % (gb_, st_))

        gather(0)
        load_w(0)
        for e_ in range(NEXP):
            wb = e_ % 2
            wgt, wut, wdt = wts[wb]
            xg = xgs[e_ % 2]
            xgk = lambda st_: "xg%d_%d" % (e_ % 2, st_)
            if e_ + 1 < NEXP:
                gather(e_ + 1)
                load_w(e_ + 1)
            for pair in range(4):
                s = pair // 2
                for hf in range(2):
                    pb = (pair * 2 + hf) % 2
                    for st2 in range(2):
                        st_ = pair * 2 + st2
                        for f4 in range(4):
                            fc = hf * 4 + f4
                            add("pe", C("transpose", out=pX[pb][:, f4, st2 * 128:(st2 + 1) * 128], in_=xg[st_][:, fc * 128:(fc + 1) * 128], identity=ident),
                                r=[xgk(st_), "cm"], w=["pX%d" % pb])
                    for f4 in range(4):
                        fc = hf * 4 + f4
                        dst = xsT[:, fc, pair * 256:(pair + 1) * 256]
                        if pb == 0:
                            add("act", C("activation", out=dst, in_=pX[pb][:, f4, :], func=AF.Identity, scale=A2[:, fc, s:s + 1], bias=B2[:, fc, s:s + 1]), r=["pX%d" % pb, "A2", "modv"], w=["xsT%d" % pair])
                        else:
                            add("dve", C("tensor_scalar", out=dst, in0=pX[pb][:, f4, :], scalar1=A2[:, fc, s:s + 1], scalar2=B2[:, fc, s:s + 1], op0=ALU.mult, op1=ALU.add), r=["pX%d" % pb, "A2", "modv"], w=["xsT%d" % pair])
            for f in range(8):
                for half in range(2):
                    gb = ctr["gu"] % 2
                    ctr["gu"] += 1
                    xk = ["xsT%d" % (half * 2), "xsT%d" % (half * 2 + 1)]
                    for kc in range(8):
                        add("pe", C("matmul", pG[gb][:, :], lhsT=wgt[:, kc, f * 128:(f + 1) * 128], rhs=xsT[:, kc, half * 512:(half + 1) * 512], start=(kc == 0), stop=(kc == 7)),
                            r=xk + ["w%d_0_0" % wb, "w%d_0_1" % wb], w=["pG%d" % gb])
                    for kc in range(8):
                        add("pe", C("matmul", pU[gb][:, :], lhsT=wut[:, kc, f * 128:(f + 1) * 128], rhs=xsT[:, kc, half * 512:(half + 1) * 512], start=(kc == 0), stop=(kc == 7)),
                            r=xk + ["w%d_1_0" % wb, "w%d_1_1" % wb], w=["pU%d" % gb])
                    add("act", C("activation", out=sg[gb], in_=pG[gb][:, :], func=AF.Silu), r=["pG%d" % gb], w=["sg%d" % gb])
                    add("dve", C("tensor_tensor", out=hidT[:, f, half * 512:(half + 1) * 512], in0=pU[gb][:, :], in1=sg[gb], op=ALU.mult), r=["pU%d" % gb, "sg%d" % gb], w=["hid%d" % half])
            for st_ in range(8):
                s, c = st_ // 4, st_ % 4
                for half in range(2):
                    for fk in range(8):
                        add("pe", C("matmul", pY[half][:, :], lhsT=hidT[:, fk, st_ * 128:(st_ + 1) * 128], rhs=wdt[:, fk, half * 512:(half + 1) * 512], start=(fk == 0), stop=(fk == 7)),
                            r=["hid%d" % s, "w%d_2_0" % wb, "w%d_2_1" % wb], w=["pY%d" % half])
                yb = ctr["y"] % 2
                ctr["y"] += 1
                gate = xg[st_][:, D:1056].bitcast(F32)[:, e_:e_ + 1]
                for half in range(2):
                    hs = slice(half * 512, (half + 1) * 512)
                    add("dve", C("scalar_tensor_tensor", out=yo[yb][:, hs], in0=pY[half][:, :], scalar=gate, in1=g2bc[s][:, hs], op0=ALU.mult, op1=ALU.mult),
                        r=["pY%d" % half, xgk(st_), "g2bc%d" % s], w=["yo%d_%d" % (yb, half)])
                prev = ["sc_%d_%d_%d" % (s, e_ - 1, cc) for cc in range(4)] if e_ > 0 else []
                add("pool", C("indirect_dma_start", out=out_d[:, :], out_offset=bass.IndirectOffsetOnAxis(ap=IDX[:, s * 16 + e_, c:c + 1], axis=0), in_=yo[yb], in_offset=None, compute_op=ALU.add, oob_is_err=True),
                    r=["yo%d_0" % yb, "yo%d_1" % yb, "IDX"] + prev, w=["sc_%d_%d_%d" % (s, e_, c)], key="sc%d" % st_)

    nsamp = 2
    stages = ["A", "B", "C", "D", "T", "full"]
    lvl = stages.index(stage) if stage in stages else -1
    if stage == "0":
        nsamp = 0
    if stage == "A1":
        nsamp = 1
    import os as _os
    if _os.environ.get("NSAMP"):
        nsamp = int(_os.environ["NSAMP"])
    for s in range(nsamp):
        phase_A(s)
        P.barrier()
        if stage == "A" and s == 0:
            add("sp", C("dma_start", out=dbg["qT"][:, :, :], in_=qT), key="dbg0")
            add("sp", C("dma_start", out=dbg["kT"][:, :], in_=kT), key="dbg1")
            add("sp", C("dma_start", out=dbg["v"][:, :, :, :], in_=vaug), key="dbg0")
            add("sp", C("dma_start", out=dbg["xr"][:, :, :], in_=xrS), key="dbg1")
            add("sp", C("dma_start", out=dbg["gr"][:, :, :], in_=grS), key="dbg0")
            P.barrier()
        if lvl >= 1:
            phase_B(s)
            P.barrier()
            if stage == "B" and s == 0:
                add("sp", C("dma_start", out=dbg["qT"][:, :, :], in_=qT), key="dbg0")
                P.barrier()
        if lvl >= 2:
            phase_C(s)
            P.barrier()
            if stage == "C" and s == 0:
                add("sp", C("dma_start", out=dbg["gr"][:, :, :], in_=grS), key="dbg0")
                add("sp", C("dma_start", out=dbg["qT"][:, :, :], in_=qT), key="dbg1")
                P.barrier()
        if lvl >= 3:
            phase_D(s)
            P.barrier()
    if lvl >= 4:
        phase_topk()
        P.barrier()
        if "aall" in dbg:
            add("sp", C("dma_start", out=dbg["aall"][:, :, :], in_=Aall[:]), key="dbg0")
    if lvl >= 5:
        phase_moe()
    P.barrier()
    P.emit()
    P.close()
    return nc, P.stats


def _swap_idx():
    d = np.arange(64)
    return np.where((d % 32) < 16, d + 16, d - 16)


def _consts():
    bf = ml_dtypes.bfloat16
    cm = np.zeros((128, NCM, 128), np.float32)
    cm[:, CM_ID, :] = np.eye(128)
    for h in range(2):
        cm[h * 64:(h + 1) * 64, CM_ONES, h * 64:(h + 1) * 64] = 1.0
    sw = _swap_idx()
    for m in range(128):
        k = (m // 64) * 64 + sw[m % 64]
        cm[k, CM_SWAP, m] = 1.0
    cm[:, CM_IOP, :] = (np.arange(128) - 127)[None, :]
    cm[:, CM_IOC, 0:4] = np.arange(1, 5)[None, :]
    identf = np.eye(128, dtype=np.float32)
    a = np.arange(128)
    prev = (a[None, :] <= a[:, None]).astype(np.float32)
    nxt = (a[:, None] <= a[None, :]).astype(np.float32)
    masks = np.stack([np.tile(prev, (1, 4)), np.tile(nxt, (1, 4))], axis=1)
    t = np.arange(SEQ)
    row = (t // 64).astype(np.float32)
    col = (t % 64).astype(np.float32)
    inv = (10000.0 ** (-np.arange(16, dtype=np.float32) / 16)).astype(np.float32)
    ar = (row[None, :] * inv[:, None]).astype(np.float32)
    ac = (col[None, :] * inv[:, None]).astype(np.float32)
    C = np.zeros((64, SEQ), np.float32)
    S2 = np.zeros((64, SEQ), np.float32)
    C[0:16] = np.cos(ar); C[16:32] = np.cos(ar); C[32:48] = np.cos(ac); C[48:64] = np.cos(ac)
    S2[0:16] = np.sin(ar); S2[16:32] = -np.sin(ar); S2[32:48] = np.sin(ac); S2[48:64] = -np.sin(ac)
    rope = np.stack([np.tile(C, (2, 1)), np.tile(S2, (2, 1))], axis=1)
    tok = np.arange(32)[None, :] * 128 + np.arange(128)[:, None]
    tcol = np.stack([tok // 64, tok % 64], axis=2).astype(np.float32)
    return dict(cm=cm.astype(bf), identf=identf, masks=masks.astype(bf), rope=rope.astype(bf), tcol=tcol)


def prep_inputs(inp):
    f = np.float32
    g = lambda k: np.asarray(inp[k], dtype=f)
    x, c, ctx, c_ctx = g("x"), g("c"), g("ctx"), g("c_ctx")
    L = 0
    w_in = g("w_in")[L]
    qperm = np.concatenate([np.r_[b * 64:(b + 1) * 64, (b + 4) * 64:(b + 5) * 64] for b in range(4)])
    w_in_p = np.ascontiguousarray(np.concatenate([w_in[:, qperm], w_in[:, 512:]], axis=1))
    w_out = g("w_out")[L]
    w_out_p = np.ascontiguousarray(np.concatenate([w_out[qperm, :], w_out[512:, :]], axis=0))
    b_ada = g("b_ada")[L]
    vecs = np.zeros((128, NV), f)
    pc = lambda v, n: v.reshape(n, 128).T
    vecs[:, V_N1G:V_N1G + 8] = pc(g("norm1_g")[L], 8)
    vecs[:, V_N2G:V_N2G + 8] = pc(g("norm2_g")[L], 8)
    vecs[:, V_BADA:V_BADA + 48] = pc(b_ada, 48)
    gq, gk = g("q_norm_g")[L], g("k_norm_g")[L]
    vecs[:, V_GQ] = np.tile(gq, 2)
    vecs[:, V_GK] = np.tile(gk, 2)
    cw = g("conv_w")[L]
    for cch in range(4):
        for j in range(4):
            vecs[:, V_CONVW + cch * 4 + j] = cw[j, cch * 128:(cch + 1) * 128]
    vecs[:, V_CONVB:V_CONVB + 4] = pc(g("conv_b")[L], 4)
    for d in range(2):
        vecs[:, V_BR + d * 4:V_BR + d * 4 + 4] = pc(g("lru_b_r")[L][d], 4)
        vecs[:, V_BI + d * 4:V_BI + d * 4 + 4] = pc(g("lru_b_i")[L][d], 4)
        vecs[:, V_LAM + d * 4:V_LAM + d * 4 + 4] = pc(g("lru_lambda")[L][d], 4)
    vecs[:, V_SINK:V_SINK + 8] = g("attn_sink")[L][None, :]
    vecs[:, V_GQROW:V_GQROW + 64] = gq[None, :]
    vecs[:, V_GKROW:V_GKROW + 64] = gk[None, :]
    lruw = np.zeros((128, 16, 128), f)
    for d in range(2):
        for gi, nm in enumerate(["lru_w_r", "lru_w_i"]):
            w = g(nm)[L][d]
            for cch in range(4):
                for nl in range(2):
                    lruw[nl * 64:(nl + 1) * 64, (d * 2 + gi) * 4 + cch, nl * 64:(nl + 1) * 64] = w[cch * 2 + nl]
    badag = np.stack([np.stack([b_ada[2 * D:3 * D], b_ada[5 * D:6 * D]])] * 3)
    wrt = np.ascontiguousarray(g("w_router")[L].reshape(8, 128, 16).transpose(1, 0, 2))
    consts = _consts()
    shared = dict(w_ada=g("w_ada")[L], badag=badag, vecs=vecs, w_in=w_in_p, lruw=lruw, w_out=w_out_p, w_router=wrt,
                  w_gate=g("w_gate")[L], w_up=g("w_up")[L], w_down=g("w_down")[L], **consts)
    maps = []
    for core in range(NCORES):
        s0 = 2 * core
        cc = np.stack([c[s0], c[s0 + 1], c_ctx])
        ccT = np.ascontiguousarray(cc.reshape(3, 8, 128).transpose(2, 1, 0))
        m = dict(shared)
        m["x"] = np.ascontiguousarray(x[s0:s0 + 2].reshape(2 * SEQ, D))
        m["ctx"] = np.ascontiguousarray(ctx[s0:s0 + 2].reshape(2 * NCTX, D))
        m["ccT"] = ccT
        maps.append(m)
    return maps


_CACHE = {}


def kernel(**inputs):
    maps = prep_inputs(inputs)
    if "nc" not in _CACHE:
        _CACHE["nc"] = build_program("full")[0]
    nc = _CACHE["nc"]
    res = run_bass_kernel_spmd(nc, maps, core_ids=list(range(NCORES)))
    outs = [np.asarray(r["out"]).reshape(2, SEQ, D) for r in res.results]
    return np.concatenate(outs, axis=0).astype(np.float32)
```

```python
import numpy as np
import ml_dtypes
import concourse.bass as bass
import concourse.mybir as mybir
from concourse.bass_utils import run_bass_kernel_spmd
from contextlib import ExitStack

AF = mybir.ActivationFunctionType
ALU = mybir.AluOpType
AX = mybir.AxisListType
F32 = mybir.dt.float32
BF16 = mybir.dt.bfloat16
I32 = mybir.dt.int32

NCORES = 8
SEQ = 4096
NCTX = 256
NTOK = SEQ + NCTX
D = 1024
EPS = 1e-6
NEXP = 16
CAP = 512


class Op:
    __slots__ = ("eng", "fn", "deps", "dma", "key", "signal", "ticket")

    def __init__(self, eng, fn, dma, key):
        self.eng = eng
        self.fn = fn
        self.dma = dma
        self.key = key
        self.deps = []
        self.signal = False
        self.ticket = 0


ENGS = ["pe", "act", "dve", "pool", "sp"]


def C(name, *a, **k):
    return (name, a, k)


class Prog:
    def __init__(self, nc):
        self.nc = nc
        self.ops = []
        self.last_w = {}
        self.readers = {}
        self.dma_last = {}
        self.last_eng = {}
        self.bank_last = {}
        self.stack = ExitStack()

    def sb(self, name, shape, dtype):
        return self.stack.enter_context(self.nc.sbuf_tensor("sb_" + name, list(shape), dtype))

    def ps(self, name, shape, dtype):
        return self.stack.enter_context(self.nc.psum_tensor("ps_" + name, list(shape), dtype))

    dead = False

    def cut(self, n):
        import os
        if int(os.environ.get("CUT", "-1")) == n:
            self.dead = True

    def add(self, eng, fn, r=(), w=(), key=None):
        op = Op(eng, fn, key is not None, key)
        if self.dead:
            return op
        deps = {}
        for k in r:
            o = self.last_w.get(k)
            if o is not None:
                deps[id(o)] = o
        for k in w:
            o = self.last_w.get(k)
            if o is not None:
                deps[id(o)] = o
            for o in self.readers.get(k, {}).values():
                deps[id(o)] = o
        if op.dma:
            o = self.dma_last.get(key)
            if o is not None:
                deps[id(o)] = o
            self.dma_last[key] = op
        nm_, a_, k_ = fn
        for v in list(a_) + list(k_.values()):
            tn = getattr(getattr(v, "tensor", None), "name", "")
            if tn.startswith("ps_bank"):
                o = self.bank_last.get(tn)
                if o is not None and o.eng != eng:
                    deps[id(o)] = o
                self.bank_last[tn] = op
        for k in r:
            self.readers.setdefault(k, {})[(eng, key)] = op
        for k in w:
            self.last_w[k] = op
            self.readers[k] = {}
        for o in deps.values():
            if o is op:
                continue
            if (not o.dma) and (not op.dma) and o.eng == "pe" and op.eng == "pe":
                continue
            op.deps.append(o)
            o.signal = True
        if not op.dma:
            self.last_eng[eng] = op
        self.ops.append(op)
        return op

    def barrier(self):
        prev = [o for o in self.last_eng.values()] + list(self.dma_last.values())
        new = []
        for e in ENGS:
            op = Op(e, C("nop", ), False, None)
            for o in prev:
                if o.eng == e and not o.dma and e == "pe":
                    continue
                op.deps.append(o)
                o.signal = True
            self.ops.append(op)
            new.append(op)
        for op in new:
            self.last_eng[op.eng] = op
        self.last_w = {}
        self.readers = {}

    def emit(self):
        nc = self.nc
        cnt = {e: 0 for e in ENGS}
        dcnt = {}
        for op in self.ops:
            if op.dma:
                dcnt[op.key] = dcnt.get(op.key, 0) + 16
                op.ticket = dcnt[op.key]
            elif op.signal:
                cnt[op.eng] += 1
                op.ticket = cnt[op.eng]
        sems = {}
        for e in ENGS:
            if cnt[e]:
                sems[("c", e)] = self.stack.enter_context(nc.semaphore("s_" + e))
        for k in dcnt:
            sems[("d", k)] = self.stack.enter_context(nc.semaphore("d_" + str(k)))

        def semof(o):
            return sems[("d", o.key)] if o.dma else sems[("c", o.eng)]

        per = {e: [] for e in ENGS}
        for op in self.ops:
            per[op.eng].append(op)

        def run(name, eng):
            seen = {}
            for op in per[name]:
                need = {}
                for d in op.deps:
                    s = semof(d)
                    if need.get(id(s), (None, 0))[1] < d.ticket:
                        need[id(s)] = (s, d.ticket)
                for kk, (s, t) in need.items():
                    if seen.get(kk, 0) >= t:
                        continue
                    eng.wait_ge(s, t)
                    seen[kk] = t
                nm, a_, k_ = op.fn
                try:
                    ins = getattr(eng, nm)(*a_, **k_)
                except Exception:
                    print("EMIT FAIL at", per[name].index(op), "of", len(per[name]), "ndma_before", sum(1 for o in per[name][:per[name].index(op)] if o.dma), flush=True)
                    print("EMIT FAIL", name, nm, [repr(x)[:300] for x in a_], {kk: repr(vv)[:300] for kk, vv in k_.items()}, flush=True)
                    raise
                if op.dma:
                    ins.then_inc(semof(op), 16)
                elif op.signal:
                    ins.then_inc(semof(op), 1)

        with nc.Block() as block:
            @block.tensor
            def _(e):
                run("pe", e)

            @block.scalar
            def _(e):
                run("act", e)

            @block.vector
            def _(e):
                run("dve", e)

            @block.gpsimd
            def _(e):
                run("pool", e)

            @block.sync
            def _(e):
                run("sp", e)
        self.stats = dict(n={e: len(per[e]) for e in ENGS}, sig=cnt, nsems=len(sems))

    def close(self):
        self.stack.close()


class Arena:
    def __init__(self, P, name, nbytes):
        self.t = P.sb(name, [128, nbytes // 2], BF16)
        self.cap = nbytes
        self.off = 0

    def alloc(self, shape, dtype):
        esz = 2 if dtype == BF16 else 4
        n = int(np.prod(shape))
        nb = n * esz
        start = self.off
        self.off += (nb + 31) // 32 * 32
        assert self.off <= self.cap, ("arena overflow", self.off, self.cap)
        ap = self.t[:, start // 2:start // 2 + nb // 2]
        if dtype != BF16:
            ap = ap.bitcast(dtype)
        if len(shape) == 2:
            ap = ap.rearrange("p (a b) -> p a b", a=shape[0])
        elif len(shape) == 3:
            ap = ap.rearrange("p (a b c) -> p a b c", a=shape[0], b=shape[1])
        elif len(shape) == 4:
            ap = ap.rearrange("p (a b c d) -> p a b c d", a=shape[0], b=shape[1], c=shape[2])
        return ap

    def reset(self, off=0):
        self.off = off


V_N1G, V_N2G, V_BADA, V_GQ, V_GK, V_CONVW, V_CONVB, V_BR, V_BI, V_LAM = 0, 8, 16, 64, 65, 66, 82, 86, 94, 102
V_SINK, V_GQROW, V_GKROW, NV = 110, 118, 182, 246
CM_ID, CM_ONES, CM_SWAP, CM_IOP, CM_IOC, NCM = 0, 1, 2, 3, 4, 5


def build_program(stage="full"):
    nc = bass.Bass("TRN2", target_bir_lowering=False)

    def din(name, shape, dt=F32):
        return nc.dram_tensor(name, list(shape), dt, kind="ExternalInput").ap()

    x_d = din("x", [2 * SEQ, D])
    ctx_d = din("ctx", [2 * NCTX, D])
    ccT_d = din("ccT", [128, 8, 3])
    wada_d = din("w_ada", [D, 6 * D])
    badag_d = din("badag", [3, 2, D])
    vecs_d = din("vecs", [128, NV])
    cm_d = din("cm", [128, NCM, 128], BF16)
    identf_d = din("identf", [128, 128])
    masks_d = din("masks", [128, 2, 512], BF16)
    rope_d = din("rope", [128, 2, SEQ], BF16)
    tcol_d = din("tcol", [128, 32, 2])
    win_d = din("w_in", [D, 1792])
    lruw_d = din("lruw", [128, 16, 128])
    wout_d = din("w_out", [D, D])
    wr_d = din("w_router", [128, 8, 16])
    wg_d = din("w_gate", [NEXP, D, D])
    wu_d = din("w_up", [NEXP, D, D])
    wd_d = din("w_down", [NEXP, D, D])
    out_d = nc.dram_tensor("out", [2 * SEQ, D], F32, kind="ExternalOutput").ap()
    xn_d = nc.dram_tensor("xn_scr", [2 * SEQ, 1056], BF16, kind="Internal").ap()
    gsc_d = nc.dram_tensor("g_scr", [3, 2, D], F32, kind="Internal").ap()
    dbg = {}
    if stage != "full":
        dbg["qT"] = nc.dram_tensor("dbg_qT", [128, 4, SEQ], BF16, kind="ExternalOutput").ap()
        dbg["kT"] = nc.dram_tensor("dbg_kT", [128, NTOK], BF16, kind="ExternalOutput").ap()
        dbg["v"] = nc.dram_tensor("dbg_v", [128, 34, 2, 65], BF16, kind="ExternalOutput").ap()
        dbg["xr"] = nc.dram_tensor("dbg_xr", [128, 4, NTOK], BF16, kind="ExternalOutput").ap()
        dbg["gr"] = nc.dram_tensor("dbg_gr", [128, 4, SEQ], BF16, kind="ExternalOutput").ap()
        dbg["modv"] = nc.dram_tensor("dbg_modv", [128, 48, 3], F32, kind="ExternalOutput").ap()
        dbg["aall"] = nc.dram_tensor("dbg_aall", [128, 32, 32], F32, kind="ExternalOutput").ap()
        dbg["idx"] = nc.dram_tensor("dbg_idx", [128, 32, 4], I32, kind="ExternalOutput").ap()

    P = Prog(nc)
    add = P.add

    vecs = P.sb("vecs", [128, NV], F32)
    cm = P.sb("cm", [128, NCM, 128], BF16)
    identf = P.sb("identf", [128, 128], F32)
    masks = P.sb("masks", [128, 2, 512], BF16)
    lruW = P.sb("lruW", [128, 16, 128], BF16)
    wr = P.sb("wr", [128, 8, 16], BF16)
    tcol = P.sb("tcol", [128, 32, 2], F32)
    modv = P.sb("modv", [128, 48, 3], F32)
    A1 = P.sb("A1", [128, 8, 3], F32)
    A2 = P.sb("A2", [128, 8, 3], F32)
    lv = P.sb("lv", [128, 40], F32)
    av = P.sb("av", [128, 16], F32)
    Aall = P.sb("Aall", [128, 32, 32], F32)
    IDX = P.sb("IDX", [128, 32, 4], I32)
    ident = cm[:, CM_ID, :]
    onesblk = cm[:, CM_ONES, :]
    pswap = cm[:, CM_SWAP, :]
    B1 = modv[:, 0:8, :]
    B2 = modv[:, 24:32, :]
    HBR, HBI, CNEG, HCNEG = 0, 8, 16, 24
    negB = av[:, 0:1]
    esink = av[:, 1:9]

    banks = [P.ps("bank%d" % i, [128, 512], F32) for i in range(8)]
    ST = Arena(P, "ST", 116 * 1024)
    WK = Arena(P, "WK", 77 * 1024)

    def ld(eng, out, in_, w, key):
        return add(eng, C("dma_start", out=out, in_=in_), w=w, key=key)

    ld("sp", vecs[:], vecs_d[:, :], ["vecs"], "c0")
    ld("sp", cm[:], cm_d[:, :, :], ["cm"], "c1")
    ld("sp", identf[:], identf_d[:, :], ["identf"], "c2")
    ld("sp", masks[:], masks_d[:, :, :], ["masks"], "c3")
    ld("sp", tcol[:], tcol_d[:, :, :], ["tcol"], "c0")
    ld("pool", lruW[:], lruw_d[:, :, :], ["lruW"], "c4")
    ld("pool", wr[:], wr_d[:, :, :], ["wr"], "c5")
    ccT = WK.alloc([8, 3], F32)
    scT = WK.alloc([8, 3], F32)
    badag = WK.alloc([2, D], F32)
    grow = WK.alloc([2, D], F32)
    wa = [WK.alloc([8, 512], F32) for _ in range(2)]
    tmpv = WK.alloc([64], F32)
    ld("sp", ccT, ccT_d[:, :, :], ["ccT"], "c2")
    ld("sp", badag[0:3], badag_d[:, :, :], ["badag"], "c3")
    add("act", C("activation", out=scT, in_=ccT, func=AF.Silu), r=["ccT"], w=["scT"])
    pm = banks[0][:, 0:192].rearrange("p (a b) -> p a b", a=48)
    prow = banks[1]
    wada_v = wada_d.rearrange("(kc p) n -> p kc n", p=128)
    P.cut(1)
    pc = 0
    for m in range(6):
        for half in range(2):
            b = pc % 2
            pc += 1
            c0 = m * D + half * 512
            ld("sp", wa[b], wada_v[:, :, c0:c0 + 512], ["wa%d" % b], "wa%d" % b)
            for jj in range(4):
                j = m * 8 + half * 4 + jj
                for kc in range(8):
                    add("pe", C("matmul", pm[:, j, 0:3], lhsT=wa[b][:, kc, jj * 128:(jj + 1) * 128], rhs=scT[:, kc, :], start=(kc == 0), stop=(kc == 7)),
                        r=["wa%d" % b, "scT"], w=["pm"])
            if m in (2, 5):
                which = 0 if m == 2 else 1
                for kc in range(8):
                    add("pe", C("matmul", prow[0:3, :], lhsT=scT[:, kc, :], rhs=wa[b][:, kc, :], start=(kc == 0), stop=(kc == 7)),
                        r=["wa%d" % b, "scT"], w=["prow"])
                add("dve", C("tensor_tensor", out=grow[0:3, which, half * 512:(half + 1) * 512], in0=prow[0:3, :], in1=badag[0:3, which, half * 512:(half + 1) * 512], op=ALU.add),
                    r=["prow", "badag"], w=["grow"])
    P.cut(2)
    add("sp", C("dma_start", out=gsc_d[:, :, :], in_=grow[0:3]), r=["grow"], w=["gsc"], key="c1")
    P.cut(3)
    add("dve", C("tensor_tensor", out=modv[:], in0=pm[:, :, 0:3], in1=vecs[:, V_BADA:V_BADA + 48].unsqueeze(2).to_broadcast([128, 48, 3]), op=ALU.add),
        r=["pm", "vecs"], w=["modv"])
    add("dve", C("scalar_tensor_tensor", out=A1[:], in0=modv[:, 8:16, :], scalar=1.0, in1=vecs[:, V_N1G:V_N1G + 8].unsqueeze(2).to_broadcast([128, 8, 3]), op0=ALU.add, op1=ALU.mult),
        r=["modv", "vecs"], w=["A1"])
    add("dve", C("scalar_tensor_tensor", out=A2[:], in0=modv[:, 32:40, :], scalar=1.0, in1=vecs[:, V_N2G:V_N2G + 8].unsqueeze(2).to_broadcast([128, 8, 3]), op0=ALU.add, op1=ALU.mult),
        r=["modv", "vecs"], w=["A2"])
    P.cut(4)
    add("dve", C("tensor_scalar", out=lv[:, HBR:HBR + 16], in0=vecs[:, V_BR:V_BR + 16], scalar1=0.5, scalar2=None, op0=ALU.mult), r=["vecs"], w=["lv0"])
    add("act", C("activation", out=tmpv[:, 0:8], in_=vecs[:, V_LAM:V_LAM + 8], func=AF.Exp, scale=-1.0), r=["vecs"], w=["tmpv"])
    add("act", C("activation", out=tmpv[:, 8:16], in_=tmpv[:, 0:8], func=AF.Ln, bias=1.0), r=["tmpv"], w=["tmpv2"])
    add("dve", C("tensor_scalar", out=lv[:, CNEG:CNEG + 8], in0=tmpv[:, 8:16], scalar1=-8.0, scalar2=None, op0=ALU.mult), r=["tmpv2"], w=["lv1"])
    add("dve", C("tensor_scalar", out=lv[:, HCNEG:HCNEG + 8], in0=tmpv[:, 8:16], scalar1=-4.0, scalar2=None, op0=ALU.mult), r=["tmpv2"], w=["lv2"])
    P.cut(5)
    add("dve", C("tensor_reduce", out=tmpv[:, 16:17], in_=vecs[:, V_GQROW:V_GQROW + 64], axis=AX.X, op=ALU.max, apply_absolute_value=True), r=["vecs"], w=["tmpv3"])
    add("dve", C("tensor_reduce", out=tmpv[:, 17:18], in_=vecs[:, V_GKROW:V_GKROW + 64], axis=AX.X, op=ALU.max, apply_absolute_value=True), r=["vecs"], w=["tmpv4"])
    add("dve", C("scalar_tensor_tensor", out=av[:, 0:1], in0=tmpv[:, 16:17], scalar=-8.0, in1=tmpv[:, 17:18], op0=ALU.mult, op1=ALU.mult), r=["tmpv3", "tmpv4"], w=["av0"])
    add("act", C("activation", out=av[:, 1:9], in_=vecs[:, V_SINK:V_SINK + 8], func=AF.Exp, bias=av[:, 0:1]), r=["vecs", "av0"], w=["av1"])
    P.cut(6)
    if "modv" in dbg:
        add("sp", C("dma_start", out=dbg["modv"][:, :, :], in_=modv[:]), r=["modv"], key="dbg0")
    P.barrier()
    P.cut(7)

    ST.reset()
    qT = ST.alloc([4, SEQ], BF16)
    kT = ST.alloc([NTOK], BF16)
    vaug = ST.alloc([34, 2, 65], BF16)
    xrS = ST.alloc([4, NTOK], BF16)
    grS = ST.alloc([4, SEQ], BF16)
    add("pool", C("memset", vaug[:, :, :, 64:65], 1.0), w=["vones"])
    P.cut(8)
    P.barrier()
    P.cut(9)

    win_v = win_d.rearrange("(kc p) n -> p kc n", p=128)
    wout_v = wout_d.rearrange("(kc p) n -> p kc n", p=128)

    def phase_A(s):
        WK.reset()
        win = WK.alloc([8, 1792], BF16)
        ropeb = [WK.alloc([2, 512], BF16) for _ in range(2)]
        xt = [WK.alloc([D], F32) for _ in range(3)]
        xn = WK.alloc([4, D], BF16)
        hT = WK.alloc([8, 512], BF16)
        NSET = 3
        tq = [[WK.alloc([512], BF16) for _ in range(3)] for _ in range(NSET)]
        rstd = [WK.alloc([512], F32) for _ in range(NSET)]
        ssb = WK.alloc([8], F32)
        for h in range(2):
            add("pool", C("dma_start", out=win[:, h * 4:(h + 1) * 4, :], in_=win_v[:, h * 4:(h + 1) * 4, :]), w=["win%d" % h], key="win%d" % h)
        add("pool", C("memset", vaug[:, :, :, 64:65], 1.0), w=["vones"])
        pT = [banks[i].bitcast(BF16).rearrange("p (a b) -> p a b", a=2) for i in range(4)]
        accs = [(banks[4], 4), (banks[5], 5), (banks[0], 0), (banks[1], 1), (banks[2], 2), (banks[3], 3)]
        pS = banks[6]
        pR = banks[7]
        st = dict(xh=0, grp=0, acc=0, tq=0, ev=0, prepped=set())

        def prep(rows_ap, ntiles):
            for t in range(ntiles):
                xb = st["xh"] % 3
                st["xh"] += 1
                add("sp", C("dma_start", out=xt[xb], in_=rows_ap[t * 128:(t + 1) * 128, :]), w=["xt%d" % xb], key="xt%d" % xb)
                sc_ = ssb[:, t:t + 1]
                add("act", C("activation", out=xn[:, t, :], in_=xt[xb], func=AF.Square, accum_out=sc_), r=["xt%d" % xb], w=["xn%d" % t, "ms%d" % t])
                add("act", C("activation", out=sc_, in_=sc_, func=AF.Ln, scale=1.0 / D, bias=EPS), r=["ms%d" % t], w=["ms%d" % t])
                add("act", C("activation", out=sc_, in_=sc_, func=AF.Exp, scale=-0.5), r=["ms%d" % t], w=["ms%d" % t])
                add("dve", C("tensor_scalar", out=xn[:, t, :], in0=xt[xb], scalar1=sc_, scalar2=None, op0=ALU.mult), r=["xt%d" % xb, "ms%d" % t], w=["xn%d" % t])

        def group(rows_ap, ntiles, is_ctx, col0, tile0, lat0, nxt=None):
            ntok = ntiles * 128
            scol = 2 if is_ctx else s
            gi = st["grp"]
            st["grp"] += 1
            hb = gi % 2
            if not is_ctx:
                add("sp", C("dma_start", out=ropeb[hb], in_=rope_d[:, :, lat0:lat0 + 512]), w=["rope%d" % hb], key="rope%d" % hb)
            if gi not in st["prepped"]:
                prep(rows_ap, ntiles)
            st["prepped"].discard(gi)

            for fc in range(8):
                bk = fc % 4
                for t in range(ntiles):
                    add("pe", C("transpose", out=pT[bk][:, fc // 4, t * 128:(t + 1) * 128], in_=xn[:, t, fc * 128:(fc + 1) * 128], identity=ident),
                        r=["xn%d" % t, "cm"], w=["pT%d" % fc, "acc%d" % bk])
                if bk % 2 == 0:
                    add("act", C("activation", out=hT[:, fc, 0:ntok], in_=pT[bk][:, fc // 4, 0:ntok], func=AF.Identity, scale=A1[:, fc, scol:scol + 1], bias=B1[:, fc, scol:scol + 1]),
                        r=["pT%d" % fc, "A1", "modv"], w=["hT%d" % fc])
                else:
                    add("dve", C("tensor_scalar", out=hT[:, fc, 0:ntok], in0=pT[bk][:, fc // 4, 0:ntok], scalar1=A1[:, fc, scol:scol + 1], scalar2=B1[:, fc, scol:scol + 1], op0=ALU.mult, op1=ALU.add),
                        r=["pT%d" % fc, "A1", "modv"], w=["hT%d" % fc])
            hkeys = ["hT%d" % fc for fc in range(8)]
            cols = {"k": 512, "v": 640}
            for b in range(4):
                cols["q%d" % b] = b * 128
                cols["x%d" % b] = 768 + b * 128
                cols["g%d" % b] = 1280 + b * 128
            if is_ctx:
                order = ["k", "v", "x0", "x1", "x2", "x3"]
            else:
                order = ["k", "v", "q0", "x0", "q1", "x1", "q2", "x2", "q3", "x3", "g0", "g1", "g2", "g3"]
            pend = []

            def evac_copy(dst, src_, rk, wk):
                st["ev"] += 1
                if st["ev"] % 2 == 0:
                    add("act", C("copy", out=dst, in_=src_), r=rk, w=wk)
                else:
                    add("dve", C("tensor_copy", out=dst, in_=src_), r=rk, w=wk)

            def qk_stage1(name, acc, akey, ts):
                sq, qg, qs = tq[ts]
                qc = sq
                rs = rstd[ts]
                gcol = vecs[:, V_GK:V_GK + 1] if name == "k" else vecs[:, V_GQ:V_GQ + 1]
                add("pe", C("matmul", pS[:, 0:ntok], lhsT=onesblk, rhs=sq[:, 0:ntok], start=True, stop=True), r=["sq%d" % ts, "cm"], w=["pS"])
                add("act", C("activation", out=rs[:, 0:ntok], in_=pS[:, 0:ntok], func=AF.Ln, scale=1.0 / 64, bias=EPS), r=["pS"], w=["rs%d" % ts])
                add("act", C("activation", out=rs[:, 0:ntok], in_=rs[:, 0:ntok], func=AF.Exp, scale=-0.5), r=["rs%d" % ts], w=["rs%d" % ts])
                if is_ctx:
                    add("dve", C("scalar_tensor_tensor", out=kT[:, col0:col0 + ntok], in0=acc[:, 0:ntok], scalar=gcol, in1=rs[:, 0:ntok], op0=ALU.mult, op1=ALU.mult),
                        r=[akey, "rs%d" % ts, "vecs"], w=["kT%d" % (col0 // 128 + t) for t in range(ntiles)])
                    return
                add("dve", C("scalar_tensor_tensor", out=qg[:, 0:ntok], in0=acc[:, 0:ntok], scalar=gcol, in1=rs[:, 0:ntok], op0=ALU.mult, op1=ALU.mult),
                    r=[akey, "rs%d" % ts, "vecs"], w=["qg%d" % ts])
                add("dve", C("tensor_tensor", out=qc[:, 0:ntok], in0=qg[:, 0:ntok], in1=ropeb[hb][:, 0, 0:ntok], op=ALU.mult), r=["qg%d" % ts, "rope%d" % hb, "sq%d" % ts], w=["sq%d" % ts])
                add("dve", C("tensor_tensor", out=qs[:, 0:ntok], in0=qg[:, 0:ntok], in1=ropeb[hb][:, 1, 0:ntok], op=ALU.mult), r=["qg%d" % ts, "rope%d" % hb], w=["qs%d" % ts])

            def qk_stage2(name, ts):
                sq, qg, qs = tq[ts]
                qc = sq
                add("pe", C("matmul", pR[:, 0:ntok], lhsT=ident, rhs=qc[:, 0:ntok], start=True, stop=False), r=["sq%d" % ts, "cm"], w=["pR"])
                add("pe", C("matmul", pR[:, 0:ntok], lhsT=pswap, rhs=qs[:, 0:ntok], start=False, stop=True), r=["qs%d" % ts, "cm"], w=["pR"])
                if name == "k":
                    add("act", C("copy", out=kT[:, col0:col0 + ntok], in_=pR[:, 0:ntok]), r=["pR"], w=["kT%d" % (col0 // 128 + t) for t in range(ntiles)])
                else:
                    b = int(name[1])
                    add("act", C("copy", out=qT[:, b, lat0:lat0 + ntok], in_=pR[:, 0:ntok]), r=["pR"], w=["qT%d" % (lat0 // 128 + t) for t in range(ntiles)])

            def run_pending(upto):
                keep = []
                for due, fn_ in pend:
                    if due <= upto:
                        fn_()
                    else:
                        keep.append((due, fn_))
                pend[:] = keep

            for bi, name in enumerate(order):
                c0 = cols[name]
                acc, bno = accs[st["acc"] % len(accs)]
                st["acc"] += 1
                akey = "acc%d" % bno
                wk = [akey] + (["pT%d" % bno, "pT%d" % (bno + 4)] if bno < 4 else [])
                if name == "v":
                    accv = acc.rearrange("p (t c) -> p t c", t=4)
                    for t in range(ntiles):
                        for kc in range(8):
                            add("pe", C("matmul", accv[:, t, :], lhsT=hT[:, kc, t * 128:(t + 1) * 128], rhs=win[:, kc, 640:768], start=(kc == 0), stop=(kc == 7)),
                                r=hkeys + ["win0", "win1"], w=wk)
                    add("dve", C("tensor_copy", out=vaug[:, tile0:tile0 + ntiles, :, 0:64], in_=accv[:, 0:ntiles, :].rearrange("p t (h d) -> p t h d", h=2)),
                        r=[akey], w=["v%d" % (tile0 + t) for t in range(ntiles)])
                else:
                    for kc in range(8):
                        add("pe", C("matmul", acc[:, 0:ntok], lhsT=win[:, kc, c0:c0 + 128], rhs=hT[:, kc, 0:ntok], start=(kc == 0), stop=(kc == 7)),
                            r=hkeys + ["win0", "win1"], w=wk)
                    if name[0] == "x":
                        c = int(name[1])
                        evac_copy(xrS[:, c, col0:col0 + ntok], acc[:, 0:ntok], [akey], ["xr%d" % c])
                    elif name[0] == "g":
                        c = int(name[1])
                        evac_copy(grS[:, c, lat0:lat0 + ntok], acc[:, 0:ntok], [akey], ["gr%d" % c])
                    else:
                        ts = st["tq"] % NSET
                        st["tq"] += 1
                        add("act", C("activation", out=tq[ts][0][:, 0:ntok], in_=acc[:, 0:ntok], func=AF.Square), r=[akey], w=["sq%d" % ts])
                        pend.append((bi + 1, (lambda name=name, acc=acc, akey=akey, ts=ts: qk_stage1(name, acc, akey, ts))))
                        if not is_ctx:
                            pend.append((bi + 3, (lambda name=name, ts=ts: qk_stage2(name, ts))))
                run_pending(bi)
                if nxt is not None and bi == min(4, len(order) - 2):
                    prep(*nxt)
                    st["prepped"].add(gi + 1)
            run_pending(10 ** 9)

        lat_rows = lambda g: x_d[s * SEQ + g * 512:s * SEQ + (g + 1) * 512, :]
        group(ctx_d[s * NCTX:(s + 1) * NCTX, :], 2, True, 0, 0, 0, nxt=(lat_rows(0), 4))
        for g in range(8):
            group(lat_rows(g), 4, False, NCTX + g * 512, 2 + g * 4, g * 512, nxt=((lat_rows(g + 1), 4) if g < 7 else None))

    def phase_B(s):
        WK.reset()
        kTz = [WK.alloc([NTOK], BF16) for _ in range(2)]
        NEX = 16
        expS = [WK.alloc([512], BF16) for _ in range(NEX)]
        On = [WK.alloc([4, 2, 64], BF16) for _ in range(2)]
        den = [WK.alloc([4], F32) for _ in range(4)]
        kkeys = ["kT%d" % t for t in range(34)]
        add("pool", C("memset", kTz[0][64:128, :], 0.0), w=["kz0"])
        add("pool", C("memset", kTz[1][0:64, :], 0.0), w=["kz1"])
        add("act", C("copy", out=kTz[0][0:64, :], in_=kT[0:64, :]), r=kkeys, w=["kz0"])
        add("dve", C("tensor_copy", out=kTz[1][64:128, :], in_=kT[64:128, :]), r=kkeys, w=["kz1"])
        pSc = banks[0:4]
        pO = [banks[4], banks[5]]
        pTr = [banks[6].bitcast(BF16)[:, 0:512].rearrange("p (g q) -> p g q", g=4), banks[7].bitcast(BF16)[:, 0:512].rearrange("p (g q) -> p g q", g=4)]
        ctr = dict(sc=0, ex=0)
        its = [(i, kvh) for i in range(32) for kvh in range(2)]
        info = {}

        def chunks_of(i):
            ch = [(0, None), (1, None)]
            if i > 0:
                ch.append((2 + i - 1, 0))
            ch.append((2 + i, None))
            if i < 31:
                ch.append((2 + i + 1, 1))
            return ch

        def stage_sc(n):
            i, kvh = its[n]
            ebs = []
            for ci, (kt, mk) in enumerate(chunks_of(i)):
                sb_ = ctr["sc"] % 4
                ctr["sc"] += 1
                eb = ctr["ex"] % NEX
                ctr["ex"] += 1
                ebs.append(eb)
                add("pe", C("matmul", pSc[sb_][:, :], lhsT=kTz[kvh][:, kt * 128:(kt + 1) * 128], rhs=qT[:, :, i * 128:(i + 1) * 128], start=True, stop=True),
                    r=["kz%d" % kvh, "qT%d" % i], w=["pSc%d" % sb_])
                add("act", C("activation", out=expS[eb], in_=pSc[sb_][:, :], func=AF.Exp, scale=0.125, bias=negB), r=["pSc%d" % sb_, "av0"], w=["ex%d" % eb])
                if mk is not None:
                    add("pool", C("tensor_tensor", out=expS[eb], in0=expS[eb], in1=masks[:, mk, :], op=ALU.mult), r=["ex%d" % eb, "masks"], w=["ex%d" % eb])
            info[n] = ebs

        def stage_pv(n):
            i, kvh = its[n]
            ob = i % 2
            chunks = chunks_of(i)
            ebs = info.pop(n)
            po = pO[n % 2]
            pok = "pO%d" % (n % 2)
            dn = den[n % 4]
            dnk = "den%d" % (n % 4)
            pov = po[:, 0:260].rearrange("p (g c) -> p g c", g=4)
            for ci, (kt, mk) in enumerate(chunks):
                eb = ebs[ci]
                for g in range(4):
                    first = (ci == 0 and g == 0)
                    add("pe", C("matmul", pov[:, g, :], lhsT=expS[eb][:, g * 128:(g + 1) * 128], rhs=vaug[:, kt, kvh, :], start=first, stop=(ci == len(chunks) - 1), skip_group_check=True),
                        r=["ex%d" % eb, "v%d" % kt, "vones"], w=[pok])
            add("dve", C("tensor_tensor", out=dn, in0=pov[:, :, 64], in1=esink[:, kvh * 4:(kvh + 1) * 4], op=ALU.add), r=[pok, "av1"], w=[dnk])
            add("dve", C("reciprocal", out=dn, in_=dn), r=[dnk], w=[dnk])
            add("dve", C("tensor_tensor", out=On[ob][:, :, kvh, :], in0=pov[:, :, 0:64], in1=dn.unsqueeze(2).to_broadcast([128, 4, 64]), op=ALU.mult),
                r=[pok, dnk], w=["On%d_%d" % (ob, kvh)])
            if kvh == 1:
                for g in range(4):
                    add("pe", C("transpose", out=pTr[ob][:, g, :], in_=On[ob][:, g, :, :].rearrange("p h d -> p (h d)"), identity=ident),
                        r=["On%d_0" % ob, "On%d_1" % ob, "cm"], w=["pTr%d" % ob])
                add("act", C("copy", out=qT[:, :, i * 128:(i + 1) * 128], in_=pTr[ob]), r=["pTr%d" % ob], w=["qT%d" % i])

        LAG = 2
        for step in range(len(its) + LAG):
            if step < len(its):
                stage_sc(step)
            if step - LAG >= 0:
                stage_pv(step - LAG)

    def phase_C(s):
        WK.reset()
        xcb = WK.alloc([NTOK], BF16)
        abuf0 = WK.alloc([NTOK], F32)
        abuf1 = ST.t[:, 16384:16384 + 8704].bitcast(F32)
        abufs = [abuf0, abuf1]
        bbuf = [WK.alloc([NTOK], F32) for _ in range(2)]
        tbuf = WK.alloc([NTOK], BF16)
        trb = [WK.alloc([512], F32) for _ in range(2)]
        tib = [WK.alloc([512], F32) for _ in range(2)]
        ctr = dict(b=0)
        deferred = []
        segs = [(0, NCTX), (NCTX, NTOK)]
        def conv(c):
            xcf = abufs[0]
            cw = lambda j: vecs[:, V_CONVW + c * 4 + j:V_CONVW + c * 4 + j + 1]
            for (lo, hi) in segs:
                add("dve", C("tensor_scalar", out=xcf[:, lo:hi], in0=xrS[:, c, lo:hi], scalar1=cw(2), scalar2=vecs[:, V_CONVB + c:V_CONVB + c + 1], op0=ALU.mult, op1=ALU.add),
                    r=["xr%d" % c, "vecs"], w=["a0"])
                for j in (0, 1, 3):
                    o = j - 2
                    a0, a1 = max(lo, lo - o), min(hi, hi - o)
                    add("dve", C("scalar_tensor_tensor", out=xcf[:, a0:a1], in0=xrS[:, c, a0 + o:a1 + o], scalar=cw(j), in1=xcf[:, a0:a1], op0=ALU.mult, op1=ALU.add),
                        r=["xr%d" % c, "vecs", "a0"], w=["a0"])
            add("act", C("copy", out=xcb, in_=xcf), r=["a0"], w=["xcb"])

        conv(0)
        for c in range(4):
            for d in range(2):
                bb = bbuf[d]
                bk = "b%d" % d
                abuf = abufs[d]
                ak = "a%d" % d
                wi_r = (d * 2 + 0) * 4 + c
                wi_i = (d * 2 + 1) * 4 + c
                col = d * 4 + c
                blocks = [(0, NCTX)] + [(NCTX + g * 512, NCTX + (g + 1) * 512) for g in range(8)]
                for (lo, hi) in blocks:
                    n = hi - lo
                    pb = ctr["b"] % 2
                    ctr["b"] += 1
                    pr, pi = banks[pb * 2], banks[pb * 2 + 1]
                    tr, ti = trb[pb], tib[pb]
                    add("pe", C("matmul", pr[:, 0:n], lhsT=lruW[:, wi_r, :], rhs=xcb[:, lo:hi], start=True, stop=True), r=["xcb", "lruW"], w=["pr%d" % pb])
                    add("pe", C("matmul", pi[:, 0:n], lhsT=lruW[:, wi_i, :], rhs=xcb[:, lo:hi], start=True, stop=True), r=["xcb", "lruW"], w=["pi%d" % pb])
                    add("act", C("activation", out=tr[:, 0:n], in_=pr[:, 0:n], func=AF.Tanh, scale=0.5, bias=lv[:, HBR + col:HBR + col + 1]), r=["pr%d" % pb, "lv0"], w=["tr%d" % pb])
                    add("act", C("activation", out=ti[:, 0:n], in_=pi[:, 0:n], func=AF.Tanh, scale=0.5, bias=lv[:, HBI + col:HBI + col + 1]), r=["pi%d" % pb, "lv0"], w=["ti%d" % pb])
                    add("act", C("activation", out=abuf[:, lo:hi], in_=tr[:, 0:n], func=AF.Exp, scale=lv[:, HCNEG + col:HCNEG + col + 1], bias=lv[:, HCNEG + col:HCNEG + col + 1]), r=["tr%d" % pb, "lv2"], w=[ak])
                    add("act", C("activation", out=bb[:, lo:hi], in_=tr[:, 0:n], func=AF.Exp, scale=lv[:, CNEG + col:CNEG + col + 1], bias=lv[:, CNEG + col:CNEG + col + 1]), r=["tr%d" % pb, "lv1"], w=[bk])
                    add("dve", C("scalar_tensor_tensor", out=tbuf[:, lo:hi], in0=ti[:, 0:n], scalar=1.0, in1=xcb[:, lo:hi], op0=ALU.add, op1=ALU.mult), r=["ti%d" % pb, "xcb"], w=["t"])
                while deferred:
                    deferred.pop(0)()
                add("act", C("activation", out=bb, in_=bb, func=AF.Sqrt, scale=-0.25, bias=0.25), r=[bk], w=[bk])
                add("dve", C("tensor_tensor", out=bb, in0=bb, in1=tbuf, op=ALU.mult), r=[bk, "t"], w=[bk])
                if d == 0:
                    def _scan0(bb=bb, abuf=abuf, bk=bk, ak=ak):
                        add("dve", C("tensor_tensor_scan", out=bb[:, 0:NCTX], data0=abuf[:, 0:NCTX], data1=bb[:, 0:NCTX], initial=0.0, op0=ALU.mult, op1=ALU.add), r=[bk, ak], w=[bk])
                        add("dve", C("tensor_tensor_scan", out=bb[:, NCTX:NTOK], data0=abuf[:, NCTX:NTOK], data1=bb[:, NCTX:NTOK], initial=bb[:, NCTX - 1:NCTX], op0=ALU.mult, op1=ALU.add), r=[bk, ak], w=[bk])
                    deferred.append(_scan0)
                else:
                    add("dve", C("tensor_tensor_scan", out=bb[:, 0:NCTX][:, ::-1], data0=abuf[:, 0:NCTX][:, ::-1], data1=bb[:, 0:NCTX][:, ::-1], initial=0.0, op0=ALU.mult, op1=ALU.add), r=[bk, ak], w=[bk])
                    add("dve", C("tensor_tensor_scan", out=bb[:, NCTX:NTOK][:, ::-1], data0=abuf[:, NCTX:NTOK][:, ::-1], data1=bb[:, NCTX:NTOK][:, ::-1], initial=bb[:, 0:1], op0=ALU.mult, op1=ALU.add), r=[bk, ak], w=[bk])
            add("pool", C("tensor_tensor", out=bbuf[0][:, NCTX:NTOK], in0=bbuf[0][:, NCTX:NTOK], in1=bbuf[1][:, NCTX:NTOK], op=ALU.add), r=["b0", "b1"], w=["b0"])
            add("act", C("activation", out=tbuf[:, 0:SEQ], in_=grS[:, c, :], func=AF.Gelu_apprx_tanh), r=["gr%d" % c, "t"], w=["t"])
            if c + 1 < 4:
                conv(c + 1)
            add("dve", C("tensor_tensor", out=grS[:, c, :], in0=bbuf[0][:, NCTX:NTOK], in1=tbuf[:, 0:SEQ], op=ALU.mult), r=["b0", "t"], w=["gr%d" % c])

    def phase_D(s):
        WK.reset()
        wout = WK.alloc([8, D], BF16)
        g1bc = WK.alloc([D], F32)
        xt2 = [WK.alloc([D], F32) for _ in range(2)]
        x1 = [WK.alloc([D], F32) for _ in range(2)]
        xn2 = [WK.alloc([1056], BF16) for _ in range(2)]
        h2T = [WK.alloc([8, 128], BF16) for _ in range(2)]
        junk = WK.alloc([D], BF16)
        sm = [WK.alloc([24], F32) for _ in range(2)]
        ex = [WK.alloc([16], F32) for _ in range(2)]
        for h in range(2):
            add("pool", C("dma_start", out=wout[:, h * 4:(h + 1) * 4, :], in_=wout_v[:, h * 4:(h + 1) * 4, :]), w=["wout%d" % h], key="win%d" % h)
        add("sp", C("dma_start", out=g1bc, in_=gsc_d[s, 0, :].partition_broadcast(128)), w=["g1bc"], key="c0")
        for kc in range(8):
            add("dve", C("tensor_tensor", out=wout[:, kc, :], in0=wout[:, kc, :], in1=g1bc, op=ALU.mult), r=["wout0", "wout1", "g1bc"], w=["wout%d" % (kc // 4)])
        py = [[banks[0], banks[1]], [banks[2], banks[3]]]
        pT2 = [banks[4].bitcast(BF16).rearrange("p (a b) -> p a b", a=8), banks[5].bitcast(BF16).rearrange("p (a b) -> p a b", a=8)]
        pl = [banks[6], banks[7]]

        def s1(tt):
            b = tt % 2
            cols = slice(tt * 128, (tt + 1) * 128)
            r0 = s * SEQ + tt * 128
            add("sp", C("dma_start", out=xt2[b], in_=x_d[r0:r0 + 128, :]), w=["xt2%d" % b], key="xt%d" % b)
            for half in range(2):
                for kc in range(8):
                    lhs = qT[:, kc, cols] if kc < 4 else grS[:, kc - 4, cols]
                    rk = "qT%d" % tt if kc < 4 else "gr%d" % (kc - 4)
                    add("pe", C("matmul", py[b][half][:, :], lhsT=lhs, rhs=wout[:, kc, half * 512:(half + 1) * 512], start=(kc == 0), stop=(kc == 7)),
                        r=[rk, "wout0", "wout1"], w=["py%d%d" % (b, half)])

        def s2(tt):
            b = tt % 2
            r0 = s * SEQ + tt * 128
            for half in range(2):
                hs = slice(half * 512, (half + 1) * 512)
                add("dve", C("tensor_tensor", out=x1[b][:, hs], in0=py[b][half][:, :], in1=xt2[b][:, hs], op=ALU.add), r=["py%d%d" % (b, half), "xt2%d" % b], w=["x1%d" % b])
            add("sp", C("dma_start", out=out_d[r0:r0 + 128, :], in_=x1[b]), r=["x1%d" % b], key="o%d" % b)
            add("act", C("activation", out=junk, in_=x1[b], func=AF.Square, accum_out=sm[b][:, 0:1]), r=["x1%d" % b], w=["junk", "sm%d" % b])

        def s3(tt):
            b = tt % 2
            add("act", C("activation", out=sm[b][:, 0:1], in_=sm[b][:, 0:1], func=AF.Ln, scale=1.0 / D, bias=EPS), r=["sm%d" % b], w=["sm%d" % b])
            add("act", C("activation", out=sm[b][:, 0:1], in_=sm[b][:, 0:1], func=AF.Exp, scale=-0.5), r=["sm%d" % b], w=["sm%d" % b])
            add("dve", C("tensor_scalar", out=xn2[b][:, 0:D], in0=x1[b], scalar1=sm[b][:, 0:1], scalar2=None, op0=ALU.mult), r=["x1%d" % b, "sm%d" % b], w=["xn2%d" % b])

        def s4(tt):
            b = tt % 2
            for fc in range(8):
                add("pe", C("transpose", out=pT2[b][:, fc, :], in_=xn2[b][:, fc * 128:(fc + 1) * 128], identity=ident), r=["xn2%d" % b, "cm"], w=["pT2%d" % b])
            for fc in range(8):
                if b == 0:
                    add("act", C("activation", out=h2T[b][:, fc, :], in_=pT2[b][:, fc, :], func=AF.Identity, scale=A2[:, fc, s:s + 1], bias=B2[:, fc, s:s + 1]), r=["pT2%d" % b, "A2", "modv"], w=["h2T%d_%d" % (b, fc)])
                else:
                    add("dve", C("tensor_scalar", out=h2T[b][:, fc, :], in0=pT2[b][:, fc, :], scalar1=A2[:, fc, s:s + 1], scalar2=B2[:, fc, s:s + 1], op0=ALU.mult, op1=ALU.add), r=["pT2%d" % b, "A2", "modv"], w=["h2T%d_%d" % (b, fc)])
            for kc in range(8):
                add("pe", C("matmul", pl[b][:, 0:16], lhsT=h2T[b][:, kc, :], rhs=wr[:, kc, :], start=(kc == 0), stop=(kc == 7)), r=["h2T%d_%d" % (b, fc) for fc in range(8)] + ["wr"], w=["pl%d" % b])

        def s5(tt):
            b = tt % 2
            r0 = s * SEQ + tt * 128
            add("dve", C("tensor_reduce", out=sm[b][:, 1:2], in_=pl[b][:, 0:16], axis=AX.X, op=ALU.max), r=["pl%d" % b], w=["mx%d" % b])
            add("dve", C("tensor_scalar", out=sm[b][:, 1:2], in0=sm[b][:, 1:2], scalar1=-1.0, scalar2=None, op0=ALU.mult), r=["mx%d" % b], w=["mx%d" % b])
            add("act", C("activation", out=ex[b], in_=pl[b][:, 0:16], func=AF.Exp, bias=sm[b][:, 1:2], accum_out=sm[b][:, 2:3]), r=["pl%d" % b, "mx%d" % b], w=["ex%d" % b, "se%d" % b])
            add("dve", C("reciprocal", out=sm[b][:, 2:3], in_=sm[b][:, 2:3]), r=["se%d" % b], w=["se%d" % b])
            add("dve", C("tensor_scalar", out=Aall[:, tt, s * 16:(s + 1) * 16], in0=ex[b], scalar1=sm[b][:, 2:3], scalar2=None, op0=ALU.mult), r=["ex%d" % b, "se%d" % b], w=["Aall"])
            add("dve", C("tensor_copy", out=xn2[b][:, D:D + 16], in_=Aall[:, tt, s * 16:(s + 1) * 16]), r=["Aall"], w=["xn2a%d" % b])
            add("dve", C("tensor_tensor", out=xn2[b][:, D + 16:1056], in0=Aall[:, tt, s * 16:(s + 1) * 16], in1=xn2[b][:, D:D + 16], op=ALU.subtract), r=["Aall", "xn2a%d" % b], w=["xn2a%d" % b])
            add("sp", C("dma_start", out=xn_d[r0:r0 + 128, :], in_=xn2[b]), r=["xn2%d" % b, "xn2a%d" % b], key="xn%d" % b)

        stages = [s1, s2, s3, s4, s5]
        NT = 32
        for step in range(NT + len(stages) - 1):
            for si in reversed(range(len(stages))):
                tt = step - si
                if 0 <= tt < NT:
                    stages[si](tt)

    def phase_topk():
        WK.reset()
        ST.reset()
        AT = ST.alloc([SEQ], F32)
        ones = ST.alloc([SEQ], F32)
        Mk = ST.alloc([SEQ], F32)
        cs = ST.alloc([SEQ], F32)
        bs = WK.alloc([8], F32)
        for tt in range(32):
            pb = banks[tt % 2]
            add("pe", C("transpose", out=pb[0:32, 0:128], in_=Aall[:, tt, :], identity=identf[:]), r=["Aall", "identf"], w=["pAT%d" % (tt % 2)])
            add("act", C("copy", out=AT[0:32, tt * 128:(tt + 1) * 128], in_=pb[0:32, 0:128]), r=["pAT%d" % (tt % 2)], w=["AT"])
        add("pool", C("memset", ones[0:32, :], 1.0), w=["ones"])
        mid, cntc, stp = bs[0:32, 0:1], bs[0:32, 1:2], bs[0:32, 2:3]
        add("dve", C("memset", mid, 0.5), w=["mid"])
        NIT = 24
        for k in range(NIT):
            wk = 2.0 ** -(k + 1)
            wn = 2.0 ** -(k + 2)
            add("dve", C("tensor_scalar", out=Mk[0:32, :], in0=AT[0:32, :], scalar1=mid, scalar2=None, op0=ALU.is_ge, op1=ALU.add, accum_out=cntc), r=["AT", "mid"], w=["Mk", "cnt"])
            add("dve", C("tensor_scalar", out=stp, in0=cntc, scalar1=float(CAP), scalar2=wk, op0=ALU.is_ge, op1=ALU.mult), r=["cnt"], w=["stp"])
            last = (k == NIT - 1)
            delta = (-wk) if last else (wn - wk)
            add("dve", C("scalar_tensor_tensor", out=mid, in0=stp, scalar=delta, in1=mid, op0=ALU.add, op1=ALU.add), r=["stp", "mid"], w=["mid"])
        add("dve", C("tensor_scalar", out=Mk[0:32, :], in0=AT[0:32, :], scalar1=mid, scalar2=None, op0=ALU.is_ge), r=["AT", "mid"], w=["Mk"])
        add("dve", C("tensor_tensor_scan", out=cs[0:32, :], data0=ones[0:32, :], data1=Mk[0:32, :], initial=0.0, op0=ALU.mult, op1=ALU.add), r=["ones", "Mk"], w=["cs"])
        add("dve", C("tensor_tensor", out=cs[0:32, :], in0=cs[0:32, :], in1=Mk[0:32, :], op=ALU.mult), r=["cs", "Mk"], w=["cs"])
        for thr in (129.0, 257.0, 385.0):
            add("dve", C("scalar_tensor_tensor", out=Mk[0:32, :], in0=cs[0:32, :], scalar=thr, in1=Mk[0:32, :], op0=ALU.is_ge, op1=ALU.add), r=["cs", "Mk"], w=["Mk"])
        add("dve", C("scalar_tensor_tensor", out=cs[0:32, :], in0=Mk[0:32, :], scalar=-128.0, in1=cs[0:32, :], op0=ALU.mult, op1=ALU.add), r=["Mk", "cs"], w=["cs"])
        PH = WK.alloc([32, 32], BF16)
        PL = WK.alloc([32, 32], BF16)
        Lb = [WK.alloc([32, 128], BF16) for _ in range(2)]
        Ht = [WK.alloc([32, 4, 2], BF16) for _ in range(2)]
        eq = [WK.alloc([32, 4], F32) for _ in range(2)]
        pph = [banks[2], banks[3]]
        for tt in range(32):
            pb = pph[tt % 2]
            add("pe", C("transpose", out=pb[:, 0:32], in_=Mk[0:32, tt * 128:(tt + 1) * 128], identity=identf[0:32, 0:32]), r=["Mk", "identf"], w=["pph%d" % (tt % 2)])
            add("pe", C("transpose", out=pb[:, 32:64], in_=cs[0:32, tt * 128:(tt + 1) * 128], identity=identf[0:32, 0:32]), r=["cs", "identf"], w=["pph%d" % (tt % 2)])
            add("act", C("copy", out=PH[:, tt, :], in_=pb[:, 0:32]), r=["pph%d" % (tt % 2)], w=["PH%d" % tt])
            add("act", C("copy", out=PL[:, tt, :], in_=pb[:, 32:64]), r=["pph%d" % (tt % 2)], w=["PL%d" % tt])
        pIdx = banks[4][:, 0:256].rearrange("p (a c k) -> p a c k", a=32, c=4)
        iop = cm[:, CM_IOP, :]
        ioc = cm[:, CM_IOC, 0:4]
        for tt in range(32):
            b = tt % 2
            add("dve", C("tensor_tensor", out=Lb[b], in0=iop.unsqueeze(1).to_broadcast([128, 32, 128]), in1=PL[:, tt, :].unsqueeze(2).to_broadcast([128, 32, 128]), op=ALU.is_equal), r=["PL%d" % tt, "cm"], w=["Lb%d" % b])
            add("dve", C("tensor_tensor", out=eq[b], in0=ioc.unsqueeze(1).to_broadcast([128, 32, 4]), in1=PH[:, tt, :].unsqueeze(2).to_broadcast([128, 32, 4]), op=ALU.is_equal), r=["PH%d" % tt, "cm"], w=["eq%d" % b])
            for k2 in range(2):
                add("dve", C("tensor_scalar", out=Ht[b][:, :, :, k2], in0=eq[b], scalar1=tcol[:, tt, k2:k2 + 1], scalar2=None, op0=ALU.mult), r=["eq%d" % b, "tcol"], w=["Ht%d_%d" % (b, k2)])
            for se in range(32):
                first = (tt == 0 and se == 0)
                add("pe", C("matmul", pIdx[:, se, :, :].rearrange("p c k -> p (c k)"), lhsT=Lb[b][:, se, :], rhs=Ht[b][:, se, :, :].rearrange("p c k -> p (c k)"), start=first, stop=(tt == 31), skip_group_check=True),
                    r=["Lb%d" % b, "Ht%d_0" % b, "Ht%d_1" % b], w=["pIdx"])
        idf = WK.alloc([32, 4], F32)
        add("dve", C("tensor_copy", out=idf, in_=pIdx[:, :, :, 1]), r=["pIdx"], w=["idf"])
        add("dve", C("scalar_tensor_tensor", out=idf, in0=pIdx[:, :, :, 0], scalar=64.0, in1=idf, op0=ALU.mult, op1=ALU.add), r=["pIdx", "idf"], w=["idf"])
        add("dve", C("tensor_scalar", out=idf[:, 16:32, :], in0=idf[:, 16:32, :], scalar1=float(SEQ), scalar2=None, op0=ALU.add), r=["idf"], w=["idf"])
        add("dve", C("tensor_copy", out=IDX[:], in_=idf), r=["idf"], w=["IDX"])
        if "idx" in dbg:
            add("sp", C("dma_start", out=dbg["idx"][:, :, :], in_=IDX[:]), r=["IDX"], key="dbg1")

    def phase_moe():
        ST.reset()
        WK.reset()
        wts = [[ST.alloc([8, D], BF16) for _ in range(3)] for _ in range(2)]
        g2bc = [ST.alloc([D], F32) for _ in range(2)]
        xgs = [[WK.alloc([1056], BF16) for _ in range(8)] for _ in range(2)]
        xsT = WK.alloc([8, 1024], BF16)
        hidT = WK.alloc([8, 1024], BF16)
        sg = [WK.alloc([512], BF16) for _ in range(2)]
        yo = [WK.alloc([D], F32) for _ in range(2)]
        gts = [WK.alloc([8], F32) for _ in range(4)]
        for s in range(2):
            add("sp", C("dma_start", out=g2bc[s], in_=gsc_d[s, 1, :].partition_broadcast(128)), w=["g2bc%d" % s], key="c%d" % s)
        pX = [banks[0].bitcast(BF16).rearrange("p (a b) -> p a b", a=4), banks[1].bitcast(BF16).rearrange("p (a b) -> p a b", a=4)]
        pG = [banks[2], banks[3]]
        pU = [banks[4], banks[5]]
        pY = [banks[6], banks[7]]
        srcs = [wg_d, wu_d, wd_d]
        ctr = dict(y=0, gu=0)

        def load_w(e_):
            wb = e_ % 2
            for m in range(3):
                v = srcs[m][e_].rearrange("(kc p) n -> p kc n", p=128)
                for h in range(2):
                    add("pool", C("dma_start", out=wts[wb][m][:, h * 4:(h + 1) * 4, :], in_=v[:, h * 4:(h + 1) * 4, :]), w=["w%d_%d_%d" % (wb, m, h)], key="w%d%d" % (m, h))

        def gather(e_):
            gb_ = e_ % 2
            for s in range(2):
                for c in range(4):
                    st_ = s * 4 + c
                    add("pool", C("indirect_dma_start", out=xgs[gb_][st_], out_offset=None, in_=xn_d[:, :], in_offset=bass.IndirectOffsetOnAxis(ap=IDX[:, s * 16 + e_, c:c + 1], axis=0)),
                        r=["IDX"], w=["xg%d_%d" % (gb_, st_)], key="g%d_%d" % (gb_, st_))

        def xpose_unit(e2, u):
            pair, hf = u // 2, u % 2
            s = pair // 2
            pb = u % 2
            xg2 = xgs[e2 % 2]
            for st2 in range(2):
                st_ = pair * 2 + st2
                for f4 in range(4):
                    fc = hf * 4 + f4
                    add("pe", C("transpose", out=pX[pb][:, f4, st2 * 128:(st2 + 1) * 128], in_=xg2[st_][:, fc * 128:(fc + 1) * 128], identity=ident),
                        r=["xg%d_%d" % (e2 % 2, st_), "cm"], w=["pX%d" % pb])
            for f4 in range(4):
                fc = hf * 4 + f4
                dst = xsT[:, fc, pair * 256:(pair + 1) * 256]
                if pb == 0:
                    add("act", C("activation", out=dst, in_=pX[pb][:, f4, :], func=AF.Identity, scale=A2[:, fc, s:s + 1], bias=B2[:, fc, s:s + 1]), r=["pX%d" % pb, "A2", "modv"], w=["xsT%d" % pair])
                else:
                    add("dve", C("tensor_scalar", out=dst, in0=pX[pb][:, f4, :], scalar1=A2[:, fc, s:s + 1], scalar2=B2[:, fc, s:s + 1], op0=ALU.mult, op1=ALU.add), r=["pX%d" % pb, "A2", "modv"], w=["xsT%d" % pair])

        gather(0)
        load_w(0)
        for u in range(8):
            xpose_unit(0, u)
        for e_ in range(NEXP):
            wb = e_ % 2
            wgt, wut, wdt = wts[wb]
            xg = xgs[e_ % 2]
            xgk = lambda st_: "xg%d_%d" % (e_ % 2, st_)
            if e_ + 1 < NEXP:
                gather(e_ + 1)
                load_w(e_ + 1)
            for f in range(8):
                for half in range(2):
                    gb = ctr["gu"] % 2
                    ctr["gu"] += 1
                    xk = ["xsT%d" % (half * 2), "xsT%d" % (half * 2 + 1)]
                    for kc in range(8):
                        add("pe", C("matmul", pG[gb][:, :], lhsT=wgt[:, kc, f * 128:(f + 1) * 128], rhs=xsT[:, kc, half * 512:(half + 1) * 512], start=(kc == 0), stop=(kc == 7)),
                            r=xk + ["w%d_0_0" % wb, "w%d_0_1" % wb], w=["pG%d" % gb])
                    for kc in range(8):
                        add("pe", C("matmul", pU[gb][:, :], lhsT=wut[:, kc, f * 128:(f + 1) * 128], rhs=xsT[:, kc, half * 512:(half + 1) * 512], start=(kc == 0), stop=(kc == 7)),
                            r=xk + ["w%d_1_0" % wb, "w%d_1_1" % wb], w=["pU%d" % gb])
                    add("act", C("activation", out=sg[gb], in_=pG[gb][:, :], func=AF.Silu), r=["pG%d" % gb], w=["sg%d" % gb])
                    add("dve", C("tensor_tensor", out=hidT[:, f, half * 512:(half + 1) * 512], in0=pU[gb][:, :], in1=sg[gb], op=ALU.mult), r=["pU%d" % gb, "sg%d" % gb], w=["hid%d" % half])
            for st_ in range(8):
                s, c = st_ // 4, st_ % 4
                for half in range(2):
                    for fk in range(8):
                        add("pe", C("matmul", pY[half][:, :], lhsT=hidT[:, fk, st_ * 128:(st_ + 1) * 128], rhs=wdt[:, fk, half * 512:(half + 1) * 512], start=(fk == 0), stop=(fk == 7)),
                            r=["hid%d" % s, "w%d_2_0" % wb, "w%d_2_1" % wb], w=["pY%d" % half])
                yb = ctr["y"] % 2
                ctr["y"] += 1
                gate = gts[ctr["y"] % 4][:, 0:1]
                gk_ = "gt%d" % (ctr["y"] % 4)
                add("dve", C("tensor_tensor", out=gate, in0=xg[st_][:, D + e_:D + e_ + 1], in1=xg[st_][:, D + 16 + e_:D + 16 + e_ + 1], op=ALU.add), r=[xgk(st_)], w=[gk_])
                for half in range(2):
                    hs = slice(half * 512, (half + 1) * 512)
                    add("dve", C("scalar_tensor_tensor", out=yo[yb][:, hs], in0=pY[half][:, :], scalar=gate, in1=g2bc[s][:, hs], op0=ALU.mult, op1=ALU.mult),
                        r=["pY%d" % half, gk_, "g2bc%d" % s], w=["yo%d_%d" % (yb, half)])
                prev = ["sc_%d_%d_%d" % (s, e_ - 1, cc) for cc in range(4)] if e_ > 0 else []
                add("pool", C("indirect_dma_start", out=out_d[:, :], out_offset=bass.IndirectOffsetOnAxis(ap=IDX[:, s * 16 + e_, c:c + 1], axis=0), in_=yo[yb], in_offset=None, compute_op=ALU.add, oob_is_err=True),
                    r=["yo%d_0" % yb, "yo%d_1" % yb, "IDX"] + prev, w=["sc_%d_%d_%d" % (s, e_, c)], key="sc%d" % st_)
                if e_ + 1 < NEXP:
                    xpose_unit(e_ + 1, st_)

    nsamp = 2
    stages = ["A", "B", "C", "D", "T", "full"]
    lvl = stages.index(stage) if stage in stages else -1
    if stage == "0":
        nsamp = 0
    if stage == "A1":
        nsamp = 1
    import os as _os
    if _os.environ.get("NSAMP"):
        nsamp = int(_os.environ["NSAMP"])
    for s in range(nsamp):
        phase_A(s)
        P.barrier()
        if stage == "A" and s == 0:
            add("sp", C("dma_start", out=dbg["qT"][:, :, :], in_=qT), key="dbg0")
            add("sp", C("dma_start", out=dbg["kT"][:, :], in_=kT), key="dbg1")
            add("sp", C("dma_start", out=dbg["v"][:, :, :, :], in_=vaug), key="dbg0")
            add("sp", C("dma_start", out=dbg["xr"][:, :, :], in_=xrS), key="dbg1")
            add("sp", C("dma_start", out=dbg["gr"][:, :, :], in_=grS), key="dbg0")
            P.barrier()
        if lvl >= 1:
            phase_B(s)
            P.barrier()
            if stage == "B" and s == 0:
                add("sp", C("dma_start", out=dbg["qT"][:, :, :], in_=qT), key="dbg0")
                P.barrier()
        if lvl >= 2:
            phase_C(s)
            P.barrier()
            if stage == "C" and s == 0:
                add("sp", C("dma_start", out=dbg["gr"][:, :, :], in_=grS), key="dbg0")
                add("sp", C("dma_start", out=dbg["qT"][:, :, :], in_=qT), key="dbg1")
                P.barrier()
        if lvl >= 3:
            phase_D(s)
            P.barrier()
    if lvl >= 4:
        phase_topk()
        P.barrier()
        if "aall" in dbg:
            add("sp", C("dma_start", out=dbg["aall"][:, :, :], in_=Aall[:]), key="dbg0")
    if lvl >= 5:
        phase_moe()
    P.barrier()
    P.emit()
    P.close()
    return nc, P.stats


def _swap_idx():
    d = np.arange(64)
    return np.where((d % 32) < 16, d + 16, d - 16)


def _consts():
    bf = ml_dtypes.bfloat16
    cm = np.zeros((128, NCM, 128), np.float32)
    cm[:, CM_ID, :] = np.eye(128)
    for h in range(2):
        cm[h * 64:(h + 1) * 64, CM_ONES, h * 64:(h + 1) * 64] = 1.0
    sw = _swap_idx()
    for m in range(128):
        k = (m // 64) * 64 + sw[m % 64]
        cm[k, CM_SWAP, m] = 1.0
    cm[:, CM_IOP, :] = (np.arange(128) - 127)[None, :]
    cm[:, CM_IOC, 0:4] = np.arange(1, 5)[None, :]
    identf = np.eye(128, dtype=np.float32)
    a = np.arange(128)
    prev = (a[None, :] <= a[:, None]).astype(np.float32)
    nxt = (a[:, None] <= a[None, :]).astype(np.float32)
    masks = np.stack([np.tile(prev, (1, 4)), np.tile(nxt, (1, 4))], axis=1)
    t = np.arange(SEQ)
    row = (t // 64).astype(np.float32)
    col = (t % 64).astype(np.float32)
    inv = (10000.0 ** (-np.arange(16, dtype=np.float32) / 16)).astype(np.float32)
    ar = (row[None, :] * inv[:, None]).astype(np.float32)
    ac = (col[None, :] * inv[:, None]).astype(np.float32)
    C = np.zeros((64, SEQ), np.float32)
    S2 = np.zeros((64, SEQ), np.float32)
    C[0:16] = np.cos(ar); C[16:32] = np.cos(ar); C[32:48] = np.cos(ac); C[48:64] = np.cos(ac)
    S2[0:16] = np.sin(ar); S2[16:32] = -np.sin(ar); S2[32:48] = np.sin(ac); S2[48:64] = -np.sin(ac)
    rope = np.stack([np.tile(C, (2, 1)), np.tile(S2, (2, 1))], axis=1)
    tok = np.arange(32)[None, :] * 128 + np.arange(128)[:, None]
    tcol = np.stack([tok // 64, tok % 64], axis=2).astype(np.float32)
    return dict(cm=cm.astype(bf), identf=identf, masks=masks.astype(bf), rope=rope.astype(bf), tcol=tcol)


def prep_inputs(inp):
    f = np.float32
    g = lambda k: np.asarray(inp[k], dtype=f)
    x, c, ctx, c_ctx = g("x"), g("c"), g("ctx"), g("c_ctx")
    L = 0
    w_in = g("w_in")[L]
    qperm = np.concatenate([np.r_[b * 64:(b + 1) * 64, (b + 4) * 64:(b + 5) * 64] for b in range(4)])
    w_in_p = np.ascontiguousarray(np.concatenate([w_in[:, qperm], w_in[:, 512:]], axis=1))
    w_out = g("w_out")[L]
    w_out_p = np.ascontiguousarray(np.concatenate([w_out[qperm, :], w_out[512:, :]], axis=0))
    b_ada = g("b_ada")[L]
    vecs = np.zeros((128, NV), f)
    pc = lambda v, n: v.reshape(n, 128).T
    vecs[:, V_N1G:V_N1G + 8] = pc(g("norm1_g")[L], 8)
    vecs[:, V_N2G:V_N2G + 8] = pc(g("norm2_g")[L], 8)
    vecs[:, V_BADA:V_BADA + 48] = pc(b_ada, 48)
    gq, gk = g("q_norm_g")[L], g("k_norm_g")[L]
    vecs[:, V_GQ] = np.tile(gq, 2)
    vecs[:, V_GK] = np.tile(gk, 2)
    cw = g("conv_w")[L]
    for cch in range(4):
        for j in range(4):
            vecs[:, V_CONVW + cch * 4 + j] = cw[j, cch * 128:(cch + 1) * 128]
    vecs[:, V_CONVB:V_CONVB + 4] = pc(g("conv_b")[L], 4)
    for d in range(2):
        vecs[:, V_BR + d * 4:V_BR + d * 4 + 4] = pc(g("lru_b_r")[L][d], 4)
        vecs[:, V_BI + d * 4:V_BI + d * 4 + 4] = pc(g("lru_b_i")[L][d], 4)
        vecs[:, V_LAM + d * 4:V_LAM + d * 4 + 4] = pc(g("lru_lambda")[L][d], 4)
    vecs[:, V_SINK:V_SINK + 8] = g("attn_sink")[L][None, :]
    vecs[:, V_GQROW:V_GQROW + 64] = gq[None, :]
    vecs[:, V_GKROW:V_GKROW + 64] = gk[None, :]
    lruw = np.zeros((128, 16, 128), f)
    for d in range(2):
        for gi, nm in enumerate(["lru_w_r", "lru_w_i"]):
            w = g(nm)[L][d]
            for cch in range(4):
                for nl in range(2):
                    lruw[nl * 64:(nl + 1) * 64, (d * 2 + gi) * 4 + cch, nl * 64:(nl + 1) * 64] = w[cch * 2 + nl]
    badag = np.stack([np.stack([b_ada[2 * D:3 * D], b_ada[5 * D:6 * D]])] * 3)
    wrt = np.ascontiguousarray(g("w_router")[L].reshape(8, 128, 16).transpose(1, 0, 2))
    consts = _consts()
    shared = dict(w_ada=g("w_ada")[L], badag=badag, vecs=vecs, w_in=w_in_p, lruw=lruw, w_out=w_out_p, w_router=wrt,
                  w_gate=g("w_gate")[L], w_up=g("w_up")[L], w_down=g("w_down")[L], **consts)
    maps = []
    for core in range(NCORES):
        s0 = 2 * core
        cc = np.stack([c[s0], c[s0 + 1], c_ctx])
        ccT = np.ascontiguousarray(cc.reshape(3, 8, 128).transpose(2, 1, 0))
        m = dict(shared)
        m["x"] = np.ascontiguousarray(x[s0:s0 + 2].reshape(2 * SEQ, D))
        m["ctx"] = np.ascontiguousarray(ctx[s0:s0 + 2].reshape(2 * NCTX, D))
        m["ccT"] = ccT
        maps.append(m)
    return maps


_CACHE = {}


def kernel(**inputs):
    maps = prep_inputs(inputs)
    if "nc" not in _CACHE:
        _CACHE["nc"] = build_program("full")[0]
    nc = _CACHE["nc"]
    res = run_bass_kernel_spmd(nc, maps, core_ids=list(range(NCORES)))
    outs = [np.asarray(r["out"]).reshape(2, SEQ, D) for r in res.results]
    return np.concatenate(outs, axis=0).astype(np.float32)
```

```python
import numpy as np
import ml_dtypes
import concourse.bass as bass
import concourse.mybir as mybir
from concourse.bass_utils import run_bass_kernel_spmd
from contextlib import ExitStack

AF = mybir.ActivationFunctionType
ALU = mybir.AluOpType
AX = mybir.AxisListType
F32 = mybir.dt.float32
BF16 = mybir.dt.bfloat16
I32 = mybir.dt.int32

NCORES = 8
SEQ = 4096
NCTX = 256
NTOK = SEQ + NCTX
D = 1024
EPS = 1e-6
NEXP = 16
CAP = 512


class Op:
    __slots__ = ("eng", "fn", "deps", "dma", "key", "signal", "ticket")

    def __init__(self, eng, fn, dma, key):
        self.eng = eng
        self.fn = fn
        self.dma = dma
        self.key = key
        self.deps = []
        self.signal = False
        self.ticket = 0


ENGS = ["pe", "act", "dve", "pool", "sp"]


def C(name, *a, **k):
    return (name, a, k)


class Prog:
    def __init__(self, nc):
        self.nc = nc
        self.ops = []
        self.last_w = {}
        self.readers = {}
        self.dma_last = {}
        self.last_eng = {}
        self.bank_last = {}
        self.stack = ExitStack()

    def sb(self, name, shape, dtype):
        return self.stack.enter_context(self.nc.sbuf_tensor("sb_" + name, list(shape), dtype))

    def ps(self, name, shape, dtype):
        return self.stack.enter_context(self.nc.psum_tensor("ps_" + name, list(shape), dtype))

    dead = False

    def cut(self, n):
        import os
        if int(os.environ.get("CUT", "-1")) == n:
            self.dead = True

    def add(self, eng, fn, r=(), w=(), key=None):
        op = Op(eng, fn, key is not None, key)
        if self.dead:
            return op
        deps = {}
        for k in r:
            o = self.last_w.get(k)
            if o is not None:
                deps[id(o)] = o
        for k in w:
            o = self.last_w.get(k)
            if o is not None:
                deps[id(o)] = o
            for o in self.readers.get(k, {}).values():
                deps[id(o)] = o
        if op.dma:
            o = self.dma_last.get(key)
            if o is not None:
                deps[id(o)] = o
            self.dma_last[key] = op
        nm_, a_, k_ = fn
        for v in list(a_) + list(k_.values()):
            tn = getattr(getattr(v, "tensor", None), "name", "")
            if tn.startswith("ps_bank"):
                o = self.bank_last.get(tn)
                if o is not None and o.eng != eng:
                    deps[id(o)] = o
                self.bank_last[tn] = op
        for k in r:
            self.readers.setdefault(k, {})[(eng, key)] = op
        for k in w:
            self.last_w[k] = op
            self.readers[k] = {}
        for o in deps.values():
            if o is op:
                continue
            if (not o.dma) and (not op.dma) and o.eng == "pe" and op.eng == "pe":
                continue
            op.deps.append(o)
            o.signal = True
        if not op.dma:
            self.last_eng[eng] = op
        self.ops.append(op)
        return op

    def barrier(self):
        prev = [o for o in self.last_eng.values()] + list(self.dma_last.values())
        new = []
        for e in ENGS:
            op = Op(e, C("nop", ), False, None)
            for o in prev:
                if o.eng == e and not o.dma and e == "pe":
                    continue
                op.deps.append(o)
                o.signal = True
            self.ops.append(op)
            new.append(op)
        for op in new:
            self.last_eng[op.eng] = op
        self.last_w = {}
        self.readers = {}

    def emit(self):
        nc = self.nc
        cnt = {e: 0 for e in ENGS}
        dcnt = {}
        for op in self.ops:
            if op.dma:
                dcnt[op.key] = dcnt.get(op.key, 0) + 16
                op.ticket = dcnt[op.key]
            elif op.signal:
                cnt[op.eng] += 1
                op.ticket = cnt[op.eng]
        sems = {}
        for e in ENGS:
            if cnt[e]:
                sems[("c", e)] = self.stack.enter_context(nc.semaphore("s_" + e))
        for k in dcnt:
            sems[("d", k)] = self.stack.enter_context(nc.semaphore("d_" + str(k)))

        def semof(o):
            return sems[("d", o.key)] if o.dma else sems[("c", o.eng)]

        per = {e: [] for e in ENGS}
        for op in self.ops:
            per[op.eng].append(op)

        def run(name, eng):
            seen = {}
            for op in per[name]:
                need = {}
                for d in op.deps:
                    s = semof(d)
                    if need.get(id(s), (None, 0))[1] < d.ticket:
                        need[id(s)] = (s, d.ticket)
                for kk, (s, t) in need.items():
                    if seen.get(kk, 0) >= t:
                        continue
                    eng.wait_ge(s, t)
                    seen[kk] = t
                nm, a_, k_ = op.fn
                try:
                    ins = getattr(eng, nm)(*a_, **k_)
                except Exception:
                    print("EMIT FAIL at", per[name].index(op), "of", len(per[name]), "ndma_before", sum(1 for o in per[name][:per[name].index(op)] if o.dma), flush=True)
                    print("EMIT FAIL", name, nm, [repr(x)[:300] for x in a_], {kk: repr(vv)[:300] for kk, vv in k_.items()}, flush=True)
                    raise
                if op.dma:
                    ins.then_inc(semof(op), 16)
                elif op.signal:
                    ins.then_inc(semof(op), 1)

        with nc.Block() as block:
            @block.tensor
            def _(e):
                run("pe", e)

            @block.scalar
            def _(e):
                run("act", e)

            @block.vector
            def _(e):
                run("dve", e)

            @block.gpsimd
            def _(e):
                run("pool", e)

            @block.sync
            def _(e):
                run("sp", e)
        self.stats = dict(n={e: len(per[e]) for e in ENGS}, sig=cnt, nsems=len(sems))

    def close(self):
        self.stack.close()


class Arena:
    def __init__(self, P, name, nbytes):
        self.t = P.sb(name, [128, nbytes // 2], BF16)
        self.cap = nbytes
        self.off = 0

    def alloc(self, shape, dtype):
        esz = 2 if dtype == BF16 else 4
        n = int(np.prod(shape))
        nb = n * esz
        start = self.off
        self.off += (nb + 31) // 32 * 32
        assert self.off <= self.cap, ("arena overflow", self.off, self.cap)
        ap = self.t[:, start // 2:start // 2 + nb // 2]
        if dtype != BF16:
            ap = ap.bitcast(dtype)
        if len(shape) == 2:
            ap = ap.rearrange("p (a b) -> p a b", a=shape[0])
        elif len(shape) == 3:
            ap = ap.rearrange("p (a b c) -> p a b c", a=shape[0], b=shape[1])
        elif len(shape) == 4:
            ap = ap.rearrange("p (a b c d) -> p a b c d", a=shape[0], b=shape[1], c=shape[2])
        return ap

    def reset(self, off=0):
        self.off = off


V_N1G, V_N2G, V_BADA, V_GQ, V_GK, V_CONVW, V_CONVB, V_BR, V_BI, V_LAM = 0, 8, 16, 64, 65, 66, 82, 86, 94, 102
V_SINK, V_GQROW, V_GKROW, NV = 110, 118, 182, 246
CM_ID, CM_ONES, CM_SWAP, CM_IOP, CM_IOC, NCM = 0, 1, 2, 3, 4, 5


def build_program(stage="full"):
    nc = bass.Bass("TRN2", target_bir_lowering=False)

    def din(name, shape, dt=F32):
        return nc.dram_tensor(name, list(shape), dt, kind="ExternalInput").ap()

    x_d = din("x", [2 * SEQ, D])
    ctx_d = din("ctx", [2 * NCTX, D])
    ccT_d = din("ccT", [128, 8, 3])
    wada_d = din("w_ada", [D, 6 * D])
    badag_d = din("badag", [3, 2, D])
    vecs_d = din("vecs", [128, NV])
    cm_d = din("cm", [128, NCM, 128], BF16)
    identf_d = din("identf", [128, 128])
    masks_d = din("masks", [128, 2, 512], BF16)
    rope_d = din("rope", [128, 2, SEQ], BF16)
    tcol_d = din("tcol", [128, 32, 2])
    win_d = din("w_in", [D, 1792])
    lruw_d = din("lruw", [128, 16, 128])
    wout_d = din("w_out", [D, D])
    wr_d = din("w_router", [128, 8, 16])
    wg_d = din("w_gate", [NEXP, D, D])
    wu_d = din("w_up", [NEXP, D, D])
    wd_d = din("w_down", [NEXP, D, D])
    out_d = nc.dram_tensor("out", [2 * SEQ, D], F32, kind="ExternalOutput").ap()
    xn_d = nc.dram_tensor("xn_scr", [2 * SEQ, 1056], BF16, kind="Internal").ap()
    gsc_d = nc.dram_tensor("g_scr", [3, 2, D], F32, kind="Internal").ap()
    dbg = {}
    if stage != "full":
        dbg["qT"] = nc.dram_tensor("dbg_qT", [128, 4, SEQ], BF16, kind="ExternalOutput").ap()
        dbg["kT"] = nc.dram_tensor("dbg_kT", [128, NTOK], BF16, kind="ExternalOutput").ap()
        dbg["v"] = nc.dram_tensor("dbg_v", [128, 34, 2, 65], BF16, kind="ExternalOutput").ap()
        dbg["xr"] = nc.dram_tensor("dbg_xr", [128, 4, NTOK], BF16, kind="ExternalOutput").ap()
        dbg["gr"] = nc.dram_tensor("dbg_gr", [128, 4, SEQ], BF16, kind="ExternalOutput").ap()
        dbg["modv"] = nc.dram_tensor("dbg_modv", [128, 48, 3], F32, kind="ExternalOutput").ap()
        dbg["aall"] = nc.dram_tensor("dbg_aall", [128, 32, 32], F32, kind="ExternalOutput").ap()
        dbg["idx"] = nc.dram_tensor("dbg_idx", [128, 32, 4], I32, kind="ExternalOutput").ap()

    P = Prog(nc)
    add = P.add

    vecs = P.sb("vecs", [128, NV], F32)
    cm = P.sb("cm", [128, NCM, 128], BF16)
    identf = P.sb("identf", [128, 128], F32)
    masks = P.sb("masks", [128, 2, 512], BF16)
    lruW = P.sb("lruW", [128, 16, 128], BF16)
    wr = P.sb("wr", [128, 8, 16], BF16)
    tcol = P.sb("tcol", [128, 32, 2], F32)
    modv = P.sb("modv", [128, 48, 3], F32)
    A1 = P.sb("A1", [128, 8, 3], F32)
    A2 = P.sb("A2", [128, 8, 3], F32)
    lv = P.sb("lv", [128, 40], F32)
    av = P.sb("av", [128, 16], F32)
    Aall = P.sb("Aall", [128, 32, 32], F32)
    IDX = P.sb("IDX", [128, 32, 4], I32)
    ident = cm[:, CM_ID, :]
    onesblk = cm[:, CM_ONES, :]
    pswap = cm[:, CM_SWAP, :]
    B1 = modv[:, 0:8, :]
    B2 = modv[:, 24:32, :]
    HBR, HBI, CNEG, HCNEG = 0, 8, 16, 24
    negB = av[:, 0:1]
    esink = av[:, 1:9]

    banks = [P.ps("bank%d" % i, [128, 512], F32) for i in range(8)]
    ST = Arena(P, "ST", 116 * 1024)
    WK = Arena(P, "WK", 77 * 1024)

    def ld(eng, out, in_, w, key):
        return add(eng, C("dma_start", out=out, in_=in_), w=w, key=key)

    ld("sp", vecs[:], vecs_d[:, :], ["vecs"], "c0")
    ld("sp", cm[:], cm_d[:, :, :], ["cm"], "c1")
    ld("sp", identf[:], identf_d[:, :], ["identf"], "c2")
    ld("sp", masks[:], masks_d[:, :, :], ["masks"], "c3")
    ld("sp", tcol[:], tcol_d[:, :, :], ["tcol"], "c0")
    ld("pool", lruW[:], lruw_d[:, :, :], ["lruW"], "c4")
    ld("pool", wr[:], wr_d[:, :, :], ["wr"], "c5")
    ccT = WK.alloc([8, 3], F32)
    scT = WK.alloc([8, 3], F32)
    badag = WK.alloc([2, D], F32)
    grow = WK.alloc([2, D], F32)
    wa = [WK.alloc([8, 512], F32) for _ in range(2)]
    tmpv = WK.alloc([64], F32)
    rowsb = [WK.alloc([512], F32) for _ in range(2)]
    ld("sp", ccT, ccT_d[:, :, :], ["ccT"], "c2")
    ld("sp", badag[0:3], badag_d[:, :, :], ["badag"], "c3")
    add("act", C("activation", out=scT, in_=ccT, func=AF.Silu), r=["ccT"], w=["scT"])
    pm = banks[0][:, 0:192].rearrange("p (a b) -> p a b", a=48)
    prow = banks[1]
    wada_v = wada_d.rearrange("(kc p) n -> p kc n", p=128)
    P.cut(1)
    pc = 0
    for m in range(6):
        for half in range(2):
            b = pc % 2
            pc += 1
            c0 = m * D + half * 512
            ld("sp", wa[b], wada_v[:, :, c0:c0 + 512], ["wa%d" % b], "wa%d" % b)
            for kc in range(8):
                add("pe", C("matmul", prow[0:3, :], lhsT=scT[:, kc, :], rhs=wa[b][:, kc, :], start=(kc == 0), stop=(kc == 7)),
                    r=["wa%d" % b, "scT"], w=["prow"])
            rb = rowsb[pc % 2]
            add("dve", C("tensor_copy", out=rb[0:3, :], in_=prow[0:3, :]), r=["prow"], w=["rowsb%d" % (pc % 2)])
            for jj in range(4):
                j = m * 8 + half * 4 + jj
                add("pe", C("transpose", out=pm[:, j, 0:3], in_=rb[0:3, jj * 128:(jj + 1) * 128], identity=identf[0:3, 0:3]), r=["rowsb%d" % (pc % 2), "identf"], w=["pm"])
            if m in (2, 5):
                which = 0 if m == 2 else 1
                add("dve", C("tensor_tensor", out=grow[0:3, which, half * 512:(half + 1) * 512], in0=rb[0:3, :], in1=badag[0:3, which, half * 512:(half + 1) * 512], op=ALU.add),
                    r=["rowsb%d" % (pc % 2), "badag"], w=["grow"])
    P.cut(2)
    add("sp", C("dma_start", out=gsc_d[:, :, :], in_=grow[0:3]), r=["grow"], w=["gsc"], key="c1")
    P.cut(3)
    add("dve", C("tensor_tensor", out=modv[:], in0=pm[:, :, 0:3], in1=vecs[:, V_BADA:V_BADA + 48].unsqueeze(2).to_broadcast([128, 48, 3]), op=ALU.add),
        r=["pm", "vecs"], w=["modv"])
    add("dve", C("scalar_tensor_tensor", out=A1[:], in0=modv[:, 8:16, :], scalar=1.0, in1=vecs[:, V_N1G:V_N1G + 8].unsqueeze(2).to_broadcast([128, 8, 3]), op0=ALU.add, op1=ALU.mult),
        r=["modv", "vecs"], w=["A1"])
    add("dve", C("scalar_tensor_tensor", out=A2[:], in0=modv[:, 32:40, :], scalar=1.0, in1=vecs[:, V_N2G:V_N2G + 8].unsqueeze(2).to_broadcast([128, 8, 3]), op0=ALU.add, op1=ALU.mult),
        r=["modv", "vecs"], w=["A2"])
    P.cut(4)
    add("dve", C("tensor_scalar", out=lv[:, HBR:HBR + 16], in0=vecs[:, V_BR:V_BR + 16], scalar1=0.5, scalar2=None, op0=ALU.mult), r=["vecs"], w=["lv0"])
    add("act", C("activation", out=tmpv[:, 0:8], in_=vecs[:, V_LAM:V_LAM + 8], func=AF.Exp, scale=-1.0), r=["vecs"], w=["tmpv"])
    add("act", C("activation", out=tmpv[:, 8:16], in_=tmpv[:, 0:8], func=AF.Ln, bias=1.0), r=["tmpv"], w=["tmpv2"])
    add("dve", C("tensor_scalar", out=lv[:, CNEG:CNEG + 8], in0=tmpv[:, 8:16], scalar1=-8.0, scalar2=None, op0=ALU.mult), r=["tmpv2"], w=["lv1"])
    add("dve", C("tensor_scalar", out=lv[:, HCNEG:HCNEG + 8], in0=tmpv[:, 8:16], scalar1=-4.0, scalar2=None, op0=ALU.mult), r=["tmpv2"], w=["lv2"])
    P.cut(5)
    add("dve", C("tensor_reduce", out=tmpv[:, 16:17], in_=vecs[:, V_GQROW:V_GQROW + 64], axis=AX.X, op=ALU.max, apply_absolute_value=True), r=["vecs"], w=["tmpv3"])
    add("dve", C("tensor_reduce", out=tmpv[:, 17:18], in_=vecs[:, V_GKROW:V_GKROW + 64], axis=AX.X, op=ALU.max, apply_absolute_value=True), r=["vecs"], w=["tmpv4"])
    add("dve", C("scalar_tensor_tensor", out=av[:, 0:1], in0=tmpv[:, 16:17], scalar=-8.0, in1=tmpv[:, 17:18], op0=ALU.mult, op1=ALU.mult), r=["tmpv3", "tmpv4"], w=["av0"])
    add("act", C("activation", out=av[:, 1:9], in_=vecs[:, V_SINK:V_SINK + 8], func=AF.Exp, bias=av[:, 0:1]), r=["vecs", "av0"], w=["av1"])
    P.cut(6)
    if "modv" in dbg:
        add("sp", C("dma_start", out=dbg["modv"][:, :, :], in_=modv[:]), r=["modv"], key="dbg0")
    P.barrier()
    P.cut(7)

    ST.reset()
    qT = ST.alloc([4, SEQ], BF16)
    kT = ST.alloc([NTOK], BF16)
    vaug = ST.alloc([34, 2, 65], BF16)
    xrS = ST.alloc([4, NTOK], BF16)
    grS = ST.alloc([4, SEQ], BF16)
    add("pool", C("memset", vaug[:, :, :, 64:65], 1.0), w=["vones"])
    P.cut(8)
    P.barrier()
    P.cut(9)

    win_v = win_d.rearrange("(kc p) n -> p kc n", p=128)
    wout_v = wout_d.rearrange("(kc p) n -> p kc n", p=128)

    def phase_A(s):
        WK.reset()
        win = WK.alloc([8, 1792], BF16)
        ropeb = [WK.alloc([2, 512], BF16) for _ in range(2)]
        xt = [WK.alloc([D], F32) for _ in range(3)]
        xn = WK.alloc([4, D], BF16)
        hT = WK.alloc([8, 512], BF16)
        NSET = 3
        tq = [[WK.alloc([512], BF16) for _ in range(3)] for _ in range(NSET)]
        rstd = [WK.alloc([512], F32) for _ in range(NSET)]
        ssb = WK.alloc([8], F32)
        for h in range(2):
            add("pool", C("dma_start", out=win[:, h * 4:(h + 1) * 4, :], in_=win_v[:, h * 4:(h + 1) * 4, :]), w=["win%d" % h], key="win%d" % h)
        add("pool", C("memset", vaug[:, :, :, 64:65], 1.0), w=["vones"])
        pT = [banks[i].bitcast(BF16).rearrange("p (a b) -> p a b", a=2) for i in range(4)]
        accs = [(banks[4], 4), (banks[5], 5), (banks[0], 0), (banks[1], 1), (banks[2], 2), (banks[3], 3)]
        pS = banks[6]
        pR = banks[7]
        st = dict(xh=0, grp=0, acc=0, tq=0, ev=0, prepped=set())

        def prep(rows_ap, ntiles):
            for t in range(ntiles):
                xb = st["xh"] % 3
                st["xh"] += 1
                add("sp", C("dma_start", out=xt[xb], in_=rows_ap[t * 128:(t + 1) * 128, :]), w=["xt%d" % xb], key="xt%d" % xb)
                sc_ = ssb[:, t:t + 1]
                add("act", C("activation", out=xn[:, t, :], in_=xt[xb], func=AF.Square, accum_out=sc_), r=["xt%d" % xb], w=["xn%d" % t, "ms%d" % t])
                add("act", C("activation", out=sc_, in_=sc_, func=AF.Ln, scale=1.0 / D, bias=EPS), r=["ms%d" % t], w=["ms%d" % t])
                add("act", C("activation", out=sc_, in_=sc_, func=AF.Exp, scale=-0.5), r=["ms%d" % t], w=["ms%d" % t])
                add("dve", C("tensor_scalar", out=xn[:, t, :], in0=xt[xb], scalar1=sc_, scalar2=None, op0=ALU.mult), r=["xt%d" % xb, "ms%d" % t], w=["xn%d" % t])

        def group(rows_ap, ntiles, is_ctx, col0, tile0, lat0, nxt=None):
            ntok = ntiles * 128
            scol = 2 if is_ctx else s
            gi = st["grp"]
            st["grp"] += 1
            hb = gi % 2
            if not is_ctx:
                add("sp", C("dma_start", out=ropeb[hb], in_=rope_d[:, :, lat0:lat0 + 512]), w=["rope%d" % hb], key="rope%d" % hb)
            if gi not in st["prepped"]:
                prep(rows_ap, ntiles)
            st["prepped"].discard(gi)

            for fc in range(8):
                bk = fc % 4
                for t in range(ntiles):
                    add("pe", C("transpose", out=pT[bk][:, fc // 4, t * 128:(t + 1) * 128], in_=xn[:, t, fc * 128:(fc + 1) * 128], identity=ident),
                        r=["xn%d" % t, "cm"], w=["pT%d" % fc, "acc%d" % bk])
                if bk % 2 == 0:
                    add("act", C("activation", out=hT[:, fc, 0:ntok], in_=pT[bk][:, fc // 4, 0:ntok], func=AF.Identity, scale=A1[:, fc, scol:scol + 1], bias=B1[:, fc, scol:scol + 1]),
                        r=["pT%d" % fc, "A1", "modv"], w=["hT%d" % fc])
                else:
                    add("dve", C("tensor_scalar", out=hT[:, fc, 0:ntok], in0=pT[bk][:, fc // 4, 0:ntok], scalar1=A1[:, fc, scol:scol + 1], scalar2=B1[:, fc, scol:scol + 1], op0=ALU.mult, op1=ALU.add),
                        r=["pT%d" % fc, "A1", "modv"], w=["hT%d" % fc])
            hkeys = ["hT%d" % fc for fc in range(8)]
            cols = {"k": 512, "v": 640}
            for b in range(4):
                cols["q%d" % b] = b * 128
                cols["x%d" % b] = 768 + b * 128
                cols["g%d" % b] = 1280 + b * 128
            if is_ctx:
                order = ["k", "v", "x0", "x1", "x2", "x3"]
            else:
                order = ["k", "v", "q0", "x0", "q1", "x1", "q2", "x2", "q3", "x3", "g0", "g1", "g2", "g3"]
            pend = []

            def evac_copy(dst, src_, rk, wk):
                st["ev"] += 1
                if st["ev"] % 2 == 0:
                    add("act", C("copy", out=dst, in_=src_), r=rk, w=wk)
                else:
                    add("dve", C("tensor_copy", out=dst, in_=src_), r=rk, w=wk)

            def qk_stage1(name, acc, akey, ts):
                sq, qg, qs = tq[ts]
                qc = sq
                rs = rstd[ts]
                gcol = vecs[:, V_GK:V_GK + 1] if name == "k" else vecs[:, V_GQ:V_GQ + 1]
                add("pe", C("matmul", pS[:, 0:ntok], lhsT=onesblk, rhs=sq[:, 0:ntok], start=True, stop=True), r=["sq%d" % ts, "cm"], w=["pS"])
                add("act", C("activation", out=rs[:, 0:ntok], in_=pS[:, 0:ntok], func=AF.Ln, scale=1.0 / 64, bias=EPS), r=["pS"], w=["rs%d" % ts])
                add("act", C("activation", out=rs[:, 0:ntok], in_=rs[:, 0:ntok], func=AF.Exp, scale=-0.5), r=["rs%d" % ts], w=["rs%d" % ts])
                if is_ctx:
                    add("dve", C("scalar_tensor_tensor", out=kT[:, col0:col0 + ntok], in0=acc[:, 0:ntok], scalar=gcol, in1=rs[:, 0:ntok], op0=ALU.mult, op1=ALU.mult),
                        r=[akey, "rs%d" % ts, "vecs"], w=["kT%d" % (col0 // 128 + t) for t in range(ntiles)])
                    return
                add("dve", C("scalar_tensor_tensor", out=qg[:, 0:ntok], in0=acc[:, 0:ntok], scalar=gcol, in1=rs[:, 0:ntok], op0=ALU.mult, op1=ALU.mult),
                    r=[akey, "rs%d" % ts, "vecs"], w=["qg%d" % ts])
                add("dve", C("tensor_tensor", out=qc[:, 0:ntok], in0=qg[:, 0:ntok], in1=ropeb[hb][:, 0, 0:ntok], op=ALU.mult), r=["qg%d" % ts, "rope%d" % hb, "sq%d" % ts], w=["sq%d" % ts])
                add("dve", C("tensor_tensor", out=qs[:, 0:ntok], in0=qg[:, 0:ntok], in1=ropeb[hb][:, 1, 0:ntok], op=ALU.mult), r=["qg%d" % ts, "rope%d" % hb], w=["qs%d" % ts])

            def qk_stage2(name, ts):
                sq, qg, qs = tq[ts]
                qc = sq
                add("pe", C("matmul", pR[:, 0:ntok], lhsT=ident, rhs=qc[:, 0:ntok], start=True, stop=False), r=["sq%d" % ts, "cm"], w=["pR"])
                add("pe", C("matmul", pR[:, 0:ntok], lhsT=pswap, rhs=qs[:, 0:ntok], start=False, stop=True), r=["qs%d" % ts, "cm"], w=["pR"])
                if name == "k":
                    add("act", C("copy", out=kT[:, col0:col0 + ntok], in_=pR[:, 0:ntok]), r=["pR"], w=["kT%d" % (col0 // 128 + t) for t in range(ntiles)])
                else:
                    b = int(name[1])
                    add("act", C("copy", out=qT[:, b, lat0:lat0 + ntok], in_=pR[:, 0:ntok]), r=["pR"], w=["qT%d" % (lat0 // 128 + t) for t in range(ntiles)])

            def run_pending(upto):
                keep = []
                for due, fn_ in pend:
                    if due <= upto:
                        fn_()
                    else:
                        keep.append((due, fn_))
                pend[:] = keep

            for bi, name in enumerate(order):
                c0 = cols[name]
                acc, bno = accs[st["acc"] % len(accs)]
                st["acc"] += 1
                akey = "acc%d" % bno
                wk = [akey] + (["pT%d" % bno, "pT%d" % (bno + 4)] if bno < 4 else [])
                if name == "v":
                    accv = acc.rearrange("p (t c) -> p t c", t=4)
                    for t in range(ntiles):
                        for kc in range(8):
                            add("pe", C("matmul", accv[:, t, :], lhsT=hT[:, kc, t * 128:(t + 1) * 128], rhs=win[:, kc, 640:768], start=(kc == 0), stop=(kc == 7)),
                                r=hkeys + ["win0", "win1"], w=wk)
                    add("dve", C("tensor_copy", out=vaug[:, tile0:tile0 + ntiles, :, 0:64], in_=accv[:, 0:ntiles, :].rearrange("p t (h d) -> p t h d", h=2)),
                        r=[akey], w=["v%d" % (tile0 + t) for t in range(ntiles)])
                else:
                    for kc in range(8):
                        add("pe", C("matmul", acc[:, 0:ntok], lhsT=win[:, kc, c0:c0 + 128], rhs=hT[:, kc, 0:ntok], start=(kc == 0), stop=(kc == 7)),
                            r=hkeys + ["win0", "win1"], w=wk)
                    if name[0] == "x":
                        c = int(name[1])
                        evac_copy(xrS[:, c, col0:col0 + ntok], acc[:, 0:ntok], [akey], ["xr%d" % c])
                    elif name[0] == "g":
                        c = int(name[1])
                        evac_copy(grS[:, c, lat0:lat0 + ntok], acc[:, 0:ntok], [akey], ["gr%d" % c])
                    else:
                        ts = st["tq"] % NSET
                        st["tq"] += 1
                        add("act", C("activation", out=tq[ts][0][:, 0:ntok], in_=acc[:, 0:ntok], func=AF.Square), r=[akey], w=["sq%d" % ts])
                        pend.append((bi + 1, (lambda name=name, acc=acc, akey=akey, ts=ts: qk_stage1(name, acc, akey, ts))))
                        if not is_ctx:
                            pend.append((bi + 3, (lambda name=name, ts=ts: qk_stage2(name, ts))))
                run_pending(bi)
                if nxt is not None and bi == min(4, len(order) - 2):
                    prep(*nxt)
                    st["prepped"].add(gi + 1)
            run_pending(10 ** 9)

        lat_rows = lambda g: x_d[s * SEQ + g * 512:s * SEQ + (g + 1) * 512, :]
        group(ctx_d[s * NCTX:(s + 1) * NCTX, :], 2, True, 0, 0, 0, nxt=(lat_rows(0), 4))
        for g in range(8):
            group(lat_rows(g), 4, False, NCTX + g * 512, 2 + g * 4, g * 512, nxt=((lat_rows(g + 1), 4) if g < 7 else None))

    def phase_B(s):
        WK.reset()
        kTz = [WK.alloc([NTOK], BF16) for _ in range(2)]
        NEX = 16
        expS = [WK.alloc([512], BF16) for _ in range(NEX)]
        On = [WK.alloc([4, 2, 64], BF16) for _ in range(2)]
        den = [WK.alloc([4], F32) for _ in range(4)]
        kkeys = ["kT%d" % t for t in range(34)]
        add("pool", C("memset", kTz[0][64:128, :], 0.0), w=["kz0"])
        add("pool", C("memset", kTz[1][0:64, :], 0.0), w=["kz1"])
        add("act", C("copy", out=kTz[0][0:64, :], in_=kT[0:64, :]), r=kkeys, w=["kz0"])
        add("dve", C("tensor_copy", out=kTz[1][64:128, :], in_=kT[64:128, :]), r=kkeys, w=["kz1"])
        pSc = banks[0:4]
        pO = [banks[4], banks[5]]
        pTr = [banks[6].bitcast(BF16)[:, 0:512].rearrange("p (g q) -> p g q", g=4), banks[7].bitcast(BF16)[:, 0:512].rearrange("p (g q) -> p g q", g=4)]
        ctr = dict(sc=0, ex=0)
        its = [(i, kvh) for i in range(32) for kvh in range(2)]
        info = {}

        def chunks_of(i):
            ch = [(0, None), (1, None)]
            if i > 0:
                ch.append((2 + i - 1, 0))
            ch.append((2 + i, None))
            if i < 31:
                ch.append((2 + i + 1, 1))
            return ch

        def stage_sc(n):
            i, kvh = its[n]
            ebs = []
            for ci, (kt, mk) in enumerate(chunks_of(i)):
                sb_ = ctr["sc"] % 4
                ctr["sc"] += 1
                eb = ctr["ex"] % NEX
                ctr["ex"] += 1
                ebs.append(eb)
                add("pe", C("matmul", pSc[sb_][:, :], lhsT=kTz[kvh][:, kt * 128:(kt + 1) * 128], rhs=qT[:, :, i * 128:(i + 1) * 128], start=True, stop=True),
                    r=["kz%d" % kvh, "qT%d" % i], w=["pSc%d" % sb_])
                add("act", C("activation", out=expS[eb], in_=pSc[sb_][:, :], func=AF.Exp, scale=0.125, bias=negB), r=["pSc%d" % sb_, "av0"], w=["ex%d" % eb])
                if mk is not None:
                    add("pool", C("tensor_tensor", out=expS[eb], in0=expS[eb], in1=masks[:, mk, :], op=ALU.mult), r=["ex%d" % eb, "masks"], w=["ex%d" % eb])
            info[n] = ebs

        def stage_pv(n):
            i, kvh = its[n]
            ob = i % 2
            chunks = chunks_of(i)
            ebs = info.pop(n)
            po = pO[n % 2]
            pok = "pO%d" % (n % 2)
            dn = den[n % 4]
            dnk = "den%d" % (n % 4)
            pov = po[:, 0:260].rearrange("p (g c) -> p g c", g=4)
            for ci, (kt, mk) in enumerate(chunks):
                eb = ebs[ci]
                for g in range(4):
                    first = (ci == 0 and g == 0)
                    add("pe", C("matmul", pov[:, g, :], lhsT=expS[eb][:, g * 128:(g + 1) * 128], rhs=vaug[:, kt, kvh, :], start=first, stop=(ci == len(chunks) - 1), skip_group_check=True),
                        r=["ex%d" % eb, "v%d" % kt, "vones"], w=[pok])
            add("dve", C("tensor_tensor", out=dn, in0=pov[:, :, 64], in1=esink[:, kvh * 4:(kvh + 1) * 4], op=ALU.add), r=[pok, "av1"], w=[dnk])
            add("dve", C("reciprocal", out=dn, in_=dn), r=[dnk], w=[dnk])
            add("dve", C("tensor_tensor", out=On[ob][:, :, kvh, :], in0=pov[:, :, 0:64], in1=dn.unsqueeze(2).to_broadcast([128, 4, 64]), op=ALU.mult),
                r=[pok, dnk], w=["On%d_%d" % (ob, kvh)])
            if kvh == 1:
                for g in range(4):
                    add("pe", C("transpose", out=pTr[ob][:, g, :], in_=On[ob][:, g, :, :].rearrange("p h d -> p (h d)"), identity=ident),
                        r=["On%d_0" % ob, "On%d_1" % ob, "cm"], w=["pTr%d" % ob])
                add("act", C("copy", out=qT[:, :, i * 128:(i + 1) * 128], in_=pTr[ob]), r=["pTr%d" % ob], w=["qT%d" % i])

        LAG = 2
        for step in range(len(its) + LAG):
            if step < len(its):
                stage_sc(step)
            if step - LAG >= 0:
                stage_pv(step - LAG)

    def phase_C(s):
        WK.reset()
        xcb = WK.alloc([NTOK], BF16)
        abuf0 = WK.alloc([NTOK], F32)
        abuf1 = ST.t[:, 16384:16384 + 8704].bitcast(F32)
        abufs = [abuf0, abuf1]
        bbuf = [WK.alloc([NTOK], F32) for _ in range(2)]
        tbuf = WK.alloc([NTOK], BF16)
        trb = [WK.alloc([512], F32) for _ in range(2)]
        tib = [WK.alloc([512], F32) for _ in range(2)]
        ctr = dict(b=0)
        deferred = []
        segs = [(0, NCTX), (NCTX, NTOK)]
        def conv(c):
            xcf = abufs[0]
            cw = lambda j: vecs[:, V_CONVW + c * 4 + j:V_CONVW + c * 4 + j + 1]
            for (lo, hi) in segs:
                add("dve", C("tensor_scalar", out=xcf[:, lo:hi], in0=xrS[:, c, lo:hi], scalar1=cw(2), scalar2=vecs[:, V_CONVB + c:V_CONVB + c + 1], op0=ALU.mult, op1=ALU.add),
                    r=["xr%d" % c, "vecs"], w=["a0"])
                for j in (0, 1, 3):
                    o = j - 2
                    a0, a1 = max(lo, lo - o), min(hi, hi - o)
                    add("dve", C("scalar_tensor_tensor", out=xcf[:, a0:a1], in0=xrS[:, c, a0 + o:a1 + o], scalar=cw(j), in1=xcf[:, a0:a1], op0=ALU.mult, op1=ALU.add),
                        r=["xr%d" % c, "vecs", "a0"], w=["a0"])
            add("act", C("copy", out=xcb, in_=xcf), r=["a0"], w=["xcb"])

        conv(0)
        for c in range(4):
            for d in range(2):
                bb = bbuf[d]
                bk = "b%d" % d
                abuf = abufs[d]
                ak = "a%d" % d
                wi_r = (d * 2 + 0) * 4 + c
                wi_i = (d * 2 + 1) * 4 + c
                col = d * 4 + c
                blocks = [(0, NCTX)] + [(NCTX + g * 512, NCTX + (g + 1) * 512) for g in range(8)]
                for (lo, hi) in blocks:
                    n = hi - lo
                    pb = ctr["b"] % 2
                    ctr["b"] += 1
                    pr, pi = banks[pb * 2], banks[pb * 2 + 1]
                    tr, ti = trb[pb], tib[pb]
                    add("pe", C("matmul", pr[:, 0:n], lhsT=lruW[:, wi_r, :], rhs=xcb[:, lo:hi], start=True, stop=True), r=["xcb", "lruW"], w=["pr%d" % pb])
                    add("pe", C("matmul", pi[:, 0:n], lhsT=lruW[:, wi_i, :], rhs=xcb[:, lo:hi], start=True, stop=True), r=["xcb", "lruW"], w=["pi%d" % pb])
                    add("act", C("activation", out=tr[:, 0:n], in_=pr[:, 0:n], func=AF.Tanh, scale=0.5, bias=lv[:, HBR + col:HBR + col + 1]), r=["pr%d" % pb, "lv0"], w=["tr%d" % pb])
                    add("act", C("activation", out=ti[:, 0:n], in_=pi[:, 0:n], func=AF.Tanh, scale=0.5, bias=lv[:, HBI + col:HBI + col + 1]), r=["pi%d" % pb, "lv0"], w=["ti%d" % pb])
                    add("act", C("activation", out=abuf[:, lo:hi], in_=tr[:, 0:n], func=AF.Exp, scale=lv[:, HCNEG + col:HCNEG + col + 1], bias=lv[:, HCNEG + col:HCNEG + col + 1]), r=["tr%d" % pb, "lv2"], w=[ak])
                    add("act", C("activation", out=bb[:, lo:hi], in_=tr[:, 0:n], func=AF.Exp, scale=lv[:, CNEG + col:CNEG + col + 1], bias=lv[:, CNEG + col:CNEG + col + 1]), r=["tr%d" % pb, "lv1"], w=[bk])
                    add("dve", C("scalar_tensor_tensor", out=tbuf[:, lo:hi], in0=ti[:, 0:n], scalar=1.0, in1=xcb[:, lo:hi], op0=ALU.add, op1=ALU.mult), r=["ti%d" % pb, "xcb"], w=["t"])
                while deferred:
                    deferred.pop(0)()
                add("act", C("activation", out=bb, in_=bb, func=AF.Sqrt, scale=-0.25, bias=0.25), r=[bk], w=[bk])
                add("dve", C("tensor_tensor", out=bb, in0=bb, in1=tbuf, op=ALU.mult), r=[bk, "t"], w=[bk])
                if d == 0:
                    def _scan0(bb=bb, abuf=abuf, bk=bk, ak=ak):
                        add("dve", C("tensor_tensor_scan", out=bb[:, 0:NCTX], data0=abuf[:, 0:NCTX], data1=bb[:, 0:NCTX], initial=0.0, op0=ALU.mult, op1=ALU.add), r=[bk, ak], w=[bk])
                        add("dve", C("tensor_tensor_scan", out=bb[:, NCTX:NTOK], data0=abuf[:, NCTX:NTOK], data1=bb[:, NCTX:NTOK], initial=bb[:, NCTX - 1:NCTX], op0=ALU.mult, op1=ALU.add), r=[bk, ak], w=[bk])
                    deferred.append(_scan0)
                else:
                    add("dve", C("tensor_tensor_scan", out=bb[:, 0:NCTX][:, ::-1], data0=abuf[:, 0:NCTX][:, ::-1], data1=bb[:, 0:NCTX][:, ::-1], initial=0.0, op0=ALU.mult, op1=ALU.add), r=[bk, ak], w=[bk])
                    add("dve", C("tensor_tensor_scan", out=bb[:, NCTX:NTOK][:, ::-1], data0=abuf[:, NCTX:NTOK][:, ::-1], data1=bb[:, NCTX:NTOK][:, ::-1], initial=bb[:, 0:1], op0=ALU.mult, op1=ALU.add), r=[bk, ak], w=[bk])
            add("pool", C("tensor_tensor", out=bbuf[0][:, NCTX:NTOK], in0=bbuf[0][:, NCTX:NTOK], in1=bbuf[1][:, NCTX:NTOK], op=ALU.add), r=["b0", "b1"], w=["b0"])
            add("act", C("activation", out=tbuf[:, 0:SEQ], in_=grS[:, c, :], func=AF.Gelu_apprx_tanh), r=["gr%d" % c, "t"], w=["t"])
            if c + 1 < 4:
                conv(c + 1)
            add("dve", C("tensor_tensor", out=grS[:, c, :], in0=bbuf[0][:, NCTX:NTOK], in1=tbuf[:, 0:SEQ], op=ALU.mult), r=["b0", "t"], w=["gr%d" % c])

    def phase_D(s):
        WK.reset()
        wout = WK.alloc([8, D], BF16)
        g1bc = WK.alloc([D], F32)
        xt2 = [WK.alloc([D], F32) for _ in range(2)]
        x1 = [WK.alloc([D], F32) for _ in range(2)]
        xn2 = [WK.alloc([1056], BF16) for _ in range(2)]
        h2T = [WK.alloc([8, 128], BF16) for _ in range(2)]
        junk = WK.alloc([D], BF16)
        sm = [WK.alloc([24], F32) for _ in range(2)]
        ex = [WK.alloc([16], F32) for _ in range(2)]
        for h in range(2):
            add("pool", C("dma_start", out=wout[:, h * 4:(h + 1) * 4, :], in_=wout_v[:, h * 4:(h + 1) * 4, :]), w=["wout%d" % h], key="win%d" % h)
        add("sp", C("dma_start", out=g1bc, in_=gsc_d[s, 0, :].partition_broadcast(128)), w=["g1bc"], key="c0")
        for kc in range(8):
            add("dve", C("tensor_tensor", out=wout[:, kc, :], in0=wout[:, kc, :], in1=g1bc, op=ALU.mult), r=["wout0", "wout1", "g1bc"], w=["wout%d" % (kc // 4)])
        py = [[banks[0], banks[1]], [banks[2], banks[3]]]
        pT2 = [banks[4].bitcast(BF16).rearrange("p (a b) -> p a b", a=8), banks[5].bitcast(BF16).rearrange("p (a b) -> p a b", a=8)]
        pl = [banks[6], banks[7]]

        def s1(tt):
            b = tt % 2
            cols = slice(tt * 128, (tt + 1) * 128)
            r0 = s * SEQ + tt * 128
            add("sp", C("dma_start", out=xt2[b], in_=x_d[r0:r0 + 128, :]), w=["xt2%d" % b], key="xt%d" % b)
            for half in range(2):
                for kc in range(8):
                    lhs = qT[:, kc, cols] if kc < 4 else grS[:, kc - 4, cols]
                    rk = "qT%d" % tt if kc < 4 else "gr%d" % (kc - 4)
                    add("pe", C("matmul", py[b][half][:, :], lhsT=lhs, rhs=wout[:, kc, half * 512:(half + 1) * 512], start=(kc == 0), stop=(kc == 7)),
                        r=[rk, "wout0", "wout1"], w=["py%d%d" % (b, half)])

        def s2(tt):
            b = tt % 2
            r0 = s * SEQ + tt * 128
            for half in range(2):
                hs = slice(half * 512, (half + 1) * 512)
                add("dve", C("tensor_tensor", out=x1[b][:, hs], in0=py[b][half][:, :], in1=xt2[b][:, hs], op=ALU.add), r=["py%d%d" % (b, half), "xt2%d" % b], w=["x1%d" % b])
            add("sp", C("dma_start", out=out_d[r0:r0 + 128, :], in_=x1[b]), r=["x1%d" % b], key="o%d" % b)
            add("act", C("activation", out=junk, in_=x1[b], func=AF.Square, accum_out=sm[b][:, 0:1]), r=["x1%d" % b], w=["junk", "sm%d" % b])

        def s3(tt):
            b = tt % 2
            add("act", C("activation", out=sm[b][:, 0:1], in_=sm[b][:, 0:1], func=AF.Ln, scale=1.0 / D, bias=EPS), r=["sm%d" % b], w=["sm%d" % b])
            add("act", C("activation", out=sm[b][:, 0:1], in_=sm[b][:, 0:1], func=AF.Exp, scale=-0.5), r=["sm%d" % b], w=["sm%d" % b])
            add("dve", C("tensor_scalar", out=xn2[b][:, 0:D], in0=x1[b], scalar1=sm[b][:, 0:1], scalar2=None, op0=ALU.mult), r=["x1%d" % b, "sm%d" % b], w=["xn2%d" % b])

        def s4(tt):
            b = tt % 2
            for fc in range(8):
                add("pe", C("transpose", out=pT2[b][:, fc, :], in_=xn2[b][:, fc * 128:(fc + 1) * 128], identity=ident), r=["xn2%d" % b, "cm"], w=["pT2%d" % b])
            for fc in range(8):
                if b == 0:
                    add("act", C("activation", out=h2T[b][:, fc, :], in_=pT2[b][:, fc, :], func=AF.Identity, scale=A2[:, fc, s:s + 1], bias=B2[:, fc, s:s + 1]), r=["pT2%d" % b, "A2", "modv"], w=["h2T%d_%d" % (b, fc)])
                else:
                    add("dve", C("tensor_scalar", out=h2T[b][:, fc, :], in0=pT2[b][:, fc, :], scalar1=A2[:, fc, s:s + 1], scalar2=B2[:, fc, s:s + 1], op0=ALU.mult, op1=ALU.add), r=["pT2%d" % b, "A2", "modv"], w=["h2T%d_%d" % (b, fc)])
            for kc in range(8):
                add("pe", C("matmul", pl[b][:, 0:16], lhsT=h2T[b][:, kc, :], rhs=wr[:, kc, :], start=(kc == 0), stop=(kc == 7)), r=["h2T%d_%d" % (b, fc) for fc in range(8)] + ["wr"], w=["pl%d" % b])

        def s5(tt):
            b = tt % 2
            r0 = s * SEQ + tt * 128
            add("dve", C("tensor_reduce", out=sm[b][:, 1:2], in_=pl[b][:, 0:16], axis=AX.X, op=ALU.max), r=["pl%d" % b], w=["mx%d" % b])
            add("dve", C("tensor_scalar", out=sm[b][:, 1:2], in0=sm[b][:, 1:2], scalar1=-1.0, scalar2=None, op0=ALU.mult), r=["mx%d" % b], w=["mx%d" % b])
            add("act", C("activation", out=ex[b], in_=pl[b][:, 0:16], func=AF.Exp, bias=sm[b][:, 1:2], accum_out=sm[b][:, 2:3]), r=["pl%d" % b, "mx%d" % b], w=["ex%d" % b, "se%d" % b])
            add("dve", C("reciprocal", out=sm[b][:, 2:3], in_=sm[b][:, 2:3]), r=["se%d" % b], w=["se%d" % b])
            add("dve", C("tensor_scalar", out=Aall[:, tt, s * 16:(s + 1) * 16], in0=ex[b], scalar1=sm[b][:, 2:3], scalar2=None, op0=ALU.mult), r=["ex%d" % b, "se%d" % b], w=["Aall"])
            add("dve", C("tensor_copy", out=xn2[b][:, D:D + 16], in_=Aall[:, tt, s * 16:(s + 1) * 16]), r=["Aall"], w=["xn2a%d" % b])
            add("dve", C("tensor_tensor", out=xn2[b][:, D + 16:1056], in0=Aall[:, tt, s * 16:(s + 1) * 16], in1=xn2[b][:, D:D + 16], op=ALU.subtract), r=["Aall", "xn2a%d" % b], w=["xn2a%d" % b])
            add("sp", C("dma_start", out=xn_d[r0:r0 + 128, :], in_=xn2[b]), r=["xn2%d" % b, "xn2a%d" % b], key="xn%d" % b)

        stages = [s1, s2, s3, s4, s5]
        NT = 32
        for step in range(NT + len(stages) - 1):
            for si in reversed(range(len(stages))):
                tt = step - si
                if 0 <= tt < NT:
                    stages[si](tt)

    def phase_topk():
        WK.reset()
        ST.reset()
        AT = ST.alloc([SEQ], F32)
        ones = ST.alloc([SEQ], F32)
        Mk = ST.alloc([SEQ], F32)
        cs = ST.alloc([SEQ], F32)
        bs = WK.alloc([8], F32)
        for tt in range(32):
            pb = banks[tt % 2]
            add("pe", C("transpose", out=pb[0:32, 0:128], in_=Aall[:, tt, :], identity=identf[:]), r=["Aall", "identf"], w=["pAT%d" % (tt % 2)])
            add("act", C("copy", out=AT[0:32, tt * 128:(tt + 1) * 128], in_=pb[0:32, 0:128]), r=["pAT%d" % (tt % 2)], w=["AT"])
        add("pool", C("memset", ones[0:32, :], 1.0), w=["ones"])
        mid, cntc, stp = bs[0:32, 0:1], bs[0:32, 1:2], bs[0:32, 2:3]
        add("dve", C("memset", mid, 0.5), w=["mid"])
        NIT = 24
        for k in range(NIT):
            wk = 2.0 ** -(k + 1)
            wn = 2.0 ** -(k + 2)
            add("dve", C("tensor_scalar", out=Mk[0:32, :], in0=AT[0:32, :], scalar1=mid, scalar2=None, op0=ALU.is_ge, op1=ALU.add, accum_out=cntc), r=["AT", "mid"], w=["Mk", "cnt"])
            add("dve", C("tensor_scalar", out=stp, in0=cntc, scalar1=float(CAP), scalar2=wk, op0=ALU.is_ge, op1=ALU.mult), r=["cnt"], w=["stp"])
            last = (k == NIT - 1)
            delta = (-wk) if last else (wn - wk)
            add("dve", C("scalar_tensor_tensor", out=mid, in0=stp, scalar=delta, in1=mid, op0=ALU.add, op1=ALU.add), r=["stp", "mid"], w=["mid"])
        add("dve", C("tensor_scalar", out=Mk[0:32, :], in0=AT[0:32, :], scalar1=mid, scalar2=None, op0=ALU.is_ge), r=["AT", "mid"], w=["Mk"])
        add("dve", C("tensor_tensor_scan", out=cs[0:32, :], data0=ones[0:32, :], data1=Mk[0:32, :], initial=0.0, op0=ALU.mult, op1=ALU.add), r=["ones", "Mk"], w=["cs"])
        add("dve", C("tensor_tensor", out=cs[0:32, :], in0=cs[0:32, :], in1=Mk[0:32, :], op=ALU.mult), r=["cs", "Mk"], w=["cs"])
        for thr in (129.0, 257.0, 385.0):
            add("dve", C("scalar_tensor_tensor", out=Mk[0:32, :], in0=cs[0:32, :], scalar=thr, in1=Mk[0:32, :], op0=ALU.is_ge, op1=ALU.add), r=["cs", "Mk"], w=["Mk"])
        add("dve", C("scalar_tensor_tensor", out=cs[0:32, :], in0=Mk[0:32, :], scalar=-128.0, in1=cs[0:32, :], op0=ALU.mult, op1=ALU.add), r=["Mk", "cs"], w=["cs"])
        PH = WK.alloc([32, 32], BF16)
        PL = WK.alloc([32, 32], BF16)
        Lb = [WK.alloc([32, 128], BF16) for _ in range(2)]
        Ht = [WK.alloc([32, 4, 2], BF16) for _ in range(2)]
        eq = [WK.alloc([32, 4], F32) for _ in range(2)]
        pph = [banks[2], banks[3]]
        for tt in range(32):
            pb = pph[tt % 2]
            add("pe", C("transpose", out=pb[:, 0:32], in_=Mk[0:32, tt * 128:(tt + 1) * 128], identity=identf[0:32, 0:32]), r=["Mk", "identf"], w=["pph%d" % (tt % 2)])
            add("pe", C("transpose", out=pb[:, 32:64], in_=cs[0:32, tt * 128:(tt + 1) * 128], identity=identf[0:32, 0:32]), r=["cs", "identf"], w=["pph%d" % (tt % 2)])
            add("act", C("copy", out=PH[:, tt, :], in_=pb[:, 0:32]), r=["pph%d" % (tt % 2)], w=["PH%d" % tt])
            add("act", C("copy", out=PL[:, tt, :], in_=pb[:, 32:64]), r=["pph%d" % (tt % 2)], w=["PL%d" % tt])
        pIdx = banks[4][:, 0:256].rearrange("p (a c k) -> p a c k", a=32, c=4)
        iop = cm[:, CM_IOP, :]
        ioc = cm[:, CM_IOC, 0:4]
        for tt in range(32):
            b = tt % 2
            add("dve", C("tensor_tensor", out=Lb[b], in0=iop.unsqueeze(1).to_broadcast([128, 32, 128]), in1=PL[:, tt, :].unsqueeze(2).to_broadcast([128, 32, 128]), op=ALU.is_equal), r=["PL%d" % tt, "cm"], w=["Lb%d" % b])
            add("dve", C("tensor_tensor", out=eq[b], in0=ioc.unsqueeze(1).to_broadcast([128, 32, 4]), in1=PH[:, tt, :].unsqueeze(2).to_broadcast([128, 32, 4]), op=ALU.is_equal), r=["PH%d" % tt, "cm"], w=["eq%d" % b])
            for k2 in range(2):
                add("dve", C("tensor_scalar", out=Ht[b][:, :, :, k2], in0=eq[b], scalar1=tcol[:, tt, k2:k2 + 1], scalar2=None, op0=ALU.mult), r=["eq%d" % b, "tcol"], w=["Ht%d_%d" % (b, k2)])
            for se in range(32):
                first = (tt == 0 and se == 0)
                add("pe", C("matmul", pIdx[:, se, :, :].rearrange("p c k -> p (c k)"), lhsT=Lb[b][:, se, :], rhs=Ht[b][:, se, :, :].rearrange("p c k -> p (c k)"), start=first, stop=(tt == 31), skip_group_check=True),
                    r=["Lb%d" % b, "Ht%d_0" % b, "Ht%d_1" % b], w=["pIdx"])
        idf = WK.alloc([32, 4], F32)
        add("dve", C("tensor_copy", out=idf, in_=pIdx[:, :, :, 1]), r=["pIdx"], w=["idf"])
        add("dve", C("scalar_tensor_tensor", out=idf, in0=pIdx[:, :, :, 0], scalar=64.0, in1=idf, op0=ALU.mult, op1=ALU.add), r=["pIdx", "idf"], w=["idf"])
        add("dve", C("tensor_scalar", out=idf[:, 16:32, :], in0=idf[:, 16:32, :], scalar1=float(SEQ), scalar2=None, op0=ALU.add), r=["idf"], w=["idf"])
        add("dve", C("tensor_copy", out=IDX[:], in_=idf), r=["idf"], w=["IDX"])
        if "idx" in dbg:
            add("sp", C("dma_start", out=dbg["idx"][:, :, :], in_=IDX[:]), r=["IDX"], key="dbg1")

    def phase_moe():
        ST.reset()
        WK.reset()
        wts = [[ST.alloc([8, D], BF16) for _ in range(3)] for _ in range(2)]
        g2bc = [ST.alloc([D], F32) for _ in range(2)]
        xgs = [[WK.alloc([1056], BF16) for _ in range(8)] for _ in range(2)]
        xsT = WK.alloc([8, 1024], BF16)
        hidT = WK.alloc([8, 1024], BF16)
        sg = [WK.alloc([512], BF16) for _ in range(2)]
        yo = [WK.alloc([D], F32) for _ in range(2)]
        gts = [WK.alloc([8], F32) for _ in range(4)]
        for s in range(2):
            add("sp", C("dma_start", out=g2bc[s], in_=gsc_d[s, 1, :].partition_broadcast(128)), w=["g2bc%d" % s], key="c%d" % s)
        pX = [banks[0].bitcast(BF16).rearrange("p (a b) -> p a b", a=4), banks[1].bitcast(BF16).rearrange("p (a b) -> p a b", a=4)]
        pG = [banks[2], banks[3]]
        pU = [banks[4], banks[5]]
        pY = [banks[6], banks[7]]
        srcs = [wg_d, wu_d, wd_d]
        ctr = dict(y=0, gu=0)

        def load_w(e_):
            wb = e_ % 2
            for m in range(3):
                v = srcs[m][e_].rearrange("(kc p) n -> p kc n", p=128)
                for h in range(2):
                    add("pool", C("dma_start", out=wts[wb][m][:, h * 4:(h + 1) * 4, :], in_=v[:, h * 4:(h + 1) * 4, :]), w=["w%d_%d_%d" % (wb, m, h)], key="w%d%d" % (m, h))

        def gather(e_):
            gb_ = e_ % 2
            for s in range(2):
                for c in range(4):
                    st_ = s * 4 + c
                    add("pool", C("indirect_dma_start", out=xgs[gb_][st_], out_offset=None, in_=xn_d[:, :], in_offset=bass.IndirectOffsetOnAxis(ap=IDX[:, s * 16 + e_, c:c + 1], axis=0)),
                        r=["IDX"], w=["xg%d_%d" % (gb_, st_)], key="g%d_%d" % (gb_, st_))

        def xpose_unit(e2, u):
            pair, hf = u // 2, u % 2
            s = pair // 2
            pb = u % 2
            xg2 = xgs[e2 % 2]
            for st2 in range(2):
                st_ = pair * 2 + st2
                for f4 in range(4):
                    fc = hf * 4 + f4
                    add("pe", C("transpose", out=pX[pb][:, f4, st2 * 128:(st2 + 1) * 128], in_=xg2[st_][:, fc * 128:(fc + 1) * 128], identity=ident),
                        r=["xg%d_%d" % (e2 % 2, st_), "cm"], w=["pX%d" % pb])
            for f4 in range(4):
                fc = hf * 4 + f4
                dst = xsT[:, fc, pair * 256:(pair + 1) * 256]
                if pb == 0:
                    add("act", C("activation", out=dst, in_=pX[pb][:, f4, :], func=AF.Identity, scale=A2[:, fc, s:s + 1], bias=B2[:, fc, s:s + 1]), r=["pX%d" % pb, "A2", "modv"], w=["xsT%d" % pair])
                else:
                    add("dve", C("tensor_scalar", out=dst, in0=pX[pb][:, f4, :], scalar1=A2[:, fc, s:s + 1], scalar2=B2[:, fc, s:s + 1], op0=ALU.mult, op1=ALU.add), r=["pX%d" % pb, "A2", "modv"], w=["xsT%d" % pair])

        gather(0)
        load_w(0)
        for u in range(8):
            xpose_unit(0, u)
        for e_ in range(NEXP):
            wb = e_ % 2
            wgt, wut, wdt = wts[wb]
            xg = xgs[e_ % 2]
            xgk = lambda st_: "xg%d_%d" % (e_ % 2, st_)
            if e_ + 1 < NEXP:
                gather(e_ + 1)
                load_w(e_ + 1)
            for f in range(8):
                for half in range(2):
                    gb = ctr["gu"] % 2
                    ctr["gu"] += 1
                    xk = ["xsT%d" % (half * 2), "xsT%d" % (half * 2 + 1)]
                    for kc in range(8):
                        add("pe", C("matmul", pG[gb][:, :], lhsT=wgt[:, kc, f * 128:(f + 1) * 128], rhs=xsT[:, kc, half * 512:(half + 1) * 512], start=(kc == 0), stop=(kc == 7)),
                            r=xk + ["w%d_0_0" % wb, "w%d_0_1" % wb], w=["pG%d" % gb])
                    for kc in range(8):
                        add("pe", C("matmul", pU[gb][:, :], lhsT=wut[:, kc, f * 128:(f + 1) * 128], rhs=xsT[:, kc, half * 512:(half + 1) * 512], start=(kc == 0), stop=(kc == 7)),
                            r=xk + ["w%d_1_0" % wb, "w%d_1_1" % wb], w=["pU%d" % gb])
                    add("act", C("activation", out=sg[gb], in_=pG[gb][:, :], func=AF.Silu), r=["pG%d" % gb], w=["sg%d" % gb])
                    add("dve", C("tensor_tensor", out=hidT[:, f, half * 512:(half + 1) * 512], in0=pU[gb][:, :], in1=sg[gb], op=ALU.mult), r=["pU%d" % gb, "sg%d" % gb], w=["hid%d" % half])
            for st_ in range(8):
                s, c = st_ // 4, st_ % 4
                for half in range(2):
                    for fk in range(8):
                        add("pe", C("matmul", pY[half][:, :], lhsT=hidT[:, fk, st_ * 128:(st_ + 1) * 128], rhs=wdt[:, fk, half * 512:(half + 1) * 512], start=(fk == 0), stop=(fk == 7)),
                            r=["hid%d" % s, "w%d_2_0" % wb, "w%d_2_1" % wb], w=["pY%d" % half])
                yb = ctr["y"] % 2
                ctr["y"] += 1
                gate = gts[ctr["y"] % 4][:, 0:1]
                gk_ = "gt%d" % (ctr["y"] % 4)
                add("dve", C("tensor_tensor", out=gate, in0=xg[st_][:, D + e_:D + e_ + 1], in1=xg[st_][:, D + 16 + e_:D + 16 + e_ + 1], op=ALU.add), r=[xgk(st_)], w=[gk_])
                for half in range(2):
                    hs = slice(half * 512, (half + 1) * 512)
                    add("dve", C("scalar_tensor_tensor", out=yo[yb][:, hs], in0=pY[half][:, :], scalar=gate, in1=g2bc[s][:, hs], op0=ALU.mult, op1=ALU.mult),
                        r=["pY%d" % half, gk_, "g2bc%d" % s], w=["yo%d_%d" % (yb, half)])
                prev = ["sc_%d_%d_%d" % (s, e_ - 1, cc) for cc in range(4)] if e_ > 0 else []
                add("pool", C("indirect_dma_start", out=out_d[:, :], out_offset=bass.IndirectOffsetOnAxis(ap=IDX[:, s * 16 + e_, c:c + 1], axis=0), in_=yo[yb], in_offset=None, compute_op=ALU.add, oob_is_err=True),
                    r=["yo%d_0" % yb, "yo%d_1" % yb, "IDX"] + prev, w=["sc_%d_%d_%d" % (s, e_, c)], key="sc%d" % st_)
                if e_ + 1 < NEXP:
                    xpose_unit(e_ + 1, st_)

    nsamp = 2
    stages = ["A", "B", "C", "D", "T", "full"]
    lvl = stages.index(stage) if stage in stages else -1
    if stage == "0":
        nsamp = 0
    if stage == "A1":
        nsamp = 1
    import os as _os
    if _os.environ.get("NSAMP"):
        nsamp = int(_os.environ["NSAMP"])
    for s in range(nsamp):
        phase_A(s)
        P.barrier()
        if stage == "A" and s == 0:
            add("sp", C("dma_start", out=dbg["qT"][:, :, :], in_=qT), key="dbg0")
            add("sp", C("dma_start", out=dbg["kT"][:, :], in_=kT), key="dbg1")
            add("sp", C("dma_start", out=dbg["v"][:, :, :, :], in_=vaug), key="dbg0")
            add("sp", C("dma_start", out=dbg["xr"][:, :, :], in_=xrS), key="dbg1")
            add("sp", C("dma_start", out=dbg["gr"][:, :, :], in_=grS), key="dbg0")
            P.barrier()
        if lvl >= 1:
            phase_B(s)
            P.barrier()
            if stage == "B" and s == 0:
                add("sp", C("dma_start", out=dbg["qT"][:, :, :], in_=qT), key="dbg0")
                P.barrier()
        if lvl >= 2:
            phase_C(s)
            P.barrier()
            if stage == "C" and s == 0:
                add("sp", C("dma_start", out=dbg["gr"][:, :, :], in_=grS), key="dbg0")
                add("sp", C("dma_start", out=dbg["qT"][:, :, :], in_=qT), key="dbg1")
                P.barrier()
        if lvl >= 3:
            phase_D(s)
            P.barrier()
    if lvl >= 4:
        phase_topk()
        P.barrier()
        if "aall" in dbg:
            add("sp", C("dma_start", out=dbg["aall"][:, :, :], in_=Aall[:]), key="dbg0")
    if lvl >= 5:
        phase_moe()
    P.barrier()
    P.emit()
    P.close()
    return nc, P.stats


def _swap_idx():
    d = np.arange(64)
    return np.where((d % 32) < 16, d + 16, d - 16)


def _consts():
    bf = ml_dtypes.bfloat16
    cm = np.zeros((128, NCM, 128), np.float32)
    cm[:, CM_ID, :] = np.eye(128)
    for h in range(2):
        cm[h * 64:(h + 1) * 64, CM_ONES, h * 64:(h + 1) * 64] = 1.0
    sw = _swap_idx()
    for m in range(128):
        k = (m // 64) * 64 + sw[m % 64]
        cm[k, CM_SWAP, m] = 1.0
    cm[:, CM_IOP, :] = (np.arange(128) - 127)[None, :]
    cm[:, CM_IOC, 0:4] = np.arange(1, 5)[None, :]
    identf = np.eye(128, dtype=np.float32)
    a = np.arange(128)
    prev = (a[None, :] <= a[:, None]).astype(np.float32)
    nxt = (a[:, None] <= a[None, :]).astype(np.float32)
    masks = np.stack([np.tile(prev, (1, 4)), np.tile(nxt, (1, 4))], axis=1)
    t = np.arange(SEQ)
    row = (t // 64).astype(np.float32)
    col = (t % 64).astype(np.float32)
    inv = (10000.0 ** (-np.arange(16, dtype=np.float32) / 16)).astype(np.float32)
    ar = (row[None, :] * inv[:, None]).astype(np.float32)
    ac = (col[None, :] * inv[:, None]).astype(np.float32)
    C = np.zeros((64, SEQ), np.float32)
    S2 = np.zeros((64, SEQ), np.float32)
    C[0:16] = np.cos(ar); C[16:32] = np.cos(ar); C[32:48] = np.cos(ac); C[48:64] = np.cos(ac)
    S2[0:16] = np.sin(ar); S2[16:32] = -np.sin(ar); S2[32:48] = np.sin(ac); S2[48:64] = -np.sin(ac)
    rope = np.stack([np.tile(C, (2, 1)), np.tile(S2, (2, 1))], axis=1)
    tok = np.arange(32)[None, :] * 128 + np.arange(128)[:, None]
    tcol = np.stack([tok // 64, tok % 64], axis=2).astype(np.float32)
    return dict(cm=cm.astype(bf), identf=identf, masks=masks.astype(bf), rope=rope.astype(bf), tcol=tcol)


def prep_inputs(inp):
    f = np.float32
    g = lambda k: np.asarray(inp[k], dtype=f)
    x, c, ctx, c_ctx = g("x"), g("c"), g("ctx"), g("c_ctx")
    L = 0
    w_in = g("w_in")[L]
    qperm = np.concatenate([np.r_[b * 64:(b + 1) * 64, (b + 4) * 64:(b + 5) * 64] for b in range(4)])
    w_in_p = np.ascontiguousarray(np.concatenate([w_in[:, qperm], w_in[:, 512:]], axis=1))
    w_out = g("w_out")[L]
    w_out_p = np.ascontiguousarray(np.concatenate([w_out[qperm, :], w_out[512:, :]], axis=0))
    b_ada = g("b_ada")[L]
    vecs = np.zeros((128, NV), f)
    pc = lambda v, n: v.reshape(n, 128).T
    vecs[:, V_N1G:V_N1G + 8] = pc(g("norm1_g")[L], 8)
    vecs[:, V_N2G:V_N2G + 8] = pc(g("norm2_g")[L], 8)
    vecs[:, V_BADA:V_BADA + 48] = pc(b_ada, 48)
    gq, gk = g("q_norm_g")[L], g("k_norm_g")[L]
    vecs[:, V_GQ] = np.tile(gq, 2)
    vecs[:, V_GK] = np.tile(gk, 2)
    cw = g("conv_w")[L]
    for cch in range(4):
        for j in range(4):
            vecs[:, V_CONVW + cch * 4 + j] = cw[j, cch * 128:(cch + 1) * 128]
    vecs[:, V_CONVB:V_CONVB + 4] = pc(g("conv_b")[L], 4)
    for d in range(2):
        vecs[:, V_BR + d * 4:V_BR + d * 4 + 4] = pc(g("lru_b_r")[L][d], 4)
        vecs[:, V_BI + d * 4:V_BI + d * 4 + 4] = pc(g("lru_b_i")[L][d], 4)
        vecs[:, V_LAM + d * 4:V_LAM + d * 4 + 4] = pc(g("lru_lambda")[L][d], 4)
    vecs[:, V_SINK:V_SINK + 8] = g("attn_sink")[L][None, :]
    vecs[:, V_GQROW:V_GQROW + 64] = gq[None, :]
    vecs[:, V_GKROW:V_GKROW + 64] = gk[None, :]
    lruw = np.zeros((128, 16, 128), f)
    for d in range(2):
        for gi, nm in enumerate(["lru_w_r", "lru_w_i"]):
            w = g(nm)[L][d]
            for cch in range(4):
                for nl in range(2):
                    lruw[nl * 64:(nl + 1) * 64, (d * 2 + gi) * 4 + cch, nl * 64:(nl + 1) * 64] = w[cch * 2 + nl]
    badag = np.stack([np.stack([b_ada[2 * D:3 * D], b_ada[5 * D:6 * D]])] * 3)
    wrt = np.ascontiguousarray(g("w_router")[L].reshape(8, 128, 16).transpose(1, 0, 2))
    consts = _consts()
    shared = dict(w_ada=g("w_ada")[L], badag=badag, vecs=vecs, w_in=w_in_p, lruw=lruw, w_out=w_out_p, w_router=wrt,
                  w_gate=g("w_gate")[L], w_up=g("w_up")[L], w_down=g("w_down")[L], **consts)
    maps = []
    for core in range(NCORES):
        s0 = 2 * core
        cc = np.stack([c[s0], c[s0 + 1], c_ctx])
        ccT = np.ascontiguousarray(cc.reshape(3, 8, 128).transpose(2, 1, 0))
        m = dict(shared)
        m["x"] = np.ascontiguousarray(x[s0:s0 + 2].reshape(2 * SEQ, D))
        m["ctx"] = np.ascontiguousarray(ctx[s0:s0 + 2].reshape(2 * NCTX, D))
        m["ccT"] = ccT
        maps.append(m)
    return maps


_CACHE = {}


def kernel(**inputs):
    maps = prep_inputs(inputs)
    if "nc" not in _CACHE:
        _CACHE["nc"] = build_program("full")[0]
    nc = _CACHE["nc"]
    res = run_bass_kernel_spmd(nc, maps, core_ids=list(range(NCORES)))
    outs = [np.asarray(r["out"]).reshape(2, SEQ, D) for r in res.results]
    return np.concatenate(outs, axis=0).astype(np.float32)
```

```python
import numpy as np
import ml_dtypes
import concourse.bass as bass
import concourse.mybir as mybir
from concourse.bass_utils import run_bass_kernel_spmd
from contextlib import ExitStack

AF = mybir.ActivationFunctionType
ALU = mybir.AluOpType
AX = mybir.AxisListType
F32 = mybir.dt.float32
BF16 = mybir.dt.bfloat16
I32 = mybir.dt.int32

NCORES = 8
SEQ = 4096
NCTX = 256
NTOK = SEQ + NCTX
D = 1024
EPS = 1e-6
NEXP = 16
CAP = 512


class Op:
    __slots__ = ("eng", "fn", "deps", "dma", "key", "signal", "ticket")

    def __init__(self, eng, fn, dma, key):
        self.eng = eng
        self.fn = fn
        self.dma = dma
        self.key = key
        self.deps = []
        self.signal = False
        self.ticket = 0


ENGS = ["pe", "act", "dve", "pool", "sp"]


def C(name, *a, **k):
    return (name, a, k)


class Prog:
    def __init__(self, nc):
        self.nc = nc
        self.ops = []
        self.last_w = {}
        self.readers = {}
        self.dma_last = {}
        self.last_eng = {}
        self.bank_last = {}
        self.stack = ExitStack()

    def sb(self, name, shape, dtype):
        return self.stack.enter_context(self.nc.sbuf_tensor("sb_" + name, list(shape), dtype))

    def ps(self, name, shape, dtype):
        return self.stack.enter_context(self.nc.psum_tensor("ps_" + name, list(shape), dtype))

    dead = False

    def cut(self, n):
        import os
        if int(os.environ.get("CUT", "-1")) == n:
            self.dead = True

    def add(self, eng, fn, r=(), w=(), key=None):
        op = Op(eng, fn, key is not None, key)
        if self.dead:
            return op
        deps = {}
        for k in r:
            o = self.last_w.get(k)
            if o is not None:
                deps[id(o)] = o
        for k in w:
            o = self.last_w.get(k)
            if o is not None:
                deps[id(o)] = o
            for o in self.readers.get(k, {}).values():
                deps[id(o)] = o
        if op.dma:
            o = self.dma_last.get(key)
            if o is not None:
                deps[id(o)] = o
            self.dma_last[key] = op
        nm_, a_, k_ = fn
        for v in list(a_) + list(k_.values()):
            tn = getattr(getattr(v, "tensor", None), "name", "")
            if tn.startswith("ps_bank"):
                o = self.bank_last.get(tn)
                if o is not None and o.eng != eng:
                    deps[id(o)] = o
                self.bank_last[tn] = op
        for k in r:
            self.readers.setdefault(k, {})[(eng, key)] = op
        for k in w:
            self.last_w[k] = op
            self.readers[k] = {}
        for o in deps.values():
            if o is op:
                continue
            if (not o.dma) and (not op.dma) and o.eng == "pe" and op.eng == "pe":
                continue
            op.deps.append(o)
            o.signal = True
        if not op.dma:
            self.last_eng[eng] = op
        self.ops.append(op)
        return op

    def barrier(self):
        prev = [o for o in self.last_eng.values()] + list(self.dma_last.values())
        new = []
        for e in ENGS:
            op = Op(e, C("nop", ), False, None)
            for o in prev:
                if o.eng == e and not o.dma and e == "pe":
                    continue
                op.deps.append(o)
                o.signal = True
            self.ops.append(op)
            new.append(op)
        for op in new:
            self.last_eng[op.eng] = op
        self.last_w = {}
        self.readers = {}

    def emit(self):
        nc = self.nc
        cnt = {e: 0 for e in ENGS}
        dcnt = {}
        for op in self.ops:
            if op.dma:
                dcnt[op.key] = dcnt.get(op.key, 0) + 16
                op.ticket = dcnt[op.key]
            elif op.signal:
                cnt[op.eng] += 1
                op.ticket = cnt[op.eng]
        sems = {}
        for e in ENGS:
            if cnt[e]:
                sems[("c", e)] = self.stack.enter_context(nc.semaphore("s_" + e))
        for k in dcnt:
            sems[("d", k)] = self.stack.enter_context(nc.semaphore("d_" + str(k)))

        def semof(o):
            return sems[("d", o.key)] if o.dma else sems[("c", o.eng)]

        per = {e: [] for e in ENGS}
        for op in self.ops:
            per[op.eng].append(op)

        def run(name, eng):
            seen = {}
            for op in per[name]:
                need = {}
                for d in op.deps:
                    s = semof(d)
                    if need.get(id(s), (None, 0))[1] < d.ticket:
                        need[id(s)] = (s, d.ticket)
                for kk, (s, t) in need.items():
                    if seen.get(kk, 0) >= t:
                        continue
                    eng.wait_ge(s, t)
                    seen[kk] = t
                nm, a_, k_ = op.fn
                try:
                    ins = getattr(eng, nm)(*a_, **k_)
                except Exception:
                    print("EMIT FAIL at", per[name].index(op), "of", len(per[name]), "ndma_before", sum(1 for o in per[name][:per[name].index(op)] if o.dma), flush=True)
                    print("EMIT FAIL", name, nm, [repr(x)[:300] for x in a_], {kk: repr(vv)[:300] for kk, vv in k_.items()}, flush=True)
                    raise
                if op.dma:
                    ins.then_inc(semof(op), 16)
                elif op.signal:
                    ins.then_inc(semof(op), 1)

        with nc.Block() as block:
            @block.tensor
            def _(e):
                run("pe", e)

            @block.scalar
            def _(e):
                run("act", e)

            @block.vector
            def _(e):
                run("dve", e)

            @block.gpsimd
            def _(e):
                run("pool", e)

            @block.sync
            def _(e):
                run("sp", e)
        self.stats = dict(n={e: len(per[e]) for e in ENGS}, sig=cnt, nsems=len(sems))

    def close(self):
        self.stack.close()


class Arena:
    def __init__(self, P, name, nbytes):
        self.t = P.sb(name, [128, nbytes // 2], BF16)
        self.cap = nbytes
        self.off = 0

    def alloc(self, shape, dtype):
        esz = 2 if dtype == BF16 else 4
        n = int(np.prod(shape))
        nb = n * esz
        start = self.off
        self.off += (nb + 31) // 32 * 32
        assert self.off <= self.cap, ("arena overflow", self.off, self.cap)
        ap = self.t[:, start // 2:start // 2 + nb // 2]
        if dtype != BF16:
            ap = ap.bitcast(dtype)
        if len(shape) == 2:
            ap = ap.rearrange("p (a b) -> p a b", a=shape[0])
        elif len(shape) == 3:
            ap = ap.rearrange("p (a b c) -> p a b c", a=shape[0], b=shape[1])
        elif len(shape) == 4:
            ap = ap.rearrange("p (a b c d) -> p a b c d", a=shape[0], b=shape[1], c=shape[2])
        return ap

    def reset(self, off=0):
        self.off = off


V_N1G, V_N2G, V_BADA, V_GQ, V_GK, V_CONVW, V_CONVB, V_BR, V_BI, V_LAM = 0, 8, 16, 64, 65, 66, 82, 86, 94, 102
V_SINK, V_GQROW, V_GKROW, NV = 110, 118, 182, 246
CM_ID, CM_ONES, CM_SWAP, CM_IOP, CM_IOC, NCM = 0, 1, 2, 3, 4, 5


def build_program(stage="full"):
    nc = bass.Bass("TRN2", target_bir_lowering=False)

    def din(name, shape, dt=F32):
        return nc.dram_tensor(name, list(shape), dt, kind="ExternalInput").ap()

    x_d = din("x", [2 * SEQ, D])
    ctx_d = din("ctx", [2 * NCTX, D])
    ccT_d = din("ccT", [128, 8, 3])
    wada_d = din("w_ada", [D, 6 * D])
    badag_d = din("badag", [3, 2, D])
    vecs_d = din("vecs", [128, NV])
    cm_d = din("cm", [128, NCM, 128], BF16)
    identf_d = din("identf", [128, 128])
    masks_d = din("masks", [128, 2, 512], BF16)
    rope_d = din("rope", [128, 2, SEQ], BF16)
    tcol_d = din("tcol", [128, 32, 2])
    win_d = din("w_in", [D, 1792])
    lruw_d = din("lruw", [128, 16, 128])
    wout_d = din("w_out", [D, D])
    wr_d = din("w_router", [128, 8, 16])
    wg_d = din("w_gate", [NEXP, D, D])
    wu_d = din("w_up", [NEXP, D, D])
    wd_d = din("w_down", [NEXP, D, D])
    out_d = nc.dram_tensor("out", [2 * SEQ, D], F32, kind="ExternalOutput").ap()
    xn_d = nc.dram_tensor("xn_scr", [2 * SEQ, 1056], BF16, kind="Internal").ap()
    gsc_d = nc.dram_tensor("g_scr", [3, 2, D], F32, kind="Internal").ap()
    dbg = {}
    if stage != "full":
        dbg["qT"] = nc.dram_tensor("dbg_qT", [128, 4, SEQ], BF16, kind="ExternalOutput").ap()
        dbg["kT"] = nc.dram_tensor("dbg_kT", [128, NTOK], BF16, kind="ExternalOutput").ap()
        dbg["v"] = nc.dram_tensor("dbg_v", [128, 34, 2, 65], BF16, kind="ExternalOutput").ap()
        dbg["xr"] = nc.dram_tensor("dbg_xr", [128, 4, NTOK], BF16, kind="ExternalOutput").ap()
        dbg["gr"] = nc.dram_tensor("dbg_gr", [128, 4, SEQ], BF16, kind="ExternalOutput").ap()
        dbg["modv"] = nc.dram_tensor("dbg_modv", [128, 48, 3], F32, kind="ExternalOutput").ap()
        dbg["aall"] = nc.dram_tensor("dbg_aall", [128, 32, 32], F32, kind="ExternalOutput").ap()
        dbg["idx"] = nc.dram_tensor("dbg_idx", [128, 32, 4], I32, kind="ExternalOutput").ap()

    P = Prog(nc)
    add = P.add

    vecs = P.sb("vecs", [128, NV], F32)
    cm = P.sb("cm", [128, NCM, 128], BF16)
    identf = P.sb("identf", [128, 128], F32)
    masks = P.sb("masks", [128, 2, 512], BF16)
    lruW = P.sb("lruW", [128, 16, 128], BF16)
    wr = P.sb("wr", [128, 8, 16], BF16)
    tcol = P.sb("tcol", [128, 32, 2], F32)
    modv = P.sb("modv", [128, 48, 3], F32)
    A1 = P.sb("A1", [128, 8, 3], F32)
    A2 = P.sb("A2", [128, 8, 3], F32)
    lv = P.sb("lv", [128, 40], F32)
    av = P.sb("av", [128, 16], F32)
    Aall = P.sb("Aall", [128, 32, 32], F32)
    IDX = P.sb("IDX", [128, 32, 4], I32)
    ident = cm[:, CM_ID, :]
    onesblk = cm[:, CM_ONES, :]
    pswap = cm[:, CM_SWAP, :]
    B1 = modv[:, 0:8, :]
    B2 = modv[:, 24:32, :]
    HBR, HBI, CNEG, HCNEG = 0, 8, 16, 24
    negB = av[:, 0:1]
    esink = av[:, 1:9]

    banks = [P.ps("bank%d" % i, [128, 512], F32) for i in range(8)]
    ST = Arena(P, "ST", 116 * 1024)
    WK = Arena(P, "WK", 77 * 1024)

    def ld(eng, out, in_, w, key):
        return add(eng, C("dma_start", out=out, in_=in_), w=w, key=key)

    ld("sp", vecs[:], vecs_d[:, :], ["vecs"], "c0")
    ld("sp", cm[:], cm_d[:, :, :], ["cm"], "c1")
    ld("sp", identf[:], identf_d[:, :], ["identf"], "c2")
    ld("sp", masks[:], masks_d[:, :, :], ["masks"], "c3")
    ld("sp", tcol[:], tcol_d[:, :, :], ["tcol"], "c0")
    ld("pool", lruW[:], lruw_d[:, :, :], ["lruW"], "c4")
    ld("pool", wr[:], wr_d[:, :, :], ["wr"], "c5")
    ccT = WK.alloc([8, 3], F32)
    scT = WK.alloc([8, 3], F32)
    badag = WK.alloc([2, D], F32)
    grow = WK.alloc([2, D], F32)
    wa = [WK.alloc([8, 512], F32) for _ in range(2)]
    tmpv = WK.alloc([64], F32)
    rowsb = [WK.alloc([512], F32) for _ in range(2)]
    ld("sp", ccT, ccT_d[:, :, :], ["ccT"], "c2")
    ld("sp", badag[0:3], badag_d[:, :, :], ["badag"], "c3")
    add("act", C("activation", out=scT, in_=ccT, func=AF.Silu), r=["ccT"], w=["scT"])
    pm = banks[0][:, 0:192].rearrange("p (a b) -> p a b", a=48)
    prow = banks[1]
    wada_v = wada_d.rearrange("(kc p) n -> p kc n", p=128)
    P.cut(1)
    pc = 0
    for m in range(6):
        for half in range(2):
            b = pc % 2
            pc += 1
            c0 = m * D + half * 512
            ld("sp", wa[b], wada_v[:, :, c0:c0 + 512], ["wa%d" % b], "wa%d" % b)
            for kc in range(8):
                add("pe", C("matmul", prow[0:3, :], lhsT=scT[:, kc, :], rhs=wa[b][:, kc, :], start=(kc == 0), stop=(kc == 7)),
                    r=["wa%d" % b, "scT"], w=["prow"])
            rb = rowsb[pc % 2]
            add("dve", C("tensor_copy", out=rb[0:3, :], in_=prow[0:3, :]), r=["prow"], w=["rowsb%d" % (pc % 2)])
            for jj in range(4):
                j = m * 8 + half * 4 + jj
                add("pe", C("transpose", out=pm[:, j, 0:3], in_=rb[0:3, jj * 128:(jj + 1) * 128], identity=identf[0:3, 0:3]), r=["rowsb%d" % (pc % 2), "identf"], w=["pm"])
            if m in (2, 5):
                which = 0 if m == 2 else 1
                add("dve", C("tensor_tensor", out=grow[0:3, which, half * 512:(half + 1) * 512], in0=rb[0:3, :], in1=badag[0:3, which, half * 512:(half + 1) * 512], op=ALU.add),
                    r=["rowsb%d" % (pc % 2), "badag"], w=["grow"])
    P.cut(2)
    add("sp", C("dma_start", out=gsc_d[:, :, :], in_=grow[0:3]), r=["grow"], w=["gsc"], key="c1")
    P.cut(3)
    add("dve", C("tensor_tensor", out=modv[:], in0=pm[:, :, 0:3], in1=vecs[:, V_BADA:V_BADA + 48].unsqueeze(2).to_broadcast([128, 48, 3]), op=ALU.add),
        r=["pm", "vecs"], w=["modv"])
    add("dve", C("scalar_tensor_tensor", out=A1[:], in0=modv[:, 8:16, :], scalar=1.0, in1=vecs[:, V_N1G:V_N1G + 8].unsqueeze(2).to_broadcast([128, 8, 3]), op0=ALU.add, op1=ALU.mult),
        r=["modv", "vecs"], w=["A1"])
    add("dve", C("scalar_tensor_tensor", out=A2[:], in0=modv[:, 32:40, :], scalar=1.0, in1=vecs[:, V_N2G:V_N2G + 8].unsqueeze(2).to_broadcast([128, 8, 3]), op0=ALU.add, op1=ALU.mult),
        r=["modv", "vecs"], w=["A2"])
    P.cut(4)
    add("dve", C("tensor_scalar", out=lv[:, HBR:HBR + 16], in0=vecs[:, V_BR:V_BR + 16], scalar1=0.5, scalar2=None, op0=ALU.mult), r=["vecs"], w=["lv0"])
    add("act", C("activation", out=tmpv[:, 0:8], in_=vecs[:, V_LAM:V_LAM + 8], func=AF.Exp, scale=-1.0), r=["vecs"], w=["tmpv"])
    add("act", C("activation", out=tmpv[:, 8:16], in_=tmpv[:, 0:8], func=AF.Ln, bias=1.0), r=["tmpv"], w=["tmpv2"])
    add("dve", C("tensor_scalar", out=lv[:, CNEG:CNEG + 8], in0=tmpv[:, 8:16], scalar1=-8.0, scalar2=None, op0=ALU.mult), r=["tmpv2"], w=["lv1"])
    add("dve", C("tensor_scalar", out=lv[:, HCNEG:HCNEG + 8], in0=tmpv[:, 8:16], scalar1=-4.0, scalar2=None, op0=ALU.mult), r=["tmpv2"], w=["lv2"])
    P.cut(5)
    add("dve", C("tensor_reduce", out=tmpv[:, 16:17], in_=vecs[:, V_GQROW:V_GQROW + 64], axis=AX.X, op=ALU.max, apply_absolute_value=True), r=["vecs"], w=["tmpv3"])
    add("dve", C("tensor_reduce", out=tmpv[:, 17:18], in_=vecs[:, V_GKROW:V_GKROW + 64], axis=AX.X, op=ALU.max, apply_absolute_value=True), r=["vecs"], w=["tmpv4"])
    add("dve", C("scalar_tensor_tensor", out=av[:, 0:1], in0=tmpv[:, 16:17], scalar=-8.0, in1=tmpv[:, 17:18], op0=ALU.mult, op1=ALU.mult), r=["tmpv3", "tmpv4"], w=["av0"])
    add("act", C("activation", out=av[:, 1:9], in_=vecs[:, V_SINK:V_SINK + 8], func=AF.Exp, bias=av[:, 0:1]), r=["vecs", "av0"], w=["av1"])
    P.cut(6)
    if "modv" in dbg:
        add("sp", C("dma_start", out=dbg["modv"][:, :, :], in_=modv[:]), r=["modv"], key="dbg0")
    P.barrier()
    P.cut(7)

    ST.reset()
    qT = ST.alloc([4, SEQ], BF16)
    kT = ST.alloc([NTOK], BF16)
    vaug = ST.alloc([34, 2, 65], BF16)
    xrS = ST.alloc([4, NTOK], BF16)
    grS = ST.alloc([4, SEQ], BF16)
    add("pool", C("memset", vaug[:, :, :, 64:65], 1.0), w=["vones"])
    P.cut(8)
    P.barrier()
    P.cut(9)

    win_v = win_d.rearrange("(kc p) n -> p kc n", p=128)
    wout_v = wout_d.rearrange("(kc p) n -> p kc n", p=128)

    def phase_A(s):
        WK.reset()
        win = WK.alloc([8, 1792], BF16)
        ropeb = [WK.alloc([2, 512], BF16) for _ in range(2)]
        xt = [WK.alloc([D], F32) for _ in range(3)]
        xn = WK.alloc([4, D], BF16)
        hT = WK.alloc([8, 512], BF16)
        NSET = 3
        tq = [[WK.alloc([512], BF16) for _ in range(3)] for _ in range(NSET)]
        rstd = [WK.alloc([512], F32) for _ in range(NSET)]
        ssb = WK.alloc([8], F32)
        for h in range(2):
            add("pool", C("dma_start", out=win[:, h * 4:(h + 1) * 4, :], in_=win_v[:, h * 4:(h + 1) * 4, :]), w=["win%d" % h], key="win%d" % h)
        add("pool", C("memset", vaug[:, :, :, 64:65], 1.0), w=["vones"])
        pT = [banks[i].bitcast(BF16).rearrange("p (a b) -> p a b", a=2) for i in range(4)]
        accs = [(banks[4], 4), (banks[5], 5), (banks[0], 0), (banks[1], 1), (banks[2], 2), (banks[3], 3)]
        pS = banks[6]
        pR = banks[7]
        st = dict(xh=0, grp=0, acc=0, tq=0, ev=0, prepped=set())

        def prep(rows_ap, ntiles):
            for t in range(ntiles):
                xb = st["xh"] % 3
                st["xh"] += 1
                add("sp", C("dma_start", out=xt[xb], in_=rows_ap[t * 128:(t + 1) * 128, :]), w=["xt%d" % xb], key="xt%d" % xb)
                sc_ = ssb[:, t:t + 1]
                add("act", C("activation", out=xn[:, t, :], in_=xt[xb], func=AF.Square, accum_out=sc_), r=["xt%d" % xb], w=["xn%d" % t, "ms%d" % t])
                add("act", C("activation", out=sc_, in_=sc_, func=AF.Ln, scale=1.0 / D, bias=EPS), r=["ms%d" % t], w=["ms%d" % t])
                add("act", C("activation", out=sc_, in_=sc_, func=AF.Exp, scale=-0.5), r=["ms%d" % t], w=["ms%d" % t])
                add("dve", C("tensor_scalar", out=xn[:, t, :], in0=xt[xb], scalar1=sc_, scalar2=None, op0=ALU.mult), r=["xt%d" % xb, "ms%d" % t], w=["xn%d" % t])

        def group(rows_ap, ntiles, is_ctx, col0, tile0, lat0, nxt=None):
            ntok = ntiles * 128
            scol = 2 if is_ctx else s
            gi = st["grp"]
            st["grp"] += 1
            hb = gi % 2
            if not is_ctx:
                add("sp", C("dma_start", out=ropeb[hb], in_=rope_d[:, :, lat0:lat0 + 512]), w=["rope%d" % hb], key="rope%d" % hb)
            if gi not in st["prepped"]:
                prep(rows_ap, ntiles)
            st["prepped"].discard(gi)

            for fc in range(8):
                bk = fc % 4
                for t in range(ntiles):
                    add("pe", C("transpose", out=pT[bk][:, fc // 4, t * 128:(t + 1) * 128], in_=xn[:, t, fc * 128:(fc + 1) * 128], identity=ident),
                        r=["xn%d" % t, "cm"], w=["pT%d" % fc, "acc%d" % bk])
                if bk % 2 == 0:
                    add("act", C("activation", out=hT[:, fc, 0:ntok], in_=pT[bk][:, fc // 4, 0:ntok], func=AF.Identity, scale=A1[:, fc, scol:scol + 1], bias=B1[:, fc, scol:scol + 1]),
                        r=["pT%d" % fc, "A1", "modv"], w=["hT%d" % fc])
                else:
                    add("dve", C("tensor_scalar", out=hT[:, fc, 0:ntok], in0=pT[bk][:, fc // 4, 0:ntok], scalar1=A1[:, fc, scol:scol + 1], scalar2=B1[:, fc, scol:scol + 1], op0=ALU.mult, op1=ALU.add),
                        r=["pT%d" % fc, "A1", "modv"], w=["hT%d" % fc])
            hkeys = ["hT%d" % fc for fc in range(8)]
            cols = {"k": 512, "v": 640}
            for b in range(4):
                cols["q%d" % b] = b * 128
                cols["x%d" % b] = 768 + b * 128
                cols["g%d" % b] = 1280 + b * 128
            if is_ctx:
                order = ["k", "v", "x0", "x1", "x2", "x3"]
            else:
                order = ["k", "v", "q0", "x0", "q1", "x1", "q2", "x2", "q3", "x3", "g0", "g1", "g2", "g3"]
            pend = []

            def evac_copy(dst, src_, rk, wk):
                st["ev"] += 1
                if st["ev"] % 2 == 0:
                    add("act", C("copy", out=dst, in_=src_), r=rk, w=wk)
                else:
                    add("dve", C("tensor_copy", out=dst, in_=src_), r=rk, w=wk)

            def qk_stage1(name, acc, akey, ts):
                sq, qg, qs = tq[ts]
                qc = sq
                rs = rstd[ts]
                gcol = vecs[:, V_GK:V_GK + 1] if name == "k" else vecs[:, V_GQ:V_GQ + 1]
                add("pe", C("matmul", pS[:, 0:ntok], lhsT=onesblk, rhs=sq[:, 0:ntok], start=True, stop=True), r=["sq%d" % ts, "cm"], w=["pS"])
                add("act", C("activation", out=rs[:, 0:ntok], in_=pS[:, 0:ntok], func=AF.Ln, scale=1.0 / 64, bias=EPS), r=["pS"], w=["rs%d" % ts])
                add("act", C("activation", out=rs[:, 0:ntok], in_=rs[:, 0:ntok], func=AF.Exp, scale=-0.5), r=["rs%d" % ts], w=["rs%d" % ts])
                if is_ctx:
                    add("dve", C("scalar_tensor_tensor", out=kT[:, col0:col0 + ntok], in0=acc[:, 0:ntok], scalar=gcol, in1=rs[:, 0:ntok], op0=ALU.mult, op1=ALU.mult),
                        r=[akey, "rs%d" % ts, "vecs"], w=["kT%d" % (col0 // 128 + t) for t in range(ntiles)])
                    return
                add("dve", C("scalar_tensor_tensor", out=qg[:, 0:ntok], in0=acc[:, 0:ntok], scalar=gcol, in1=rs[:, 0:ntok], op0=ALU.mult, op1=ALU.mult),
                    r=[akey, "rs%d" % ts, "vecs"], w=["qg%d" % ts])
                add("dve", C("tensor_tensor", out=qc[:, 0:ntok], in0=qg[:, 0:ntok], in1=ropeb[hb][:, 0, 0:ntok], op=ALU.mult), r=["qg%d" % ts, "rope%d" % hb, "sq%d" % ts], w=["sq%d" % ts])
                add("dve", C("tensor_tensor", out=qs[:, 0:ntok], in0=qg[:, 0:ntok], in1=ropeb[hb][:, 1, 0:ntok], op=ALU.mult), r=["qg%d" % ts, "rope%d" % hb], w=["qs%d" % ts])

            def qk_stage2(name, ts):
                sq, qg, qs = tq[ts]
                qc = sq
                add("pe", C("matmul", pR[:, 0:ntok], lhsT=ident, rhs=qc[:, 0:ntok], start=True, stop=False), r=["sq%d" % ts, "cm"], w=["pR"])
                add("pe", C("matmul", pR[:, 0:ntok], lhsT=pswap, rhs=qs[:, 0:ntok], start=False, stop=True), r=["qs%d" % ts, "cm"], w=["pR"])
                if name == "k":
                    add("act", C("copy", out=kT[:, col0:col0 + ntok], in_=pR[:, 0:ntok]), r=["pR"], w=["kT%d" % (col0 // 128 + t) for t in range(ntiles)])
                else:
                    b = int(name[1])
                    add("act", C("copy", out=qT[:, b, lat0:lat0 + ntok], in_=pR[:, 0:ntok]), r=["pR"], w=["qT%d" % (lat0 // 128 + t) for t in range(ntiles)])

            def run_pending(upto):
                keep = []
                for due, fn_ in pend:
                    if due <= upto:
                        fn_()
                    else:
                        keep.append((due, fn_))
                pend[:] = keep

            for bi, name in enumerate(order):
                c0 = cols[name]
                acc, bno = accs[st["acc"] % len(accs)]
                st["acc"] += 1
                akey = "acc%d" % bno
                wk = [akey] + (["pT%d" % bno, "pT%d" % (bno + 4)] if bno < 4 else [])
                if name == "v":
                    accv = acc.rearrange("p (t c) -> p t c", t=4)
                    for t in range(ntiles):
                        for kc in range(8):
                            add("pe", C("matmul", accv[:, t, :], lhsT=hT[:, kc, t * 128:(t + 1) * 128], rhs=win[:, kc, 640:768], start=(kc == 0), stop=(kc == 7)),
                                r=hkeys + ["win0", "win1"], w=wk)
                    add("dve", C("tensor_copy", out=vaug[:, tile0:tile0 + ntiles, :, 0:64], in_=accv[:, 0:ntiles, :].rearrange("p t (h d) -> p t h d", h=2)),
                        r=[akey], w=["v%d" % (tile0 + t) for t in range(ntiles)])
                else:
                    for kc in range(8):
                        add("pe", C("matmul", acc[:, 0:ntok], lhsT=win[:, kc, c0:c0 + 128], rhs=hT[:, kc, 0:ntok], start=(kc == 0), stop=(kc == 7)),
                            r=hkeys + ["win0", "win1"], w=wk)
                    if name[0] == "x":
                        c = int(name[1])
                        evac_copy(xrS[:, c, col0:col0 + ntok], acc[:, 0:ntok], [akey], ["xr%d" % c])
                    elif name[0] == "g":
                        c = int(name[1])
                        evac_copy(grS[:, c, lat0:lat0 + ntok], acc[:, 0:ntok], [akey], ["gr%d" % c])
                    else:
                        ts = st["tq"] % NSET
                        st["tq"] += 1
                        add("act", C("activation", out=tq[ts][0][:, 0:ntok], in_=acc[:, 0:ntok], func=AF.Square), r=[akey], w=["sq%d" % ts])
                        pend.append((bi + 1, (lambda name=name, acc=acc, akey=akey, ts=ts: qk_stage1(name, acc, akey, ts))))
                        if not is_ctx:
                            pend.append((bi + 3, (lambda name=name, ts=ts: qk_stage2(name, ts))))
                run_pending(bi)
                if nxt is not None and bi == min(4, len(order) - 2):
                    prep(*nxt)
                    st["prepped"].add(gi + 1)
            run_pending(10 ** 9)

        lat_rows = lambda g: x_d[s * SEQ + g * 512:s * SEQ + (g + 1) * 512, :]
        group(ctx_d[s * NCTX:(s + 1) * NCTX, :], 2, True, 0, 0, 0, nxt=(lat_rows(0), 4))
        for g in range(8):
            group(lat_rows(g), 4, False, NCTX + g * 512, 2 + g * 4, g * 512, nxt=((lat_rows(g + 1), 4) if g < 7 else None))

    def phase_B(s):
        WK.reset()
        kTz = [WK.alloc([NTOK], BF16) for _ in range(2)]
        NEX = 16
        expS = [WK.alloc([512], BF16) for _ in range(NEX)]
        On = [WK.alloc([4, 2, 64], BF16) for _ in range(2)]
        den = [WK.alloc([4], F32) for _ in range(4)]
        kkeys = ["kT%d" % t for t in range(34)]
        add("pool", C("memset", kTz[0][64:128, :], 0.0), w=["kz0"])
        add("pool", C("memset", kTz[1][0:64, :], 0.0), w=["kz1"])
        add("act", C("copy", out=kTz[0][0:64, :], in_=kT[0:64, :]), r=kkeys, w=["kz0"])
        add("dve", C("tensor_copy", out=kTz[1][64:128, :], in_=kT[64:128, :]), r=kkeys, w=["kz1"])
        pSc = banks[0:4]
        pO = [banks[4], banks[5]]
        pTr = [banks[6].bitcast(BF16)[:, 0:512].rearrange("p (g q) -> p g q", g=4), banks[7].bitcast(BF16)[:, 0:512].rearrange("p (g q) -> p g q", g=4)]
        ctr = dict(sc=0, ex=0)
        its = [(i, kvh) for i in range(32) for kvh in range(2)]
        info = {}

        def chunks_of(i):
            ch = [(0, None), (1, None)]
            if i > 0:
                ch.append((2 + i - 1, 0))
            ch.append((2 + i, None))
            if i < 31:
                ch.append((2 + i + 1, 1))
            return ch

        def stage_sc(n):
            i, kvh = its[n]
            ebs = []
            for ci, (kt, mk) in enumerate(chunks_of(i)):
                sb_ = ctr["sc"] % 4
                ctr["sc"] += 1
                eb = ctr["ex"] % NEX
                ctr["ex"] += 1
                ebs.append(eb)
                add("pe", C("matmul", pSc[sb_][:, :], lhsT=kTz[kvh][:, kt * 128:(kt + 1) * 128], rhs=qT[:, :, i * 128:(i + 1) * 128], start=True, stop=True),
                    r=["kz%d" % kvh, "qT%d" % i], w=["pSc%d" % sb_])
                add("act", C("activation", out=expS[eb], in_=pSc[sb_][:, :], func=AF.Exp, scale=0.125, bias=negB), r=["pSc%d" % sb_, "av0"], w=["ex%d" % eb])
                if mk is not None:
                    add("pool", C("tensor_tensor", out=expS[eb], in0=expS[eb], in1=masks[:, mk, :], op=ALU.mult), r=["ex%d" % eb, "masks"], w=["ex%d" % eb])
            info[n] = ebs

        def stage_pv(n):
            i, kvh = its[n]
            ob = i % 2
            chunks = chunks_of(i)
            ebs = info.pop(n)
            po = pO[n % 2]
            pok = "pO%d" % (n % 2)
            dn = den[n % 4]
            dnk = "den%d" % (n % 4)
            pov = po[:, 0:260].rearrange("p (g c) -> p g c", g=4)
            for ci, (kt, mk) in enumerate(chunks):
                eb = ebs[ci]
                for g in range(4):
                    first = (ci == 0 and g == 0)
                    add("pe", C("matmul", pov[:, g, :], lhsT=expS[eb][:, g * 128:(g + 1) * 128], rhs=vaug[:, kt, kvh, :], start=first, stop=(ci == len(chunks) - 1), skip_group_check=True),
                        r=["ex%d" % eb, "v%d" % kt, "vones"], w=[pok])
            add("dve", C("tensor_tensor", out=dn, in0=pov[:, :, 64], in1=esink[:, kvh * 4:(kvh + 1) * 4], op=ALU.add), r=[pok, "av1"], w=[dnk])
            add("dve", C("reciprocal", out=dn, in_=dn), r=[dnk], w=[dnk])
            add("dve", C("tensor_tensor", out=On[ob][:, :, kvh, :], in0=pov[:, :, 0:64], in1=dn.unsqueeze(2).to_broadcast([128, 4, 64]), op=ALU.mult),
                r=[pok, dnk], w=["On%d_%d" % (ob, kvh)])
            if kvh == 1:
                for g in range(4):
                    add("pe", C("transpose", out=pTr[ob][:, g, :], in_=On[ob][:, g, :, :].rearrange("p h d -> p (h d)"), identity=ident),
                        r=["On%d_0" % ob, "On%d_1" % ob, "cm"], w=["pTr%d" % ob])
                add("act", C("copy", out=qT[:, :, i * 128:(i + 1) * 128], in_=pTr[ob]), r=["pTr%d" % ob], w=["qT%d" % i])

        LAG = 2
        for step in range(len(its) + LAG):
            if step < len(its):
                stage_sc(step)
            if step - LAG >= 0:
                stage_pv(step - LAG)

    def phase_C(s):
        WK.reset()
        xcb = WK.alloc([NTOK], BF16)
        abuf0 = WK.alloc([NTOK], F32)
        abuf1 = ST.t[:, 16384:16384 + 8704].bitcast(F32)
        abufs = [abuf0, abuf1]
        bbuf = [WK.alloc([NTOK], F32) for _ in range(2)]
        tbuf = WK.alloc([NTOK], BF16)
        trb = [WK.alloc([512], F32) for _ in range(2)]
        tib = [WK.alloc([512], F32) for _ in range(2)]
        ctr = dict(b=0)
        deferred = []
        segs = [(0, NCTX), (NCTX, NTOK)]
        def conv(c):
            xcf = abufs[0]
            cw = lambda j: vecs[:, V_CONVW + c * 4 + j:V_CONVW + c * 4 + j + 1]
            for (lo, hi) in segs:
                add("dve", C("tensor_scalar", out=xcf[:, lo:hi], in0=xrS[:, c, lo:hi], scalar1=cw(2), scalar2=vecs[:, V_CONVB + c:V_CONVB + c + 1], op0=ALU.mult, op1=ALU.add),
                    r=["xr%d" % c, "vecs"], w=["a0"])
                for j in (0, 1, 3):
                    o = j - 2
                    a0, a1 = max(lo, lo - o), min(hi, hi - o)
                    add("dve", C("scalar_tensor_tensor", out=xcf[:, a0:a1], in0=xrS[:, c, a0 + o:a1 + o], scalar=cw(j), in1=xcf[:, a0:a1], op0=ALU.mult, op1=ALU.add),
                        r=["xr%d" % c, "vecs", "a0"], w=["a0"])
            add("act", C("copy", out=xcb, in_=xcf), r=["a0"], w=["xcb"])

        conv(0)
        for c in range(4):
            for d in range(2):
                bb = bbuf[d]
                bk = "b%d" % d
                abuf = abufs[d]
                ak = "a%d" % d
                wi_r = (d * 2 + 0) * 4 + c
                wi_i = (d * 2 + 1) * 4 + c
                col = d * 4 + c
                blocks = [(0, NCTX)] + [(NCTX + g * 512, NCTX + (g + 1) * 512) for g in range(8)]
                for (lo, hi) in blocks:
                    n = hi - lo
                    pb = ctr["b"] % 2
                    ctr["b"] += 1
                    pr, pi = banks[pb * 2], banks[pb * 2 + 1]
                    tr, ti = trb[pb], tib[pb]
                    add("pe", C("matmul", pr[:, 0:n], lhsT=lruW[:, wi_r, :], rhs=xcb[:, lo:hi], start=True, stop=True), r=["xcb", "lruW"], w=["pr%d" % pb])
                    add("pe", C("matmul", pi[:, 0:n], lhsT=lruW[:, wi_i, :], rhs=xcb[:, lo:hi], start=True, stop=True), r=["xcb", "lruW"], w=["pi%d" % pb])
                    add("act", C("activation", out=tr[:, 0:n], in_=pr[:, 0:n], func=AF.Tanh, scale=0.5, bias=lv[:, HBR + col:HBR + col + 1]), r=["pr%d" % pb, "lv0"], w=["tr%d" % pb])
                    add("act", C("activation", out=ti[:, 0:n], in_=pi[:, 0:n], func=AF.Tanh, scale=0.5, bias=lv[:, HBI + col:HBI + col + 1]), r=["pi%d" % pb, "lv0"], w=["ti%d" % pb])
                    add("act", C("activation", out=abuf[:, lo:hi], in_=tr[:, 0:n], func=AF.Exp, scale=lv[:, HCNEG + col:HCNEG + col + 1], bias=lv[:, HCNEG + col:HCNEG + col + 1]), r=["tr%d" % pb, "lv2"], w=[ak])
                    add("act", C("activation", out=bb[:, lo:hi], in_=tr[:, 0:n], func=AF.Exp, scale=lv[:, CNEG + col:CNEG + col + 1], bias=lv[:, CNEG + col:CNEG + col + 1]), r=["tr%d" % pb, "lv1"], w=[bk])
                    add("dve", C("scalar_tensor_tensor", out=tbuf[:, lo:hi], in0=ti[:, 0:n], scalar=1.0, in1=xcb[:, lo:hi], op0=ALU.add, op1=ALU.mult), r=["ti%d" % pb, "xcb"], w=["t"])
                while deferred:
                    deferred.pop(0)()
                add("act", C("activation", out=bb, in_=bb, func=AF.Sqrt, scale=-0.25, bias=0.25), r=[bk], w=[bk])
                add("dve", C("tensor_tensor", out=bb, in0=bb, in1=tbuf, op=ALU.mult), r=[bk, "t"], w=[bk])
                if d == 0:
                    def _scan0(bb=bb, abuf=abuf, bk=bk, ak=ak):
                        add("dve", C("tensor_tensor_scan", out=bb[:, 0:NCTX], data0=abuf[:, 0:NCTX], data1=bb[:, 0:NCTX], initial=0.0, op0=ALU.mult, op1=ALU.add), r=[bk, ak], w=[bk])
                        add("dve", C("tensor_tensor_scan", out=bb[:, NCTX:NTOK], data0=abuf[:, NCTX:NTOK], data1=bb[:, NCTX:NTOK], initial=bb[:, NCTX - 1:NCTX], op0=ALU.mult, op1=ALU.add), r=[bk, ak], w=[bk])
                    deferred.append(_scan0)
                else:
                    add("dve", C("tensor_tensor_scan", out=bb[:, 0:NCTX][:, ::-1], data0=abuf[:, 0:NCTX][:, ::-1], data1=bb[:, 0:NCTX][:, ::-1], initial=0.0, op0=ALU.mult, op1=ALU.add), r=[bk, ak], w=[bk])
                    add("dve", C("tensor_tensor_scan", out=bb[:, NCTX:NTOK][:, ::-1], data0=abuf[:, NCTX:NTOK][:, ::-1], data1=bb[:, NCTX:NTOK][:, ::-1], initial=bb[:, 0:1], op0=ALU.mult, op1=ALU.add), r=[bk, ak], w=[bk])
            add("pool", C("tensor_tensor", out=bbuf[0][:, NCTX:NTOK], in0=bbuf[0][:, NCTX:NTOK], in1=bbuf[1][:, NCTX:NTOK], op=ALU.add), r=["b0", "b1"], w=["b0"])
            add("act", C("activation", out=tbuf[:, 0:SEQ], in_=grS[:, c, :], func=AF.Gelu_apprx_tanh), r=["gr%d" % c, "t"], w=["t"])
            if c + 1 < 4:
                conv(c + 1)
            add("dve", C("tensor_tensor", out=grS[:, c, :], in0=bbuf[0][:, NCTX:NTOK], in1=tbuf[:, 0:SEQ], op=ALU.mult), r=["b0", "t"], w=["gr%d" % c])

    def phase_D(s):
        WK.reset()
        wout = WK.alloc([8, D], BF16)
        g1bc = WK.alloc([D], F32)
        xt2 = [WK.alloc([D], F32) for _ in range(2)]
        x1 = [WK.alloc([D], F32) for _ in range(2)]
        xn2 = [WK.alloc([1056], BF16) for _ in range(2)]
        h2T = [WK.alloc([8, 128], BF16) for _ in range(2)]
        junk = WK.alloc([D], BF16)
        sm = [WK.alloc([24], F32) for _ in range(2)]
        ex = [WK.alloc([16], F32) for _ in range(2)]
        for h in range(2):
            add("pool", C("dma_start", out=wout[:, h * 4:(h + 1) * 4, :], in_=wout_v[:, h * 4:(h + 1) * 4, :]), w=["wout%d" % h], key="win%d" % h)
        add("sp", C("dma_start", out=g1bc, in_=gsc_d[s, 0, :].partition_broadcast(128)), w=["g1bc"], key="c0")
        for kc in range(8):
            add("dve", C("tensor_tensor", out=wout[:, kc, :], in0=wout[:, kc, :], in1=g1bc, op=ALU.mult), r=["wout0", "wout1", "g1bc"], w=["wout%d" % (kc // 4)])
        py = [[banks[0], banks[1]], [banks[2], banks[3]]]
        pT2 = [banks[4].bitcast(BF16).rearrange("p (a b) -> p a b", a=8), banks[5].bitcast(BF16).rearrange("p (a b) -> p a b", a=8)]
        pl = [banks[6], banks[7]]

        def s1(tt):
            b = tt % 2
            cols = slice(tt * 128, (tt + 1) * 128)
            r0 = s * SEQ + tt * 128
            add("sp", C("dma_start", out=xt2[b], in_=x_d[r0:r0 + 128, :]), w=["xt2%d" % b], key="xt%d" % b)
            for half in range(2):
                for kc in range(8):
                    lhs = qT[:, kc, cols] if kc < 4 else grS[:, kc - 4, cols]
                    rk = "qT%d" % tt if kc < 4 else "gr%d" % (kc - 4)
                    add("pe", C("matmul", py[b][half][:, :], lhsT=lhs, rhs=wout[:, kc, half * 512:(half + 1) * 512], start=(kc == 0), stop=(kc == 7)),
                        r=[rk, "wout0", "wout1"], w=["py%d%d" % (b, half)])

        def s2(tt):
            b = tt % 2
            r0 = s * SEQ + tt * 128
            for half in range(2):
                hs = slice(half * 512, (half + 1) * 512)
                add("dve", C("tensor_tensor", out=x1[b][:, hs], in0=py[b][half][:, :], in1=xt2[b][:, hs], op=ALU.add), r=["py%d%d" % (b, half), "xt2%d" % b], w=["x1%d" % b])
            add("sp", C("dma_start", out=out_d[r0:r0 + 128, :], in_=x1[b]), r=["x1%d" % b], key="o%d" % b)
            add("act", C("activation", out=junk, in_=x1[b], func=AF.Square, accum_out=sm[b][:, 0:1]), r=["x1%d" % b], w=["junk", "sm%d" % b])

        def s3(tt):
            b = tt % 2
            add("act", C("activation", out=sm[b][:, 0:1], in_=sm[b][:, 0:1], func=AF.Ln, scale=1.0 / D, bias=EPS), r=["sm%d" % b], w=["sm%d" % b])
            add("act", C("activation", out=sm[b][:, 0:1], in_=sm[b][:, 0:1], func=AF.Exp, scale=-0.5), r=["sm%d" % b], w=["sm%d" % b])
            add("dve", C("tensor_scalar", out=xn2[b][:, 0:D], in0=x1[b], scalar1=sm[b][:, 0:1], scalar2=None, op0=ALU.mult), r=["x1%d" % b, "sm%d" % b], w=["xn2%d" % b])

        def s4(tt):
            b = tt % 2
            for fc in range(8):
                add("pe", C("transpose", out=pT2[b][:, fc, :], in_=xn2[b][:, fc * 128:(fc + 1) * 128], identity=ident), r=["xn2%d" % b, "cm"], w=["pT2%d" % b])
            for fc in range(8):
                if b == 0:
                    add("act", C("activation", out=h2T[b][:, fc, :], in_=pT2[b][:, fc, :], func=AF.Identity, scale=A2[:, fc, s:s + 1], bias=B2[:, fc, s:s + 1]), r=["pT2%d" % b, "A2", "modv"], w=["h2T%d_%d" % (b, fc)])
                else:
                    add("dve", C("tensor_scalar", out=h2T[b][:, fc, :], in0=pT2[b][:, fc, :], scalar1=A2[:, fc, s:s + 1], scalar2=B2[:, fc, s:s + 1], op0=ALU.mult, op1=ALU.add), r=["pT2%d" % b, "A2", "modv"], w=["h2T%d_%d" % (b, fc)])
            for kc in range(8):
                add("pe", C("matmul", pl[b][:, 0:16], lhsT=h2T[b][:, kc, :], rhs=wr[:, kc, :], start=(kc == 0), stop=(kc == 7)), r=["h2T%d_%d" % (b, fc) for fc in range(8)] + ["wr"], w=["pl%d" % b])

        def s5(tt):
            b = tt % 2
            r0 = s * SEQ + tt * 128
            add("dve", C("tensor_reduce", out=sm[b][:, 1:2], in_=pl[b][:, 0:16], axis=AX.X, op=ALU.max), r=["pl%d" % b], w=["mx%d" % b])
            add("dve", C("tensor_scalar", out=sm[b][:, 1:2], in0=sm[b][:, 1:2], scalar1=-1.0, scalar2=None, op0=ALU.mult), r=["mx%d" % b], w=["mx%d" % b])
            add("act", C("activation", out=ex[b], in_=pl[b][:, 0:16], func=AF.Exp, bias=sm[b][:, 1:2], accum_out=sm[b][:, 2:3]), r=["pl%d" % b, "mx%d" % b], w=["ex%d" % b, "se%d" % b])
            add("dve", C("reciprocal", out=sm[b][:, 2:3], in_=sm[b][:, 2:3]), r=["se%d" % b], w=["se%d" % b])
            add("dve", C("tensor_scalar", out=Aall[:, tt, s * 16:(s + 1) * 16], in0=ex[b], scalar1=sm[b][:, 2:3], scalar2=None, op0=ALU.mult), r=["ex%d" % b, "se%d" % b], w=["Aall"])
            add("dve", C("tensor_copy", out=xn2[b][:, D:D + 16], in_=Aall[:, tt, s * 16:(s + 1) * 16]), r=["Aall"], w=["xn2a%d" % b])
            add("dve", C("tensor_tensor", out=xn2[b][:, D + 16:1056], in0=Aall[:, tt, s * 16:(s + 1) * 16], in1=xn2[b][:, D:D + 16], op=ALU.subtract), r=["Aall", "xn2a%d" % b], w=["xn2a%d" % b])
            add("sp", C("dma_start", out=xn_d[r0:r0 + 128, :], in_=xn2[b]), r=["xn2%d" % b, "xn2a%d" % b], key="xn%d" % b)

        stages = [s1, s2, s3, s4, s5]
        NT = 32
        for step in range(NT + len(stages) - 1):
            for si in reversed(range(len(stages))):
                tt = step - si
                if 0 <= tt < NT:
                    stages[si](tt)

    def phase_topk():
        WK.reset()
        ST.reset()
        AT = ST.alloc([SEQ], F32)
        ones = ST.alloc([SEQ], F32)
        Mk = ST.alloc([SEQ], F32)
        cs = ST.alloc([SEQ], F32)
        bs = WK.alloc([8], F32)
        for tt in range(32):
            pb = banks[tt % 2]
            add("pe", C("transpose", out=pb[0:32, 0:128], in_=Aall[:, tt, :], identity=identf[:]), r=["Aall", "identf"], w=["pAT%d" % (tt % 2)])
            add("act", C("copy", out=AT[0:32, tt * 128:(tt + 1) * 128], in_=pb[0:32, 0:128]), r=["pAT%d" % (tt % 2)], w=["AT"])
        add("pool", C("memset", ones[0:32, :], 1.0), w=["ones"])
        mid, cntc, stp = bs[0:32, 0:1], bs[0:32, 1:2], bs[0:32, 2:3]
        add("dve", C("memset", mid, 0.5), w=["mid"])
        NIT = 20
        for k in range(NIT):
            wk = 2.0 ** -(k + 1)
            wn = 2.0 ** -(k + 2)
            add("dve", C("tensor_scalar", out=Mk[0:32, :], in0=AT[0:32, :], scalar1=mid, scalar2=None, op0=ALU.is_ge, op1=ALU.add, accum_out=cntc), r=["AT", "mid"], w=["Mk", "cnt"])
            add("dve", C("tensor_scalar", out=stp, in0=cntc, scalar1=float(CAP), scalar2=wk, op0=ALU.is_ge, op1=ALU.mult), r=["cnt"], w=["stp"])
            last = (k == NIT - 1)
            delta = (-wk) if last else (wn - wk)
            add("dve", C("scalar_tensor_tensor", out=mid, in0=stp, scalar=delta, in1=mid, op0=ALU.add, op1=ALU.add), r=["stp", "mid"], w=["mid"])
        add("dve", C("tensor_scalar", out=Mk[0:32, :], in0=AT[0:32, :], scalar1=mid, scalar2=None, op0=ALU.is_ge), r=["AT", "mid"], w=["Mk"])
        add("dve", C("tensor_tensor_scan", out=cs[0:32, :], data0=ones[0:32, :], data1=Mk[0:32, :], initial=0.0, op0=ALU.mult, op1=ALU.add), r=["ones", "Mk"], w=["cs"])
        add("dve", C("tensor_tensor", out=cs[0:32, :], in0=cs[0:32, :], in1=Mk[0:32, :], op=ALU.mult), r=["cs", "Mk"], w=["cs"])
        for thr in (129.0, 257.0, 385.0):
            add("dve", C("scalar_tensor_tensor", out=Mk[0:32, :], in0=cs[0:32, :], scalar=thr, in1=Mk[0:32, :], op0=ALU.is_ge, op1=ALU.add), r=["cs", "Mk"], w=["Mk"])
        add("dve", C("scalar_tensor_tensor", out=cs[0:32, :], in0=Mk[0:32, :], scalar=-128.0, in1=cs[0:32, :], op0=ALU.mult, op1=ALU.add), r=["Mk", "cs"], w=["cs"])
        PH = WK.alloc([32, 32], BF16)
        PL = WK.alloc([32, 32], BF16)
        Lb = [WK.alloc([32, 128], BF16) for _ in range(2)]
        Ht = [WK.alloc([32, 4, 2], BF16) for _ in range(2)]
        eq = [WK.alloc([32, 4], F32) for _ in range(2)]
        pph = [banks[2], banks[3]]
        for tt in range(32):
            pb = pph[tt % 2]
            add("pe", C("transpose", out=pb[:, 0:32], in_=Mk[0:32, tt * 128:(tt + 1) * 128], identity=identf[0:32, 0:32]), r=["Mk", "identf"], w=["pph%d" % (tt % 2)])
            add("pe", C("transpose", out=pb[:, 32:64], in_=cs[0:32, tt * 128:(tt + 1) * 128], identity=identf[0:32, 0:32]), r=["cs", "identf"], w=["pph%d" % (tt % 2)])
            add("act", C("copy", out=PH[:, tt, :], in_=pb[:, 0:32]), r=["pph%d" % (tt % 2)], w=["PH%d" % tt])
            add("act", C("copy", out=PL[:, tt, :], in_=pb[:, 32:64]), r=["pph%d" % (tt % 2)], w=["PL%d" % tt])
        pIdx = banks[4][:, 0:256].rearrange("p (a c k) -> p a c k", a=32, c=4)
        iop = cm[:, CM_IOP, :]
        ioc = cm[:, CM_IOC, 0:4]
        for tt in range(32):
            b = tt % 2
            add("dve", C("tensor_tensor", out=Lb[b], in0=iop.unsqueeze(1).to_broadcast([128, 32, 128]), in1=PL[:, tt, :].unsqueeze(2).to_broadcast([128, 32, 128]), op=ALU.is_equal), r=["PL%d" % tt, "cm"], w=["Lb%d" % b])
            add("dve", C("tensor_tensor", out=eq[b], in0=ioc.unsqueeze(1).to_broadcast([128, 32, 4]), in1=PH[:, tt, :].unsqueeze(2).to_broadcast([128, 32, 4]), op=ALU.is_equal), r=["PH%d" % tt, "cm"], w=["eq%d" % b])
            for k2 in range(2):
                add("dve", C("tensor_scalar", out=Ht[b][:, :, :, k2], in0=eq[b], scalar1=tcol[:, tt, k2:k2 + 1], scalar2=None, op0=ALU.mult), r=["eq%d" % b, "tcol"], w=["Ht%d_%d" % (b, k2)])
            for se in range(32):
                first = (tt == 0 and se == 0)
                add("pe", C("matmul", pIdx[:, se, :, :].rearrange("p c k -> p (c k)"), lhsT=Lb[b][:, se, :], rhs=Ht[b][:, se, :, :].rearrange("p c k -> p (c k)"), start=first, stop=(tt == 31), skip_group_check=True),
                    r=["Lb%d" % b, "Ht%d_0" % b, "Ht%d_1" % b], w=["pIdx"])
        idf = WK.alloc([32, 4], F32)
        add("dve", C("tensor_copy", out=idf, in_=pIdx[:, :, :, 1]), r=["pIdx"], w=["idf"])
        add("dve", C("scalar_tensor_tensor", out=idf, in0=pIdx[:, :, :, 0], scalar=64.0, in1=idf, op0=ALU.mult, op1=ALU.add), r=["pIdx", "idf"], w=["idf"])
        add("dve", C("tensor_scalar", out=idf[:, 16:32, :], in0=idf[:, 16:32, :], scalar1=float(SEQ), scalar2=None, op0=ALU.add), r=["idf"], w=["idf"])
        add("dve", C("tensor_copy", out=IDX[:], in_=idf), r=["idf"], w=["IDX"])
        if "idx" in dbg:
            add("sp", C("dma_start", out=dbg["idx"][:, :, :], in_=IDX[:]), r=["IDX"], key="dbg1")

    def phase_moe():
        ST.reset()
        WK.reset()
        wts = [[ST.alloc([8, D], BF16) for _ in range(3)] for _ in range(2)]
        g2bc = [ST.alloc([D], F32) for _ in range(2)]
        xgs = [[WK.alloc([1056], BF16) for _ in range(8)] for _ in range(2)]
        xsT = WK.alloc([8, 1024], BF16)
        hidT = WK.alloc([8, 1024], BF16)
        sg = [WK.alloc([512], BF16) for _ in range(2)]
        yo = [WK.alloc([D], F32) for _ in range(2)]
        gts = [WK.alloc([8], F32) for _ in range(4)]
        for s in range(2):
            add("sp", C("dma_start", out=g2bc[s], in_=gsc_d[s, 1, :].partition_broadcast(128)), w=["g2bc%d" % s], key="c%d" % s)
        pX = [banks[0].bitcast(BF16).rearrange("p (a b) -> p a b", a=4), banks[1].bitcast(BF16).rearrange("p (a b) -> p a b", a=4)]
        pG = [banks[2], banks[3]]
        pU = [banks[4], banks[5]]
        pY = [banks[6], banks[7]]
        srcs = [wg_d, wu_d, wd_d]
        ctr = dict(y=0, gu=0)

        def load_w(e_):
            wb = e_ % 2
            for m in range(3):
                v = srcs[m][e_].rearrange("(kc p) n -> p kc n", p=128)
                for h in range(2):
                    add("pool", C("dma_start", out=wts[wb][m][:, h * 4:(h + 1) * 4, :], in_=v[:, h * 4:(h + 1) * 4, :]), w=["w%d_%d_%d" % (wb, m, h)], key="w%d%d" % (m, h))

        def gather(e_):
            gb_ = e_ % 2
            for s in range(2):
                for c in range(4):
                    st_ = s * 4 + c
                    add("pool", C("indirect_dma_start", out=xgs[gb_][st_], out_offset=None, in_=xn_d[:, :], in_offset=bass.IndirectOffsetOnAxis(ap=IDX[:, s * 16 + e_, c:c + 1], axis=0)),
                        r=["IDX"], w=["xg%d_%d" % (gb_, st_)], key="g%d_%d" % (gb_, st_))

        def xpose_unit(e2, u):
            pair, hf = u // 2, u % 2
            s = pair // 2
            pb = u % 2
            xg2 = xgs[e2 % 2]
            for st2 in range(2):
                st_ = pair * 2 + st2
                for f4 in range(4):
                    fc = hf * 4 + f4
                    add("pe", C("transpose", out=pX[pb][:, f4, st2 * 128:(st2 + 1) * 128], in_=xg2[st_][:, fc * 128:(fc + 1) * 128], identity=ident),
                        r=["xg%d_%d" % (e2 % 2, st_), "cm"], w=["pX%d" % pb])
            for f4 in range(4):
                fc = hf * 4 + f4
                dst = xsT[:, fc, pair * 256:(pair + 1) * 256]
                if pb == 0:
                    add("act", C("activation", out=dst, in_=pX[pb][:, f4, :], func=AF.Identity, scale=A2[:, fc, s:s + 1], bias=B2[:, fc, s:s + 1]), r=["pX%d" % pb, "A2", "modv"], w=["xsT%d" % pair])
                else:
                    add("dve", C("tensor_scalar", out=dst, in0=pX[pb][:, f4, :], scalar1=A2[:, fc, s:s + 1], scalar2=B2[:, fc, s:s + 1], op0=ALU.mult, op1=ALU.add), r=["pX%d" % pb, "A2", "modv"], w=["xsT%d" % pair])

        gather(0)
        load_w(0)
        for u in range(8):
            xpose_unit(0, u)
        for e_ in range(NEXP):
            wb = e_ % 2
            wgt, wut, wdt = wts[wb]
            xg = xgs[e_ % 2]
            xgk = lambda st_: "xg%d_%d" % (e_ % 2, st_)
            if e_ + 1 < NEXP:
                gather(e_ + 1)
                load_w(e_ + 1)
            for f in range(8):
                for half in range(2):
                    gb = ctr["gu"] % 2
                    ctr["gu"] += 1
                    xk = ["xsT%d" % (half * 2), "xsT%d" % (half * 2 + 1)]
                    for kc in range(8):
                        add("pe", C("matmul", pG[gb][:, :], lhsT=wgt[:, kc, f * 128:(f + 1) * 128], rhs=xsT[:, kc, half * 512:(half + 1) * 512], start=(kc == 0), stop=(kc == 7)),
                            r=xk + ["w%d_0_0" % wb, "w%d_0_1" % wb], w=["pG%d" % gb])
                    for kc in range(8):
                        add("pe", C("matmul", pU[gb][:, :], lhsT=wut[:, kc, f * 128:(f + 1) * 128], rhs=xsT[:, kc, half * 512:(half + 1) * 512], start=(kc == 0), stop=(kc == 7)),
                            r=xk + ["w%d_1_0" % wb, "w%d_1_1" % wb], w=["pU%d" % gb])
                    add("act", C("activation", out=sg[gb], in_=pG[gb][:, :], func=AF.Silu), r=["pG%d" % gb], w=["sg%d" % gb])
                    add("dve", C("tensor_tensor", out=hidT[:, f, half * 512:(half + 1) * 512], in0=pU[gb][:, :], in1=sg[gb], op=ALU.mult), r=["pU%d" % gb, "sg%d" % gb], w=["hid%d" % half])
            for st_ in range(8):
                s, c = st_ // 4, st_ % 4
                for half in range(2):
                    for fk in range(8):
                        add("pe", C("matmul", pY[half][:, :], lhsT=hidT[:, fk, st_ * 128:(st_ + 1) * 128], rhs=wdt[:, fk, half * 512:(half + 1) * 512], start=(fk == 0), stop=(fk == 7)),
                            r=["hid%d" % s, "w%d_2_0" % wb, "w%d_2_1" % wb], w=["pY%d" % half])
                yb = ctr["y"] % 2
                ctr["y"] += 1
                gate = gts[ctr["y"] % 4][:, 0:1]
                gk_ = "gt%d" % (ctr["y"] % 4)
                add("dve", C("tensor_tensor", out=gate, in0=xg[st_][:, D + e_:D + e_ + 1], in1=xg[st_][:, D + 16 + e_:D + 16 + e_ + 1], op=ALU.add), r=[xgk(st_)], w=[gk_])
                for half in range(2):
                    hs = slice(half * 512, (half + 1) * 512)
                    add("dve", C("scalar_tensor_tensor", out=yo[yb][:, hs], in0=pY[half][:, :], scalar=gate, in1=g2bc[s][:, hs], op0=ALU.mult, op1=ALU.mult),
                        r=["pY%d" % half, gk_, "g2bc%d" % s], w=["yo%d_%d" % (yb, half)])
                prev = ["sc_%d_%d_%d" % (s, e_ - 1, cc) for cc in range(4)] if e_ > 0 else []
                add("pool", C("indirect_dma_start", out=out_d[:, :], out_offset=bass.IndirectOffsetOnAxis(ap=IDX[:, s * 16 + e_, c:c + 1], axis=0), in_=yo[yb], in_offset=None, compute_op=ALU.add, oob_is_err=True),
                    r=["yo%d_0" % yb, "yo%d_1" % yb, "IDX"] + prev, w=["sc_%d_%d_%d" % (s, e_, c)], key="sc%d" % st_)
                if e_ + 1 < NEXP:
                    xpose_unit(e_ + 1, st_)

    nsamp = 2
    stages = ["A", "B", "C", "D", "T", "full"]
    lvl = stages.index(stage) if stage in stages else -1
    if stage == "0":
        nsamp = 0
    if stage == "A1":
        nsamp = 1
    import os as _os
    if _os.environ.get("NSAMP"):
        nsamp = int(_os.environ["NSAMP"])
    for s in range(nsamp):
        phase_A(s)
        P.barrier()
        if stage == "A" and s == 0:
            add("sp", C("dma_start", out=dbg["qT"][:, :, :], in_=qT), key="dbg0")
            add("sp", C("dma_start", out=dbg["kT"][:, :], in_=kT), key="dbg1")
            add("sp", C("dma_start", out=dbg["v"][:, :, :, :], in_=vaug), key="dbg0")
            add("sp", C("dma_start", out=dbg["xr"][:, :, :], in_=xrS), key="dbg1")
            add("sp", C("dma_start", out=dbg["gr"][:, :, :], in_=grS), key="dbg0")
            P.barrier()
        if lvl >= 1:
            phase_B(s)
            P.barrier()
            if stage == "B" and s == 0:
                add("sp", C("dma_start", out=dbg["qT"][:, :, :], in_=qT), key="dbg0")
                P.barrier()
        if lvl >= 2:
            phase_C(s)
            P.barrier()
            if stage == "C" and s == 0:
                add("sp", C("dma_start", out=dbg["gr"][:, :, :], in_=grS), key="dbg0")
                add("sp", C("dma_start", out=dbg["qT"][:, :, :], in_=qT), key="dbg1")
                P.barrier()
        if lvl >= 3:
            phase_D(s)
            P.barrier()
    if lvl >= 4:
        phase_topk()
        P.barrier()
        if "aall" in dbg:
            add("sp", C("dma_start", out=dbg["aall"][:, :, :], in_=Aall[:]), key="dbg0")
    if lvl >= 5:
        phase_moe()
    P.barrier()
    P.emit()
    P.close()
    return nc, P.stats


def _swap_idx():
    d = np.arange(64)
    return np.where((d % 32) < 16, d + 16, d - 16)


def _consts():
    bf = ml_dtypes.bfloat16
    cm = np.zeros((128, NCM, 128), np.float32)
    cm[:, CM_ID, :] = np.eye(128)
    for h in range(2):
        cm[h * 64:(h + 1) * 64, CM_ONES, h * 64:(h + 1) * 64] = 1.0
    sw = _swap_idx()
    for m in range(128):
        k = (m // 64) * 64 + sw[m % 64]
        cm[k, CM_SWAP, m] = 1.0
    cm[:, CM_IOP, :] = (np.arange(128) - 127)[None, :]
    cm[:, CM_IOC, 0:4] = np.arange(1, 5)[None, :]
    identf = np.eye(128, dtype=np.float32)
    a = np.arange(128)
    prev = (a[None, :] <= a[:, None]).astype(np.float32)
    nxt = (a[:, None] <= a[None, :]).astype(np.float32)
    masks = np.stack([np.tile(prev, (1, 4)), np.tile(nxt, (1, 4))], axis=1)
    t = np.arange(SEQ)
    row = (t // 64).astype(np.float32)
    col = (t % 64).astype(np.float32)
    inv = (10000.0 ** (-np.arange(16, dtype=np.float32) / 16)).astype(np.float32)
    ar = (row[None, :] * inv[:, None]).astype(np.float32)
    ac = (col[None, :] * inv[:, None]).astype(np.float32)
    C = np.zeros((64, SEQ), np.float32)
    S2 = np.zeros((64, SEQ), np.float32)
    C[0:16] = np.cos(ar); C[16:32] = np.cos(ar); C[32:48] = np.cos(ac); C[48:64] = np.cos(ac)
    S2[0:16] = np.sin(ar); S2[16:32] = -np.sin(ar); S2[32:48] = np.sin(ac); S2[48:64] = -np.sin(ac)
    rope = np.stack([np.tile(C, (2, 1)), np.tile(S2, (2, 1))], axis=1)
    tok = np.arange(32)[None, :] * 128 + np.arange(128)[:, None]
    tcol = np.stack([tok // 64, tok % 64], axis=2).astype(np.float32)
    return dict(cm=cm.astype(bf), identf=identf, masks=masks.astype(bf), rope=rope.astype(bf), tcol=tcol)


def prep_inputs(inp):
    f = np.float32
    g = lambda k: np.asarray(inp[k], dtype=f)
    x, c, ctx, c_ctx = g("x"), g("c"), g("ctx"), g("c_ctx")
    L = 0
    w_in = g("w_in")[L]
    qperm = np.concatenate([np.r_[b * 64:(b + 1) * 64, (b + 4) * 64:(b + 5) * 64] for b in range(4)])
    w_in_p = np.ascontiguousarray(np.concatenate([w_in[:, qperm], w_in[:, 512:]], axis=1))
    w_out = g("w_out")[L]
    w_out_p = np.ascontiguousarray(np.concatenate([w_out[qperm, :], w_out[512:, :]], axis=0))
    b_ada = g("b_ada")[L]
    vecs = np.zeros((128, NV), f)
    pc = lambda v, n: v.reshape(n, 128).T
    vecs[:, V_N1G:V_N1G + 8] = pc(g("norm1_g")[L], 8)
    vecs[:, V_N2G:V_N2G + 8] = pc(g("norm2_g")[L], 8)
    vecs[:, V_BADA:V_BADA + 48] = pc(b_ada, 48)
    gq, gk = g("q_norm_g")[L], g("k_norm_g")[L]
    vecs[:, V_GQ] = np.tile(gq, 2)
    vecs[:, V_GK] = np.tile(gk, 2)
    cw = g("conv_w")[L]
    for cch in range(4):
        for j in range(4):
            vecs[:, V_CONVW + cch * 4 + j] = cw[j, cch * 128:(cch + 1) * 128]
    vecs[:, V_CONVB:V_CONVB + 4] = pc(g("conv_b")[L], 4)
    for d in range(2):
        vecs[:, V_BR + d * 4:V_BR + d * 4 + 4] = pc(g("lru_b_r")[L][d], 4)
        vecs[:, V_BI + d * 4:V_BI + d * 4 + 4] = pc(g("lru_b_i")[L][d], 4)
        vecs[:, V_LAM + d * 4:V_LAM + d * 4 + 4] = pc(g("lru_lambda")[L][d], 4)
    vecs[:, V_SINK:V_SINK + 8] = g("attn_sink")[L][None, :]
    vecs[:, V_GQROW:V_GQROW + 64] = gq[None, :]
    vecs[:, V_GKROW:V_GKROW + 64] = gk[None, :]
    lruw = np.zeros((128, 16, 128), f)
    for d in range(2):
        for gi, nm in enumerate(["lru_w_r", "lru_w_i"]):
            w = g(nm)[L][d]
            for cch in range(4):
                for nl in range(2):
                    lruw[nl * 64:(nl + 1) * 64, (d * 2 + gi) * 4 + cch, nl * 64:(nl + 1) * 64] = w[cch * 2 + nl]
    badag = np.stack([np.stack([b_ada[2 * D:3 * D], b_ada[5 * D:6 * D]])] * 3)
    wrt = np.ascontiguousarray(g("w_router")[L].reshape(8, 128, 16).transpose(1, 0, 2))
    consts = _consts()
    shared = dict(w_ada=g("w_ada")[L], badag=badag, vecs=vecs, w_in=w_in_p, lruw=lruw, w_out=w_out_p, w_router=wrt,
                  w_gate=g("w_gate")[L], w_up=g("w_up")[L], w_down=g("w_down")[L], **consts)
    maps = []
    for core in range(NCORES):
        s0 = 2 * core
        cc = np.stack([c[s0], c[s0 + 1], c_ctx])
        ccT = np.ascontiguousarray(cc.reshape(3, 8, 128).transpose(2, 1, 0))
        m = dict(shared)
        m["x"] = np.ascontiguousarray(x[s0:s0 + 2].reshape(2 * SEQ, D))
        m["ctx"] = np.ascontiguousarray(ctx[s0:s0 + 2].reshape(2 * NCTX, D))
        m["ccT"] = ccT
        maps.append(m)
    return maps


_CACHE = {}


def kernel(**inputs):
    maps = prep_inputs(inputs)
    if "nc" not in _CACHE:
        _CACHE["nc"] = build_program("full")[0]
    nc = _CACHE["nc"]
    res = run_bass_kernel_spmd(nc, maps, core_ids=list(range(NCORES)))
    outs = [np.asarray(r["out"]).reshape(2, SEQ, D) for r in res.results]
    return np.concatenate(outs, axis=0).astype(np.float32)
```
